# Optimizing a Trainium2 kernel written in Bass

```python
import jax, jax.numpy as jnp
from jax import lax
import numpy as np

D_MODEL = 2048
BATCH = 4
SEQ = 2048
DEPTH = 2
DEC_BATCH = 128
DEC_SEQ = 1
PAST_LEN = 16384
PAGE_SIZE = 128

BRANCH_W = D_MODEL // 2
N_BRANCH = 3
R_HEAD = 64
R_HEADS = BRANCH_W // R_HEAD
R_LORA_W = 64
R_LORA_A = 64
R_LORA_G = 64
R_GN_EPS = 64e-5
R_COLS = 3 * BRANCH_W + R_LORA_W + R_LORA_A + R_LORA_G
R_SPLITS = [BRANCH_W, 2 * BRANCH_W, 3 * BRANCH_W, 3 * BRANCH_W + R_LORA_W, 3 * BRANCH_W + R_LORA_W + R_LORA_A]
M_HEADS = 4
M_DK = BRANCH_W // M_HEADS
M_DV = BRANCH_W // M_HEADS
CONV_W = 4
M_COLS = 4 * BRANCH_W + 2 * M_HEADS
M_SPLITS = [2 * BRANCH_W, 3 * BRANCH_W, 3 * BRANCH_W + M_HEADS, 3 * BRANCH_W + 2 * M_HEADS]
G_HEADS = 4
G_DK = BRANCH_W // (2 * G_HEADS)
G_DV = BRANCH_W // G_HEADS
G_LR = 16
G_GATE_NORM = 16.0
G_COLS = 2 * G_HEADS * G_DK + 2 * BRANCH_W + G_LR
G_SPLITS = [G_HEADS * G_DK, 2 * G_HEADS * G_DK, 2 * G_HEADS * G_DK + BRANCH_W, 2 * G_HEADS * G_DK + BRANCH_W + G_LR]
GATE_COLS = N_BRANCH * D_MODEL
N_IN = R_COLS + M_COLS + G_COLS + GATE_COLS
IN_SPLITS = [R_COLS, R_COLS + M_COLS, R_COLS + M_COLS + G_COLS]
D_FF = -((-8 * D_MODEL) // (3 * 256)) * 256
CHUNK = 64
EPS = 1e-6

kernel_name = 'hybrid_rwkv7_mlstm_gla_step'


def rmsnorm(x, g):
    xf = x.astype(jnp.float32)
    y = xf * lax.rsqrt(jnp.mean(xf * xf, axis=-1, keepdims=True) + EPS)
    return (y * g).astype(x.dtype)


def _heads(t, n_heads):
    return t.reshape(*t.shape[:-1], n_heads, -1)


def head_rmsnorm(h, g):
    h = h * lax.rsqrt(jnp.mean(h * h, axis=-1, keepdims=True) + EPS)
    return h.reshape(*h.shape[:-2], -1) * g


def _chunk_len(T):
    return CHUNK if T % CHUNK == 0 else T


def _to_chunks(t, L):
    B, H, T = t.shape[:3]
    t = t.reshape(B, H, T // L, L, *t.shape[3:])
    return jnp.moveaxis(t, 2, 0)


def _from_chunks(t):
    t = jnp.moveaxis(t, 0, 2)
    return t.reshape(t.shape[0], t.shape[1], -1, *t.shape[4:])


def rwkv7_branch(p, prev, S0, mu, w0, w2, a0, a2, g2, k_k, k_a, r_k, ln_g, ln_b):
    B, T, _ = p.shape
    f32 = jnp.float32
    p_prev = jnp.concatenate([prev.astype(p.dtype)[:, None, :], p[:, :-1, :]], axis=1)
    xs = p + (p_prev - p) * mu
    r, k, v, xw, xa, xg = jnp.split(xs, R_SPLITS, axis=-1)
    w = -jax.nn.softplus(-(w0 + jnp.tanh(xw) @ w2)) - 0.5
    log_decay = -jnp.exp(w.astype(f32))
    a = jax.nn.sigmoid(a0 + xa @ a2)
    g = jax.nn.sigmoid(xg) @ g2
    kk = _heads((k * k_k).astype(f32), R_HEADS)
    kk = kk / jnp.maximum(jnp.linalg.norm(kk, axis=-1, keepdims=True), 1e-12)
    k = k * (1.0 + (a - 1.0) * k_a)
    rh, kh, vh, ah, ldh = [_heads(t.astype(f32), R_HEADS) for t in (r, k, v, a, log_decay)]

    def step(S, inp):
        r_t, k_t, v_t, kk_t, a_t, ld_t = inp
        S = (S * jnp.exp(ld_t)[:, :, None, :]
             - jnp.einsum('bhvk,bhk->bhv', S, kk_t)[..., None] * (kk_t * a_t)[:, :, None, :]
             + v_t[..., :, None] * k_t[..., None, :])
        return S, jnp.einsum('bhvk,bhk->bhv', S, r_t)

    seq = tuple(jnp.moveaxis(t, 1, 0) for t in (rh, kh, vh, kk, ah, ldh))
    S_T, out = lax.scan(step, S0.astype(f32), seq)
    out = jnp.moveaxis(out, 0, 1)
    mean = jnp.mean(out, axis=-1, keepdims=True)
    var = jnp.mean(jnp.square(out - mean), axis=-1, keepdims=True)
    out = ((out - mean) * lax.rsqrt(var + R_GN_EPS)).reshape(B, T, BRANCH_W) * ln_g + ln_b
    bonus = jnp.sum(rh * kh * r_k, axis=-1, keepdims=True) * vh
    out = (out + bonus.reshape(B, T, BRANCH_W)) * g
    return out.astype(p.dtype), S_T, p[:, -1, :]


def mlstm_chunks(q, k, v, ig, lf, C0, n0, m0):
    L = _chunk_len(q.shape[2])
    mask = jnp.tril(jnp.ones((L, L), dtype=bool))

    def step(carry, inp):
        C, n, m = carry
        qc, kc, vc, ic, fc = inp
        F = jnp.cumsum(fc, axis=-1)
        D = jnp.where(mask, F[..., :, None] - F[..., None, :] + ic[..., None, :], -jnp.inf)
        inter = F + m[..., None]
        m_t = jnp.maximum(inter, jnp.max(D, axis=-1))
        w_inter = jnp.exp(inter - m_t)
        S = jnp.einsum('bhtd,bhsd->bhts', qc, kc) * jnp.exp(D - m_t[..., None])
        num = w_inter[..., None] * jnp.einsum('bhtd,bhde->bhte', qc, C) + jnp.einsum('bhts,bhse->bhte', S, vc)
        den = w_inter * jnp.einsum('bhtd,bhd->bht', qc, n) + jnp.sum(S, axis=-1)
        h = num / jnp.maximum(jnp.abs(den), jnp.exp(-m_t))[..., None]
        FL = F[..., -1]
        g_s = FL[..., None] - F + ic
        m_new = jnp.maximum(FL + m, jnp.max(g_s, axis=-1))
        a_c = jnp.exp(FL + m - m_new)
        w_s = jnp.exp(g_s - m_new[..., None])
        C = a_c[..., None, None] * C + jnp.einsum('bhs,bhsd,bhse->bhde', w_s, kc, vc)
        n = a_c[..., None] * n + jnp.einsum('bhs,bhsd->bhd', w_s, kc)
        return (C, n, m_new), h

    f32 = jnp.float32
    (C, n, m), h = lax.scan(step, (C0.astype(f32), n0.astype(f32), m0.astype(f32)),
                            tuple(_to_chunks(t, L) for t in (q, k, v, ig, lf)))
    return _from_chunks(h), C, n, m


def mlstm_branch(p, conv_buf, C0, n0, m0, conv_w, conv_b, i_b, f_b, norm_g):
    B, T, _ = p.shape
    f32 = jnp.float32
    qk_raw, v, i_pre, f_pre, o = jnp.split(p, M_SPLITS, axis=-1)
    full = jnp.concatenate([conv_buf.astype(p.dtype), qk_raw], axis=1)
    conv = conv_b + sum(full[:, j:j + T, :] * conv_w[j] for j in range(CONV_W))
    q, k = jnp.split(jax.nn.silu(conv), 2, axis=-1)
    qh = _heads(q.astype(f32), M_HEADS).transpose(0, 2, 1, 3)
    kh = _heads(k.astype(f32), M_HEADS).transpose(0, 2, 1, 3) * (M_DK ** -0.5)
    vh = _heads(v.astype(f32), M_HEADS).transpose(0, 2, 1, 3)
    ig = (i_pre + i_b).astype(f32).transpose(0, 2, 1)
    lf = jax.nn.log_sigmoid((f_pre + f_b).astype(f32)).transpose(0, 2, 1)
    h, C, n, m = mlstm_chunks(qh, kh, vh, ig, lf, C0, n0, m0)
    h = head_rmsnorm(h.transpose(0, 2, 1, 3), norm_g)
    out = jax.nn.sigmoid(o) * h
    return out.astype(p.dtype), full[:, -(CONV_W - 1):, :], C, n, m


def gla_branch(p, S0, a2, a_b, norm_g):
    B, T, _ = p.shape
    f32 = jnp.float32
    q, k, v, xa, og = jnp.split(p, G_SPLITS, axis=-1)
    lg = jax.nn.log_sigmoid((xa @ a2 + a_b).astype(f32)) / G_GATE_NORM
    qh = _heads(q.astype(f32), G_HEADS).transpose(0, 2, 1, 3) * (G_DK ** -0.5)
    kh = _heads(k.astype(f32), G_HEADS).transpose(0, 2, 1, 3)
    vh = _heads(v.astype(f32), G_HEADS).transpose(0, 2, 1, 3)
    gh = _heads(lg, G_HEADS).transpose(0, 2, 1, 3)
    L = _chunk_len(T)
    mask = jnp.tril(jnp.ones((L, L), dtype=bool))[..., None]

    def step(S, inp):
        qc, kc, vc, gc = inp
        b = jnp.cumsum(gc, axis=2)
        decay = jnp.exp(jnp.where(mask, b[:, :, :, None, :] - b[:, :, None, :, :], -jnp.inf))
        A = jnp.einsum('bhtd,bhsd,bhtsd->bhts', qc, kc, decay)
        o = jnp.einsum('bhtd,bhde->bhte', qc * jnp.exp(b), S) + jnp.einsum('bhts,bhse->bhte', A, vc)
        bL = b[:, :, -1:, :]
        S = jnp.exp(bL[:, :, 0, :])[..., None] * S + jnp.einsum('bhsd,bhse->bhde', kc * jnp.exp(bL - b), vc)
        return S, o

    S_T, o = lax.scan(step, S0.astype(f32), tuple(_to_chunks(t, L) for t in (qh, kh, vh, gh)))
    o = _from_chunks(o).transpose(0, 2, 1, 3)
    out = head_rmsnorm(o, norm_g) * jax.nn.silu(og)
    return out.astype(p.dtype), S_T


def decoder_layer(x, rw_prev, rw_S, m_conv, m_C, m_n, m_m, g_S,
                  norm1_g, w_in, gate_b, rwkv_mu, rwkv_w0, rwkv_w2, rwkv_a0, rwkv_a2, rwkv_g2,
                  rwkv_k_k, rwkv_k_a, rwkv_r_k, rwkv_ln_g, rwkv_ln_b,
                  mlstm_conv_w, mlstm_conv_b, mlstm_i_b, mlstm_f_b, mlstm_norm_g,
                  gla_a2, gla_a_b, gla_norm_g, w_branch, w_out, norm2_g, ffn_w_gu, ffn_w_down):
    B, T, _ = x.shape
    h = rmsnorm(x, norm1_g)
    proj = h @ w_in
    p_r, p_m, p_g, p_gate = jnp.split(proj, IN_SPLITS, axis=-1)
    o_r, rw_S, rw_prev = rwkv7_branch(p_r, rw_prev, rw_S, rwkv_mu, rwkv_w0, rwkv_w2, rwkv_a0, rwkv_a2,
                                      rwkv_g2, rwkv_k_k, rwkv_k_a, rwkv_r_k, rwkv_ln_g, rwkv_ln_b)
    o_m, m_conv, m_C, m_n, m_m = mlstm_branch(p_m, m_conv, m_C, m_n, m_m, mlstm_conv_w, mlstm_conv_b,
                                              mlstm_i_b, mlstm_f_b, mlstm_norm_g)
    o_g, g_S = gla_branch(p_g, g_S, gla_a2, gla_a_b, gla_norm_g)
    branches = jnp.stack([o_r, o_m, o_g], axis=2)
    gates = jax.nn.sigmoid(p_gate.reshape(B, T, N_BRANCH, D_MODEL) + gate_b)
    widened = jnp.einsum('btjc,jcd->btjd', branches, w_branch)
    x = x + jnp.sum(gates * widened, axis=2) @ w_out
    gg, uu = jnp.split(rmsnorm(x, norm2_g) @ ffn_w_gu, 2, axis=-1)
    x = x + (jax.nn.silu(gg) * uu) @ ffn_w_down
    return x, (rw_prev, rw_S, m_conv, m_C, m_n, m_m, g_S)


def run_trunk(x, states, layer_params, final_norm_g):
    per_layer = []
    for l in range(DEPTH):
        x, new = decoder_layer(x, *[s[l] for s in states], *[w[l] for w in layer_params])
        per_layer.append(new)
    new_states = [jnp.stack([st[i] for st in per_layer], axis=0).astype(x.dtype) for i in range(len(states))]
    return rmsnorm(x, final_norm_g), new_states


def zero_states(batch, dtype):
    return (jnp.zeros((DEPTH, batch, R_COLS), dtype),
            jnp.zeros((DEPTH, batch, R_HEADS, R_HEAD, R_HEAD), dtype),
            jnp.zeros((DEPTH, batch, CONV_W - 1, 2 * BRANCH_W), dtype),
            jnp.zeros((DEPTH, batch, M_HEADS, M_DK, M_DV), dtype),
            jnp.zeros((DEPTH, batch, M_HEADS, M_DK), dtype),
            jnp.zeros((DEPTH, batch, M_HEADS), dtype),
            jnp.zeros((DEPTH, batch, G_HEADS, G_DK, G_DV), dtype))


def setup_inputs(seed: int = 0) -> dict:
    key = jax.random.key(seed)
    ks = iter(jax.random.split(key, 64))
    f32 = jnp.float32
    L = DEPTH

    def nrm(shape, scale):
        return scale * jax.random.normal(next(ks), shape, f32)

    def unif(shape):
        return jax.random.uniform(next(ks), shape, f32)

    return {
        'x_prompt': nrm((BATCH, SEQ, D_MODEL), 1.0),
        'x_sample': nrm((DEC_BATCH, DEC_SEQ, D_MODEL), 1.0),
        'state_rwkv_shift': nrm((L, DEC_BATCH, R_COLS), 1.0),
        'state_rwkv_wkv': nrm((L, DEC_BATCH, R_HEADS, R_HEAD, R_HEAD), 0.5),
        'state_mlstm_conv': nrm((L, DEC_BATCH, CONV_W - 1, 2 * BRANCH_W), 1.0),
        'state_mlstm_C': nrm((L, DEC_BATCH, M_HEADS, M_DK, M_DV), 0.5),
        'state_mlstm_n': nrm((L, DEC_BATCH, M_HEADS, M_DK), 0.5),
        'state_mlstm_m': nrm((L, DEC_BATCH, M_HEADS), 1.0),
        'state_gla_S': nrm((L, DEC_BATCH, G_HEADS, G_DK, G_DV), 0.5),
        'norm1_g': 1.0 + nrm((L, D_MODEL), 0.02),
        'w_in': nrm((L, D_MODEL, N_IN), D_MODEL ** -0.5),
        'gate_b': nrm((L, N_BRANCH, D_MODEL), 0.01),
        'rwkv_mu': unif((L, R_COLS)),
        'rwkv_w0': jnp.linspace(-6.5, -1.5, BRANCH_W, dtype=f32) + nrm((L, BRANCH_W), 0.1),
        'rwkv_w2': nrm((L, R_LORA_W, BRANCH_W), 0.1),
        'rwkv_a0': nrm((L, BRANCH_W), 0.1),
        'rwkv_a2': nrm((L, R_LORA_A, BRANCH_W), 0.1),
        'rwkv_g2': nrm((L, R_LORA_G, BRANCH_W), 0.25),
        'rwkv_k_k': 0.85 + nrm((L, BRANCH_W), 0.05),
        'rwkv_k_a': 1.0 + nrm((L, BRANCH_W), 0.05),
        'rwkv_r_k': -0.04 + nrm((L, R_HEADS, R_HEAD), 0.1),
        'rwkv_ln_g': 1.0 + nrm((L, BRANCH_W), 0.02),
        'rwkv_ln_b': nrm((L, BRANCH_W), 0.02),
        'mlstm_conv_w': nrm((L, CONV_W, 2 * BRANCH_W), CONV_W ** -0.5),
        'mlstm_conv_b': nrm((L, 2 * BRANCH_W), 0.02),
        'mlstm_i_b': nrm((L, M_HEADS), 0.1),
        'mlstm_f_b': jnp.linspace(3.0, 6.0, M_HEADS, dtype=f32) + nrm((L, M_HEADS), 0.1),
        'mlstm_norm_g': 1.0 + nrm((L, BRANCH_W), 0.02),
        'gla_a2': nrm((L, G_LR, G_HEADS * G_DK), G_LR ** -0.5),
        'gla_a_b': nrm((L, G_HEADS * G_DK), 0.1),
        'gla_norm_g': 1.0 + nrm((L, BRANCH_W), 0.02),
        'w_branch': nrm((L, N_BRANCH, BRANCH_W, D_MODEL), BRANCH_W ** -0.5),
        'w_out': nrm((L, D_MODEL, D_MODEL), D_MODEL ** -0.5),
        'norm2_g': 1.0 + nrm((L, D_MODEL), 0.02),
        'ffn_w_gu': nrm((L, D_MODEL, 2 * D_FF), D_MODEL ** -0.5),
        'ffn_w_down': nrm((L, D_FF, D_MODEL), D_FF ** -0.5),
        'final_norm_g': 1.0 + nrm((D_MODEL,), 0.02),
    }


def reference(x_prompt, x_sample, state_rwkv_shift, state_rwkv_wkv, state_mlstm_conv, state_mlstm_C,
              state_mlstm_n, state_mlstm_m, state_gla_S,
              norm1_g, w_in, gate_b, rwkv_mu, rwkv_w0, rwkv_w2, rwkv_a0, rwkv_a2, rwkv_g2,
              rwkv_k_k, rwkv_k_a, rwkv_r_k, rwkv_ln_g, rwkv_ln_b,
              mlstm_conv_w, mlstm_conv_b, mlstm_i_b, mlstm_f_b, mlstm_norm_g,
              gla_a2, gla_a_b, gla_norm_g, w_branch, w_out, norm2_g, ffn_w_gu, ffn_w_down,
              final_norm_g):
    layer_params = (norm1_g, w_in, gate_b, rwkv_mu, rwkv_w0, rwkv_w2, rwkv_a0, rwkv_a2, rwkv_g2,
                    rwkv_k_k, rwkv_k_a, rwkv_r_k, rwkv_ln_g, rwkv_ln_b,
                    mlstm_conv_w, mlstm_conv_b, mlstm_i_b, mlstm_f_b, mlstm_norm_g,
                    gla_a2, gla_a_b, gla_norm_g, w_branch, w_out, norm2_g, ffn_w_gu, ffn_w_down)
    y_prompt, p_states = run_trunk(x_prompt, zero_states(x_prompt.shape[0], x_prompt.dtype),
                                   layer_params, final_norm_g)
    p_shift, p_wkv, p_conv, p_C, p_n, p_m, p_gla = p_states
    sample_states = (state_rwkv_shift, state_rwkv_wkv, state_mlstm_conv, state_mlstm_C,
                     state_mlstm_n, state_mlstm_m, state_gla_S)
    y_sample, s_states = run_trunk(x_sample, sample_states, layer_params, final_norm_g)
    s_shift, s_wkv, s_conv, s_C, s_n, s_m, s_gla = s_states
    return (y_prompt, y_sample, p_shift, p_wkv, p_conv, p_C, p_n, p_m, p_gla,
            s_shift, s_wkv, s_conv, s_C, s_n, s_m, s_gla)
```

```python
from contextlib import ExitStack
import numpy as np
import concourse.bass as bass
import concourse.mybir as mybir
from concourse.bass_utils import run_bass_kernel_spmd

F32 = mybir.dt.float32
BF16 = mybir.dt.bfloat16
AF = mybir.ActivationFunctionType
ALU = mybir.AluOpType
AX = mybir.AxisListType
_ESZ = {F32: 4, BF16: 2, mybir.dt.int32: 4, mybir.dt.uint8: 1}

D = 2048
BW = 1024
R_COLS = 3264
M_COLS = 4104
G_COLS = 3088
N_IN = 16600
DFF = 5632
RB, MB, GB_, GTB = 0, 3264, 7368, 10456
EPS = 1e-6
R_GN_EPS = 64e-5
NCORES = 8


class _Rec:
    __slots__ = ("sp", "plo", "phi", "lo", "hi", "lw", "rde", "rdd", "cells", "dead")


class _Op:
    __slots__ = ("eng", "fn", "dma", "deps", "signal", "semval", "semidx")


def _region(ap):
    t = ap.tensor
    tn = type(t).__name__
    esz = _ESZ[ap.dtype]
    pairs = ap.ap
    if tn == "DRamTensorHandle":
        ext = 1
        for st, cnt in pairs:
            ext += (cnt - 1) * abs(st)
        lo = ap.offset * esz
        return ("d" + t.name, 0, 1, lo, lo + ext * esz)
    tshape = t.shape
    rowb = 1
    for s in tshape[1:]:
        rowb *= s
    rowb *= _ESZ[t.dtype]
    off = ap.offset * esz
    p0 = off // rowb
    c0 = off % rowb
    ext = 1
    for st, cnt in pairs[1:]:
        ext += (cnt - 1) * abs(st)
    np_ = pairs[0][1]
    if tn == "SBTensorHandle":
        return ("s", p0, p0 + np_, c0, c0 + ext * esz)
    lo = (c0 // 2048) * 2048
    hi = ((c0 + ext * esz + 2047) // 2048) * 2048
    return ("p", 0, 128, lo, hi)


class Sched:
    ENGS = ("pe", "act", "dve", "pool", "sp")

    def __init__(self, nc, n_dma_sems=64):
        self.nc = nc
        self.ops = []
        self.nds = n_dma_sems
        self.eng_obj = {"pe": nc.tensor, "act": nc.scalar, "dve": nc.vector, "pool": nc.gpsimd, "sp": nc.sync}
        self.recs = {}
        self.grid = {}
        self.cellsz = {}

    def _cells(self, key):
        sp, plo, phi, lo, hi = key
        if sp == "s":
            cs = 2048
        elif sp == "p":
            cs = 2048
        else:
            cs = 1 << 20
        return [(sp, c) for c in range(lo // cs, (hi - 1) // cs + 1)]

    def _get(self, key):
        r = self.recs.get(key)
        if r is None:
            r = _Rec()
            r.sp, r.plo, r.phi, r.lo, r.hi = key
            r.lw = -1
            r.rde = {}
            r.rdd = []
            r.cells = self._cells(key)
            r.dead = False
            self.recs[key] = r
            for c in r.cells:
                self.grid.setdefault(c, set()).add(key)
        return r

    def _overlaps(self, key):
        sp, plo, phi, lo, hi = key
        out = set()
        for c in self._cells(key):
            g = self.grid.get(c)
            if g:
                out |= g
        res = []
        for k in out:
            if k[1] < phi and plo < k[2] and k[3] < hi and lo < k[4]:
                res.append(k)
        return res

    def op(self, eng, fn, reads=(), writes=(), dma=False):
        i = len(self.ops)
        o = _Op()
        o.eng, o.fn, o.dma = eng, fn, dma
        o.signal = False
        o.semval = 0
        o.semidx = -1
        deps = set()
        rkeys = [_region(a) for a in reads]
        wkeys = [_region(a) for a in writes]
        for k in rkeys:
            for q in self._overlaps(k):
                r = self.recs[q]
                if r.lw >= 0:
                    deps.add(r.lw)
        for k in wkeys:
            for q in self._overlaps(k):
                r = self.recs[q]
                if r.lw >= 0:
                    deps.add(r.lw)
                for v in r.rde.values():
                    deps.add(v)
                for v in r.rdd:
                    deps.add(v)
        keep = []
        for d in deps:
            od = self.ops[d]
            if (not od.dma) and (not dma) and od.eng == eng and eng == "pe":
                continue
            keep.append(d)
            od.signal = True
        o.deps = keep
        self.ops.append(o)
        for k in rkeys:
            r = self._get(k)
            if dma:
                r.rdd.append(i)
            else:
                r.rde[eng] = i
        for k in wkeys:
            for q in self._overlaps(k):
                if q != k and q[1] >= k[1] and q[2] <= k[2] and q[3] >= k[3] and q[4] <= k[4]:
                    r = self.recs.pop(q)
                    for c in r.cells:
                        self.grid[c].discard(q)
            r = self._get(k)
            r.lw = i
            r.rde = {}
            r.rdd = []
        if dma:
            o.signal = True
        return o

    def dma(self, out, in_, eng="sp", **kw):
        return self.op(eng, lambda e: e.dma_start(out=out, in_=in_, **kw), [in_], [out], dma=True)

    def emit(self, stack):
        nc = self.nc
        cnt = {e: 0 for e in self.ENGS}
        dcnt = [0] * self.nds
        di = 0
        for o in self.ops:
            if not o.signal:
                continue
            if o.dma:
                o.semidx = di % self.nds
                dcnt[o.semidx] += 16
                o.semval = dcnt[o.semidx]
                di += 1
            else:
                cnt[o.eng] += 1
                o.semval = cnt[o.eng]
        sems = {e: stack.enter_context(nc.semaphore("s_" + e)) for e in self.ENGS}
        dsems = [stack.enter_context(nc.semaphore("d_%d" % i)) for i in range(self.nds)]
        waited = {e: {} for e in self.ENGS}
        nw = 0
        for o in self.ops:
            e = self.eng_obj[o.eng]
            need = {}
            for d in o.deps:
                od = self.ops[d]
                key = ("d", od.semidx) if od.dma else ("e", od.eng)
                if od.semval > need.get(key, 0):
                    need[key] = od.semval
            wd = waited[o.eng]
            if o.dma and o.semval > 16:
                k0 = ("d", o.semidx)
                if o.semval - 16 > need.get(k0, 0):
                    need[k0] = o.semval - 16
            for key, val in need.items():
                if wd.get(key, 0) >= val:
                    continue
                wd[key] = val
                e.wait_ge(dsems[key[1]] if key[0] == "d" else sems[key[1]], val)
                nw += 1
            ins = o.fn(e)
            if o.signal:
                if o.dma:
                    ins.then_inc(dsems[o.semidx], 16)
                else:
                    ins.then_inc(sems[o.eng], 1)
        for i, v in enumerate(dcnt):
            if v:
                nc.sync.wait_ge(dsems[i], v)
        return nw


class Arena:
    def __init__(self, t, ncols):
        self.t = t
        self.ncols = ncols
        self.top = 0

    def mark(self):
        return self.top

    def release(self, m):
        self.top = m

    def f32(self, cols, parts=128):
        c0 = self.top
        self.top += cols
        assert self.top <= self.ncols, ("arena overflow", self.top, self.ncols)
        return self.t[0:parts, c0:c0 + cols]

    def bf(self, cols, parts=128):
        n32 = (cols + 1) // 2
        a = self.f32(n32, parts)
        return a.bitcast(BF16)[:, 0:cols]


class Psum:
    def __init__(self, t):
        self.t = t
        self.nb = 0

    def bank(self):
        b = self.nb % 8
        self.nb += 1
        return self.t[:, b * 512:(b + 1) * 512]

    def banks(self, n):
        return [self.bank() for _ in range(n)]


def ttiles(NT):
    out = []
    c = 0
    while c < NT:
        n = min(512, NT - c)
        out.append((c, n))
        c += n
    return out


VEC_OFF = {}
_o = 0
for _n, _c in [("n1g", 16), ("n2g", 16), ("gate_b", 48), ("mu", 24), ("mul", 3), ("w0", 8), ("a0", 8), ("k_k", 8),
               ("k_a", 8), ("r_k", 8), ("ln_g", 8), ("ln_b", 8), ("conv_w", 64), ("conv_b", 16), ("m_ng", 8),
               ("g_ng", 8), ("ga_b", 4)]:
    VEC_OFF[_n] = (_o, _c)
    _o += _c
VEC_N = _o
DV_OFF = {"omu": (0, 24), "omul": (24, 3), "omk_a": (27, 8)}
DV_N = 35


class Psum2(Psum):
    def __init__(self, t):
        Psum.__init__(self, t)
        self.reserved = set()

    def bank(self):
        while True:
            b = self.nb % 8
            self.nb += 1
            if b not in self.reserved:
                return self.t[:, b * 512:(b + 1) * 512]

    def reserve(self):
        while True:
            b = self.nb % 8
            self.nb += 1
            if b not in self.reserved:
                self.reserved.add(b)
                return b, self.t[:, b * 512:(b + 1) * 512]

    def unreserve(self, b):
        self.reserved.discard(b)


class StopBuild(Exception):
    pass


class Builder:
    def chk(self, name):
        import os
        if os.environ.get("SUBSTOP") == name:
            raise StopBuild()

    def __init__(self, T, NS, depth, dbg=(), stop=None):
        self.T, self.NS, self.NT, self.depth = T, NS, T + NS, depth
        self.dbg = set(dbg)
        self.stop = stop
        assert T % 128 == 0

    def dram(self, name, shape, dt=F32, kind=None):
        if kind is None:
            kind = "ExternalOutput" if name in self.dbg else "Internal"
        return self.nc.dram_tensor(name, list(shape), dt, kind=kind).ap()

    def build(self):
        nc = bass.Bass("TRN2", target_bir_lowering=False)
        self.nc = nc
        T, NS, NT, L = self.T, self.NS, self.NT, self.depth
        inp = lambda n, s: self.dram(n, s, kind="ExternalInput")
        outp = lambda n, s: self.dram(n, s, kind="ExternalOutput")
        I = self.I = {}
        I["xT"] = inp("xT", [D, NT])
        I["vec"] = inp("vec", [L, 128, VEC_N])
        I["rows"] = inp("rows", [L, 4, 2])
        I["w_in"] = inp("w_in", [L, D, N_IN])
        I["rw2"] = inp("rw2", [L, 3, 64, BW])
        I["ga2"] = inp("ga2", [L, 16, 512])
        I["w_branch"] = inp("w_branch", [L, 3, BW, D])
        I["w_out"] = inp("w_out", [L, D, D])
        I["w_gu"] = inp("w_gu", [L, D, 2 * DFF])
        I["w_down"] = inp("w_down", [L, DFF, D])
        I["fng"] = inp("fng", [128, 16])
        I["s_shift"] = inp("s_shift", [L, R_COLS, NS])
        I["s_wkv"] = inp("s_wkv", [L, NS, 16, 64, 64])
        I["s_conv"] = inp("s_conv", [L, 3, D, NS])
        I["s_C"] = inp("s_C", [L, NS, 4, 256, 256])
        I["s_n"] = inp("s_n", [L, NS, 4, 256])
        I["s_m"] = inp("s_m", [L, 4, NS])
        I["s_gla"] = inp("s_gla", [L, NS, 4, 128, 256])
        O = self.O = {}
        O["yT"] = outp("yT", [D, NT])
        O["o_shift"] = outp("o_shift", [L, R_COLS, 1 + NS])
        O["p_wkv"] = outp("p_wkv", [L, 16, 64, 64])
        O["s_wkv"] = outp("o_s_wkv", [L, NS, 16, 64, 64])
        O["o_conv"] = outp("o_conv", [L, 3, D, 1 + NS])
        O["p_C"] = outp("p_C", [L, 4, 256, 256])
        O["p_n"] = outp("p_n", [L, 4, 256])
        O["s_C"] = outp("o_s_C", [L, NS, 4, 256, 256])
        O["s_n"] = outp("o_s_n", [L, NS, 4, 256])
        O["o_m"] = outp("o_m", [L, 4, 1 + NS])
        O["p_gla"] = outp("p_gla", [L, 4, 128, 256])
        O["s_gla"] = outp("o_s_gla", [L, NS, 4, 128, 256])
        self.xT = self.dram("x_scr", [D, NT])
        self.projT = self.dram("projT", [N_IN, NT])
        self.obT = self.dram("obT", [3 * BW, NT], BF16)
        self.mgT = self.dram("mgT", [D, NT], BF16)
        self.actT = self.dram("actT", [DFF, NT], BF16)

        with ExitStack() as st:
            ACOLS = 52800
            at = st.enter_context(nc.sbuf_tensor("arena", [128, ACOLS], F32))
            pt = st.enter_context(nc.psum_tensor("psum", [128, 4096], F32))
            self.A = Arena(at, ACOLS)
            self.P = Psum2(pt)
            self.S = Sched(nc)
            self.consts()
            done = True
            try:
                for l in range(L):
                    if not self.layer(l):
                        done = False
                        break
            except StopBuild:
                done = False
            if done:
                self.S.dma(self.C["vec"][:, 0:16], I["fng"])
                self.rmsnorm(self.xT, "n1g", out_dram=O["yT"])
            self.nwaits = self.S.emit(st)
        return nc

    def act(self, out, in_, func, bias=None, scale=1.0):
        kw = {}
        rd = [in_]
        if bias is not None:
            kw["bias"] = bias
            if not isinstance(bias, (int, float)):
                rd.append(bias)
        if not isinstance(scale, (int, float)):
            rd.append(scale)
        self.S.op("act", lambda e: e.activation(out=out, in_=in_, func=func, scale=scale, **kw), rd, [out])

    def ts(self, eng, out, in0, s1, s2, op0, op1=None):
        rd = [in0] + [s for s in (s1, s2) if s is not None and not isinstance(s, (int, float))]
        if op1 is None:
            self.S.op(eng, lambda e: e.tensor_scalar(out=out, in0=in0, scalar1=s1, scalar2=None, op0=op0), rd, [out])
        else:
            self.S.op(eng, lambda e: e.tensor_scalar(out=out, in0=in0, scalar1=s1, scalar2=s2, op0=op0, op1=op1),
                      rd, [out])

    def stt(self, out, in0, sc, in1, op0, op1):
        rd = [in0, in1] + ([] if isinstance(sc, (int, float)) else [sc])
        self.S.op("dve", lambda e: e.scalar_tensor_tensor(out=out, in0=in0, scalar=sc, in1=in1, op0=op0, op1=op1),
                  rd, [out])

    def tt(self, eng, out, in0, in1, op):
        self.S.op(eng, lambda e: e.tensor_tensor(out=out, in0=in0, in1=in1, op=op), [in0, in1], [out])

    def cp(self, eng, out, in_):
        if eng == "act":
            self.S.op("act", lambda e: e.copy(out=out, in_=in_), [in_], [out])
        else:
            self.S.op(eng, lambda e: e.tensor_copy(out=out, in_=in_), [in_], [out])

    def mm(self, out, lhsT, rhs, start=True, stop=True):
        self.S.op("pe", lambda e: e.matmul(out, lhsT=lhsT, rhs=rhs, start=start, stop=stop), [lhsT, rhs], [out])

    def tr(self, out, in_, ident):
        self.S.op("pe", lambda e: e.transpose(out, in_, ident), [in_, ident], [out])

    def memset(self, eng, ap, v):
        self.S.op(eng, lambda e: e.memset(ap, v), [], [ap])

    def recip(self, out, in_):
        self.S.op("dve", lambda e: e.reciprocal(out=out, in_=in_), [in_], [out])

    def scan(self, out, d0, d1, init, op0, op1):
        self.S.op("dve", lambda e: e.tensor_tensor_scan(out=out, data0=d0, data1=d1, initial=init, op0=op0, op1=op1),
                  [d0, d1], [out])

    def dmas(self, out, in_, eng="sp"):
        self.S.op(eng, lambda e: e.dma_start(out=out, in_=in_, allow_slow_non_contiguous=True), [in_], [out], dma=True)

    def V(self, name, j=0, parts=128):
        off, n = VEC_OFF[name]
        return self.C["vec"][0:parts, off + j:off + j + 1]

    def DV(self, name, j=0, parts=128):
        off, n = DV_OFF[name]
        return self.C["dvec"][0:parts, off + j:off + j + 1]

    def consts(self):
        A, S, T = self.A, self.S, self.T
        C = self.C = {}
        io = A.f32(128)
        S.op("pool", lambda e: e.iota(io, [[1, 128]], base=0, channel_multiplier=-1,
                                      allow_small_or_imprecise_dtypes=True), [], [io])
        C["ident_f"] = A.f32(128)
        self.ts("dve", C["ident_f"], io, 0.0, None, ALU.is_equal)
        C["ident_b"] = A.bf(128)
        self.ts("dve", C["ident_b"], io, 0.0, None, ALU.is_equal)
        C["ones_b"] = A.bf(128)
        self.memset("dve", C["ones_b"], 1.0)
        C["m_le_b"] = A.bf(128)
        self.ts("dve", C["m_le_b"], io, 0.0, None, ALU.is_ge)
        C["negm"] = A.f32(128)
        self.ts("dve", C["negm"], io, 0.0, -30000.0, ALU.is_lt, ALU.mult)
        C["bo_b"] = A.bf(128)
        self.memset("dve", C["bo_b"], 0.0)
        self.memset("dve", C["bo_b"][0:64, 0:64], 1.0)
        self.memset("dve", C["bo_b"][64:128, 64:128], 1.0)
        io2 = A.f32(64)
        S.op("pool", lambda e: e.iota(io2[0:64, :], [[1, 64]], base=0, channel_multiplier=-1,
                                      allow_small_or_imprecise_dtypes=True), [], [io2[0:64, :]])
        S.op("pool", lambda e: e.iota(io2[64:128, :], [[1, 64]], base=0, channel_multiplier=-1,
                                      allow_small_or_imprecise_dtypes=True), [], [io2[64:128, :]])
        C["mk192"] = A.bf(192)
        self.ts("dve", C["mk192"][:, 0:64], io2, 0.0, None, ALU.is_gt)
        self.ts("dve", C["mk192"][:, 64:128], io2, 0.0, None, ALU.is_gt)
        self.ts("dve", C["mk192"][:, 128:192], io2, 0.0, None, ALU.is_ge)
        C["mk_lt"] = A.bf(128)
        self.ts("dve", C["mk_lt"][:, 0:64], io2, 0.0, None, ALU.is_lt)
        self.ts("dve", C["mk_lt"][:, 64:128], io2, 0.0, None, ALU.is_lt)
        C["rst64"] = A.bf(T)
        self.memset("pool", C["rst64"], 1.0)
        self.memset("pool", C["rst64"].rearrange("p (c j) -> p c j", j=64)[:, :, 0:1], 0.0)
        C["rst128"] = A.bf(T)
        self.memset("pool", C["rst128"], 1.0)
        self.memset("pool", C["rst128"].rearrange("p (c j) -> p c j", j=128)[:, :, 0:1], 0.0)
        C["sel"] = A.f32(4 * 128, parts=4).rearrange("p (h m) -> p h m", h=4)
        for h in range(4):
            self.cp("dve", C["sel"][:, h, :], C["ident_f"][0:4, h:h + 1].to_broadcast([4, 128]))
        C["vec"] = A.f32(VEC_N)
        C["dvec"] = A.f32(DV_N)
        C["rows"] = A.f32(2, parts=4)
        C["eps"] = A.f32(1)
        self.memset("dve", C["eps"], EPS)
        C["gneps"] = A.f32(1)
        self.memset("dve", C["gneps"], R_GN_EPS)
        self.abase = A.mark()

    def layer(self, l):
        S, I, O, C = self.S, self.I, self.O, self.C
        T, NS, NT = self.T, self.NS, self.NT
        S.dma(C["vec"], I["vec"][l])
        S.dma(C["rows"], I["rows"][l])
        mo, mn = VEC_OFF["mu"]
        self.ts("dve", C["dvec"][:, 0:27], C["vec"][:, mo:mo + 27], -1.0, 1.0, ALU.mult, ALU.add)
        ko, kn = VEC_OFF["k_a"]
        self.ts("dve", C["dvec"][:, 27:35], C["vec"][:, ko:ko + 8], -1.0, 1.0, ALU.mult, ALU.add)
        src = I["xT"] if l == 0 else self.xT
        self.src = src
        self.rmsnorm(src, "n1g")
        self.dense(I["w_in"][l], N_IN, 16, self.hT, self.proj_sink)
        if self.stop == (l, "proj"):
            return False
        for r0 in range(0, R_COLS, 816):
            self.dmas(O["o_shift"][l, r0:r0 + 816, 0:1], self.projT[r0:r0 + 816, T - 1:T])
            self.dmas(O["o_shift"][l, r0:r0 + 816, 1:1 + NS], self.projT[r0:r0 + 816, T:NT])
        for r0 in range(0, D, 1024):
            for lag in range(3):
                self.dmas(O["o_conv"][l, lag, r0:r0 + 1024, 0:1],
                          self.projT[MB + r0:MB + r0 + 1024, T - 3 + lag:T - 2 + lag])
            self.dmas(O["o_conv"][l, 0, r0:r0 + 1024, 1:1 + NS], I["s_conv"][l, 1, r0:r0 + 1024, :])
            self.dmas(O["o_conv"][l, 1, r0:r0 + 1024, 1:1 + NS], I["s_conv"][l, 2, r0:r0 + 1024, :])
            self.dmas(O["o_conv"][l, 2, r0:r0 + 1024, 1:1 + NS], self.projT[MB + r0:MB + r0 + 1024, T:NT])
        self.chk("rw0")
        self.rwkv(l)
        if self.stop == (l, "rwkv"):
            return False
        self.mlstm(l)
        if self.stop == (l, "mlstm"):
            return False
        self.gla(l)
        if self.stop == (l, "gla"):
            return False
        self.merge(l)
        self.outproj(l)
        if self.stop == (l, "attn"):
            return False
        self.rmsnorm(self.xT, "n2g")
        self.ffn(l)
        return True

    def rmsnorm(self, src, gname, out_dram=None):
        A, S, C = self.A, self.S, self.C
        NT = self.NT
        A.release(self.abase)
        if out_dram is None:
            self.hT = A.bf(16 * NT).rearrange("p (c t) -> p c t", c=16)
        m0 = A.mark()
        srcv = src.rearrange("(c p) t -> p c t", p=128)
        for (t0, tn) in ttiles(NT):
            A.release(m0)
            xt = A.f32(16 * tn).rearrange("p (c t) -> p c t", c=16)
            sq = A.bf(16 * tn).rearrange("p (c t) -> p c t", c=16)
            rs = A.f32(tn)
            S.dma(xt, srcv[:, :, t0:t0 + tn])
            ps = self.P.bank()[:, 0:tn]
            for c in range(16):
                self.act(sq[:, c, :], xt[:, c, :], AF.Square)
                self.mm(ps, C["ones_b"], sq[:, c, :], start=(c == 0), stop=(c == 15))
            self.act(rs, ps, AF.Sqrt, bias=C["eps"], scale=1.0 / D)
            self.recip(rs, rs)
            for c in range(16):
                dst = self.hT[:, c, t0:t0 + tn] if out_dram is None else xt[:, c, :]
                self.stt(dst, xt[:, c, :], self.V(gname, c), rs, ALU.mult, ALU.mult)
            if out_dram is not None:
                S.dma(out_dram.rearrange("(c p) t -> p c t", p=128)[:, :, t0:t0 + tn], xt, eng="sp")
        A.release(m0)

    def dense(self, W, N, KC, rhs, sink, order=None):
        A, S = self.A, self.S
        NT = self.NT
        m0 = A.mark()
        nm = (N + 127) // 128
        Wv = W.rearrange("(kc p) n -> p kc n", p=128)
        wst = [A.f32(KC * 128).rearrange("p (k n) -> p k n", k=KC) for _ in range(2)]
        wbf = [A.bf(KC * 128).rearrange("p (k n) -> p k n", k=KC) for _ in range(2)]
        self._dense_m0 = A.mark()
        tl = ttiles(NT)
        order = order if order is not None else list(range(nm))
        for it, m in enumerate(order):
            rows = min(128, N - m * 128)
            ws, wb = wst[it % 2], wbf[it % 2]
            S.dma(ws[:, :, 0:rows], Wv[:, :, m * 128:m * 128 + rows])
            self.cp("pool" if it % 2 else "dve", wb[:, :, 0:rows], ws[:, :, 0:rows])
            banks = self.P.banks(len(tl))
            for k in range(KC):
                for bi, (t0, tn) in enumerate(tl):
                    self.mm(banks[bi][0:rows, 0:tn], wb[:, k, 0:rows], rhs[:, k, t0:t0 + tn],
                            start=(k == 0), stop=(k == KC - 1))
            sink(it, m, rows, banks, tl)
        A.release(m0)

    def proj_sink(self, it, m, rows, banks, tl):
        A, S = self.A, self.S
        A.release(self._dense_m0)
        stg = [A.f32(self.NT) for _ in range(2)]
        o = stg[it % 2]
        for bi, (t0, tn) in enumerate(tl):
            self.cp("act", o[0:rows, t0:t0 + tn], banks[bi][0:rows, 0:tn])
        S.dma(self.projT[m * 128:m * 128 + rows, :], o[0:rows, :], eng="act")

    def shiftmix(self, dst, p, mu, omu, st, parts=128):
        T, NT = self.T, self.NT
        self.ts("dve", dst[0:parts, :], p[0:parts, :], omu, None, ALU.mult)
        self.stt(dst[0:parts, 1:T], p[0:parts, 0:T - 1], mu, dst[0:parts, 1:T], ALU.mult, ALU.add)
        self.stt(dst[0:parts, T:NT], st, mu, dst[0:parts, T:NT], ALU.mult, ALU.add)

    def rwkv(self, l):
        A, S, C, I, O = self.A, self.S, self.C, self.I, self.O
        T, NS, NT = self.T, self.NS, self.NT
        A.release(self.abase)
        lw = A.bf(3 * BW, parts=64).rearrange("p (j n) -> p j n", j=3)
        lin = [A.bf(NT, parts=64) for _ in range(3)]
        m1 = A.mark()
        lw_st = A.f32(3 * BW, parts=64).rearrange("p (j n) -> p j n", j=3)
        S.dma(lw_st, I["rw2"][l].rearrange("j k n -> k j n"))
        self.cp("dve", lw, lw_st)
        self.chk("rw1a")
        raw = A.f32(NT, parts=64)
        xs = A.f32(NT, parts=64)
        sst = A.f32(NS, parts=64)
        for j in range(3):
            S.dma(raw, self.projT[3072 + 64 * j:3072 + 64 * (j + 1), :])
            S.dma(sst, I["s_shift"][l, 3072 + 64 * j:3072 + 64 * (j + 1), :])
            self.chk("rw1b")
            self.shiftmix(xs, raw, self.V("mul", j, 64), self.DV("omul", j, 64), sst, 64)
            self.chk("rw1c")
            if j == 1:
                self.cp("dve", lin[j], xs)
            else:
                self.act(lin[j], xs, [AF.Tanh, AF.Copy, AF.Sigmoid][j])
            self.chk("rw1d")
        A.release(m1)
        self._rw_base = A.mark()
        self.chk("rw1")
        for hp in range(8):
            self.rwkv_pair(l, hp, lw, lin)

    def rwkv_pair(self, l, hp, lw, lin):
        A, S, C, I, O, P = self.A, self.S, self.C, self.I, self.O, self.P
        T, NS, NT = self.T, self.NS, self.NT
        NCH = T // 64
        tl = ttiles(NT)
        A.release(self._rw_base)
        c0 = hp * 128
        H = [slice(0, 64), slice(64, 128)]
        bonus = A.f32(NT)
        g = A.f32(NT)
        Ofm = A.f32(NT)
        smp = A.f32(6 * NS).rearrange("p (q i) -> p q i", q=6)
        WL = A.f32(NCH)
        AR = A.bf(NCH * 192).rearrange("p (c n) -> p c n", c=NCH)
        Bb = A.bf(NCH * 128).rearrange("p (c n) -> p c n", c=NCH)
        Kb = A.bf(NCH * 128).rearrange("p (c n) -> p c n", c=NCH)
        Vb = A.bf(NCH * 128).rearrange("p (c n) -> p c n", c=NCH)
        mk = A.mark()
        t_r, t_k, t_v, xr, xk, xv, ld, a, kap, b, cc, e = [A.f32(NT) for _ in range(12)]
        sqb = A.bf(NT)
        sst = A.f32(3 * NS).rearrange("p (q i) -> p q i", q=3)
        for z in (AR, Bb, Kb, Vb):
            self.memset("pool", z, 0.0)
        for q, (tt_, xx) in enumerate([(t_r, xr), (t_k, xk), (t_v, xv)]):
            S.dma(tt_, self.projT[q * BW + c0:q * BW + c0 + 128, :])
            S.dma(sst[:, q, :], I["s_shift"][l, q * BW + c0:q * BW + c0 + 128, :])
            self.shiftmix(xx, tt_, self.V("mu", q * 8 + hp), self.DV("omu", q * 8 + hp), sst[:, q, :])
        for (t0, tn) in tl:
            ps = P.bank()
            self.mm(ps[:, 0:tn], lw[:, 0, c0:c0 + 128], lin[0][:, t0:t0 + tn])
            self.act(ld[:, t0:t0 + tn], ps[:, 0:tn], AF.Sigmoid, bias=self.V("w0", hp))
            ps = P.bank()
            self.mm(ps[:, 0:tn], lw[:, 1, c0:c0 + 128], lin[1][:, t0:t0 + tn])
            self.act(a[:, t0:t0 + tn], ps[:, 0:tn], AF.Sigmoid, bias=self.V("a0", hp))
            ps = P.bank()
            self.mm(ps[:, 0:tn], lw[:, 2, c0:c0 + 128], lin[2][:, t0:t0 + tn])
            self.cp("act", g[:, t0:t0 + tn], ps[:, 0:tn])
        self.ts("pool", ld, ld, -0.6065306597126334, None, ALU.mult)
        self.ts("dve", kap, xk, self.V("k_k", hp), None, ALU.mult)
        self.act(sqb, kap, AF.Square)
        for (t0, tn) in tl:
            ps = P.bank()
            self.mm(ps[:, 0:tn], C["bo_b"], sqb[:, t0:t0 + tn])
            self.act(e[:, t0:t0 + tn], ps[:, 0:tn], AF.Sqrt)
        self.ts("dve", e, e, 1e-12, None, ALU.max)
        self.recip(e, e)
        self.tt("dve", kap, kap, e, ALU.mult)
        self.ts("dve", e, a, self.V("k_a", hp), self.DV("omk_a", hp), ALU.mult, ALU.add)
        self.tt("dve", xk, xk, e, ALU.mult)
        self.tt("pool", b, kap, a, ALU.mult)
        self.stt(sqb, xr, self.V("r_k", hp), xk, ALU.mult, ALU.mult)
        for (t0, tn) in tl:
            ps = P.bank()
            self.mm(ps[:, 0:tn], C["bo_b"], sqb[:, t0:t0 + tn])
            self.tt("dve", bonus[:, t0:t0 + tn], ps[:, 0:tn], xv[:, t0:t0 + tn], ALU.mult)
        for q, src in enumerate([kap, b, xk, xv, xr]):
            self.cp("pool", smp[:, q, :], src[:, T:NT])
        self.act(smp[:, 5, :], ld[:, T:NT], AF.Exp)
        self.scan(cc[:, 0:T], C["rst64"], ld[:, 0:T], 0.0, ALU.mult, ALU.add)
        v3 = lambda ap: ap[:, 0:T].rearrange("p (c j) -> p c j", j=64)
        self.act(WL, v3(cc)[:, :, 63], AF.Exp)
        e1, e2, e3 = t_r, t_k, t_v
        self.act(e1[:, 0:T], cc[:, 0:T], AF.Exp)
        self.tt("dve", AR[:, :, 128:192], v3(xr), v3(e1), ALU.mult)
        self.tt("pool", e2[:, 0:T], cc[:, 0:T], ld[:, 0:T], ALU.subtract)
        self.act(e2[:, 0:T], e2[:, 0:T], AF.Exp)
        self.act(e3[:, 0:T], cc[:, 0:T], AF.Exp, scale=-1.0)
        for hh in range(2):
            hs = H[hh]
            self.stt(AR[hs, :, hh * 64:hh * 64 + 64], v3(kap)[hs], -1.0, v3(e2)[hs], ALU.mult, ALU.mult)
            self.tt("dve", Bb[hs, :, hh * 64:hh * 64 + 64], v3(b)[hs], v3(e3)[hs], ALU.mult)
            self.tt("dve", Kb[hs, :, hh * 64:hh * 64 + 64], v3(xk)[hs], v3(e3)[hs], ALU.mult)
            self.cp("pool", Vb[hs, :, hh * 64:hh * 64 + 64], v3(xv)[hs])
        self.chk("rw2")
        A.release(mk)
        Vtm = A.bf(NCH * 128).rearrange("p (c n) -> p c n", c=NCH)
        Btm = A.bf(NCH * 128).rearrange("p (c n) -> p c n", c=NCH)
        Ktm = A.bf(NCH * 128).rearrange("p (c n) -> p c n", c=NCH)
        TT = A.bf(NCH * 128).rearrange("p (c n) -> p c n", c=NCH)
        Aak = A.bf(NCH * 128).rearrange("p (c n) -> p c n", c=NCH)
        Abk = A.bf(NCH * 128).rearrange("p (c n) -> p c n", c=NCH)
        GW = 4
        PP = [[A.bf(256) for _ in range(2)] for _ in range(GW)]
        ZT = A.f32(128)
        ZTs = A.f32(128)
        ZTb = A.bf(128)
        Xsb = A.bf(128)
        Usb = A.bf(128)
        self.memset("dve", ZT, 0.0)
        self.memset("dve", ZTb, 0.0)
        ident = C["ident_b"]

        def st0(c, w):
            ps = P.bank()
            self.mm(ps[:, 0:192], Bb[:, c, :], AR[:, c, :])
            self.tt("dve", PP[w][0][:, 0:128], ps[:, 0:128], C["mk192"][:, 0:128], ALU.mult)
            self.tt("dve", Abk[:, c, 0:64], ps[:, 128:192], C["mk192"][:, 128:192], ALU.mult)
            self.chk("s0_1")
            ps2 = P.bank()
            self.mm(ps2[:, 0:192], Kb[:, c, :], AR[:, c, :])
            self.tt("dve", Aak[:, c, :], ps2[:, 0:128], C["mk192"][:, 0:128], ALU.mult)
            self.tt("dve", Abk[:, c, 64:128], ps2[:, 128:192], C["mk192"][:, 128:192], ALU.mult)
            ps3 = P.bank()
            self.mm(ps3[:, 0:128], AR[:, c, 0:128], Bb[:, c, :])
            self.tt("dve", PP[w][0][:, 128:256], ps3[:, 0:128], C["mk_lt"], ALU.mult)
            self.chk("s0_2")
            self.tt("pool", TT[:, c, :], PP[w][0][:, 0:128], ident, ALU.add)
            self.chk("s0_3")
            ps4 = P.bank()
            self.mm(ps4[:, 0:128], Vb[:, c, :], ident)
            self.mm(ps4[:, 128:256], Bb[:, c, :], ident)
            self.mm(ps4[:, 256:384], Kb[:, c, :], ident)
            self.cp("act", Vtm[:, c, :], ps4[:, 0:128])
            self.cp("act", Btm[:, c, :], ps4[:, 128:256])
            self.cp("act", Ktm[:, c, :], ps4[:, 256:384])
            self.chk("s0_4")

        def sq(c, w, j):
            src = PP[w][(j - 1) % 2]
            dst = PP[w][j % 2]
            ps = P.bank()
            self.mm(ps[:, 0:128], src[:, 128:256], src[:, 0:128])
            self.mm(ps[:, 128:256], src[:, 0:128], src[:, 128:256])
            self.cp("act", dst, ps[:, 0:256])

        def acc(c, w, j):
            dst = PP[w][j % 2]
            ps = P.bank()
            self.mm(ps[:, 0:128], dst[:, 128:256], TT[:, c, :])
            self.tt("dve", TT[:, c, :], TT[:, c, :], ps[:, 0:128], ALU.add)

        def chain(c):
            ps = P.bank()
            self.mm(ps[:, 0:128], AR[:, c, 0:128], ZTb, start=True, stop=False)
            self.mm(ps[:, 0:128], Aak[:, c, :], Vtm[:, c, :], start=False, stop=True)
            self.cp("act", Xsb, ps[:, 0:128])
            pu = P.bank()
            self.mm(pu[:, 0:128], TT[:, c, :], Xsb)
            self.cp("dve", Usb, pu[:, 0:128])
            po = P.bank()
            self.mm(po[:, 0:64], Usb, Abk[:, c, 0:64], start=True, stop=False)
            self.mm(po[:, 0:64], Vtm[:, c, :], Abk[:, c, 64:128], start=False, stop=False)
            self.mm(po[:, 0:64], ZTb, AR[:, c, 128:192], start=False, stop=True)
            self.cp("act", Ofm[:, c * 64:(c + 1) * 64], po[:, 0:64])
            pz = P.bank()
            self.mm(pz[:, 0:128], Btm[:, c, :], Usb, start=True, stop=False)
            self.mm(pz[:, 0:128], Ktm[:, c, :], Vtm[:, c, :], start=False, stop=True)
            self.ts("pool", ZTs, ZT, WL[:, c:c + 1], None, ALU.mult)
            self.stt(ZT, pz[:, 0:128], WL[:, c:c + 1], ZTs, ALU.mult, ALU.add)
            self.cp("act", ZTb, ZT)

        for c0_ in range(0, NCH, GW):
            wave = list(range(c0_, min(NCH, c0_ + GW)))
            for w, c in enumerate(wave):
                st0(c, w)
            self.chk("rw3a")
            for j in range(1, 6):
                for w, c in enumerate(wave):
                    sq(c, w, j)
                for w, c in enumerate(wave):
                    acc(c, w, j)
            self.chk("rw3b")
            for c in wave:
                chain(c)
        self.chk("rw3")
        ps = P.bank()
        self.tr(ps[:, 0:128], ZT, C["ident_f"])
        So = A.f32(128)
        self.cp("act", So, ps[:, 0:128])
        for hh in range(2):
            S.dma(O["p_wkv"][l, 2 * hp + hh], So[H[hh], hh * 64:hh * 64 + 64], eng="act")
        m2 = A.mark()
        tm3 = A.f32(384, parts=NS)
        ps = P.bank()
        for q, idx in enumerate([1, 2, 3]):
            self.tr(ps[0:NS, q * 128:(q + 1) * 128], smp[:, idx, :], C["ident_f"])
        self.cp("act", tm3, ps[0:NS, 0:384])
        eye = C["ident_f"][0:NS, 0:NS].unsqueeze(2).to_broadcast([NS, NS, 128])
        Bex = A.f32(NS * 128, parts=NS).rearrange("p (i c) -> p i c", i=NS)
        Kex = A.f32(NS * 128, parts=NS).rearrange("p (i c) -> p i c", i=NS)
        self.tt("dve", Bex, tm3[:, 0:128].unsqueeze(1).to_broadcast([NS, NS, 128]), eye, ALU.mult)
        self.tt("dve", Kex, tm3[:, 128:256].unsqueeze(1).to_broadcast([NS, NS, 128]), eye, ALU.mult)
        vtm = tm3[:, 256:384]
        Sbl = [A.f32(128) for _ in range(2)]
        STt = [A.f32(128) for _ in range(2)]
        SnT = [A.f32(128) for _ in range(2)]
        Sou = [A.f32(128) for _ in range(2)]
        nSK = [A.f32(128, parts=NS) for _ in range(2)]
        for z in Sbl + SnT:
            self.memset("pool", z, 0.0)
        rb, pO = P.reserve()
        for i in range(NS):
            k2 = i % 2
            for hh in range(2):
                S.dma(Sbl[k2][H[hh], hh * 64:hh * 64 + 64], I["s_wkv"][l, i, 2 * hp + hh])
            ps = P.bank()
            self.tr(ps[:, 0:128], Sbl[k2], C["ident_f"])
            self.cp("act", STt[k2], ps[:, 0:128])
            p1 = P.bank()
            self.mm(p1[0:NS, 0:128], smp[:, 0, :], STt[k2])
            self.ts("dve", nSK[k2], p1[0:NS, 0:128], -1.0, None, ALU.mult)
            p2 = P.bank()
            self.mm(p2[:, 0:128], Bex[:, i, :], nSK[k2], start=True, stop=False)
            self.mm(p2[:, 0:128], Kex[:, i, :], vtm, start=False, stop=True)
            for hh in range(2):
                hs = H[hh]
                cs = slice(hh * 64, hh * 64 + 64)
                self.stt(SnT[k2][hs, cs], STt[k2][hs, cs], smp[hs, 5, i:i + 1], p2[hs, cs], ALU.mult, ALU.add)
            self.mm(pO[:, i:i + 1], SnT[k2], smp[:, 4, i:i + 1])
            p3 = P.bank()
            self.tr(p3[:, 0:128], SnT[k2], C["ident_f"])
            self.cp("act", Sou[k2], p3[:, 0:128])
            for hh in range(2):
                S.dma(O["s_wkv"][l, i, 2 * hp + hh], Sou[k2][H[hh], hh * 64:hh * 64 + 64], eng="act")
        self.cp("act", Ofm[:, T:NT], pO[:, 0:NS])
        P.unreserve(rb)
        A.release(m2)
        self.chk("rw4")
        cen = A.f32(NT)
        rs = A.f32(NT)
        sb2 = A.bf(NT)
        obf = A.bf(NT)
        self.cp("pool", sb2, Ofm)
        for (t0, tn) in tl:
            ps = P.bank()
            self.mm(ps[:, 0:tn], C["bo_b"], sb2[:, t0:t0 + tn])
            self.stt(cen[:, t0:t0 + tn], ps[:, 0:tn], -1.0 / 64, Ofm[:, t0:t0 + tn], ALU.mult, ALU.add)
        self.act(sb2, cen, AF.Square)
        for (t0, tn) in tl:
            ps = P.bank()
            self.mm(ps[:, 0:tn], C["bo_b"], sb2[:, t0:t0 + tn])
            self.act(rs[:, t0:t0 + tn], ps[:, 0:tn], AF.Sqrt, bias=C["gneps"], scale=1.0 / 64)
        self.recip(rs, rs)
        self.tt("dve", cen, cen, rs, ALU.mult)
        self.ts("dve", cen, cen, self.V("ln_g", hp), self.V("ln_b", hp), ALU.mult, ALU.add)
        self.tt("pool", cen, cen, bonus, ALU.add)
        self.tt("dve", obf, cen, g, ALU.mult)
        S.dma(self.obT[c0:c0 + 128, :], obf, eng="sp")
        self.chk("rw5")

    def mlstm(self, l):
        A, S, C, I, O, P = self.A, self.S, self.C, self.I, self.O, self.P
        T, NS, NT = self.T, self.NS, self.NT
        NB = T // 128
        tl = ttiles(NT)
        A.release(self.abase)
        R4 = lambda n: A.f32(n, parts=4)
        Ac, wrow, emrow = [R4(NT) for _ in range(3)]
        dec, acs = R4(NB), R4(NS)
        acol = A.f32(NB * 4).rearrange("p (b h) -> p b h", h=4)
        ecol = A.f32(NB * 4).rearrange("p (b h) -> p b h", h=4)
        wscol = A.f32(4, parts=NS)
        mkeep = A.mark()
        ig, lf, G, aa, mt, erow = [R4(NT) for _ in range(6)]
        ones4 = R4(T)
        Ast, Aend = R4(NB), R4(NB)
        mold, wss = R4(NS), R4(NS)
        S.dma(ig, self.projT[MB + 3072:MB + 3076, :])
        S.dma(lf, self.projT[MB + 3076:MB + 3080, :])
        S.dma(mold, I["s_m"][l])
        self.memset("dve", ones4, 1.0)
        self.ts("dve", ig, ig, C["rows"][:, 0:1], None, ALU.add)
        self.act(lf, lf, AF.Sigmoid, bias=C["rows"][:, 1:2])
        self.act(lf, lf, AF.Ln)
        self.scan(G[:, 0:T], ones4, lf[:, 0:T], 0.0, ALU.mult, ALU.add)
        self.tt("dve", aa[:, 0:T], ig[:, 0:T], G[:, 0:T], ALU.subtract)
        self.scan(Ac[:, 0:T], ones4, aa[:, 0:T], 0.0, ALU.mult, ALU.max)
        self.tt("dve", mt[:, 0:T], G[:, 0:T], Ac[:, 0:T], ALU.add)
        v3 = lambda ap: ap[:, 0:T].rearrange("p (c j) -> p c j", j=128)
        self.cp("dve", Aend, v3(Ac)[:, :, 127])
        self.memset("dve", Ast[:, 0:1], 0.0)
        if NB > 1:
            self.cp("dve", Ast[:, 1:NB], Aend[:, 0:NB - 1])
        self.tt("dve", v3(wrow), Ast.unsqueeze(2).to_broadcast([4, NB, 128]), v3(Ac), ALU.subtract)
        self.act(wrow[:, 0:T], wrow[:, 0:T], AF.Exp)
        self.tt("dve", v3(erow), v3(aa), Aend.unsqueeze(2).to_broadcast([4, NB, 128]), ALU.subtract)
        self.act(erow[:, 0:T], erow[:, 0:T], AF.Exp)
        self.tt("dve", dec, Ast, Aend, ALU.subtract)
        self.act(dec, dec, AF.Exp)
        self.tt("dve", mold, lf[:, T:NT], mold, ALU.add)
        self.tt("dve", mt[:, T:NT], mold, ig[:, T:NT], ALU.max)
        self.tt("dve", acs, mold, mt[:, T:NT], ALU.subtract)
        self.act(acs, acs, AF.Exp)
        self.tt("dve", wss, ig[:, T:NT], mt[:, T:NT], ALU.subtract)
        self.act(wss, wss, AF.Exp)
        self.act(emrow, mt, AF.Exp, scale=-1.0)
        self.dmas(O["o_m"][l, :, 0:1], mt[:, T - 1:T], eng="act")
        S.dma(O["o_m"][l, :, 1:1 + NS], mt[:, T:NT], eng="act")
        ps = P.bank()
        i4 = C["ident_f"][0:4, 0:4]
        for bb in range(NB):
            self.mm(ps[:, bb * 4:bb * 4 + 4], aa[:, bb * 128:(bb + 1) * 128], i4)
            self.mm(ps[:, 256 + bb * 4:256 + bb * 4 + 4], erow[:, bb * 128:(bb + 1) * 128], i4)
        self.mm(ps[0:NS, 500:504], wss, i4)
        self.cp("act", acol, ps[:, 0:NB * 4].rearrange("p (b h) -> p b h", h=4))
        self.cp("act", ecol, ps[:, 256:256 + NB * 4].rearrange("p (b h) -> p b h", h=4))
        self.cp("act", wscol, ps[0:NS, 500:504])
        A.release(mkeep)
        self._ml_base = A.mark()
        for h in range(4):
            self.mlstm_head(l, h, dict(Ac=Ac, wrow=wrow, emrow=emrow, dec=dec, acs=acs, acol=acol, ecol=ecol,
                                       wscol=wscol))

    def bcast_rows(self, dst, rows4, h, ncols):
        c = 0
        while c < ncols:
            n = min(512, ncols - c)
            ps = self.P.bank()
            self.mm(ps[:, 0:n], self.C["sel"][:, h, :], rows4[:, c:c + n])
            self.cp("act", dst[:, c:c + n], ps[:, 0:n])
            c += n

    def conv_silu(self, l, ch0, cidx, raw, acc, cs):
        S, I = self.S, self.I
        T, NS, NT = self.T, self.NS, self.NT
        S.dma(raw, self.projT[MB + ch0:MB + ch0 + 128, :])
        S.dma(cs, I["s_conv"][l, :, ch0:ch0 + 128, :].rearrange("g p i -> p g i"))
        w = lambda j: self.V("conv_w", j * 16 + cidx)
        self.ts("dve", acc, raw, w(3), self.V("conv_b", cidx), ALU.mult, ALU.add)
        for lag in (1, 2, 3):
            self.stt(acc[:, lag:T], raw[:, 0:T - lag], w(3 - lag), acc[:, lag:T], ALU.mult, ALU.add)
            self.stt(acc[:, T:NT], cs[:, 3 - lag, :], w(3 - lag), acc[:, T:NT], ALU.mult, ALU.add)

    def mlstm_head(self, l, h, R):
        A, S, C, I, O, P = self.A, self.S, self.C, self.I, self.O, self.P
        T, NS, NT = self.T, self.NS, self.NT
        NB = T // 128
        tl = ttiles(NT)
        A.release(self._ml_base)
        GBA, GBw = A.f32(T), A.f32(T)
        GBem = A.f32(NT)
        GBd = A.f32(NB)
        GBac = A.f32(NS)
        self.bcast_rows(GBA, R["Ac"], h, T)
        self.bcast_rows(GBw, R["wrow"], h, T)
        self.bcast_rows(GBem, R["emrow"], h, NT)
        self.bcast_rows(GBd, R["dec"], h, NB)
        self.bcast_rows(GBac, R["acs"], h, NS)
        Qb = [A.bf(NT) for _ in range(2)]
        Kb = [A.bf(NT) for _ in range(2)]
        Qw = [A.bf(T) for _ in range(2)]
        qs = A.f32(2 * NS).rearrange("p (k i) -> p k i", k=2)
        ks = A.f32(2 * NS).rearrange("p (k i) -> p k i", k=2)
        vs = A.f32(2 * NS).rearrange("p (k i) -> p k i", k=2)
        Vtm = A.bf(NB * 257).rearrange("p (b n) -> p b n", b=NB)
        Ktm = A.bf(NB * 256).rearrange("p (b n) -> p b n", b=NB)
        Hh = [A.f32(NT) for _ in range(2)]
        m1 = A.mark()
        raw, acc = A.f32(NT), A.f32(NT)
        cs = A.f32(3 * NS).rearrange("p (g i) -> p g i", g=3)
        vb = A.bf(T)
        for kc in range(2):
            self.conv_silu(l, h * 256 + kc * 128, h * 2 + kc, raw, acc, cs)
            self.act(acc, acc, AF.Silu)
            self.cp("dve", Qb[kc], acc)
            self.cp("pool", qs[:, kc, :], acc[:, T:NT])
            self.tt("dve", Qw[kc], acc[:, 0:T], GBw, ALU.mult)
            self.conv_silu(l, BW + h * 256 + kc * 128, 8 + h * 2 + kc, raw, acc, cs)
            self.act(acc, acc, AF.Silu)
            self.ts("dve", Kb[kc], acc, 0.0625, None, ALU.mult)
            self.ts("pool", ks[:, kc, :], acc[:, T:NT], 0.0625, None, ALU.mult)
            for bb in range(NB):
                ps = P.bank()
                self.mm(ps[:, 0:128], Kb[kc][:, bb * 128:(bb + 1) * 128], C["ident_b"])
                self.ts("dve", Ktm[:, bb, kc * 128:(kc + 1) * 128], ps[:, 0:128], R["ecol"][:, bb, h:h + 1], None, ALU.mult)
        self.memset("pool", Vtm[:, :, 256:257], 1.0)
        for vc in range(2):
            S.dma(raw, self.projT[MB + 2048 + h * 256 + vc * 128:MB + 2048 + h * 256 + (vc + 1) * 128, :])
            self.cp("dve", vb, raw[:, 0:T])
            self.cp("pool", vs[:, vc, :], raw[:, T:NT])
            for bb in range(NB):
                ps = P.bank()
                self.mm(ps[:, 0:128], vb[:, bb * 128:(bb + 1) * 128], C["ident_b"])
                self.cp("act", Vtm[:, bb, vc * 128:(vc + 1) * 128], ps[:, 0:128])
        A.release(m1)
        CM = A.f32(2 * 257).rearrange("p (k n) -> p k n", k=2)
        Cb = A.bf(2 * 256).rearrange("p (k n) -> p k n", k=2)
        nbc = A.bf(2 * 128).rearrange("p (k n) -> p k n", k=2)
        tmp = A.f32(128)
        ET = A.f32(128)
        PT = A.bf(128)
        absd = A.f32(128)
        self.memset("dve", CM, 0.0)
        self.memset("dve", Cb, 0.0)
        self.memset("dve", nbc, 0.0)
        for bb in range(NB):
            blk = slice(bb * 128, (bb + 1) * 128)
            ps = P.bank()
            self.mm(ps[:, 0:128], Kb[0][:, blk], Qb[0][:, blk], start=True, stop=False)
            self.mm(ps[:, 0:128], Kb[1][:, blk], Qb[1][:, blk], start=False, stop=True)
            self.stt(tmp, GBA[:, blk], -1.0, C["negm"], ALU.mult, ALU.add)
            self.act(ET, tmp, AF.Exp, bias=R["acol"][:, bb, h:h + 1])
            self.tt("dve", PT, ps[:, 0:128], ET, ALU.mult)
            pn = P.bank()
            for vc in range(2):
                o = pn[:, vc * 128:(vc + 1) * 128]
                self.mm(o, Vtm[:, bb, vc * 128:(vc + 1) * 128], PT, start=True, stop=False)
                self.mm(o, Cb[:, 0, vc * 128:(vc + 1) * 128], Qw[0][:, blk], start=False, stop=False)
                self.mm(o, Cb[:, 1, vc * 128:(vc + 1) * 128], Qw[1][:, blk], start=False, stop=True)
            pd = P.bank()
            self.mm(pd[:, 0:128], C["ones_b"], PT, start=True, stop=False)
            self.mm(pd[:, 0:128], nbc[:, 0, :], Qw[0][:, blk], start=False, stop=False)
            self.mm(pd[:, 0:128], nbc[:, 1, :], Qw[1][:, blk], start=False, stop=True)
            self.act(absd, pd[:, 0:128], AF.Abs)
            self.tt("dve", absd, absd, GBem[:, blk], ALU.max)
            self.recip(absd, absd)
            for vc in range(2):
                self.tt("dve", Hh[vc][:, blk], pn[:, vc * 128:(vc + 1) * 128], absd, ALU.mult)
            for kc in range(2):
                pc = P.bank()
                self.mm(pc[:, 0:257], Ktm[:, bb, kc * 128:(kc + 1) * 128], Vtm[:, bb, :])
                self.stt(CM[:, kc, :], CM[:, kc, :], GBd[:, bb:bb + 1], pc[:, 0:257], ALU.mult, ALU.add)
                self.cp("act", Cb[:, kc, :], CM[:, kc, 0:256])
                self.cp("pool", nbc[:, kc, :], CM[:, kc, 256:257].to_broadcast([128, 128]))
        for kc in range(2):
            S.dma(O["p_C"][l, h, kc * 128:(kc + 1) * 128, :], CM[:, kc, 0:256], eng="sp")
            self.dmas(O["p_n"][l, h, kc * 128:(kc + 1) * 128].unsqueeze(1), CM[:, kc, 256:257], eng="sp")
        m2 = A.mark()
        ktm = A.f32(256, parts=NS)
        vau = A.f32(257, parts=NS)
        ps = P.bank()
        for kc in range(2):
            self.tr(ps[0:NS, kc * 128:(kc + 1) * 128], ks[:, kc, :], C["ident_f"])
            self.tr(ps[0:NS, 256 + kc * 128:256 + (kc + 1) * 128], vs[:, kc, :], C["ident_f"])
        self.ts("dve", ktm, ps[0:NS, 0:256], R["wscol"][:, h:h + 1], None, ALU.mult)
        self.cp("act", vau[:, 0:256], ps[0:NS, 256:512])
        self.memset("dve", vau[:, 256:257], 1.0)
        Kex = A.f32(NS * 256, parts=NS).rearrange("p (i c) -> p i c", i=NS)
        eye = C["ident_f"][0:NS, 0:NS].unsqueeze(2).to_broadcast([NS, NS, 256])
        self.tt("dve", Kex, ktm.unsqueeze(1).to_broadcast([NS, NS, 256]), eye, ALU.mult)
        CS = [A.f32(2 * 257).rearrange("p (k n) -> p k n", k=2) for _ in range(2)]
        nb2 = [A.f32(2 * 128).rearrange("p (k n) -> p k n", k=2) for _ in range(2)]
        rb, pN = P.reserve()
        for i in range(NS):
            k2 = i % 2
            cs_ = CS[k2]
            for kc in range(2):
                S.dma(cs_[:, kc, 0:256], I["s_C"][l, i, h, kc * 128:(kc + 1) * 128, :])
                self.dmas(cs_[:, kc, 256:257], I["s_n"][l, i, h, kc * 128:(kc + 1) * 128].unsqueeze(1))
            for kc in range(2):
                pc = P.bank()
                self.mm(pc[:, 0:257], Kex[:, i, kc * 128:(kc + 1) * 128], vau)
                self.stt(cs_[:, kc, :], cs_[:, kc, :], GBac[:, i:i + 1], pc[:, 0:257], ALU.mult, ALU.add)
                S.dma(O["s_C"][l, i, h, kc * 128:(kc + 1) * 128, :], cs_[:, kc, 0:256], eng="sp")
                self.dmas(O["s_n"][l, i, h, kc * 128:(kc + 1) * 128].unsqueeze(1), cs_[:, kc, 256:257], eng="sp")
                self.cp("pool", nb2[k2][:, kc, :], cs_[:, kc, 256:257].to_broadcast([128, 128]))
            for vc in range(2):
                o = pN[:, vc * NS + i:vc * NS + i + 1]
                self.mm(o, cs_[:, 0, vc * 128:(vc + 1) * 128], qs[:, 0, i:i + 1], start=True, stop=False)
                self.mm(o, cs_[:, 1, vc * 128:(vc + 1) * 128], qs[:, 1, i:i + 1], start=False, stop=True)
            o = pN[:, 2 * NS + i:2 * NS + i + 1]
            self.mm(o, nb2[k2][:, 0, :], qs[:, 0, i:i + 1], start=True, stop=False)
            self.mm(o, nb2[k2][:, 1, :], qs[:, 1, i:i + 1], start=False, stop=True)
        ad = A.f32(NS)
        self.act(ad, pN[:, 2 * NS:3 * NS], AF.Abs)
        self.tt("dve", ad, ad, GBem[:, T:NT], ALU.max)
        self.recip(ad, ad)
        for vc in range(2):
            self.tt("dve", Hh[vc][:, T:NT], pN[:, vc * NS:(vc + 1) * NS], ad, ALU.mult)
        P.unreserve(rb)
        A.release(m2)
        self.head_norm_out(Hh, MB + 3080 + h * 256, "m_ng", h * 2, AF.Sigmoid, BW + h * 256)

    def head_norm_out(self, Hh, gate_row0, gname, gidx0, gfunc, ob_row0):
        A, S, C, P = self.A, self.S, self.C, self.P
        NT = self.NT
        tl = ttiles(NT)
        m = A.mark()
        sq = [A.bf(NT) for _ in range(2)]
        rs = A.f32(NT)
        graw = A.f32(NT)
        obf = A.bf(NT)
        for vc in range(2):
            self.act(sq[vc], Hh[vc], AF.Square)
        for (t0, tn) in tl:
            ps = P.bank()
            self.mm(ps[:, 0:tn], C["ones_b"], sq[0][:, t0:t0 + tn], start=True, stop=False)
            self.mm(ps[:, 0:tn], C["ones_b"], sq[1][:, t0:t0 + tn], start=False, stop=True)
            self.act(rs[:, t0:t0 + tn], ps[:, 0:tn], AF.Sqrt, bias=C["eps"], scale=1.0 / 256)
        self.recip(rs, rs)
        for vc in range(2):
            S.dma(graw, self.projT[gate_row0 + vc * 128:gate_row0 + (vc + 1) * 128, :])
            self.act(graw, graw, gfunc)
            self.tt("dve", Hh[vc], Hh[vc], rs, ALU.mult)
            self.stt(obf, Hh[vc], self.V(gname, gidx0 + vc), graw, ALU.mult, ALU.mult)
            S.dma(self.obT[ob_row0 + vc * 128:ob_row0 + (vc + 1) * 128, :], obf, eng="sp")
        A.release(m)

    def gla(self, l):
        A, S, C, I, O, P = self.A, self.S, self.C, self.I, self.O, self.P
        T, NS, NT = self.T, self.NS, self.NT
        NB = T // 128
        tl = ttiles(NT)
        A.release(self.abase)
        a2b = A.bf(512, parts=16)
        xab = A.bf(NT, parts=16)
        m1 = A.mark()
        a2s = A.f32(512, parts=16)
        xas = A.f32(NT, parts=16)
        S.dma(a2s, I["ga2"][l])
        S.dma(xas, self.projT[GB_ + 2048:GB_ + 2064, :])
        self.cp("dve", a2b, a2s)
        self.cp("dve", xab, xas)
        A.release(m1)
        base = A.mark()
        v3 = lambda ap: ap[:, 0:T].rearrange("p (c j) -> p c j", j=128)
        for h in range(4):
            A.release(base)
            lg, bc, e1 = A.f32(NT), A.f32(NT), A.f32(NT)
            raw = A.f32(NT)
            Qh, Kh = A.bf(T), A.bf(T)
            WLg = A.f32(NB)
            qs, ksm, dgs = A.f32(NS), A.f32(NS), A.f32(NS)
            vs = A.f32(2 * NS).rearrange("p (k i) -> p k i", k=2)
            Vtm = A.bf(NB * 256).rearrange("p (b n) -> p b n", b=NB)
            Ktm = A.bf(NB * 128).rearrange("p (b n) -> p b n", b=NB)
            OG = [A.f32(NT) for _ in range(2)]
            vb = A.bf(T)
            for (t0, tn) in tl:
                ps = P.bank()
                self.mm(ps[:, 0:tn], a2b[:, h * 128:(h + 1) * 128], xab[:, t0:t0 + tn])
                self.act(lg[:, t0:t0 + tn], ps[:, 0:tn], AF.Sigmoid, bias=self.V("ga_b", h))
            self.act(lg, lg, AF.Ln)
            self.ts("pool", lg, lg, 1.0 / 16, None, ALU.mult)
            self.scan(bc[:, 0:T], C["rst128"], lg[:, 0:T], 0.0, ALU.mult, ALU.add)
            self.act(WLg, v3(bc)[:, :, 127], AF.Exp)
            self.act(dgs, lg[:, T:NT], AF.Exp)
            S.dma(raw, self.projT[GB_ + h * 128:GB_ + (h + 1) * 128, :])
            self.act(e1[:, 0:T], bc[:, 0:T], AF.Exp)
            self.stt(Qh, raw[:, 0:T], 128.0 ** -0.5, e1[:, 0:T], ALU.mult, ALU.mult)
            self.ts("pool", qs, raw[:, T:NT], 128.0 ** -0.5, None, ALU.mult)
            S.dma(raw, self.projT[GB_ + 512 + h * 128:GB_ + 512 + (h + 1) * 128, :])
            self.act(e1[:, 0:T], bc[:, 0:T], AF.Exp, scale=-1.0)
            self.tt("dve", Kh, raw[:, 0:T], e1[:, 0:T], ALU.mult)
            self.cp("pool", ksm, raw[:, T:NT])
            for bb in range(NB):
                ps = P.bank()
                self.mm(ps[:, 0:128], Kh[:, bb * 128:(bb + 1) * 128], C["ident_b"])
                self.cp("act", Ktm[:, bb, :], ps[:, 0:128])
            for vc in range(2):
                S.dma(raw, self.projT[GB_ + 1024 + h * 256 + vc * 128:GB_ + 1024 + h * 256 + (vc + 1) * 128, :])
                self.cp("dve", vb, raw[:, 0:T])
                self.cp("pool", vs[:, vc, :], raw[:, T:NT])
                for bb in range(NB):
                    ps = P.bank()
                    self.mm(ps[:, 0:128], vb[:, bb * 128:(bb + 1) * 128], C["ident_b"])
                    self.cp("act", Vtm[:, bb, vc * 128:(vc + 1) * 128], ps[:, 0:128])
            SM, SMs = A.f32(256), A.f32(256)
            Sb = A.bf(256)
            PTg = A.bf(128)
            self.memset("dve", SM, 0.0)
            self.memset("dve", Sb, 0.0)
            for bb in range(NB):
                blk = slice(bb * 128, (bb + 1) * 128)
                ps = P.bank()
                self.mm(ps[:, 0:128], Kh[:, blk], Qh[:, blk])
                self.tt("dve", PTg, ps[:, 0:128], C["m_le_b"], ALU.mult)
                po = P.bank()
                for vc in range(2):
                    o = po[:, vc * 128:(vc + 1) * 128]
                    self.mm(o, Vtm[:, bb, vc * 128:(vc + 1) * 128], PTg, start=True, stop=False)
                    self.mm(o, Sb[:, vc * 128:(vc + 1) * 128], Qh[:, blk], start=False, stop=True)
                    self.cp("act", OG[vc][:, blk], o)
                pz = P.bank()
                self.mm(pz[:, 0:256], Ktm[:, bb, :], Vtm[:, bb, :])
                self.ts("pool", SMs, SM, WLg[:, bb:bb + 1], None, ALU.mult)
                self.stt(SM, pz[:, 0:256], WLg[:, bb:bb + 1], SMs, ALU.mult, ALU.add)
                self.cp("act", Sb, SM)
            S.dma(O["p_gla"][l, h], SM, eng="sp")
            ktm = A.f32(128, parts=NS)
            vtm = A.f32(256, parts=NS)
            ps = P.bank()
            self.tr(ps[0:NS, 0:128], ksm, C["ident_f"])
            for vc in range(2):
                self.tr(ps[0:NS, 128 + vc * 128:128 + (vc + 1) * 128], vs[:, vc, :], C["ident_f"])
            self.cp("act", ktm, ps[0:NS, 0:128])
            self.cp("act", vtm, ps[0:NS, 128:384])
            Kex = A.f32(NS * 128, parts=NS).rearrange("p (i c) -> p i c", i=NS)
            eye = C["ident_f"][0:NS, 0:NS].unsqueeze(2).to_broadcast([NS, NS, 128])
            self.tt("dve", Kex, ktm.unsqueeze(1).to_broadcast([NS, NS, 128]), eye, ALU.mult)
            SS = [A.f32(256) for _ in range(2)]
            rb, pO = P.reserve()
            for i in range(NS):
                ss = SS[i % 2]
                S.dma(ss, I["s_gla"][l, i, h])
                pz = P.bank()
                self.mm(pz[:, 0:256], Kex[:, i, :], vtm)
                self.stt(ss, ss, dgs[:, i:i + 1], pz[:, 0:256], ALU.mult, ALU.add)
                S.dma(O["s_gla"][l, i, h], ss, eng="sp")
                for vc in range(2):
                    self.mm(pO[:, vc * NS + i:vc * NS + i + 1], ss[:, vc * 128:(vc + 1) * 128], qs[:, i:i + 1])
            for vc in range(2):
                self.cp("act", OG[vc][:, T:NT], pO[:, vc * NS:(vc + 1) * NS])
            P.unreserve(rb)
            self.head_norm_out(OG, GB_ + 2064 + h * 256, "g_ng", h * 2, AF.Silu, 2 * BW + h * 256)

    def merge(self, l):
        A, S, C, I, P = self.A, self.S, self.C, self.I, self.P
        NT = self.NT
        tl = ttiles(NT)
        A.release(self.abase)
        OB = A.bf(24 * NT).rearrange("p (c t) -> p c t", c=24)
        S.dma(OB, self.obT.rearrange("(c p) t -> p c t", p=128))
        wst = [A.f32(8 * 128).rearrange("p (k n) -> p k n", k=8) for _ in range(2)]
        wbf = [A.bf(8 * 128).rearrange("p (k n) -> p k n", k=8) for _ in range(2)]
        graw = [A.f32(NT) for _ in range(2)]
        accs = [A.f32(NT) for _ in range(2)]
        tmp = A.f32(512)
        mgb = [A.bf(NT) for _ in range(2)]
        it = 0
        for dc in range(16):
            acc = accs[dc % 2]
            for j in range(3):
                ws, wb, gr = wst[it % 2], wbf[it % 2], graw[it % 2]
                it += 1
                S.dma(ws, I["w_branch"][l, j, :, dc * 128:(dc + 1) * 128].rearrange("(k p) n -> p k n", p=128))
                self.cp("pool", wb, ws)
                S.dma(gr, self.projT[GTB + j * D + dc * 128:GTB + j * D + (dc + 1) * 128, :])
                self.act(gr, gr, AF.Sigmoid, bias=self.V("gate_b", j * 16 + dc))
                banks = P.banks(len(tl))
                for k in range(8):
                    for bi, (t0, tn) in enumerate(tl):
                        self.mm(banks[bi][:, 0:tn], wb[:, k, :], OB[:, j * 8 + k, t0:t0 + tn],
                                start=(k == 0), stop=(k == 7))
                for bi, (t0, tn) in enumerate(tl):
                    if j == 0:
                        self.tt("dve", acc[:, t0:t0 + tn], banks[bi][:, 0:tn], gr[:, t0:t0 + tn], ALU.mult)
                    else:
                        self.tt("dve", tmp[:, 0:tn], banks[bi][:, 0:tn], gr[:, t0:t0 + tn], ALU.mult)
                        self.tt("pool", acc[:, t0:t0 + tn], acc[:, t0:t0 + tn], tmp[:, 0:tn], ALU.add)
            self.cp("act", mgb[dc % 2], acc)
            S.dma(self.mgT[dc * 128:(dc + 1) * 128, :], mgb[dc % 2], eng="act")

    def resid_sink(self, it, m, rows, banks, tl):
        A, S = self.A, self.S
        A.release(self._dense_m0)
        stg = [A.f32(self.NT) for _ in range(2)]
        o = stg[it % 2]
        S.dma(o, self._res_src[m * 128:(m + 1) * 128, :])
        for bi, (t0, tn) in enumerate(tl):
            self.tt("dve", o[:, t0:t0 + tn], o[:, t0:t0 + tn], banks[bi][:, 0:tn], ALU.add)
        S.dma(self.xT[m * 128:(m + 1) * 128, :], o, eng="sp")

    def outproj(self, l):
        A, S, I = self.A, self.S, self.I
        NT = self.NT
        A.release(self.abase)
        MG = A.bf(16 * NT).rearrange("p (c t) -> p c t", c=16)
        S.dma(MG, self.mgT.rearrange("(c p) t -> p c t", p=128))
        self._res_src = self.src
        self.dense(I["w_out"][l], D, 16, MG, self.resid_sink)

    def ffn(self, l):
        A, S, I = self.A, self.S, self.I
        NT = self.NT
        NF = DFF // 128
        order = []
        for fc in range(NF):
            order += [fc, NF + fc]
        self.dense(I["w_gu"][l], 2 * DFF, 16, self.hT, self.gu_sink, order=order)
        self._res_src = self.xT
        for half in range(2):
            A.release(self.abase)
            AH = A.bf((NF // 2) * NT).rearrange("p (c t) -> p c t", c=NF // 2)
            r0 = half * (DFF // 2)
            S.dma(AH, self.actT[r0:r0 + DFF // 2, :].rearrange("(c p) t -> p c t", p=128))
            self.dense(I["w_down"][l, r0:r0 + DFF // 2, :], D, NF // 2, AH, self.resid_sink)

    def gu_sink(self, it, m, rows, banks, tl):
        A, S = self.A, self.S
        A.release(self._dense_m0)
        sil = A.f32(self.NT)
        ab = [A.bf(self.NT) for _ in range(2)]
        if it % 2 == 0:
            for bi, (t0, tn) in enumerate(tl):
                self.act(sil[:, t0:t0 + tn], banks[bi][:, 0:tn], AF.Silu)
        else:
            o = ab[(it // 2) % 2]
            fc = m - DFF // 128
            for bi, (t0, tn) in enumerate(tl):
                self.tt("dve", o[:, t0:t0 + tn], sil[:, t0:t0 + tn], banks[bi][:, 0:tn], ALU.mult)
            S.dma(self.actT[fc * 128:(fc + 1) * 128, :], o, eng="sp")


def _cols(v, n):
    return np.ascontiguousarray(np.asarray(v).reshape(n, 128).T)


def pack_vec(inp, l):
    out = np.zeros((128, VEC_N), np.float32)

    def put(name, arr):
        off, n = VEC_OFF[name]
        out[:arr.shape[0], off:off + n] = arr

    put("n1g", _cols(inp["norm1_g"][l], 16))
    put("n2g", _cols(inp["norm2_g"][l], 16))
    put("gate_b", _cols(inp["gate_b"][l].reshape(-1), 48))
    mu = np.asarray(inp["rwkv_mu"][l])
    put("mu", _cols(mu[:3072], 24))
    put("mul", np.ascontiguousarray(mu[3072:3264].reshape(3, 64).T))
    for nm, key in [("w0", "rwkv_w0"), ("a0", "rwkv_a0"), ("k_k", "rwkv_k_k"), ("k_a", "rwkv_k_a"),
                    ("ln_g", "rwkv_ln_g"), ("ln_b", "rwkv_ln_b"), ("m_ng", "mlstm_norm_g"), ("g_ng", "gla_norm_g")]:
        put(nm, _cols(inp[key][l], 8))
    put("r_k", _cols(np.asarray(inp["rwkv_r_k"][l]).reshape(-1), 8))
    put("conv_w", _cols(np.asarray(inp["mlstm_conv_w"][l]).reshape(-1), 64))
    put("conv_b", _cols(inp["mlstm_conv_b"][l], 16))
    put("ga_b", _cols(inp["gla_a_b"][l], 4))
    return out


_NC_CACHE = {}


def make_in_map(inp, L, xT, samp):
    f = lambda a: np.ascontiguousarray(np.asarray(a, dtype=np.float32))
    m = {}
    m["xT"] = f(xT)
    m["vec"] = np.stack([pack_vec(inp, l) for l in range(L)])
    m["rows"] = f(np.stack([np.stack([inp["mlstm_i_b"][l], inp["mlstm_f_b"][l]], axis=1) for l in range(L)]))
    m["w_in"] = f(inp["w_in"][:L])
    m["rw2"] = f(np.stack([np.stack([inp["rwkv_w2"][l], inp["rwkv_a2"][l], inp["rwkv_g2"][l]]) for l in range(L)]))
    m["ga2"] = f(inp["gla_a2"][:L])
    m["w_branch"] = f(inp["w_branch"][:L])
    m["w_out"] = f(inp["w_out"][:L])
    m["w_gu"] = f(inp["ffn_w_gu"][:L])
    m["w_down"] = f(inp["ffn_w_down"][:L])
    m["fng"] = _cols(inp["final_norm_g"], 16)
    m["s_shift"] = f(np.transpose(inp["state_rwkv_shift"][:L, samp], (0, 2, 1)))
    m["s_wkv"] = f(inp["state_rwkv_wkv"][:L, samp])
    m["s_conv"] = f(np.transpose(inp["state_mlstm_conv"][:L, samp], (0, 2, 3, 1)))
    m["s_C"] = f(inp["state_mlstm_C"][:L, samp])
    m["s_n"] = f(inp["state_mlstm_n"][:L, samp])
    m["s_m"] = f(np.transpose(inp["state_mlstm_m"][:L, samp], (0, 2, 1)))
    m["s_gla"] = f(inp["state_gla_S"][:L, samp])
    return m


def kernel(**inputs):
    inp = {k: np.asarray(v) for k, v in inputs.items()}
    B, T, _ = inp["x_prompt"].shape
    NSALL = inp["x_sample"].shape[0]
    L = inp["w_in"].shape[0]
    NS = NSALL // NCORES
    key = (T, NS, L)
    if key not in _NC_CACHE:
        _NC_CACHE[key] = Builder(T, NS, L).build()
    nc = _NC_CACHE[key]
    in_maps = []
    for c in range(NCORES):
        seq = c // 2
        samp = slice(c * NS, (c + 1) * NS)
        xT = np.concatenate([inp["x_prompt"][seq].T, inp["x_sample"][samp, 0, :].T], axis=1)
        in_maps.append(make_in_map(inp, L, xT, samp))
    res = run_bass_kernel_spmd(nc, in_maps, core_ids=list(range(NCORES)))
    R = res.results
    P = [R[2 * s] for s in range(B)]
    f32 = np.float32
    y_prompt = np.stack([P[s]["yT"][:, :T].T for s in range(B)]).astype(f32)
    y_sample = np.concatenate([R[c]["yT"][:, T:].T for c in range(NCORES)])[:, None, :].astype(f32)
    p_shift = np.stack([P[s]["o_shift"][:, :, 0] for s in range(B)], axis=1)
    s_shift = np.concatenate([np.transpose(R[c]["o_shift"][:, :, 1:], (0, 2, 1)) for c in range(NCORES)], axis=1)
    p_wkv = np.stack([P[s]["p_wkv"] for s in range(B)], axis=1)
    s_wkv = np.concatenate([R[c]["o_s_wkv"] for c in range(NCORES)], axis=1)
    p_conv = np.stack([np.transpose(P[s]["o_conv"][:, :, :, 0], (0, 1, 2)) for s in range(B)], axis=1)
    s_conv = np.concatenate([np.transpose(R[c]["o_conv"][:, :, :, 1:], (0, 3, 1, 2)) for c in range(NCORES)], axis=1)
    p_C = np.stack([P[s]["p_C"] for s in range(B)], axis=1)
    s_C = np.concatenate([R[c]["o_s_C"] for c in range(NCORES)], axis=1)
    p_n = np.stack([P[s]["p_n"] for s in range(B)], axis=1)
    s_n = np.concatenate([R[c]["o_s_n"] for c in range(NCORES)], axis=1)
    p_m = np.stack([P[s]["o_m"][:, :, 0] for s in range(B)], axis=1)
    s_m = np.concatenate([np.transpose(R[c]["o_m"][:, :, 1:], (0, 2, 1)) for c in range(NCORES)], axis=1)
    p_gla = np.stack([P[s]["p_gla"] for s in range(B)], axis=1)
    s_gla = np.concatenate([R[c]["o_s_gla"] for c in range(NCORES)], axis=1)
    outs = (y_prompt, y_sample, p_shift, p_wkv, p_conv, p_C, p_n, p_m, p_gla,
            s_shift, s_wkv, s_conv, s_C, s_n, s_m, s_gla)
    return tuple(np.ascontiguousarray(o, dtype=f32) for o in outs)
```

```python
from contextlib import ExitStack
import numpy as np
import concourse.bass as bass
import concourse.mybir as mybir
from concourse.bass_utils import run_bass_kernel_spmd

F32 = mybir.dt.float32
BF16 = mybir.dt.bfloat16
AF = mybir.ActivationFunctionType
ALU = mybir.AluOpType
AX = mybir.AxisListType
_ESZ = {F32: 4, BF16: 2, mybir.dt.int32: 4, mybir.dt.uint8: 1}

D = 2048
BW = 1024
R_COLS = 3264
M_COLS = 4104
G_COLS = 3088
N_IN = 16600
DFF = 5632
RB, MB, GB_, GTB = 0, 3264, 7368, 10456
EPS = 1e-6
R_GN_EPS = 64e-5
NCORES = 8


class _Rec:
    __slots__ = ("sp", "plo", "phi", "lo", "hi", "lw", "rde", "rdd", "cells", "dead")


class _Op:
    __slots__ = ("eng", "fn", "dma", "deps", "signal", "semval", "semidx")


def _region(ap):
    t = ap.tensor
    tn = type(t).__name__
    esz = _ESZ[ap.dtype]
    pairs = ap.ap
    if tn == "DRamTensorHandle":
        ext = 1
        for st, cnt in pairs:
            ext += (cnt - 1) * abs(st)
        lo = ap.offset * esz
        return ("d" + t.name, 0, 1, lo, lo + ext * esz)
    tshape = t.shape
    rowb = 1
    for s in tshape[1:]:
        rowb *= s
    rowb *= _ESZ[t.dtype]
    off = ap.offset * esz
    p0 = off // rowb
    c0 = off % rowb
    ext = 1
    for st, cnt in pairs[1:]:
        ext += (cnt - 1) * abs(st)
    np_ = pairs[0][1]
    if tn == "SBTensorHandle":
        return ("s", p0, p0 + np_, c0, c0 + ext * esz)
    lo = (c0 // 2048) * 2048
    hi = ((c0 + ext * esz + 2047) // 2048) * 2048
    return ("p", 0, 128, lo, hi)


class Sched:
    ENGS = ("pe", "act", "dve", "pool", "sp")

    def __init__(self, nc, n_dma_sems=64):
        self.nc = nc
        self.ops = []
        self.nds = n_dma_sems
        self.eng_obj = {"pe": nc.tensor, "act": nc.scalar, "dve": nc.vector, "pool": nc.gpsimd, "sp": nc.sync}
        self.recs = {}
        self.grid = {}
        self.cellsz = {}

    def _cells(self, key):
        sp, plo, phi, lo, hi = key
        if sp == "s":
            cs = 2048
        elif sp == "p":
            cs = 2048
        else:
            cs = 1 << 20
        return [(sp, c) for c in range(lo // cs, (hi - 1) // cs + 1)]

    def _get(self, key):
        r = self.recs.get(key)
        if r is None:
            r = _Rec()
            r.sp, r.plo, r.phi, r.lo, r.hi = key
            r.lw = -1
            r.rde = {}
            r.rdd = []
            r.cells = self._cells(key)
            r.dead = False
            self.recs[key] = r
            for c in r.cells:
                self.grid.setdefault(c, set()).add(key)
        return r

    def _overlaps(self, key):
        sp, plo, phi, lo, hi = key
        out = set()
        for c in self._cells(key):
            g = self.grid.get(c)
            if g:
                out |= g
        res = []
        for k in out:
            if k[1] < phi and plo < k[2] and k[3] < hi and lo < k[4]:
                res.append(k)
        return res

    def op(self, eng, fn, reads=(), writes=(), dma=False):
        i = len(self.ops)
        o = _Op()
        o.eng, o.fn, o.dma = eng, fn, dma
        o.signal = False
        o.semval = 0
        o.semidx = -1
        deps = set()
        rkeys = [_region(a) for a in reads]
        wkeys = [_region(a) for a in writes]
        for k in rkeys:
            for q in self._overlaps(k):
                r = self.recs[q]
                if r.lw >= 0:
                    deps.add(r.lw)
        for k in wkeys:
            for q in self._overlaps(k):
                r = self.recs[q]
                if r.lw >= 0:
                    deps.add(r.lw)
                for v in r.rde.values():
                    deps.add(v)
                for v in r.rdd:
                    deps.add(v)
        keep = []
        for d in deps:
            od = self.ops[d]
            if (not od.dma) and (not dma) and od.eng == eng and eng == "pe":
                continue
            keep.append(d)
            od.signal = True
        o.deps = keep
        self.ops.append(o)
        for k in rkeys:
            r = self._get(k)
            if dma:
                r.rdd.append(i)
            else:
                r.rde[eng] = i
        for k in wkeys:
            for q in self._overlaps(k):
                if q != k and q[1] >= k[1] and q[2] <= k[2] and q[3] >= k[3] and q[4] <= k[4]:
                    r = self.recs.pop(q)
                    for c in r.cells:
                        self.grid[c].discard(q)
            r = self._get(k)
            r.lw = i
            r.rde = {}
            r.rdd = []
        if dma:
            o.signal = True
        return o

    def dma(self, out, in_, eng="sp", **kw):
        return self.op(eng, lambda e: e.dma_start(out=out, in_=in_, **kw), [in_], [out], dma=True)

    def emit(self, stack):
        nc = self.nc
        cnt = {e: 0 for e in self.ENGS}
        dcnt = [0] * self.nds
        di = 0
        for o in self.ops:
            if not o.signal:
                continue
            if o.dma:
                o.semidx = di % self.nds
                dcnt[o.semidx] += 16
                o.semval = dcnt[o.semidx]
                di += 1
            else:
                cnt[o.eng] += 1
                o.semval = cnt[o.eng]
        sems = {e: stack.enter_context(nc.semaphore("s_" + e)) for e in self.ENGS}
        dsems = [stack.enter_context(nc.semaphore("d_%d" % i)) for i in range(self.nds)]
        waited = {e: {} for e in self.ENGS}
        nw = 0
        for o in self.ops:
            e = self.eng_obj[o.eng]
            need = {}
            for d in o.deps:
                od = self.ops[d]
                key = ("d", od.semidx) if od.dma else ("e", od.eng)
                if od.semval > need.get(key, 0):
                    need[key] = od.semval
            wd = waited[o.eng]
            if o.dma and o.semval > 16:
                k0 = ("d", o.semidx)
                if o.semval - 16 > need.get(k0, 0):
                    need[k0] = o.semval - 16
            for key, val in need.items():
                if wd.get(key, 0) >= val:
                    continue
                wd[key] = val
                e.wait_ge(dsems[key[1]] if key[0] == "d" else sems[key[1]], val)
                nw += 1
            ins = o.fn(e)
            if o.signal:
                if o.dma:
                    ins.then_inc(dsems[o.semidx], 16)
                else:
                    ins.then_inc(sems[o.eng], 1)
        for i, v in enumerate(dcnt):
            if v:
                nc.sync.wait_ge(dsems[i], v)
        return nw


class Arena:
    def __init__(self, t, ncols):
        self.t = t
        self.ncols = ncols
        self.top = 0

    def mark(self):
        return self.top

    def release(self, m):
        self.top = m

    def f32(self, cols, parts=128):
        c0 = self.top
        self.top += cols
        assert self.top <= self.ncols, ("arena overflow", self.top, self.ncols)
        return self.t[0:parts, c0:c0 + cols]

    def bf(self, cols, parts=128):
        n32 = (cols + 1) // 2
        a = self.f32(n32, parts)
        return a.bitcast(BF16)[:, 0:cols]


class Psum:
    def __init__(self, t):
        self.t = t
        self.nb = 0

    def bank(self):
        b = self.nb % 8
        self.nb += 1
        return self.t[:, b * 512:(b + 1) * 512]

    def banks(self, n):
        return [self.bank() for _ in range(n)]


def ttiles(NT):
    out = []
    c = 0
    while c < NT:
        n = min(512, NT - c)
        out.append((c, n))
        c += n
    return out


VEC_OFF = {}
_o = 0
for _n, _c in [("n1g", 16), ("n2g", 16), ("gate_b", 48), ("mu", 24), ("mul", 3), ("w0", 8), ("a0", 8), ("k_k", 8),
               ("k_a", 8), ("r_k", 8), ("ln_g", 8), ("ln_b", 8), ("conv_w", 64), ("conv_b", 16), ("m_ng", 8),
               ("g_ng", 8), ("ga_b", 4)]:
    VEC_OFF[_n] = (_o, _c)
    _o += _c
VEC_N = _o
DV_OFF = {"omu": (0, 24), "omul": (24, 3), "omk_a": (27, 8)}
DV_N = 35


class Psum2(Psum):
    def __init__(self, t):
        Psum.__init__(self, t)
        self.reserved = set()

    def bank(self):
        while True:
            b = self.nb % 8
            self.nb += 1
            if b not in self.reserved:
                return self.t[:, b * 512:(b + 1) * 512]

    def reserve(self):
        while True:
            b = self.nb % 8
            self.nb += 1
            if b not in self.reserved:
                self.reserved.add(b)
                return b, self.t[:, b * 512:(b + 1) * 512]

    def unreserve(self, b):
        self.reserved.discard(b)


class StopBuild(Exception):
    pass


class Builder:
    def chk(self, name):
        import os
        if os.environ.get("SUBSTOP") == name:
            raise StopBuild()

    def __init__(self, T, NS, depth, dbg=(), stop=None):
        self.T, self.NS, self.NT, self.depth = T, NS, T + NS, depth
        self.dbg = set(dbg)
        self.stop = stop
        assert T % 128 == 0

    def dram(self, name, shape, dt=F32, kind=None):
        if kind is None:
            kind = "ExternalOutput" if name in self.dbg else "Internal"
        return self.nc.dram_tensor(name, list(shape), dt, kind=kind).ap()

    def build(self):
        nc = bass.Bass("TRN2", target_bir_lowering=False)
        self.nc = nc
        T, NS, NT, L = self.T, self.NS, self.NT, self.depth
        inp = lambda n, s: self.dram(n, s, kind="ExternalInput")
        outp = lambda n, s: self.dram(n, s, kind="ExternalOutput")
        I = self.I = {}
        I["xT"] = inp("xT", [D, NT])
        I["vec"] = inp("vec", [L, 128, VEC_N])
        I["rows"] = inp("rows", [L, 4, 2])
        I["w_in"] = inp("w_in", [L, D, N_IN])
        I["rw2"] = inp("rw2", [L, 3, 64, BW])
        I["ga2"] = inp("ga2", [L, 16, 512])
        I["w_branch"] = inp("w_branch", [L, 3, BW, D])
        I["w_out"] = inp("w_out", [L, D, D])
        I["w_gu"] = inp("w_gu", [L, D, 2 * DFF])
        I["w_down"] = inp("w_down", [L, DFF, D])
        I["fng"] = inp("fng", [128, 16])
        I["s_shift"] = inp("s_shift", [L, R_COLS, NS])
        I["s_wkv"] = inp("s_wkv", [L, NS, 16, 64, 64])
        I["s_conv"] = inp("s_conv", [L, 3, D, NS])
        I["s_C"] = inp("s_C", [L, NS, 4, 256, 256])
        I["s_n"] = inp("s_n", [L, NS, 4, 256])
        I["s_m"] = inp("s_m", [L, 4, NS])
        I["s_gla"] = inp("s_gla", [L, NS, 4, 128, 256])
        O = self.O = {}
        O["yT"] = outp("yT", [D, NT])
        O["o_shift"] = outp("o_shift", [L, R_COLS, 1 + NS])
        O["p_wkv"] = outp("p_wkv", [L, 16, 64, 64])
        O["s_wkv"] = outp("o_s_wkv", [L, NS, 16, 64, 64])
        O["o_conv"] = outp("o_conv", [L, 3, D, 1 + NS])
        O["p_C"] = outp("p_C", [L, 4, 256, 256])
        O["p_n"] = outp("p_n", [L, 4, 256])
        O["s_C"] = outp("o_s_C", [L, NS, 4, 256, 256])
        O["s_n"] = outp("o_s_n", [L, NS, 4, 256])
        O["o_m"] = outp("o_m", [L, 4, 1 + NS])
        O["p_gla"] = outp("p_gla", [L, 4, 128, 256])
        O["s_gla"] = outp("o_s_gla", [L, NS, 4, 128, 256])
        self.xT = self.dram("x_scr", [D, NT])
        self.projT = self.dram("projT", [N_IN, NT])
        self.obT = self.dram("obT", [3 * BW, NT], BF16)
        self.mgT = self.dram("mgT", [D, NT], BF16)
        self.actT = self.dram("actT", [DFF, NT], BF16)

        with ExitStack() as st:
            ACOLS = 52800
            at = st.enter_context(nc.sbuf_tensor("arena", [128, ACOLS], F32))
            pt = st.enter_context(nc.psum_tensor("psum", [128, 4096], F32))
            self.A = Arena(at, ACOLS)
            self.P = Psum2(pt)
            self.S = Sched(nc)
            self.consts()
            done = True
            try:
                for l in range(L):
                    if not self.layer(l):
                        done = False
                        break
            except StopBuild:
                done = False
            if done:
                self.S.dma(self.C["vec"][:, 0:16], I["fng"])
                self.rmsnorm(self.xT, "n1g", out_dram=O["yT"])
            self.nwaits = self.S.emit(st)
        return nc

    def act(self, out, in_, func, bias=None, scale=1.0):
        kw = {}
        rd = [in_]
        if bias is not None:
            kw["bias"] = bias
            if not isinstance(bias, (int, float)):
                rd.append(bias)
        if not isinstance(scale, (int, float)):
            rd.append(scale)
        self.S.op("act", lambda e: e.activation(out=out, in_=in_, func=func, scale=scale, **kw), rd, [out])

    def ts(self, eng, out, in0, s1, s2, op0, op1=None):
        rd = [in0] + [s for s in (s1, s2) if s is not None and not isinstance(s, (int, float))]
        if op1 is None:
            self.S.op(eng, lambda e: e.tensor_scalar(out=out, in0=in0, scalar1=s1, scalar2=None, op0=op0), rd, [out])
        else:
            self.S.op(eng, lambda e: e.tensor_scalar(out=out, in0=in0, scalar1=s1, scalar2=s2, op0=op0, op1=op1),
                      rd, [out])

    def stt(self, out, in0, sc, in1, op0, op1):
        rd = [in0, in1] + ([] if isinstance(sc, (int, float)) else [sc])
        self.S.op("dve", lambda e: e.scalar_tensor_tensor(out=out, in0=in0, scalar=sc, in1=in1, op0=op0, op1=op1),
                  rd, [out])

    def tt(self, eng, out, in0, in1, op):
        self.S.op(eng, lambda e: e.tensor_tensor(out=out, in0=in0, in1=in1, op=op), [in0, in1], [out])

    def cp(self, eng, out, in_):
        if eng == "act":
            self.S.op("act", lambda e: e.copy(out=out, in_=in_), [in_], [out])
        else:
            self.S.op(eng, lambda e: e.tensor_copy(out=out, in_=in_), [in_], [out])

    def mm(self, out, lhsT, rhs, start=True, stop=True):
        self.S.op("pe", lambda e: e.matmul(out, lhsT=lhsT, rhs=rhs, start=start, stop=stop), [lhsT, rhs], [out])

    def tr(self, out, in_, ident):
        self.S.op("pe", lambda e: e.transpose(out, in_, ident), [in_, ident], [out])

    def memset(self, eng, ap, v):
        self.S.op(eng, lambda e: e.memset(ap, v), [], [ap])

    def rsqrt(self, out, in_, scale=1.0, bias=None):
        self.act(out, in_, AF.Ln, bias=bias, scale=scale)
        self.act(out, out, AF.Exp, scale=-0.5)

    def recip(self, out, in_):
        self.S.op("dve", lambda e: e.reciprocal(out=out, in_=in_), [in_], [out])

    def scan(self, out, d0, d1, init, op0, op1):
        self.S.op("dve", lambda e: e.tensor_tensor_scan(out=out, data0=d0, data1=d1, initial=init, op0=op0, op1=op1),
                  [d0, d1], [out])

    def dmas(self, out, in_, eng="sp"):
        self.S.op(eng, lambda e: e.dma_start(out=out, in_=in_, allow_slow_non_contiguous=True), [in_], [out], dma=True)

    def V(self, name, j=0, parts=128):
        off, n = VEC_OFF[name]
        return self.C["vec"][0:parts, off + j:off + j + 1]

    def DV(self, name, j=0, parts=128):
        off, n = DV_OFF[name]
        return self.C["dvec"][0:parts, off + j:off + j + 1]

    def consts(self):
        A, S, T = self.A, self.S, self.T
        C = self.C = {}
        io = A.f32(128)
        S.op("pool", lambda e: e.iota(io, [[1, 128]], base=0, channel_multiplier=-1,
                                      allow_small_or_imprecise_dtypes=True), [], [io])
        C["ident_f"] = A.f32(128)
        self.ts("dve", C["ident_f"], io, 0.0, None, ALU.is_equal)
        C["ident_b"] = A.bf(128)
        self.ts("dve", C["ident_b"], io, 0.0, None, ALU.is_equal)
        C["ones_b"] = A.bf(128)
        self.memset("dve", C["ones_b"], 1.0)
        C["m_le_b"] = A.bf(128)
        self.ts("dve", C["m_le_b"], io, 0.0, None, ALU.is_ge)
        C["negm"] = A.f32(128)
        self.ts("dve", C["negm"], io, 0.0, -30000.0, ALU.is_lt, ALU.mult)
        C["bo_b"] = A.bf(128)
        self.memset("dve", C["bo_b"], 0.0)
        self.memset("dve", C["bo_b"][0:64, 0:64], 1.0)
        self.memset("dve", C["bo_b"][64:128, 64:128], 1.0)
        io2 = A.f32(64)
        S.op("pool", lambda e: e.iota(io2[0:64, :], [[1, 64]], base=0, channel_multiplier=-1,
                                      allow_small_or_imprecise_dtypes=True), [], [io2[0:64, :]])
        S.op("pool", lambda e: e.iota(io2[64:128, :], [[1, 64]], base=0, channel_multiplier=-1,
                                      allow_small_or_imprecise_dtypes=True), [], [io2[64:128, :]])
        C["mk192"] = A.bf(192)
        self.ts("dve", C["mk192"][:, 0:64], io2, 0.0, None, ALU.is_gt)
        self.ts("dve", C["mk192"][:, 64:128], io2, 0.0, None, ALU.is_gt)
        self.ts("dve", C["mk192"][:, 128:192], io2, 0.0, None, ALU.is_ge)
        C["mk_lt"] = A.bf(128)
        self.ts("dve", C["mk_lt"][:, 0:64], io2, 0.0, None, ALU.is_lt)
        self.ts("dve", C["mk_lt"][:, 64:128], io2, 0.0, None, ALU.is_lt)
        C["rst64"] = A.bf(T)
        self.memset("pool", C["rst64"], 1.0)
        self.memset("pool", C["rst64"].rearrange("p (c j) -> p c j", j=64)[:, :, 0:1], 0.0)
        C["rst128"] = A.bf(T)
        self.memset("pool", C["rst128"], 1.0)
        self.memset("pool", C["rst128"].rearrange("p (c j) -> p c j", j=128)[:, :, 0:1], 0.0)
        C["sel"] = A.f32(4 * 128, parts=4).rearrange("p (h m) -> p h m", h=4)
        for h in range(4):
            self.cp("dve", C["sel"][:, h, :], C["ident_f"][0:4, h:h + 1].to_broadcast([4, 128]))
        C["vec"] = A.f32(VEC_N)
        C["dvec"] = A.f32(DV_N)
        C["rows"] = A.f32(2, parts=4)
        C["eps"] = A.f32(1)
        self.memset("dve", C["eps"], EPS)
        C["gneps"] = A.f32(1)
        self.memset("dve", C["gneps"], R_GN_EPS)
        self.abase = A.mark()

    def layer(self, l):
        S, I, O, C = self.S, self.I, self.O, self.C
        T, NS, NT = self.T, self.NS, self.NT
        S.dma(C["vec"], I["vec"][l])
        S.dma(C["rows"], I["rows"][l])
        mo, mn = VEC_OFF["mu"]
        self.ts("dve", C["dvec"][:, 0:27], C["vec"][:, mo:mo + 27], -1.0, 1.0, ALU.mult, ALU.add)
        ko, kn = VEC_OFF["k_a"]
        self.ts("dve", C["dvec"][:, 27:35], C["vec"][:, ko:ko + 8], -1.0, 1.0, ALU.mult, ALU.add)
        src = I["xT"] if l == 0 else self.xT
        self.src = src
        self.rmsnorm(src, "n1g")
        self.dense(I["w_in"][l], N_IN, 16, self.hT, self.proj_sink)
        if self.stop == (l, "proj"):
            return False
        for r0 in range(0, R_COLS, 816):
            self.dmas(O["o_shift"][l, r0:r0 + 816, 0:1], self.projT[r0:r0 + 816, T - 1:T])
            self.dmas(O["o_shift"][l, r0:r0 + 816, 1:1 + NS], self.projT[r0:r0 + 816, T:NT])
        for r0 in range(0, D, 1024):
            for lag in range(3):
                self.dmas(O["o_conv"][l, lag, r0:r0 + 1024, 0:1],
                          self.projT[MB + r0:MB + r0 + 1024, T - 3 + lag:T - 2 + lag])
            self.dmas(O["o_conv"][l, 0, r0:r0 + 1024, 1:1 + NS], I["s_conv"][l, 1, r0:r0 + 1024, :])
            self.dmas(O["o_conv"][l, 1, r0:r0 + 1024, 1:1 + NS], I["s_conv"][l, 2, r0:r0 + 1024, :])
            self.dmas(O["o_conv"][l, 2, r0:r0 + 1024, 1:1 + NS], self.projT[MB + r0:MB + r0 + 1024, T:NT])
        self.chk("rw0")
        self.rwkv(l)
        if self.stop == (l, "rwkv"):
            return False
        self.mlstm(l)
        if self.stop == (l, "mlstm"):
            return False
        self.gla(l)
        if self.stop == (l, "gla"):
            return False
        self.merge(l)
        self.outproj(l)
        if self.stop == (l, "attn"):
            return False
        self.rmsnorm(self.xT, "n2g")
        self.ffn(l)
        return True

    def rmsnorm(self, src, gname, out_dram=None):
        A, S, C = self.A, self.S, self.C
        NT = self.NT
        A.release(self.abase)
        if out_dram is None:
            self.hT = A.bf(16 * NT).rearrange("p (c t) -> p c t", c=16)
        m0 = A.mark()
        srcv = src.rearrange("(c p) t -> p c t", p=128)
        for (t0, tn) in ttiles(NT):
            A.release(m0)
            xt = A.f32(16 * tn).rearrange("p (c t) -> p c t", c=16)
            sq = A.bf(16 * tn).rearrange("p (c t) -> p c t", c=16)
            rs = A.f32(tn)
            S.dma(xt, srcv[:, :, t0:t0 + tn])
            ps = self.P.bank()[:, 0:tn]
            for c in range(16):
                self.act(sq[:, c, :], xt[:, c, :], AF.Square)
                self.mm(ps, C["ones_b"], sq[:, c, :], start=(c == 0), stop=(c == 15))
            self.rsqrt(rs, ps, scale=1.0 / D, bias=C["eps"])
            for c in range(16):
                dst = self.hT[:, c, t0:t0 + tn] if out_dram is None else xt[:, c, :]
                self.stt(dst, xt[:, c, :], self.V(gname, c), rs, ALU.mult, ALU.mult)
            if out_dram is not None:
                S.dma(out_dram.rearrange("(c p) t -> p c t", p=128)[:, :, t0:t0 + tn], xt, eng="sp")
        A.release(m0)

    def dense(self, W, N, KC, rhs, sink, order=None):
        A, S = self.A, self.S
        NT = self.NT
        m0 = A.mark()
        nm = (N + 127) // 128
        Wv = W.rearrange("(kc p) n -> p kc n", p=128)
        NWB = 3
        wst = [A.f32(KC * 128).rearrange("p (k n) -> p k n", k=KC) for _ in range(NWB)]
        wbf = [A.bf(KC * 128).rearrange("p (k n) -> p k n", k=KC) for _ in range(NWB)]
        self._dense_m0 = A.mark()
        tl = ttiles(NT)
        order = order if order is not None else list(range(nm))
        for it, m in enumerate(order):
            rows = min(128, N - m * 128)
            ws, wb = wst[it % NWB], wbf[it % NWB]
            S.dma(ws[:, :, 0:rows], Wv[:, :, m * 128:m * 128 + rows])
            self.cp("pool" if it % 2 else "dve", wb[:, :, 0:rows], ws[:, :, 0:rows])
            banks = self.P.banks(len(tl))
            for k in range(KC):
                for bi, (t0, tn) in enumerate(tl):
                    self.mm(banks[bi][0:rows, 0:tn], wb[:, k, 0:rows], rhs[:, k, t0:t0 + tn],
                            start=(k == 0), stop=(k == KC - 1))
            sink(it, m, rows, banks, tl)
        A.release(m0)

    def proj_sink(self, it, m, rows, banks, tl):
        A, S = self.A, self.S
        A.release(self._dense_m0)
        stg = [A.f32(self.NT) for _ in range(2)]
        o = stg[it % 2]
        for bi, (t0, tn) in enumerate(tl):
            self.cp("act", o[0:rows, t0:t0 + tn], banks[bi][0:rows, 0:tn])
        S.dma(self.projT[m * 128:m * 128 + rows, :], o[0:rows, :], eng="act")

    def shiftmix(self, dst, p, mu, omu, st, parts=128):
        T, NT = self.T, self.NT
        self.ts("dve", dst[0:parts, :], p[0:parts, :], omu, None, ALU.mult)
        self.stt(dst[0:parts, 1:T], p[0:parts, 0:T - 1], mu, dst[0:parts, 1:T], ALU.mult, ALU.add)
        self.stt(dst[0:parts, T:NT], st, mu, dst[0:parts, T:NT], ALU.mult, ALU.add)

    def rwkv(self, l):
        A, S, C, I, O = self.A, self.S, self.C, self.I, self.O
        T, NS, NT = self.T, self.NS, self.NT
        A.release(self.abase)
        lw = A.bf(3 * BW, parts=64).rearrange("p (j n) -> p j n", j=3)
        lin = [A.bf(NT, parts=64) for _ in range(3)]
        m1 = A.mark()
        lw_st = A.f32(3 * BW, parts=64).rearrange("p (j n) -> p j n", j=3)
        S.dma(lw_st, I["rw2"][l].rearrange("j k n -> k j n"))
        self.cp("dve", lw, lw_st)
        self.chk("rw1a")
        raw = A.f32(NT, parts=64)
        xs = A.f32(NT, parts=64)
        sst = A.f32(NS, parts=64)
        for j in range(3):
            S.dma(raw, self.projT[3072 + 64 * j:3072 + 64 * (j + 1), :])
            S.dma(sst, I["s_shift"][l, 3072 + 64 * j:3072 + 64 * (j + 1), :])
            self.chk("rw1b")
            self.shiftmix(xs, raw, self.V("mul", j, 64), self.DV("omul", j, 64), sst, 64)
            self.chk("rw1c")
            if j == 1:
                self.cp("dve", lin[j], xs)
            else:
                self.act(lin[j], xs, [AF.Tanh, AF.Copy, AF.Sigmoid][j])
            self.chk("rw1d")
        A.release(m1)
        self._rw_base = A.mark()
        self.chk("rw1")
        for hp in range(8):
            self.rwkv_pair(l, hp, lw, lin)

    def rwkv_pair(self, l, hp, lw, lin):
        A, S, C, I, O, P = self.A, self.S, self.C, self.I, self.O, self.P
        T, NS, NT = self.T, self.NS, self.NT
        NCH = T // 64
        tl = ttiles(NT)
        A.release(self._rw_base)
        c0 = hp * 128
        H = [slice(0, 64), slice(64, 128)]
        bonus = A.f32(NT)
        g = A.f32(NT)
        Ofm = A.f32(NT)
        smp = A.f32(6 * NS).rearrange("p (q i) -> p q i", q=6)
        WL = A.f32(NCH)
        AR = A.bf(NCH * 192).rearrange("p (c n) -> p c n", c=NCH)
        Bb = A.bf(NCH * 128).rearrange("p (c n) -> p c n", c=NCH)
        Kb = A.bf(NCH * 128).rearrange("p (c n) -> p c n", c=NCH)
        Vb = A.bf(NCH * 128).rearrange("p (c n) -> p c n", c=NCH)
        mk = A.mark()
        t_r, t_k, t_v, xr, xk, xv, ld, a, kap, b, cc, e = [A.f32(NT) for _ in range(12)]
        sqb = A.bf(NT)
        sst = A.f32(3 * NS).rearrange("p (q i) -> p q i", q=3)
        for z in (AR, Bb, Kb, Vb):
            self.memset("pool", z, 0.0)
        for q, (tt_, xx) in enumerate([(t_r, xr), (t_k, xk), (t_v, xv)]):
            S.dma(tt_, self.projT[q * BW + c0:q * BW + c0 + 128, :])
            S.dma(sst[:, q, :], I["s_shift"][l, q * BW + c0:q * BW + c0 + 128, :])
            self.shiftmix(xx, tt_, self.V("mu", q * 8 + hp), self.DV("omu", q * 8 + hp), sst[:, q, :])
        for (t0, tn) in tl:
            ps = P.bank()
            self.mm(ps[:, 0:tn], lw[:, 0, c0:c0 + 128], lin[0][:, t0:t0 + tn])
            self.act(ld[:, t0:t0 + tn], ps[:, 0:tn], AF.Sigmoid, bias=self.V("w0", hp))
            ps = P.bank()
            self.mm(ps[:, 0:tn], lw[:, 1, c0:c0 + 128], lin[1][:, t0:t0 + tn])
            self.act(a[:, t0:t0 + tn], ps[:, 0:tn], AF.Sigmoid, bias=self.V("a0", hp))
            ps = P.bank()
            self.mm(ps[:, 0:tn], lw[:, 2, c0:c0 + 128], lin[2][:, t0:t0 + tn])
            self.cp("act", g[:, t0:t0 + tn], ps[:, 0:tn])
        self.ts("pool", ld, ld, -0.6065306597126334, None, ALU.mult)
        self.ts("dve", kap, xk, self.V("k_k", hp), None, ALU.mult)
        self.act(sqb, kap, AF.Square)
        for (t0, tn) in tl:
            ps = P.bank()
            self.mm(ps[:, 0:tn], C["bo_b"], sqb[:, t0:t0 + tn])
            self.ts("dve", e[:, t0:t0 + tn], ps[:, 0:tn], 1e-24, None, ALU.max)
        self.rsqrt(e, e)
        self.tt("dve", kap, kap, e, ALU.mult)
        self.ts("dve", e, a, self.V("k_a", hp), self.DV("omk_a", hp), ALU.mult, ALU.add)
        self.tt("dve", xk, xk, e, ALU.mult)
        self.tt("pool", b, kap, a, ALU.mult)
        self.stt(sqb, xr, self.V("r_k", hp), xk, ALU.mult, ALU.mult)
        for (t0, tn) in tl:
            ps = P.bank()
            self.mm(ps[:, 0:tn], C["bo_b"], sqb[:, t0:t0 + tn])
            self.tt("dve", bonus[:, t0:t0 + tn], ps[:, 0:tn], xv[:, t0:t0 + tn], ALU.mult)
        for q, src in enumerate([kap, b, xk, xv, xr]):
            self.cp("pool", smp[:, q, :], src[:, T:NT])
        self.act(smp[:, 5, :], ld[:, T:NT], AF.Exp)
        self.scan(cc[:, 0:T], C["rst64"], ld[:, 0:T], 0.0, ALU.mult, ALU.add)
        v3 = lambda ap: ap[:, 0:T].rearrange("p (c j) -> p c j", j=64)
        self.act(WL, v3(cc)[:, :, 63], AF.Exp)
        e1, e2, e3 = t_r, t_k, t_v
        self.act(e1[:, 0:T], cc[:, 0:T], AF.Exp)
        self.tt("dve", AR[:, :, 128:192], v3(xr), v3(e1), ALU.mult)
        self.tt("pool", e2[:, 0:T], cc[:, 0:T], ld[:, 0:T], ALU.subtract)
        self.act(e2[:, 0:T], e2[:, 0:T], AF.Exp)
        self.act(e3[:, 0:T], cc[:, 0:T], AF.Exp, scale=-1.0)
        for hh in range(2):
            hs = H[hh]
            self.stt(AR[hs, :, hh * 64:hh * 64 + 64], v3(kap)[hs], -1.0, v3(e2)[hs], ALU.mult, ALU.mult)
            self.tt("dve", Bb[hs, :, hh * 64:hh * 64 + 64], v3(b)[hs], v3(e3)[hs], ALU.mult)
            self.tt("dve", Kb[hs, :, hh * 64:hh * 64 + 64], v3(xk)[hs], v3(e3)[hs], ALU.mult)
            self.cp("pool", Vb[hs, :, hh * 64:hh * 64 + 64], v3(xv)[hs])
        self.chk("rw2")
        A.release(mk)
        Vtm = A.bf(NCH * 128).rearrange("p (c n) -> p c n", c=NCH)
        Btm = A.bf(NCH * 128).rearrange("p (c n) -> p c n", c=NCH)
        Ktm = A.bf(NCH * 128).rearrange("p (c n) -> p c n", c=NCH)
        TT = A.bf(NCH * 128).rearrange("p (c n) -> p c n", c=NCH)
        Aak = A.bf(NCH * 128).rearrange("p (c n) -> p c n", c=NCH)
        Abk = A.bf(NCH * 128).rearrange("p (c n) -> p c n", c=NCH)
        GW = 8
        PP = [[A.bf(256) for _ in range(2)] for _ in range(GW)]
        ZT = A.f32(128)
        ZTs = A.f32(128)
        ZTb = A.bf(128)
        Xsb = A.bf(128)
        Usb = A.bf(128)
        self.memset("dve", ZT, 0.0)
        self.memset("dve", ZTb, 0.0)
        ident = C["ident_b"]

        def st0(c, w):
            ps = P.bank()
            self.mm(ps[:, 0:192], Bb[:, c, :], AR[:, c, :])
            self.tt("dve", PP[w][0][:, 0:128], ps[:, 0:128], C["mk192"][:, 0:128], ALU.mult)
            self.tt("dve", Abk[:, c, 0:64], ps[:, 128:192], C["mk192"][:, 128:192], ALU.mult)
            self.chk("s0_1")
            ps2 = P.bank()
            self.mm(ps2[:, 0:192], Kb[:, c, :], AR[:, c, :])
            self.tt("dve", Aak[:, c, :], ps2[:, 0:128], C["mk192"][:, 0:128], ALU.mult)
            self.tt("dve", Abk[:, c, 64:128], ps2[:, 128:192], C["mk192"][:, 128:192], ALU.mult)
            ps3 = P.bank()
            self.mm(ps3[:, 0:128], AR[:, c, 0:128], Bb[:, c, :])
            self.tt("dve", PP[w][0][:, 128:256], ps3[:, 0:128], C["mk_lt"], ALU.mult)
            self.chk("s0_2")
            self.tt("pool", TT[:, c, :], PP[w][0][:, 0:128], ident, ALU.add)
            self.chk("s0_3")
            ps4 = P.bank()
            self.mm(ps4[:, 0:128], Vb[:, c, :], ident)
            self.mm(ps4[:, 128:256], Bb[:, c, :], ident)
            self.mm(ps4[:, 256:384], Kb[:, c, :], ident)
            self.cp("act", Vtm[:, c, :], ps4[:, 0:128])
            self.cp("act", Btm[:, c, :], ps4[:, 128:256])
            self.cp("act", Ktm[:, c, :], ps4[:, 256:384])
            self.chk("s0_4")

        def sq(c, w, j):
            src = PP[w][(j - 1) % 2]
            dst = PP[w][j % 2]
            ps = P.bank()
            self.mm(ps[:, 0:128], src[:, 128:256], src[:, 0:128])
            self.mm(ps[:, 128:256], src[:, 0:128], src[:, 128:256])
            self.cp("act", dst, ps[:, 0:256])

        def acc(c, w, j):
            dst = PP[w][j % 2]
            ps = P.bank()
            self.mm(ps[:, 0:128], dst[:, 128:256], TT[:, c, :])
            self.tt("dve", TT[:, c, :], TT[:, c, :], ps[:, 0:128], ALU.add)

        def chain(c):
            ps = P.bank()
            self.mm(ps[:, 0:128], AR[:, c, 0:128], ZTb, start=True, stop=False)
            self.mm(ps[:, 0:128], Aak[:, c, :], Vtm[:, c, :], start=False, stop=True)
            self.cp("act", Xsb, ps[:, 0:128])
            pu = P.bank()
            self.mm(pu[:, 0:128], TT[:, c, :], Xsb)
            self.cp("dve", Usb, pu[:, 0:128])
            po = P.bank()
            self.mm(po[:, 0:64], Usb, Abk[:, c, 0:64], start=True, stop=False)
            self.mm(po[:, 0:64], Vtm[:, c, :], Abk[:, c, 64:128], start=False, stop=False)
            self.mm(po[:, 0:64], ZTb, AR[:, c, 128:192], start=False, stop=True)
            self.cp("act", Ofm[:, c * 64:(c + 1) * 64], po[:, 0:64])
            pz = P.bank()
            self.mm(pz[:, 0:128], Btm[:, c, :], Usb, start=True, stop=False)
            self.mm(pz[:, 0:128], Ktm[:, c, :], Vtm[:, c, :], start=False, stop=True)
            self.ts("pool", ZTs, ZT, WL[:, c:c + 1], None, ALU.mult)
            self.stt(ZT, pz[:, 0:128], WL[:, c:c + 1], ZTs, ALU.mult, ALU.add)
            self.cp("act", ZTb, ZT)

        for c0_ in range(0, NCH, GW):
            wave = list(range(c0_, min(NCH, c0_ + GW)))
            for w, c in enumerate(wave):
                st0(c, w)
            self.chk("rw3a")
            for j in range(1, 6):
                for w, c in enumerate(wave):
                    sq(c, w, j)
                for w, c in enumerate(wave):
                    acc(c, w, j)
            self.chk("rw3b")
            for c in wave:
                chain(c)
        self.chk("rw3")
        ps = P.bank()
        self.tr(ps[:, 0:128], ZT, C["ident_f"])
        So = A.f32(128)
        self.cp("act", So, ps[:, 0:128])
        for hh in range(2):
            S.dma(O["p_wkv"][l, 2 * hp + hh], So[H[hh], hh * 64:hh * 64 + 64], eng="act")
        m2 = A.mark()
        tm3 = A.f32(384, parts=NS)
        ps = P.bank()
        for q, idx in enumerate([1, 2, 3]):
            self.tr(ps[0:NS, q * 128:(q + 1) * 128], smp[:, idx, :], C["ident_f"])
        self.cp("act", tm3, ps[0:NS, 0:384])
        eye = C["ident_f"][0:NS, 0:NS].unsqueeze(2).to_broadcast([NS, NS, 128])
        Bex = A.f32(NS * 128, parts=NS).rearrange("p (i c) -> p i c", i=NS)
        Kex = A.f32(NS * 128, parts=NS).rearrange("p (i c) -> p i c", i=NS)
        self.tt("dve", Bex, tm3[:, 0:128].unsqueeze(1).to_broadcast([NS, NS, 128]), eye, ALU.mult)
        self.tt("dve", Kex, tm3[:, 128:256].unsqueeze(1).to_broadcast([NS, NS, 128]), eye, ALU.mult)
        vtm = tm3[:, 256:384]
        NBUF = 4
        Sbl = [A.f32(128) for _ in range(NBUF)]
        STt = [A.f32(128) for _ in range(NBUF)]
        SnT = [A.f32(128) for _ in range(NBUF)]
        Sou = [A.f32(128) for _ in range(NBUF)]
        nSK = [A.f32(128, parts=NS) for _ in range(NBUF)]
        for z in Sbl + SnT:
            self.memset("pool", z, 0.0)
        rb, pO = P.reserve()
        for i in range(NS):
            k2 = i % NBUF
            for hh in range(2):
                S.dma(Sbl[k2][H[hh], hh * 64:hh * 64 + 64], I["s_wkv"][l, i, 2 * hp + hh])
            ps = P.bank()
            self.tr(ps[:, 0:128], Sbl[k2], C["ident_f"])
            self.cp("act", STt[k2], ps[:, 0:128])
            p1 = P.bank()
            self.mm(p1[0:NS, 0:128], smp[:, 0, :], STt[k2])
            self.ts("dve", nSK[k2], p1[0:NS, 0:128], -1.0, None, ALU.mult)
            p2 = P.bank()
            self.mm(p2[:, 0:128], Bex[:, i, :], nSK[k2], start=True, stop=False)
            self.mm(p2[:, 0:128], Kex[:, i, :], vtm, start=False, stop=True)
            for hh in range(2):
                hs = H[hh]
                cs = slice(hh * 64, hh * 64 + 64)
                self.stt(SnT[k2][hs, cs], STt[k2][hs, cs], smp[hs, 5, i:i + 1], p2[hs, cs], ALU.mult, ALU.add)
            self.mm(pO[:, i:i + 1], SnT[k2], smp[:, 4, i:i + 1])
            p3 = P.bank()
            self.tr(p3[:, 0:128], SnT[k2], C["ident_f"])
            self.cp("act", Sou[k2], p3[:, 0:128])
            for hh in range(2):
                S.dma(O["s_wkv"][l, i, 2 * hp + hh], Sou[k2][H[hh], hh * 64:hh * 64 + 64], eng="act")
        self.cp("act", Ofm[:, T:NT], pO[:, 0:NS])
        P.unreserve(rb)
        A.release(m2)
        self.chk("rw4")
        cen = A.f32(NT)
        rs = A.f32(NT)
        sb2 = A.bf(NT)
        obf = A.bf(NT)
        self.cp("pool", sb2, Ofm)
        for (t0, tn) in tl:
            ps = P.bank()
            self.mm(ps[:, 0:tn], C["bo_b"], sb2[:, t0:t0 + tn])
            self.stt(cen[:, t0:t0 + tn], ps[:, 0:tn], -1.0 / 64, Ofm[:, t0:t0 + tn], ALU.mult, ALU.add)
        self.act(sb2, cen, AF.Square)
        for (t0, tn) in tl:
            ps = P.bank()
            self.mm(ps[:, 0:tn], C["bo_b"], sb2[:, t0:t0 + tn])
            self.rsqrt(rs[:, t0:t0 + tn], ps[:, 0:tn], scale=1.0 / 64, bias=C["gneps"])
        self.tt("dve", cen, cen, rs, ALU.mult)
        self.ts("dve", cen, cen, self.V("ln_g", hp), self.V("ln_b", hp), ALU.mult, ALU.add)
        self.tt("pool", cen, cen, bonus, ALU.add)
        self.tt("dve", obf, cen, g, ALU.mult)
        S.dma(self.obT[c0:c0 + 128, :], obf, eng="sp")
        self.chk("rw5")

    def mlstm(self, l):
        A, S, C, I, O, P = self.A, self.S, self.C, self.I, self.O, self.P
        T, NS, NT = self.T, self.NS, self.NT
        NB = T // 128
        tl = ttiles(NT)
        A.release(self.abase)
        R4 = lambda n: A.f32(n, parts=4)
        Ac, wrow, emrow = [R4(NT) for _ in range(3)]
        dec, acs = R4(NB), R4(NS)
        acol = A.f32(NB * 4).rearrange("p (b h) -> p b h", h=4)
        ecol = A.f32(NB * 4).rearrange("p (b h) -> p b h", h=4)
        wscol = A.f32(4, parts=NS)
        mkeep = A.mark()
        ig, lf, G, aa, mt, erow = [R4(NT) for _ in range(6)]
        ones4 = R4(T)
        Ast, Aend = R4(NB), R4(NB)
        mold, wss = R4(NS), R4(NS)
        S.dma(ig, self.projT[MB + 3072:MB + 3076, :])
        S.dma(lf, self.projT[MB + 3076:MB + 3080, :])
        S.dma(mold, I["s_m"][l])
        self.memset("dve", ones4, 1.0)
        self.ts("dve", ig, ig, C["rows"][:, 0:1], None, ALU.add)
        self.act(lf, lf, AF.Sigmoid, bias=C["rows"][:, 1:2])
        self.act(lf, lf, AF.Ln)
        self.scan(G[:, 0:T], ones4, lf[:, 0:T], 0.0, ALU.mult, ALU.add)
        self.tt("dve", aa[:, 0:T], ig[:, 0:T], G[:, 0:T], ALU.subtract)
        self.scan(Ac[:, 0:T], ones4, aa[:, 0:T], 0.0, ALU.mult, ALU.max)
        self.tt("dve", mt[:, 0:T], G[:, 0:T], Ac[:, 0:T], ALU.add)
        v3 = lambda ap: ap[:, 0:T].rearrange("p (c j) -> p c j", j=128)
        self.cp("dve", Aend, v3(Ac)[:, :, 127])
        self.memset("dve", Ast[:, 0:1], 0.0)
        if NB > 1:
            self.cp("dve", Ast[:, 1:NB], Aend[:, 0:NB - 1])
        self.tt("dve", v3(wrow), Ast.unsqueeze(2).to_broadcast([4, NB, 128]), v3(Ac), ALU.subtract)
        self.act(wrow[:, 0:T], wrow[:, 0:T], AF.Exp)
        self.tt("dve", v3(erow), v3(aa), Aend.unsqueeze(2).to_broadcast([4, NB, 128]), ALU.subtract)
        self.act(erow[:, 0:T], erow[:, 0:T], AF.Exp)
        self.tt("dve", dec, Ast, Aend, ALU.subtract)
        self.act(dec, dec, AF.Exp)
        self.tt("dve", mold, lf[:, T:NT], mold, ALU.add)
        self.tt("dve", mt[:, T:NT], mold, ig[:, T:NT], ALU.max)
        self.tt("dve", acs, mold, mt[:, T:NT], ALU.subtract)
        self.act(acs, acs, AF.Exp)
        self.tt("dve", wss, ig[:, T:NT], mt[:, T:NT], ALU.subtract)
        self.act(wss, wss, AF.Exp)
        self.act(emrow, mt, AF.Exp, scale=-1.0)
        self.dmas(O["o_m"][l, :, 0:1], mt[:, T - 1:T], eng="act")
        S.dma(O["o_m"][l, :, 1:1 + NS], mt[:, T:NT], eng="act")
        ps = P.bank()
        i4 = C["ident_f"][0:4, 0:4]
        for bb in range(NB):
            self.mm(ps[:, bb * 4:bb * 4 + 4], aa[:, bb * 128:(bb + 1) * 128], i4)
            self.mm(ps[:, 256 + bb * 4:256 + bb * 4 + 4], erow[:, bb * 128:(bb + 1) * 128], i4)
        self.mm(ps[0:NS, 500:504], wss, i4)
        self.cp("act", acol, ps[:, 0:NB * 4].rearrange("p (b h) -> p b h", h=4))
        self.cp("act", ecol, ps[:, 256:256 + NB * 4].rearrange("p (b h) -> p b h", h=4))
        self.cp("act", wscol, ps[0:NS, 500:504])
        A.release(mkeep)
        self._ml_base = A.mark()
        for h in range(4):
            self.mlstm_head(l, h, dict(Ac=Ac, wrow=wrow, emrow=emrow, dec=dec, acs=acs, acol=acol, ecol=ecol,
                                       wscol=wscol))

    def bcast_rows(self, dst, rows4, h, ncols):
        c = 0
        while c < ncols:
            n = min(512, ncols - c)
            ps = self.P.bank()
            self.mm(ps[:, 0:n], self.C["sel"][:, h, :], rows4[:, c:c + n])
            self.cp("act", dst[:, c:c + n], ps[:, 0:n])
            c += n

    def conv_silu(self, l, ch0, cidx, raw, acc, cs):
        S, I = self.S, self.I
        T, NS, NT = self.T, self.NS, self.NT
        S.dma(raw, self.projT[MB + ch0:MB + ch0 + 128, :])
        S.dma(cs, I["s_conv"][l, :, ch0:ch0 + 128, :].rearrange("g p i -> p g i"))
        w = lambda j: self.V("conv_w", j * 16 + cidx)
        self.ts("dve", acc, raw, w(3), self.V("conv_b", cidx), ALU.mult, ALU.add)
        for lag in (1, 2, 3):
            self.stt(acc[:, lag:T], raw[:, 0:T - lag], w(3 - lag), acc[:, lag:T], ALU.mult, ALU.add)
            self.stt(acc[:, T:NT], cs[:, 3 - lag, :], w(3 - lag), acc[:, T:NT], ALU.mult, ALU.add)

    def mlstm_head(self, l, h, R):
        A, S, C, I, O, P = self.A, self.S, self.C, self.I, self.O, self.P
        T, NS, NT = self.T, self.NS, self.NT
        NB = T // 128
        tl = ttiles(NT)
        A.release(self._ml_base)
        GBA, GBw = A.f32(T), A.f32(T)
        GBem = A.f32(NT)
        GBd = A.f32(NB)
        GBac = A.f32(NS)
        self.bcast_rows(GBA, R["Ac"], h, T)
        self.bcast_rows(GBw, R["wrow"], h, T)
        self.bcast_rows(GBem, R["emrow"], h, NT)
        self.bcast_rows(GBd, R["dec"], h, NB)
        self.bcast_rows(GBac, R["acs"], h, NS)
        Qb = [A.bf(NT) for _ in range(2)]
        Kb = [A.bf(NT) for _ in range(2)]
        Qw = [A.bf(T) for _ in range(2)]
        qs = A.f32(2 * NS).rearrange("p (k i) -> p k i", k=2)
        ks = A.f32(2 * NS).rearrange("p (k i) -> p k i", k=2)
        vs = A.f32(2 * NS).rearrange("p (k i) -> p k i", k=2)
        Vtm = A.bf(NB * 257).rearrange("p (b n) -> p b n", b=NB)
        Ktm = A.bf(NB * 256).rearrange("p (b n) -> p b n", b=NB)
        Hh = [A.f32(NT) for _ in range(2)]
        m1 = A.mark()
        raw, acc = A.f32(NT), A.f32(NT)
        cs = A.f32(3 * NS).rearrange("p (g i) -> p g i", g=3)
        vb = A.bf(T)
        for kc in range(2):
            self.conv_silu(l, h * 256 + kc * 128, h * 2 + kc, raw, acc, cs)
            self.act(acc, acc, AF.Silu)
            self.cp("dve", Qb[kc], acc)
            self.cp("pool", qs[:, kc, :], acc[:, T:NT])
            self.tt("dve", Qw[kc], acc[:, 0:T], GBw, ALU.mult)
            self.conv_silu(l, BW + h * 256 + kc * 128, 8 + h * 2 + kc, raw, acc, cs)
            self.act(acc, acc, AF.Silu)
            self.ts("dve", Kb[kc], acc, 0.0625, None, ALU.mult)
            self.ts("pool", ks[:, kc, :], acc[:, T:NT], 0.0625, None, ALU.mult)
            for bb in range(NB):
                ps = P.bank()
                self.mm(ps[:, 0:128], Kb[kc][:, bb * 128:(bb + 1) * 128], C["ident_b"])
                self.ts("dve", Ktm[:, bb, kc * 128:(kc + 1) * 128], ps[:, 0:128], R["ecol"][:, bb, h:h + 1], None, ALU.mult)
        self.memset("pool", Vtm[:, :, 256:257], 1.0)
        for vc in range(2):
            S.dma(raw, self.projT[MB + 2048 + h * 256 + vc * 128:MB + 2048 + h * 256 + (vc + 1) * 128, :])
            self.cp("dve", vb, raw[:, 0:T])
            self.cp("pool", vs[:, vc, :], raw[:, T:NT])
            for bb in range(NB):
                ps = P.bank()
                self.mm(ps[:, 0:128], vb[:, bb * 128:(bb + 1) * 128], C["ident_b"])
                self.cp("act", Vtm[:, bb, vc * 128:(vc + 1) * 128], ps[:, 0:128])
        A.release(m1)
        CM = A.f32(2 * 257).rearrange("p (k n) -> p k n", k=2)
        Cb = A.bf(2 * 256).rearrange("p (k n) -> p k n", k=2)
        nbc = A.bf(2 * 128).rearrange("p (k n) -> p k n", k=2)
        tmp = A.f32(128)
        ET = A.f32(128)
        PT = A.bf(128)
        absd = A.f32(128)
        self.memset("dve", CM, 0.0)
        self.memset("dve", Cb, 0.0)
        self.memset("dve", nbc, 0.0)
        for bb in range(NB):
            blk = slice(bb * 128, (bb + 1) * 128)
            ps = P.bank()
            self.mm(ps[:, 0:128], Kb[0][:, blk], Qb[0][:, blk], start=True, stop=False)
            self.mm(ps[:, 0:128], Kb[1][:, blk], Qb[1][:, blk], start=False, stop=True)
            self.stt(tmp, GBA[:, blk], -1.0, C["negm"], ALU.mult, ALU.add)
            self.act(ET, tmp, AF.Exp, bias=R["acol"][:, bb, h:h + 1])
            self.tt("dve", PT, ps[:, 0:128], ET, ALU.mult)
            pn = P.bank()
            for vc in range(2):
                o = pn[:, vc * 128:(vc + 1) * 128]
                self.mm(o, Vtm[:, bb, vc * 128:(vc + 1) * 128], PT, start=True, stop=False)
                self.mm(o, Cb[:, 0, vc * 128:(vc + 1) * 128], Qw[0][:, blk], start=False, stop=False)
                self.mm(o, Cb[:, 1, vc * 128:(vc + 1) * 128], Qw[1][:, blk], start=False, stop=True)
            pd = P.bank()
            self.mm(pd[:, 0:128], C["ones_b"], PT, start=True, stop=False)
            self.mm(pd[:, 0:128], nbc[:, 0, :], Qw[0][:, blk], start=False, stop=False)
            self.mm(pd[:, 0:128], nbc[:, 1, :], Qw[1][:, blk], start=False, stop=True)
            self.act(absd, pd[:, 0:128], AF.Abs)
            self.tt("dve", absd, absd, GBem[:, blk], ALU.max)
            self.recip(absd, absd)
            for vc in range(2):
                self.tt("dve", Hh[vc][:, blk], pn[:, vc * 128:(vc + 1) * 128], absd, ALU.mult)
            for kc in range(2):
                pc = P.bank()
                self.mm(pc[:, 0:257], Ktm[:, bb, kc * 128:(kc + 1) * 128], Vtm[:, bb, :])
                self.stt(CM[:, kc, :], CM[:, kc, :], GBd[:, bb:bb + 1], pc[:, 0:257], ALU.mult, ALU.add)
                self.cp("act", Cb[:, kc, :], CM[:, kc, 0:256])
                self.cp("pool", nbc[:, kc, :], CM[:, kc, 256:257].to_broadcast([128, 128]))
        for kc in range(2):
            S.dma(O["p_C"][l, h, kc * 128:(kc + 1) * 128, :], CM[:, kc, 0:256], eng="sp")
            self.dmas(O["p_n"][l, h, kc * 128:(kc + 1) * 128].unsqueeze(1), CM[:, kc, 256:257], eng="sp")
        m2 = A.mark()
        ktm = A.f32(256, parts=NS)
        vau = A.f32(257, parts=NS)
        ps = P.bank()
        for kc in range(2):
            self.tr(ps[0:NS, kc * 128:(kc + 1) * 128], ks[:, kc, :], C["ident_f"])
            self.tr(ps[0:NS, 256 + kc * 128:256 + (kc + 1) * 128], vs[:, kc, :], C["ident_f"])
        self.ts("dve", ktm, ps[0:NS, 0:256], R["wscol"][:, h:h + 1], None, ALU.mult)
        self.cp("act", vau[:, 0:256], ps[0:NS, 256:512])
        self.memset("dve", vau[:, 256:257], 1.0)
        Kex = A.f32(NS * 256, parts=NS).rearrange("p (i c) -> p i c", i=NS)
        eye = C["ident_f"][0:NS, 0:NS].unsqueeze(2).to_broadcast([NS, NS, 256])
        self.tt("dve", Kex, ktm.unsqueeze(1).to_broadcast([NS, NS, 256]), eye, ALU.mult)
        CS = [A.f32(2 * 257).rearrange("p (k n) -> p k n", k=2) for _ in range(4)]
        nb2 = [A.f32(2 * 128).rearrange("p (k n) -> p k n", k=2) for _ in range(4)]
        rb, pN = P.reserve()
        for i in range(NS):
            k2 = i % 4
            cs_ = CS[k2]
            for kc in range(2):
                S.dma(cs_[:, kc, 0:256], I["s_C"][l, i, h, kc * 128:(kc + 1) * 128, :])
                self.dmas(cs_[:, kc, 256:257], I["s_n"][l, i, h, kc * 128:(kc + 1) * 128].unsqueeze(1))
            for kc in range(2):
                pc = P.bank()
                self.mm(pc[:, 0:257], Kex[:, i, kc * 128:(kc + 1) * 128], vau)
                self.stt(cs_[:, kc, :], cs_[:, kc, :], GBac[:, i:i + 1], pc[:, 0:257], ALU.mult, ALU.add)
                S.dma(O["s_C"][l, i, h, kc * 128:(kc + 1) * 128, :], cs_[:, kc, 0:256], eng="sp")
                self.dmas(O["s_n"][l, i, h, kc * 128:(kc + 1) * 128].unsqueeze(1), cs_[:, kc, 256:257], eng="sp")
                self.cp("pool", nb2[k2][:, kc, :], cs_[:, kc, 256:257].to_broadcast([128, 128]))
            for vc in range(2):
                o = pN[:, vc * NS + i:vc * NS + i + 1]
                self.mm(o, cs_[:, 0, vc * 128:(vc + 1) * 128], qs[:, 0, i:i + 1], start=True, stop=False)
                self.mm(o, cs_[:, 1, vc * 128:(vc + 1) * 128], qs[:, 1, i:i + 1], start=False, stop=True)
            o = pN[:, 2 * NS + i:2 * NS + i + 1]
            self.mm(o, nb2[k2][:, 0, :], qs[:, 0, i:i + 1], start=True, stop=False)
            self.mm(o, nb2[k2][:, 1, :], qs[:, 1, i:i + 1], start=False, stop=True)
        ad = A.f32(NS)
        self.act(ad, pN[:, 2 * NS:3 * NS], AF.Abs)
        self.tt("dve", ad, ad, GBem[:, T:NT], ALU.max)
        self.recip(ad, ad)
        for vc in range(2):
            self.tt("dve", Hh[vc][:, T:NT], pN[:, vc * NS:(vc + 1) * NS], ad, ALU.mult)
        P.unreserve(rb)
        A.release(m2)
        self.head_norm_out(Hh, MB + 3080 + h * 256, "m_ng", h * 2, AF.Sigmoid, BW + h * 256)

    def head_norm_out(self, Hh, gate_row0, gname, gidx0, gfunc, ob_row0):
        A, S, C, P = self.A, self.S, self.C, self.P
        NT = self.NT
        tl = ttiles(NT)
        m = A.mark()
        sq = [A.bf(NT) for _ in range(2)]
        rs = A.f32(NT)
        graw = A.f32(NT)
        obf = A.bf(NT)
        for vc in range(2):
            self.act(sq[vc], Hh[vc], AF.Square)
        for (t0, tn) in tl:
            ps = P.bank()
            self.mm(ps[:, 0:tn], C["ones_b"], sq[0][:, t0:t0 + tn], start=True, stop=False)
            self.mm(ps[:, 0:tn], C["ones_b"], sq[1][:, t0:t0 + tn], start=False, stop=True)
            self.rsqrt(rs[:, t0:t0 + tn], ps[:, 0:tn], scale=1.0 / 256, bias=C["eps"])
        for vc in range(2):
            S.dma(graw, self.projT[gate_row0 + vc * 128:gate_row0 + (vc + 1) * 128, :])
            self.act(graw, graw, gfunc)
            self.tt("dve", Hh[vc], Hh[vc], rs, ALU.mult)
            self.stt(obf, Hh[vc], self.V(gname, gidx0 + vc), graw, ALU.mult, ALU.mult)
            S.dma(self.obT[ob_row0 + vc * 128:ob_row0 + (vc + 1) * 128, :], obf, eng="sp")
        A.release(m)

    def gla(self, l):
        A, S, C, I, O, P = self.A, self.S, self.C, self.I, self.O, self.P
        T, NS, NT = self.T, self.NS, self.NT
        NB = T // 128
        tl = ttiles(NT)
        A.release(self.abase)
        a2b = A.bf(512, parts=16)
        xab = A.bf(NT, parts=16)
        m1 = A.mark()
        a2s = A.f32(512, parts=16)
        xas = A.f32(NT, parts=16)
        S.dma(a2s, I["ga2"][l])
        S.dma(xas, self.projT[GB_ + 2048:GB_ + 2064, :])
        self.cp("dve", a2b, a2s)
        self.cp("dve", xab, xas)
        A.release(m1)
        base = A.mark()
        v3 = lambda ap: ap[:, 0:T].rearrange("p (c j) -> p c j", j=128)
        for h in range(4):
            A.release(base)
            lg, bc, e1 = A.f32(NT), A.f32(NT), A.f32(NT)
            raw = A.f32(NT)
            Qh, Kh = A.bf(T), A.bf(T)
            WLg = A.f32(NB)
            qs, ksm, dgs = A.f32(NS), A.f32(NS), A.f32(NS)
            vs = A.f32(2 * NS).rearrange("p (k i) -> p k i", k=2)
            Vtm = A.bf(NB * 256).rearrange("p (b n) -> p b n", b=NB)
            Ktm = A.bf(NB * 128).rearrange("p (b n) -> p b n", b=NB)
            OG = [A.f32(NT) for _ in range(2)]
            vb = A.bf(T)
            for (t0, tn) in tl:
                ps = P.bank()
                self.mm(ps[:, 0:tn], a2b[:, h * 128:(h + 1) * 128], xab[:, t0:t0 + tn])
                self.act(lg[:, t0:t0 + tn], ps[:, 0:tn], AF.Sigmoid, bias=self.V("ga_b", h))
            self.act(lg, lg, AF.Ln)
            self.ts("pool", lg, lg, 1.0 / 16, None, ALU.mult)
            self.scan(bc[:, 0:T], C["rst128"], lg[:, 0:T], 0.0, ALU.mult, ALU.add)
            self.act(WLg, v3(bc)[:, :, 127], AF.Exp)
            self.act(dgs, lg[:, T:NT], AF.Exp)
            S.dma(raw, self.projT[GB_ + h * 128:GB_ + (h + 1) * 128, :])
            self.act(e1[:, 0:T], bc[:, 0:T], AF.Exp)
            self.stt(Qh, raw[:, 0:T], 128.0 ** -0.5, e1[:, 0:T], ALU.mult, ALU.mult)
            self.ts("pool", qs, raw[:, T:NT], 128.0 ** -0.5, None, ALU.mult)
            S.dma(raw, self.projT[GB_ + 512 + h * 128:GB_ + 512 + (h + 1) * 128, :])
            self.act(e1[:, 0:T], bc[:, 0:T], AF.Exp, scale=-1.0)
            self.tt("dve", Kh, raw[:, 0:T], e1[:, 0:T], ALU.mult)
            self.cp("pool", ksm, raw[:, T:NT])
            for bb in range(NB):
                ps = P.bank()
                self.mm(ps[:, 0:128], Kh[:, bb * 128:(bb + 1) * 128], C["ident_b"])
                self.cp("act", Ktm[:, bb, :], ps[:, 0:128])
            for vc in range(2):
                S.dma(raw, self.projT[GB_ + 1024 + h * 256 + vc * 128:GB_ + 1024 + h * 256 + (vc + 1) * 128, :])
                self.cp("dve", vb, raw[:, 0:T])
                self.cp("pool", vs[:, vc, :], raw[:, T:NT])
                for bb in range(NB):
                    ps = P.bank()
                    self.mm(ps[:, 0:128], vb[:, bb * 128:(bb + 1) * 128], C["ident_b"])
                    self.cp("act", Vtm[:, bb, vc * 128:(vc + 1) * 128], ps[:, 0:128])
            SM, SMs = A.f32(256), A.f32(256)
            Sb = A.bf(256)
            PTg = A.bf(128)
            self.memset("dve", SM, 0.0)
            self.memset("dve", Sb, 0.0)
            for bb in range(NB):
                blk = slice(bb * 128, (bb + 1) * 128)
                ps = P.bank()
                self.mm(ps[:, 0:128], Kh[:, blk], Qh[:, blk])
                self.tt("dve", PTg, ps[:, 0:128], C["m_le_b"], ALU.mult)
                po = P.bank()
                for vc in range(2):
                    o = po[:, vc * 128:(vc + 1) * 128]
                    self.mm(o, Vtm[:, bb, vc * 128:(vc + 1) * 128], PTg, start=True, stop=False)
                    self.mm(o, Sb[:, vc * 128:(vc + 1) * 128], Qh[:, blk], start=False, stop=True)
                    self.cp("act", OG[vc][:, blk], o)
                pz = P.bank()
                self.mm(pz[:, 0:256], Ktm[:, bb, :], Vtm[:, bb, :])
                self.ts("pool", SMs, SM, WLg[:, bb:bb + 1], None, ALU.mult)
                self.stt(SM, pz[:, 0:256], WLg[:, bb:bb + 1], SMs, ALU.mult, ALU.add)
                self.cp("act", Sb, SM)
            S.dma(O["p_gla"][l, h], SM, eng="sp")
            ktm = A.f32(128, parts=NS)
            vtm = A.f32(256, parts=NS)
            ps = P.bank()
            self.tr(ps[0:NS, 0:128], ksm, C["ident_f"])
            for vc in range(2):
                self.tr(ps[0:NS, 128 + vc * 128:128 + (vc + 1) * 128], vs[:, vc, :], C["ident_f"])
            self.cp("act", ktm, ps[0:NS, 0:128])
            self.cp("act", vtm, ps[0:NS, 128:384])
            Kex = A.f32(NS * 128, parts=NS).rearrange("p (i c) -> p i c", i=NS)
            eye = C["ident_f"][0:NS, 0:NS].unsqueeze(2).to_broadcast([NS, NS, 128])
            self.tt("dve", Kex, ktm.unsqueeze(1).to_broadcast([NS, NS, 128]), eye, ALU.mult)
            SS = [A.f32(256) for _ in range(4)]
            rb, pO = P.reserve()
            for i in range(NS):
                ss = SS[i % 4]
                S.dma(ss, I["s_gla"][l, i, h])
                pz = P.bank()
                self.mm(pz[:, 0:256], Kex[:, i, :], vtm)
                self.stt(ss, ss, dgs[:, i:i + 1], pz[:, 0:256], ALU.mult, ALU.add)
                S.dma(O["s_gla"][l, i, h], ss, eng="sp")
                for vc in range(2):
                    self.mm(pO[:, vc * NS + i:vc * NS + i + 1], ss[:, vc * 128:(vc + 1) * 128], qs[:, i:i + 1])
            for vc in range(2):
                self.cp("act", OG[vc][:, T:NT], pO[:, vc * NS:(vc + 1) * NS])
            P.unreserve(rb)
            self.head_norm_out(OG, GB_ + 2064 + h * 256, "g_ng", h * 2, AF.Silu, 2 * BW + h * 256)

    def merge(self, l):
        A, S, C, I, P = self.A, self.S, self.C, self.I, self.P
        NT = self.NT
        tl = ttiles(NT)
        A.release(self.abase)
        OB = A.bf(24 * NT).rearrange("p (c t) -> p c t", c=24)
        S.dma(OB, self.obT.rearrange("(c p) t -> p c t", p=128))
        wst = [A.f32(8 * 128).rearrange("p (k n) -> p k n", k=8) for _ in range(2)]
        wbf = [A.bf(8 * 128).rearrange("p (k n) -> p k n", k=8) for _ in range(2)]
        graw = [A.f32(NT) for _ in range(2)]
        accs = [A.f32(NT) for _ in range(2)]
        tmp = A.f32(512)
        mgb = [A.bf(NT) for _ in range(2)]
        it = 0
        for dc in range(16):
            acc = accs[dc % 2]
            for j in range(3):
                ws, wb, gr = wst[it % 2], wbf[it % 2], graw[it % 2]
                it += 1
                S.dma(ws, I["w_branch"][l, j, :, dc * 128:(dc + 1) * 128].rearrange("(k p) n -> p k n", p=128))
                self.cp("pool", wb, ws)
                S.dma(gr, self.projT[GTB + j * D + dc * 128:GTB + j * D + (dc + 1) * 128, :])
                self.act(gr, gr, AF.Sigmoid, bias=self.V("gate_b", j * 16 + dc))
                banks = P.banks(len(tl))
                for k in range(8):
                    for bi, (t0, tn) in enumerate(tl):
                        self.mm(banks[bi][:, 0:tn], wb[:, k, :], OB[:, j * 8 + k, t0:t0 + tn],
                                start=(k == 0), stop=(k == 7))
                for bi, (t0, tn) in enumerate(tl):
                    if j == 0:
                        self.tt("dve", acc[:, t0:t0 + tn], banks[bi][:, 0:tn], gr[:, t0:t0 + tn], ALU.mult)
                    else:
                        self.tt("dve", tmp[:, 0:tn], banks[bi][:, 0:tn], gr[:, t0:t0 + tn], ALU.mult)
                        self.tt("pool", acc[:, t0:t0 + tn], acc[:, t0:t0 + tn], tmp[:, 0:tn], ALU.add)
            self.cp("act", mgb[dc % 2], acc)
            S.dma(self.mgT[dc * 128:(dc + 1) * 128, :], mgb[dc % 2], eng="act")

    def resid_sink(self, it, m, rows, banks, tl):
        A, S = self.A, self.S
        A.release(self._dense_m0)
        stg = [A.f32(self.NT) for _ in range(2)]
        o = stg[it % 2]
        S.dma(o, self._res_src[m * 128:(m + 1) * 128, :])
        for bi, (t0, tn) in enumerate(tl):
            self.tt("dve", o[:, t0:t0 + tn], o[:, t0:t0 + tn], banks[bi][:, 0:tn], ALU.add)
        S.dma(self.xT[m * 128:(m + 1) * 128, :], o, eng="sp")

    def outproj(self, l):
        A, S, I = self.A, self.S, self.I
        NT = self.NT
        A.release(self.abase)
        MG = A.bf(16 * NT).rearrange("p (c t) -> p c t", c=16)
        S.dma(MG, self.mgT.rearrange("(c p) t -> p c t", p=128))
        self._res_src = self.src
        self.dense(I["w_out"][l], D, 16, MG, self.resid_sink)

    def ffn(self, l):
        A, S, I = self.A, self.S, self.I
        NT = self.NT
        NF = DFF // 128
        order = []
        for fc in range(NF):
            order += [fc, NF + fc]
        self.dense(I["w_gu"][l], 2 * DFF, 16, self.hT, self.gu_sink, order=order)
        self._res_src = self.xT
        for half in range(2):
            A.release(self.abase)
            AH = A.bf((NF // 2) * NT).rearrange("p (c t) -> p c t", c=NF // 2)
            r0 = half * (DFF // 2)
            S.dma(AH, self.actT[r0:r0 + DFF // 2, :].rearrange("(c p) t -> p c t", p=128))
            self.dense(I["w_down"][l, r0:r0 + DFF // 2, :], D, NF // 2, AH, self.resid_sink)

    def gu_sink(self, it, m, rows, banks, tl):
        A, S = self.A, self.S
        A.release(self._dense_m0)
        sil = A.f32(self.NT)
        ab = [A.bf(self.NT) for _ in range(2)]
        if it % 2 == 0:
            for bi, (t0, tn) in enumerate(tl):
                self.act(sil[:, t0:t0 + tn], banks[bi][:, 0:tn], AF.Silu)
        else:
            o = ab[(it // 2) % 2]
            fc = m - DFF // 128
            for bi, (t0, tn) in enumerate(tl):
                self.tt("dve", o[:, t0:t0 + tn], sil[:, t0:t0 + tn], banks[bi][:, 0:tn], ALU.mult)
            S.dma(self.actT[fc * 128:(fc + 1) * 128, :], o, eng="sp")


def _cols(v, n):
    return np.ascontiguousarray(np.asarray(v).reshape(n, 128).T)


def pack_vec(inp, l):
    out = np.zeros((128, VEC_N), np.float32)

    def put(name, arr):
        off, n = VEC_OFF[name]
        out[:arr.shape[0], off:off + n] = arr

    put("n1g", _cols(inp["norm1_g"][l], 16))
    put("n2g", _cols(inp["norm2_g"][l], 16))
    put("gate_b", _cols(inp["gate_b"][l].reshape(-1), 48))
    mu = np.asarray(inp["rwkv_mu"][l])
    put("mu", _cols(mu[:3072], 24))
    put("mul", np.ascontiguousarray(mu[3072:3264].reshape(3, 64).T))
    for nm, key in [("w0", "rwkv_w0"), ("a0", "rwkv_a0"), ("k_k", "rwkv_k_k"), ("k_a", "rwkv_k_a"),
                    ("ln_g", "rwkv_ln_g"), ("ln_b", "rwkv_ln_b"), ("m_ng", "mlstm_norm_g"), ("g_ng", "gla_norm_g")]:
        put(nm, _cols(inp[key][l], 8))
    put("r_k", _cols(np.asarray(inp["rwkv_r_k"][l]).reshape(-1), 8))
    put("conv_w", _cols(np.asarray(inp["mlstm_conv_w"][l]).reshape(-1), 64))
    put("conv_b", _cols(inp["mlstm_conv_b"][l], 16))
    put("ga_b", _cols(inp["gla_a_b"][l], 4))
    return out


_NC_CACHE = {}


def make_in_map(inp, L, xT, samp):
    f = lambda a: np.ascontiguousarray(np.asarray(a, dtype=np.float32))
    m = {}
    m["xT"] = f(xT)
    m["vec"] = np.stack([pack_vec(inp, l) for l in range(L)])
    m["rows"] = f(np.stack([np.stack([inp["mlstm_i_b"][l], inp["mlstm_f_b"][l]], axis=1) for l in range(L)]))
    m["w_in"] = f(inp["w_in"][:L])
    m["rw2"] = f(np.stack([np.stack([inp["rwkv_w2"][l], inp["rwkv_a2"][l], inp["rwkv_g2"][l]]) for l in range(L)]))
    m["ga2"] = f(inp["gla_a2"][:L])
    m["w_branch"] = f(inp["w_branch"][:L])
    m["w_out"] = f(inp["w_out"][:L])
    m["w_gu"] = f(inp["ffn_w_gu"][:L])
    m["w_down"] = f(inp["ffn_w_down"][:L])
    m["fng"] = _cols(inp["final_norm_g"], 16)
    m["s_shift"] = f(np.transpose(inp["state_rwkv_shift"][:L, samp], (0, 2, 1)))
    m["s_wkv"] = f(inp["state_rwkv_wkv"][:L, samp])
    m["s_conv"] = f(np.transpose(inp["state_mlstm_conv"][:L, samp], (0, 2, 3, 1)))
    m["s_C"] = f(inp["state_mlstm_C"][:L, samp])
    m["s_n"] = f(inp["state_mlstm_n"][:L, samp])
    m["s_m"] = f(np.transpose(inp["state_mlstm_m"][:L, samp], (0, 2, 1)))
    m["s_gla"] = f(inp["state_gla_S"][:L, samp])
    return m


def kernel(**inputs):
    inp = {k: np.asarray(v) for k, v in inputs.items()}
    B, T, _ = inp["x_prompt"].shape
    NSALL = inp["x_sample"].shape[0]
    L = inp["w_in"].shape[0]
    NS = NSALL // NCORES
    key = (T, NS, L)
    if key not in _NC_CACHE:
        _NC_CACHE[key] = Builder(T, NS, L).build()
    nc = _NC_CACHE[key]
    in_maps = []
    for c in range(NCORES):
        seq = c // 2
        samp = slice(c * NS, (c + 1) * NS)
        xT = np.concatenate([inp["x_prompt"][seq].T, inp["x_sample"][samp, 0, :].T], axis=1)
        in_maps.append(make_in_map(inp, L, xT, samp))
    res = run_bass_kernel_spmd(nc, in_maps, core_ids=list(range(NCORES)))
    R = res.results
    P = [R[2 * s] for s in range(B)]
    f32 = np.float32
    y_prompt = np.stack([P[s]["yT"][:, :T].T for s in range(B)]).astype(f32)
    y_sample = np.concatenate([R[c]["yT"][:, T:].T for c in range(NCORES)])[:, None, :].astype(f32)
    p_shift = np.stack([P[s]["o_shift"][:, :, 0] for s in range(B)], axis=1)
    s_shift = np.concatenate([np.transpose(R[c]["o_shift"][:, :, 1:], (0, 2, 1)) for c in range(NCORES)], axis=1)
    p_wkv = np.stack([P[s]["p_wkv"] for s in range(B)], axis=1)
    s_wkv = np.concatenate([R[c]["o_s_wkv"] for c in range(NCORES)], axis=1)
    p_conv = np.stack([np.transpose(P[s]["o_conv"][:, :, :, 0], (0, 1, 2)) for s in range(B)], axis=1)
    s_conv = np.concatenate([np.transpose(R[c]["o_conv"][:, :, :, 1:], (0, 3, 1, 2)) for c in range(NCORES)], axis=1)
    p_C = np.stack([P[s]["p_C"] for s in range(B)], axis=1)
    s_C = np.concatenate([R[c]["o_s_C"] for c in range(NCORES)], axis=1)
    p_n = np.stack([P[s]["p_n"] for s in range(B)], axis=1)
    s_n = np.concatenate([R[c]["o_s_n"] for c in range(NCORES)], axis=1)
    p_m = np.stack([P[s]["o_m"][:, :, 0] for s in range(B)], axis=1)
    s_m = np.concatenate([np.transpose(R[c]["o_m"][:, :, 1:], (0, 2, 1)) for c in range(NCORES)], axis=1)
    p_gla = np.stack([P[s]["p_gla"] for s in range(B)], axis=1)
    s_gla = np.concatenate([R[c]["o_s_gla"] for c in range(NCORES)], axis=1)
    outs = (y_prompt, y_sample, p_shift, p_wkv, p_conv, p_C, p_n, p_m, p_gla,
            s_shift, s_wkv, s_conv, s_C, s_n, s_m, s_gla)
    return tuple(np.ascontiguousarray(o, dtype=f32) for o in outs)
```

```python
from contextlib import ExitStack
import numpy as np
import concourse.bass as bass
import concourse.mybir as mybir
from concourse.bass_utils import run_bass_kernel_spmd

F32 = mybir.dt.float32
BF16 = mybir.dt.bfloat16
AF = mybir.ActivationFunctionType
ALU = mybir.AluOpType
AX = mybir.AxisListType
_ESZ = {F32: 4, BF16: 2, mybir.dt.int32: 4, mybir.dt.uint8: 1}

D = 2048
BW = 1024
R_COLS = 3264
M_COLS = 4104
G_COLS = 3088
N_IN = 16600
DFF = 5632
RB, MB, GB_, GTB = 0, 3264, 7368, 10456
EPS = 1e-6
R_GN_EPS = 64e-5
NCORES = 8


class _Rec:
    __slots__ = ("sp", "plo", "phi", "lo", "hi", "lw", "rde", "rdd", "cells", "dead")


class _Op:
    __slots__ = ("eng", "fn", "dma", "deps", "signal", "semval", "semidx")


def _region(ap):
    t = ap.tensor
    tn = type(t).__name__
    esz = _ESZ[ap.dtype]
    pairs = ap.ap
    if tn == "DRamTensorHandle":
        ext = 1
        for st, cnt in pairs:
            ext += (cnt - 1) * abs(st)
        lo = ap.offset * esz
        return ("d" + t.name, 0, 1, lo, lo + ext * esz)
    tshape = t.shape
    rowb = 1
    for s in tshape[1:]:
        rowb *= s
    rowb *= _ESZ[t.dtype]
    off = ap.offset * esz
    p0 = off // rowb
    c0 = off % rowb
    ext = 1
    for st, cnt in pairs[1:]:
        ext += (cnt - 1) * abs(st)
    np_ = pairs[0][1]
    if tn == "SBTensorHandle":
        return ("s", p0, p0 + np_, c0, c0 + ext * esz)
    lo = (c0 // 2048) * 2048
    hi = ((c0 + ext * esz + 2047) // 2048) * 2048
    return ("p", 0, 128, lo, hi)


class Sched:
    ENGS = ("pe", "act", "dve", "pool", "sp")

    def __init__(self, nc, n_dma_sems=64):
        self.nc = nc
        self.ops = []
        self.nds = n_dma_sems
        self.eng_obj = {"pe": nc.tensor, "act": nc.scalar, "dve": nc.vector, "pool": nc.gpsimd, "sp": nc.sync}
        self.recs = {}
        self.grid = {}
        self.cellsz = {}

    def _cells(self, key):
        sp, plo, phi, lo, hi = key
        if sp == "s":
            cs = 2048
        elif sp == "p":
            cs = 2048
        else:
            cs = 1 << 20
        return [(sp, c) for c in range(lo // cs, (hi - 1) // cs + 1)]

    def _get(self, key):
        r = self.recs.get(key)
        if r is None:
            r = _Rec()
            r.sp, r.plo, r.phi, r.lo, r.hi = key
            r.lw = -1
            r.rde = {}
            r.rdd = []
            r.cells = self._cells(key)
            r.dead = False
            self.recs[key] = r
            for c in r.cells:
                self.grid.setdefault(c, set()).add(key)
        return r

    def _overlaps(self, key):
        sp, plo, phi, lo, hi = key
        out = set()
        for c in self._cells(key):
            g = self.grid.get(c)
            if g:
                out |= g
        res = []
        for k in out:
            if k[1] < phi and plo < k[2] and k[3] < hi and lo < k[4]:
                res.append(k)
        return res

    def op(self, eng, fn, reads=(), writes=(), dma=False):
        i = len(self.ops)
        o = _Op()
        o.eng, o.fn, o.dma = eng, fn, dma
        o.signal = False
        o.semval = 0
        o.semidx = -1
        deps = set()
        rkeys = [_region(a) for a in reads]
        wkeys = [_region(a) for a in writes]
        for k in rkeys:
            for q in self._overlaps(k):
                r = self.recs[q]
                if r.lw >= 0:
                    deps.add(r.lw)
        for k in wkeys:
            for q in self._overlaps(k):
                r = self.recs[q]
                if r.lw >= 0:
                    deps.add(r.lw)
                for v in r.rde.values():
                    deps.add(v)
                for v in r.rdd:
                    deps.add(v)
        keep = []
        for d in deps:
            od = self.ops[d]
            if (not od.dma) and (not dma) and od.eng == eng and eng == "pe":
                continue
            keep.append(d)
            od.signal = True
        o.deps = keep
        self.ops.append(o)
        for k in rkeys:
            r = self._get(k)
            if dma:
                r.rdd.append(i)
            else:
                r.rde[eng] = i
        for k in wkeys:
            for q in self._overlaps(k):
                if q != k and q[1] >= k[1] and q[2] <= k[2] and q[3] >= k[3] and q[4] <= k[4]:
                    r = self.recs.pop(q)
                    for c in r.cells:
                        self.grid[c].discard(q)
            r = self._get(k)
            r.lw = i
            r.rde = {}
            r.rdd = []
        if dma:
            o.signal = True
        return o

    def dma(self, out, in_, eng="sp", **kw):
        return self.op(eng, lambda e: e.dma_start(out=out, in_=in_, **kw), [in_], [out], dma=True)

    def emit(self, stack):
        nc = self.nc
        cnt = {e: 0 for e in self.ENGS}
        dcnt = [0] * self.nds
        di = 0
        for o in self.ops:
            if not o.signal:
                continue
            if o.dma:
                o.semidx = di % self.nds
                dcnt[o.semidx] += 16
                o.semval = dcnt[o.semidx]
                di += 1
            else:
                cnt[o.eng] += 1
                o.semval = cnt[o.eng]
        sems = {e: stack.enter_context(nc.semaphore("s_" + e)) for e in self.ENGS}
        dsems = [stack.enter_context(nc.semaphore("d_%d" % i)) for i in range(self.nds)]
        waited = {e: {} for e in self.ENGS}
        nw = 0
        for o in self.ops:
            e = self.eng_obj[o.eng]
            need = {}
            for d in o.deps:
                od = self.ops[d]
                key = ("d", od.semidx) if od.dma else ("e", od.eng)
                if od.semval > need.get(key, 0):
                    need[key] = od.semval
            wd = waited[o.eng]
            if o.dma and o.semval > 16:
                k0 = ("d", o.semidx)
                if o.semval - 16 > need.get(k0, 0):
                    need[k0] = o.semval - 16
            for key, val in need.items():
                if wd.get(key, 0) >= val:
                    continue
                wd[key] = val
                e.wait_ge(dsems[key[1]] if key[0] == "d" else sems[key[1]], val)
                nw += 1
            ins = o.fn(e)
            if o.signal:
                if o.dma:
                    ins.then_inc(dsems[o.semidx], 16)
                else:
                    ins.then_inc(sems[o.eng], 1)
        for i, v in enumerate(dcnt):
            if v:
                nc.sync.wait_ge(dsems[i], v)
        return nw


class Arena:
    def __init__(self, t, ncols):
        self.t = t
        self.ncols = ncols
        self.top = 0

    def mark(self):
        return self.top

    def release(self, m):
        self.top = m

    def f32(self, cols, parts=128):
        c0 = self.top
        self.top += cols
        assert self.top <= self.ncols, ("arena overflow", self.top, self.ncols)
        return self.t[0:parts, c0:c0 + cols]

    def bf(self, cols, parts=128):
        n32 = (cols + 1) // 2
        a = self.f32(n32, parts)
        return a.bitcast(BF16)[:, 0:cols]


class Psum:
    def __init__(self, t):
        self.t = t
        self.nb = 0

    def bank(self):
        b = self.nb % 8
        self.nb += 1
        return self.t[:, b * 512:(b + 1) * 512]

    def banks(self, n):
        return [self.bank() for _ in range(n)]


def ttiles(NT):
    out = []
    c = 0
    while c < NT:
        n = min(512, NT - c)
        out.append((c, n))
        c += n
    return out


VEC_OFF = {}
_o = 0
for _n, _c in [("n1g", 16), ("n2g", 16), ("gate_b", 48), ("mu", 24), ("mul", 3), ("w0", 8), ("a0", 8), ("k_k", 8),
               ("k_a", 8), ("r_k", 8), ("ln_g", 8), ("ln_b", 8), ("conv_w", 64), ("conv_b", 16), ("m_ng", 8),
               ("g_ng", 8), ("ga_b", 4)]:
    VEC_OFF[_n] = (_o, _c)
    _o += _c
VEC_N = _o
DV_OFF = {"omu": (0, 24), "omul": (24, 3), "omk_a": (27, 8)}
DV_N = 35


class Psum2(Psum):
    def __init__(self, t):
        Psum.__init__(self, t)
        self.reserved = set()

    def bank(self):
        while True:
            b = self.nb % 8
            self.nb += 1
            if b not in self.reserved:
                return self.t[:, b * 512:(b + 1) * 512]

    def reserve(self):
        while True:
            b = self.nb % 8
            self.nb += 1
            if b not in self.reserved:
                self.reserved.add(b)
                return b, self.t[:, b * 512:(b + 1) * 512]

    def unreserve(self, b):
        self.reserved.discard(b)


class StopBuild(Exception):
    pass


class Builder:
    def chk(self, name):
        import os
        if os.environ.get("SUBSTOP") == name:
            raise StopBuild()

    def __init__(self, T, NS, depth, dbg=(), stop=None):
        self.T, self.NS, self.NT, self.depth = T, NS, T + NS, depth
        self.dbg = set(dbg)
        self.stop = stop
        assert T % 128 == 0

    def dram(self, name, shape, dt=F32, kind=None):
        if kind is None:
            kind = "ExternalOutput" if name in self.dbg else "Internal"
        return self.nc.dram_tensor(name, list(shape), dt, kind=kind).ap()

    def build(self):
        nc = bass.Bass("TRN2", target_bir_lowering=False)
        self.nc = nc
        T, NS, NT, L = self.T, self.NS, self.NT, self.depth
        inp = lambda n, s: self.dram(n, s, kind="ExternalInput")
        outp = lambda n, s: self.dram(n, s, kind="ExternalOutput")
        I = self.I = {}
        I["xT"] = inp("xT", [D, NT])
        I["vec"] = inp("vec", [L, 128, VEC_N])
        I["rows"] = inp("rows", [L, 4, 2])
        I["w_in"] = inp("w_in", [L, D, N_IN])
        I["rw2"] = inp("rw2", [L, 3, 64, BW])
        I["ga2"] = inp("ga2", [L, 16, 512])
        I["w_branch"] = inp("w_branch", [L, 3, BW, D])
        I["w_out"] = inp("w_out", [L, D, D])
        I["w_gu"] = inp("w_gu", [L, D, 2 * DFF])
        I["w_down"] = inp("w_down", [L, DFF, D])
        I["fng"] = inp("fng", [128, 16])
        I["s_shift"] = inp("s_shift", [L, R_COLS, NS])
        I["s_wkv"] = inp("s_wkv", [L, NS, 16, 64, 64])
        I["s_conv"] = inp("s_conv", [L, 3, D, NS])
        I["s_C"] = inp("s_C", [L, NS, 4, 256, 256])
        I["s_n"] = inp("s_n", [L, NS, 4, 256])
        I["s_m"] = inp("s_m", [L, 4, NS])
        I["s_gla"] = inp("s_gla", [L, NS, 4, 128, 256])
        O = self.O = {}
        O["yT"] = outp("yT", [D, NT])
        O["o_shift"] = outp("o_shift", [L, R_COLS, 1 + NS])
        O["p_wkv"] = outp("p_wkv", [L, 16, 64, 64])
        O["s_wkv"] = outp("o_s_wkv", [L, NS, 16, 64, 64])
        O["o_conv"] = outp("o_conv", [L, 3, D, 1 + NS])
        O["p_C"] = outp("p_C", [L, 4, 256, 256])
        O["p_n"] = outp("p_n", [L, 4, 256])
        O["s_C"] = outp("o_s_C", [L, NS, 4, 256, 256])
        O["s_n"] = outp("o_s_n", [L, NS, 4, 256])
        O["o_m"] = outp("o_m", [L, 4, 1 + NS])
        O["p_gla"] = outp("p_gla", [L, 4, 128, 256])
        O["s_gla"] = outp("o_s_gla", [L, NS, 4, 128, 256])
        self.xT = self.dram("x_scr", [D, NT])
        self.projT = self.dram("projT", [N_IN, NT])
        self.obT = self.dram("obT", [3 * BW, NT], BF16)
        self.mgT = self.dram("mgT", [D, NT], BF16)
        self.actT = self.dram("actT", [DFF, NT], BF16)

        with ExitStack() as st:
            ACOLS = 52800
            at = st.enter_context(nc.sbuf_tensor("arena", [128, ACOLS], F32))
            pt = st.enter_context(nc.psum_tensor("psum", [128, 4096], F32))
            self.A = Arena(at, ACOLS)
            self.P = Psum2(pt)
            self.S = Sched(nc)
            self.consts()
            done = True
            try:
                for l in range(L):
                    if not self.layer(l):
                        done = False
                        break
            except StopBuild:
                done = False
            if done:
                self.S.dma(self.C["vec"][:, 0:16], I["fng"])
                self.rmsnorm(self.xT, "n1g", out_dram=O["yT"])
            self.nwaits = self.S.emit(st)
        return nc

    def act(self, out, in_, func, bias=None, scale=1.0):
        kw = {}
        rd = [in_]
        if bias is not None:
            kw["bias"] = bias
            if not isinstance(bias, (int, float)):
                rd.append(bias)
        if not isinstance(scale, (int, float)):
            rd.append(scale)
        self.S.op("act", lambda e: e.activation(out=out, in_=in_, func=func, scale=scale, **kw), rd, [out])

    def ts(self, eng, out, in0, s1, s2, op0, op1=None):
        rd = [in0] + [s for s in (s1, s2) if s is not None and not isinstance(s, (int, float))]
        if op1 is None:
            self.S.op(eng, lambda e: e.tensor_scalar(out=out, in0=in0, scalar1=s1, scalar2=None, op0=op0), rd, [out])
        else:
            self.S.op(eng, lambda e: e.tensor_scalar(out=out, in0=in0, scalar1=s1, scalar2=s2, op0=op0, op1=op1),
                      rd, [out])

    def stt(self, out, in0, sc, in1, op0, op1):
        rd = [in0, in1] + ([] if isinstance(sc, (int, float)) else [sc])
        self.S.op("dve", lambda e: e.scalar_tensor_tensor(out=out, in0=in0, scalar=sc, in1=in1, op0=op0, op1=op1),
                  rd, [out])

    def tt(self, eng, out, in0, in1, op):
        self.S.op(eng, lambda e: e.tensor_tensor(out=out, in0=in0, in1=in1, op=op), [in0, in1], [out])

    def cp(self, eng, out, in_):
        if eng == "act":
            self.S.op("act", lambda e: e.copy(out=out, in_=in_), [in_], [out])
        else:
            self.S.op(eng, lambda e: e.tensor_copy(out=out, in_=in_), [in_], [out])

    def mm(self, out, lhsT, rhs, start=True, stop=True):
        self.S.op("pe", lambda e: e.matmul(out, lhsT=lhsT, rhs=rhs, start=start, stop=stop), [lhsT, rhs], [out])

    def tr(self, out, in_, ident):
        self.S.op("pe", lambda e: e.transpose(out, in_, ident), [in_, ident], [out])

    def memset(self, eng, ap, v):
        self.S.op(eng, lambda e: e.memset(ap, v), [], [ap])

    def rsqrt(self, out, in_, scale=1.0, bias=None):
        self.act(out, in_, AF.Ln, bias=bias, scale=scale)
        self.act(out, out, AF.Exp, scale=-0.5)

    def recip(self, out, in_):
        self.S.op("dve", lambda e: e.reciprocal(out=out, in_=in_), [in_], [out])

    def scan(self, out, d0, d1, init, op0, op1):
        self.S.op("dve", lambda e: e.tensor_tensor_scan(out=out, data0=d0, data1=d1, initial=init, op0=op0, op1=op1),
                  [d0, d1], [out])

    def dmas(self, out, in_, eng="sp"):
        self.S.op(eng, lambda e: e.dma_start(out=out, in_=in_, allow_slow_non_contiguous=True), [in_], [out], dma=True)

    def V(self, name, j=0, parts=128):
        off, n = VEC_OFF[name]
        return self.C["vec"][0:parts, off + j:off + j + 1]

    def DV(self, name, j=0, parts=128):
        off, n = DV_OFF[name]
        return self.C["dvec"][0:parts, off + j:off + j + 1]

    def consts(self):
        A, S, T = self.A, self.S, self.T
        C = self.C = {}
        io = A.f32(128)
        S.op("pool", lambda e: e.iota(io, [[1, 128]], base=0, channel_multiplier=-1,
                                      allow_small_or_imprecise_dtypes=True), [], [io])
        C["ident_f"] = A.f32(128)
        self.ts("dve", C["ident_f"], io, 0.0, None, ALU.is_equal)
        C["ident_b"] = A.bf(128)
        self.ts("dve", C["ident_b"], io, 0.0, None, ALU.is_equal)
        C["ones_b"] = A.bf(128)
        self.memset("dve", C["ones_b"], 1.0)
        C["m_le_b"] = A.bf(128)
        self.ts("dve", C["m_le_b"], io, 0.0, None, ALU.is_ge)
        C["negm"] = A.f32(128)
        self.ts("dve", C["negm"], io, 0.0, -30000.0, ALU.is_lt, ALU.mult)
        C["bo_b"] = A.bf(128)
        self.memset("dve", C["bo_b"], 0.0)
        self.memset("dve", C["bo_b"][0:64, 0:64], 1.0)
        self.memset("dve", C["bo_b"][64:128, 64:128], 1.0)
        io2 = A.f32(64)
        S.op("pool", lambda e: e.iota(io2[0:64, :], [[1, 64]], base=0, channel_multiplier=-1,
                                      allow_small_or_imprecise_dtypes=True), [], [io2[0:64, :]])
        S.op("pool", lambda e: e.iota(io2[64:128, :], [[1, 64]], base=0, channel_multiplier=-1,
                                      allow_small_or_imprecise_dtypes=True), [], [io2[64:128, :]])
        C["mk192"] = A.bf(192)
        self.ts("dve", C["mk192"][:, 0:64], io2, 0.0, None, ALU.is_gt)
        self.ts("dve", C["mk192"][:, 64:128], io2, 0.0, None, ALU.is_gt)
        self.ts("dve", C["mk192"][:, 128:192], io2, 0.0, None, ALU.is_ge)
        C["mk_lt"] = A.bf(128)
        self.ts("dve", C["mk_lt"][:, 0:64], io2, 0.0, None, ALU.is_lt)
        self.ts("dve", C["mk_lt"][:, 64:128], io2, 0.0, None, ALU.is_lt)
        C["rst64"] = A.bf(T)
        self.memset("pool", C["rst64"], 1.0)
        self.memset("pool", C["rst64"].rearrange("p (c j) -> p c j", j=64)[:, :, 0:1], 0.0)
        C["rst128"] = A.bf(T)
        self.memset("pool", C["rst128"], 1.0)
        self.memset("pool", C["rst128"].rearrange("p (c j) -> p c j", j=128)[:, :, 0:1], 0.0)
        C["sel"] = A.f32(4 * 128, parts=4).rearrange("p (h m) -> p h m", h=4)
        for h in range(4):
            self.cp("dve", C["sel"][:, h, :], C["ident_f"][0:4, h:h + 1].to_broadcast([4, 128]))
        C["vec"] = A.f32(VEC_N)
        C["dvec"] = A.f32(DV_N)
        C["rows"] = A.f32(2, parts=4)
        C["eps"] = A.f32(1)
        self.memset("dve", C["eps"], EPS)
        C["gneps"] = A.f32(1)
        self.memset("dve", C["gneps"], R_GN_EPS)
        self.abase = A.mark()

    def layer(self, l):
        S, I, O, C = self.S, self.I, self.O, self.C
        T, NS, NT = self.T, self.NS, self.NT
        S.dma(C["vec"], I["vec"][l])
        S.dma(C["rows"], I["rows"][l])
        mo, mn = VEC_OFF["mu"]
        self.ts("dve", C["dvec"][:, 0:27], C["vec"][:, mo:mo + 27], -1.0, 1.0, ALU.mult, ALU.add)
        ko, kn = VEC_OFF["k_a"]
        self.ts("dve", C["dvec"][:, 27:35], C["vec"][:, ko:ko + 8], -1.0, 1.0, ALU.mult, ALU.add)
        src = I["xT"] if l == 0 else self.xT
        self.src = src
        self.rmsnorm(src, "n1g")
        self.dense(I["w_in"][l], N_IN, 16, self.hT, self.proj_sink)
        if self.stop == (l, "proj"):
            return False
        for r0 in range(0, R_COLS, 816):
            self.dmas(O["o_shift"][l, r0:r0 + 816, 0:1], self.projT[r0:r0 + 816, T - 1:T])
            self.dmas(O["o_shift"][l, r0:r0 + 816, 1:1 + NS], self.projT[r0:r0 + 816, T:NT])
        for r0 in range(0, D, 1024):
            for lag in range(3):
                self.dmas(O["o_conv"][l, lag, r0:r0 + 1024, 0:1],
                          self.projT[MB + r0:MB + r0 + 1024, T - 3 + lag:T - 2 + lag])
            self.dmas(O["o_conv"][l, 0, r0:r0 + 1024, 1:1 + NS], I["s_conv"][l, 1, r0:r0 + 1024, :])
            self.dmas(O["o_conv"][l, 1, r0:r0 + 1024, 1:1 + NS], I["s_conv"][l, 2, r0:r0 + 1024, :])
            self.dmas(O["o_conv"][l, 2, r0:r0 + 1024, 1:1 + NS], self.projT[MB + r0:MB + r0 + 1024, T:NT])
        self.chk("rw0")
        self.rwkv(l)
        if self.stop == (l, "rwkv"):
            return False
        self.mlstm(l)
        if self.stop == (l, "mlstm"):
            return False
        self.gla(l)
        if self.stop == (l, "gla"):
            return False
        self.merge(l)
        self.outproj(l)
        if self.stop == (l, "attn"):
            return False
        self.rmsnorm(self.xT, "n2g")
        self.ffn(l)
        return True

    def rmsnorm(self, src, gname, out_dram=None):
        A, S, C = self.A, self.S, self.C
        NT = self.NT
        A.release(self.abase)
        if out_dram is None:
            self.hT = A.bf(16 * NT).rearrange("p (c t) -> p c t", c=16)
        m0 = A.mark()
        srcv = src.rearrange("(c p) t -> p c t", p=128)
        for (t0, tn) in ttiles(NT):
            A.release(m0)
            xt = A.f32(16 * tn).rearrange("p (c t) -> p c t", c=16)
            sq = A.bf(16 * tn).rearrange("p (c t) -> p c t", c=16)
            rs = A.f32(tn)
            S.dma(xt, srcv[:, :, t0:t0 + tn])
            ps = self.P.bank()[:, 0:tn]
            for c in range(16):
                self.act(sq[:, c, :], xt[:, c, :], AF.Square)
                self.mm(ps, C["ones_b"], sq[:, c, :], start=(c == 0), stop=(c == 15))
            self.rsqrt(rs, ps, scale=1.0 / D, bias=C["eps"])
            for c in range(16):
                dst = self.hT[:, c, t0:t0 + tn] if out_dram is None else xt[:, c, :]
                self.stt(dst, xt[:, c, :], self.V(gname, c), rs, ALU.mult, ALU.mult)
            if out_dram is not None:
                S.dma(out_dram.rearrange("(c p) t -> p c t", p=128)[:, :, t0:t0 + tn], xt, eng="sp")
        A.release(m0)

    def dense(self, W, N, KC, rhs, sink, order=None):
        A, S = self.A, self.S
        NT = self.NT
        m0 = A.mark()
        nm = (N + 127) // 128
        Wv = W.rearrange("(kc p) n -> p kc n", p=128)
        NWB = 3
        wst = [A.f32(KC * 128).rearrange("p (k n) -> p k n", k=KC) for _ in range(NWB)]
        wbf = [A.bf(KC * 128).rearrange("p (k n) -> p k n", k=KC) for _ in range(NWB)]
        self._dense_m0 = A.mark()
        tl = ttiles(NT)
        order = order if order is not None else list(range(nm))
        for it, m in enumerate(order):
            rows = min(128, N - m * 128)
            ws, wb = wst[it % NWB], wbf[it % NWB]
            S.dma(ws[:, :, 0:rows], Wv[:, :, m * 128:m * 128 + rows])
            self.cp("pool" if it % 2 else "dve", wb[:, :, 0:rows], ws[:, :, 0:rows])
            banks = self.P.banks(len(tl))
            for k in range(KC):
                for bi, (t0, tn) in enumerate(tl):
                    self.mm(banks[bi][0:rows, 0:tn], wb[:, k, 0:rows], rhs[:, k, t0:t0 + tn],
                            start=(k == 0), stop=(k == KC - 1))
            sink(it, m, rows, banks, tl)
        A.release(m0)

    def proj_sink(self, it, m, rows, banks, tl):
        A, S = self.A, self.S
        A.release(self._dense_m0)
        stg = [A.f32(self.NT) for _ in range(2)]
        o = stg[it % 2]
        for bi, (t0, tn) in enumerate(tl):
            self.cp("act", o[0:rows, t0:t0 + tn], banks[bi][0:rows, 0:tn])
        S.dma(self.projT[m * 128:m * 128 + rows, :], o[0:rows, :], eng="act")

    def shiftmix(self, dst, p, mu, omu, st, parts=128):
        T, NT = self.T, self.NT
        self.ts("dve", dst[0:parts, :], p[0:parts, :], omu, None, ALU.mult)
        self.stt(dst[0:parts, 1:T], p[0:parts, 0:T - 1], mu, dst[0:parts, 1:T], ALU.mult, ALU.add)
        self.stt(dst[0:parts, T:NT], st, mu, dst[0:parts, T:NT], ALU.mult, ALU.add)

    def rwkv(self, l):
        A, S, C, I, O = self.A, self.S, self.C, self.I, self.O
        T, NS, NT = self.T, self.NS, self.NT
        A.release(self.abase)
        lw = A.bf(3 * BW, parts=64).rearrange("p (j n) -> p j n", j=3)
        lin = [A.bf(NT, parts=64) for _ in range(3)]
        m1 = A.mark()
        lw_st = A.f32(3 * BW, parts=64).rearrange("p (j n) -> p j n", j=3)
        S.dma(lw_st, I["rw2"][l].rearrange("j k n -> k j n"))
        self.cp("dve", lw, lw_st)
        self.chk("rw1a")
        raw = A.f32(NT, parts=64)
        xs = A.f32(NT, parts=64)
        sst = A.f32(NS, parts=64)
        for j in range(3):
            S.dma(raw, self.projT[3072 + 64 * j:3072 + 64 * (j + 1), :])
            S.dma(sst, I["s_shift"][l, 3072 + 64 * j:3072 + 64 * (j + 1), :])
            self.chk("rw1b")
            self.shiftmix(xs, raw, self.V("mul", j, 64), self.DV("omul", j, 64), sst, 64)
            self.chk("rw1c")
            if j == 1:
                self.cp("dve", lin[j], xs)
            else:
                self.act(lin[j], xs, [AF.Tanh, AF.Copy, AF.Sigmoid][j])
            self.chk("rw1d")
        A.release(m1)
        self._rw_base = A.mark()
        self.chk("rw1")
        for hp in range(8):
            self.rwkv_pair(l, hp, lw, lin)

    def rwkv_pair(self, l, hp, lw, lin):
        A, S, C, I, O, P = self.A, self.S, self.C, self.I, self.O, self.P
        T, NS, NT = self.T, self.NS, self.NT
        NCH = T // 64
        tl = ttiles(NT)
        A.release(self._rw_base)
        c0 = hp * 128
        H = [slice(0, 64), slice(64, 128)]
        bonus = A.f32(NT)
        g = A.f32(NT)
        Ofm = A.f32(NT)
        smp = A.f32(6 * NS).rearrange("p (q i) -> p q i", q=6)
        WL = A.f32(NCH)
        AR = A.bf(NCH * 192).rearrange("p (c n) -> p c n", c=NCH)
        Bb = A.bf(NCH * 128).rearrange("p (c n) -> p c n", c=NCH)
        Kb = A.bf(NCH * 128).rearrange("p (c n) -> p c n", c=NCH)
        Vb = A.bf(NCH * 128).rearrange("p (c n) -> p c n", c=NCH)
        mk = A.mark()
        t_r, t_k, t_v, xr, xk, xv, ld, a, kap, b, cc, e = [A.f32(NT) for _ in range(12)]
        sqb = A.bf(NT)
        sst = A.f32(3 * NS).rearrange("p (q i) -> p q i", q=3)
        for z in (AR, Bb, Kb, Vb):
            self.memset("pool", z, 0.0)
        for q, (tt_, xx) in enumerate([(t_r, xr), (t_k, xk), (t_v, xv)]):
            S.dma(tt_, self.projT[q * BW + c0:q * BW + c0 + 128, :])
            S.dma(sst[:, q, :], I["s_shift"][l, q * BW + c0:q * BW + c0 + 128, :])
            self.shiftmix(xx, tt_, self.V("mu", q * 8 + hp), self.DV("omu", q * 8 + hp), sst[:, q, :])
        for (t0, tn) in tl:
            ps = P.bank()
            self.mm(ps[:, 0:tn], lw[:, 0, c0:c0 + 128], lin[0][:, t0:t0 + tn])
            self.act(ld[:, t0:t0 + tn], ps[:, 0:tn], AF.Sigmoid, bias=self.V("w0", hp))
            ps = P.bank()
            self.mm(ps[:, 0:tn], lw[:, 1, c0:c0 + 128], lin[1][:, t0:t0 + tn])
            self.act(a[:, t0:t0 + tn], ps[:, 0:tn], AF.Sigmoid, bias=self.V("a0", hp))
            ps = P.bank()
            self.mm(ps[:, 0:tn], lw[:, 2, c0:c0 + 128], lin[2][:, t0:t0 + tn])
            self.cp("act", g[:, t0:t0 + tn], ps[:, 0:tn])
        self.ts("pool", ld, ld, -0.6065306597126334, None, ALU.mult)
        self.ts("dve", kap, xk, self.V("k_k", hp), None, ALU.mult)
        self.act(sqb, kap, AF.Square)
        for (t0, tn) in tl:
            ps = P.bank()
            self.mm(ps[:, 0:tn], C["bo_b"], sqb[:, t0:t0 + tn])
            self.ts("dve", e[:, t0:t0 + tn], ps[:, 0:tn], 1e-24, None, ALU.max)
        self.rsqrt(e, e)
        self.tt("dve", kap, kap, e, ALU.mult)
        self.ts("dve", e, a, self.V("k_a", hp), self.DV("omk_a", hp), ALU.mult, ALU.add)
        self.tt("dve", xk, xk, e, ALU.mult)
        self.tt("pool", b, kap, a, ALU.mult)
        self.stt(sqb, xr, self.V("r_k", hp), xk, ALU.mult, ALU.mult)
        for (t0, tn) in tl:
            ps = P.bank()
            self.mm(ps[:, 0:tn], C["bo_b"], sqb[:, t0:t0 + tn])
            self.tt("dve", bonus[:, t0:t0 + tn], ps[:, 0:tn], xv[:, t0:t0 + tn], ALU.mult)
        for q, src in enumerate([kap, b, xk, xv, xr]):
            self.cp("pool", smp[:, q, :], src[:, T:NT])
        self.act(smp[:, 5, :], ld[:, T:NT], AF.Exp)
        self.scan(cc[:, 0:T], C["rst64"], ld[:, 0:T], 0.0, ALU.mult, ALU.add)
        v3 = lambda ap: ap[:, 0:T].rearrange("p (c j) -> p c j", j=64)
        self.act(WL, v3(cc)[:, :, 63], AF.Exp)
        e1, e2, e3 = t_r, t_k, t_v
        self.act(e1[:, 0:T], cc[:, 0:T], AF.Exp)
        self.tt("dve", AR[:, :, 128:192], v3(xr), v3(e1), ALU.mult)
        self.tt("pool", e2[:, 0:T], cc[:, 0:T], ld[:, 0:T], ALU.subtract)
        self.act(e2[:, 0:T], e2[:, 0:T], AF.Exp)
        self.act(e3[:, 0:T], cc[:, 0:T], AF.Exp, scale=-1.0)
        for hh in range(2):
            hs = H[hh]
            self.stt(AR[hs, :, hh * 64:hh * 64 + 64], v3(kap)[hs], -1.0, v3(e2)[hs], ALU.mult, ALU.mult)
            self.tt("dve", Bb[hs, :, hh * 64:hh * 64 + 64], v3(b)[hs], v3(e3)[hs], ALU.mult)
            self.tt("dve", Kb[hs, :, hh * 64:hh * 64 + 64], v3(xk)[hs], v3(e3)[hs], ALU.mult)
            self.cp("pool", Vb[hs, :, hh * 64:hh * 64 + 64], v3(xv)[hs])
        self.chk("rw2")
        A.release(mk)
        Vtm = A.bf(NCH * 128).rearrange("p (c n) -> p c n", c=NCH)
        Btm = A.bf(NCH * 128).rearrange("p (c n) -> p c n", c=NCH)
        Ktm = A.bf(NCH * 128).rearrange("p (c n) -> p c n", c=NCH)
        TT = A.bf(NCH * 128).rearrange("p (c n) -> p c n", c=NCH)
        Aak = A.bf(NCH * 128).rearrange("p (c n) -> p c n", c=NCH)
        Abk = A.bf(NCH * 128).rearrange("p (c n) -> p c n", c=NCH)
        GW = 8
        PP = [[A.bf(256) for _ in range(2)] for _ in range(GW)]
        ZT = A.f32(128)
        ZTs = A.f32(128)
        ZTb = A.bf(128)
        Xsb = A.bf(128)
        Usb = A.bf(128)
        self.memset("dve", ZT, 0.0)
        self.memset("dve", ZTb, 0.0)
        ident = C["ident_b"]

        def st0(c, w):
            ps = P.bank()
            self.mm(ps[:, 0:192], Bb[:, c, :], AR[:, c, :])
            self.tt("dve", PP[w][0][:, 0:128], ps[:, 0:128], C["mk192"][:, 0:128], ALU.mult)
            self.tt("dve", Abk[:, c, 0:64], ps[:, 128:192], C["mk192"][:, 128:192], ALU.mult)
            self.chk("s0_1")
            ps2 = P.bank()
            self.mm(ps2[:, 0:192], Kb[:, c, :], AR[:, c, :])
            self.tt("dve", Aak[:, c, :], ps2[:, 0:128], C["mk192"][:, 0:128], ALU.mult)
            self.tt("dve", Abk[:, c, 64:128], ps2[:, 128:192], C["mk192"][:, 128:192], ALU.mult)
            ps3 = P.bank()
            self.mm(ps3[:, 0:128], AR[:, c, 0:128], Bb[:, c, :])
            self.tt("dve", PP[w][0][:, 128:256], ps3[:, 0:128], C["mk_lt"], ALU.mult)
            self.chk("s0_2")
            self.tt("pool", TT[:, c, :], PP[w][0][:, 0:128], ident, ALU.add)
            self.chk("s0_3")
            ps4 = P.bank()
            self.mm(ps4[:, 0:128], Vb[:, c, :], ident)
            self.mm(ps4[:, 128:256], Bb[:, c, :], ident)
            self.mm(ps4[:, 256:384], Kb[:, c, :], ident)
            self.cp("act", Vtm[:, c, :], ps4[:, 0:128])
            self.cp("act", Btm[:, c, :], ps4[:, 128:256])
            self.cp("act", Ktm[:, c, :], ps4[:, 256:384])
            self.chk("s0_4")

        def sq(c, w, j):
            src = PP[w][(j - 1) % 2]
            dst = PP[w][j % 2]
            ps = P.bank()
            self.mm(ps[:, 0:128], src[:, 128:256], src[:, 0:128])
            self.mm(ps[:, 128:256], src[:, 0:128], src[:, 128:256])
            self.cp("act", dst, ps[:, 0:256])

        def acc(c, w, j):
            dst = PP[w][j % 2]
            ps = P.bank()
            self.mm(ps[:, 0:128], dst[:, 128:256], TT[:, c, :])
            self.tt("dve", TT[:, c, :], TT[:, c, :], ps[:, 0:128], ALU.add)

        def chain(c):
            ps = P.bank()
            self.mm(ps[:, 0:128], AR[:, c, 0:128], ZTb, start=True, stop=False)
            self.mm(ps[:, 0:128], Aak[:, c, :], Vtm[:, c, :], start=False, stop=True)
            self.cp("act", Xsb, ps[:, 0:128])
            pu = P.bank()
            self.mm(pu[:, 0:128], TT[:, c, :], Xsb)
            self.cp("dve", Usb, pu[:, 0:128])
            po = P.bank()
            self.mm(po[:, 0:64], Usb, Abk[:, c, 0:64], start=True, stop=False)
            self.mm(po[:, 0:64], Vtm[:, c, :], Abk[:, c, 64:128], start=False, stop=False)
            self.mm(po[:, 0:64], ZTb, AR[:, c, 128:192], start=False, stop=True)
            self.cp("act", Ofm[:, c * 64:(c + 1) * 64], po[:, 0:64])
            pz = P.bank()
            self.mm(pz[:, 0:128], Btm[:, c, :], Usb, start=True, stop=False)
            self.mm(pz[:, 0:128], Ktm[:, c, :], Vtm[:, c, :], start=False, stop=True)
            self.ts("pool", ZTs, ZT, WL[:, c:c + 1], None, ALU.mult)
            self.stt(ZT, pz[:, 0:128], WL[:, c:c + 1], ZTs, ALU.mult, ALU.add)
            self.cp("act", ZTb, ZT)

        tm3 = A.f32(384, parts=NS)
        ps = P.bank()
        for q, idx in enumerate([1, 2, 3]):
            self.tr(ps[0:NS, q * 128:(q + 1) * 128], smp[:, idx, :], C["ident_f"])
        self.cp("act", tm3, ps[0:NS, 0:384])
        eye = C["ident_f"][0:NS, 0:NS].unsqueeze(2).to_broadcast([NS, NS, 128])
        Bex = A.f32(NS * 128, parts=NS).rearrange("p (i c) -> p i c", i=NS)
        Kex = A.f32(NS * 128, parts=NS).rearrange("p (i c) -> p i c", i=NS)
        self.tt("dve", Bex, tm3[:, 0:128].unsqueeze(1).to_broadcast([NS, NS, 128]), eye, ALU.mult)
        self.tt("dve", Kex, tm3[:, 128:256].unsqueeze(1).to_broadcast([NS, NS, 128]), eye, ALU.mult)
        vtm = tm3[:, 256:384]
        NBUF = 4
        Sbl = [A.f32(128) for _ in range(NBUF)]
        STt = [A.f32(128) for _ in range(NBUF)]
        SnT = [A.f32(128) for _ in range(NBUF)]
        Sou = [A.f32(128) for _ in range(NBUF)]
        nSK = [A.f32(128, parts=NS) for _ in range(NBUF)]
        for z in Sbl + SnT:
            self.memset("pool", z, 0.0)
        rb, pO = P.reserve()

        def samp_subs(i):
            k2 = i % NBUF

            def g1():
                for hh in range(2):
                    S.dma(Sbl[k2][H[hh], hh * 64:hh * 64 + 64], I["s_wkv"][l, i, 2 * hp + hh])
                ps_ = P.bank()
                self.tr(ps_[:, 0:128], Sbl[k2], C["ident_f"])
                self.cp("act", STt[k2], ps_[:, 0:128])

            def g2():
                p1 = P.bank()
                self.mm(p1[0:NS, 0:128], smp[:, 0, :], STt[k2])
                self.ts("dve", nSK[k2], p1[0:NS, 0:128], -1.0, None, ALU.mult)

            def g3():
                p2 = P.bank()
                self.mm(p2[:, 0:128], Bex[:, i, :], nSK[k2], start=True, stop=False)
                self.mm(p2[:, 0:128], Kex[:, i, :], vtm, start=False, stop=True)
                for hh in range(2):
                    hs = H[hh]
                    cs = slice(hh * 64, hh * 64 + 64)
                    self.stt(SnT[k2][hs, cs], STt[k2][hs, cs], smp[hs, 5, i:i + 1], p2[hs, cs], ALU.mult, ALU.add)

            def g4():
                self.mm(pO[:, i:i + 1], SnT[k2], smp[:, 4, i:i + 1])
                p3 = P.bank()
                self.tr(p3[:, 0:128], SnT[k2], C["ident_f"])
                self.cp("act", Sou[k2], p3[:, 0:128])
                for hh in range(2):
                    S.dma(O["s_wkv"][l, i, 2 * hp + hh], Sou[k2][H[hh], hh * 64:hh * 64 + 64], eng="act")
            return [g1, g2, g3, g4]

        def chain_subs(c):
            def f1():
                ps_ = P.bank()
                self.mm(ps_[:, 0:128], AR[:, c, 0:128], ZTb, start=True, stop=False)
                self.mm(ps_[:, 0:128], Aak[:, c, :], Vtm[:, c, :], start=False, stop=True)
                self.cp("act", Xsb, ps_[:, 0:128])

            def f2():
                pu = P.bank()
                self.mm(pu[:, 0:128], TT[:, c, :], Xsb)
                self.cp("dve", Usb, pu[:, 0:128])

            def f3():
                pz = P.bank()
                self.mm(pz[:, 0:128], Btm[:, c, :], Usb, start=True, stop=False)
                self.mm(pz[:, 0:128], Ktm[:, c, :], Vtm[:, c, :], start=False, stop=True)
                po = P.bank()
                self.mm(po[:, 0:64], Usb, Abk[:, c, 0:64], start=True, stop=False)
                self.mm(po[:, 0:64], Vtm[:, c, :], Abk[:, c, 64:128], start=False, stop=False)
                self.mm(po[:, 0:64], ZTb, AR[:, c, 128:192], start=False, stop=True)
                self.ts("pool", ZTs, ZT, WL[:, c:c + 1], None, ALU.mult)
                self.stt(ZT, pz[:, 0:128], WL[:, c:c + 1], ZTs, ALU.mult, ALU.add)
                self.cp("act", ZTb, ZT)
                self.cp("act", Ofm[:, c * 64:(c + 1) * 64], po[:, 0:64])
            return [f1, f2, f3]

        def bulk_ops(wave):
            ops_ = []
            for w, c in enumerate(wave):
                ops_.append((lambda c=c, w=w: st0(c, w)))
            for j in range(1, 6):
                for w, c in enumerate(wave):
                    ops_.append((lambda c=c, w=w, j=j: sq(c, w, j)))
                for w, c in enumerate(wave):
                    ops_.append((lambda c=c, w=w, j=j: acc(c, w, j)))
            return ops_

        waves = [list(range(c0_, min(NCH, c0_ + GW))) for c0_ in range(0, NCH, GW)]
        samp_all = []
        for i in range(NS):
            samp_all += samp_subs(i)
        for f in bulk_ops(waves[0]):
            f()
        sp_pos = 0
        for wi, wave in enumerate(waves):
            Al = []
            for c in wave:
                Al += chain_subs(c)
            Bl = bulk_ops(waves[wi + 1]) if wi + 1 < len(waves) else []
            nC = (len(samp_all) - sp_pos + (len(waves) - wi) - 1) // (len(waves) - wi)
            Cl = samp_all[sp_pos:sp_pos + nC]
            sp_pos += nC
            nA = len(Al)
            bi = ci = 0
            for k, f in enumerate(Al):
                f()
                tb = (len(Bl) * (k + 1) + nA - 1) // nA
                while bi < min(tb, len(Bl)):
                    Bl[bi]()
                    bi += 1
                tc = (len(Cl) * (k + 1) + nA - 1) // nA
                while ci < min(tc, len(Cl)):
                    Cl[ci]()
                    ci += 1
        self.chk("rw3")
        ps = P.bank()
        self.tr(ps[:, 0:128], ZT, C["ident_f"])
        So = A.f32(128)
        self.cp("act", So, ps[:, 0:128])
        for hh in range(2):
            S.dma(O["p_wkv"][l, 2 * hp + hh], So[H[hh], hh * 64:hh * 64 + 64], eng="act")
        self.cp("act", Ofm[:, T:NT], pO[:, 0:NS])
        P.unreserve(rb)
        self.chk("rw4")
        cen = A.f32(NT)
        rs = A.f32(NT)
        sb2 = A.bf(NT)
        obf = A.bf(NT)
        self.cp("pool", sb2, Ofm)
        for (t0, tn) in tl:
            ps = P.bank()
            self.mm(ps[:, 0:tn], C["bo_b"], sb2[:, t0:t0 + tn])
            self.stt(cen[:, t0:t0 + tn], ps[:, 0:tn], -1.0 / 64, Ofm[:, t0:t0 + tn], ALU.mult, ALU.add)
        self.act(sb2, cen, AF.Square)
        for (t0, tn) in tl:
            ps = P.bank()
            self.mm(ps[:, 0:tn], C["bo_b"], sb2[:, t0:t0 + tn])
            self.rsqrt(rs[:, t0:t0 + tn], ps[:, 0:tn], scale=1.0 / 64, bias=C["gneps"])
        self.tt("dve", cen, cen, rs, ALU.mult)
        self.ts("dve", cen, cen, self.V("ln_g", hp), self.V("ln_b", hp), ALU.mult, ALU.add)
        self.tt("pool", cen, cen, bonus, ALU.add)
        self.tt("dve", obf, cen, g, ALU.mult)
        S.dma(self.obT[c0:c0 + 128, :], obf, eng="sp")
        self.chk("rw5")

    def mlstm(self, l):
        A, S, C, I, O, P = self.A, self.S, self.C, self.I, self.O, self.P
        T, NS, NT = self.T, self.NS, self.NT
        NB = T // 128
        tl = ttiles(NT)
        A.release(self.abase)
        R4 = lambda n: A.f32(n, parts=4)
        Ac, wrow, emrow = [R4(NT) for _ in range(3)]
        dec, acs = R4(NB), R4(NS)
        acol = A.f32(NB * 4).rearrange("p (b h) -> p b h", h=4)
        ecol = A.f32(NB * 4).rearrange("p (b h) -> p b h", h=4)
        wscol = A.f32(4, parts=NS)
        mkeep = A.mark()
        ig, lf, G, aa, mt, erow = [R4(NT) for _ in range(6)]
        ones4 = R4(T)
        Ast, Aend = R4(NB), R4(NB)
        mold, wss = R4(NS), R4(NS)
        S.dma(ig, self.projT[MB + 3072:MB + 3076, :])
        S.dma(lf, self.projT[MB + 3076:MB + 3080, :])
        S.dma(mold, I["s_m"][l])
        self.memset("dve", ones4, 1.0)
        self.ts("dve", ig, ig, C["rows"][:, 0:1], None, ALU.add)
        self.act(lf, lf, AF.Sigmoid, bias=C["rows"][:, 1:2])
        self.act(lf, lf, AF.Ln)
        self.scan(G[:, 0:T], ones4, lf[:, 0:T], 0.0, ALU.mult, ALU.add)
        self.tt("dve", aa[:, 0:T], ig[:, 0:T], G[:, 0:T], ALU.subtract)
        self.scan(Ac[:, 0:T], ones4, aa[:, 0:T], 0.0, ALU.mult, ALU.max)
        self.tt("dve", mt[:, 0:T], G[:, 0:T], Ac[:, 0:T], ALU.add)
        v3 = lambda ap: ap[:, 0:T].rearrange("p (c j) -> p c j", j=128)
        self.cp("dve", Aend, v3(Ac)[:, :, 127])
        self.memset("dve", Ast[:, 0:1], 0.0)
        if NB > 1:
            self.cp("dve", Ast[:, 1:NB], Aend[:, 0:NB - 1])
        self.tt("dve", v3(wrow), Ast.unsqueeze(2).to_broadcast([4, NB, 128]), v3(Ac), ALU.subtract)
        self.act(wrow[:, 0:T], wrow[:, 0:T], AF.Exp)
        self.tt("dve", v3(erow), v3(aa), Aend.unsqueeze(2).to_broadcast([4, NB, 128]), ALU.subtract)
        self.act(erow[:, 0:T], erow[:, 0:T], AF.Exp)
        self.tt("dve", dec, Ast, Aend, ALU.subtract)
        self.act(dec, dec, AF.Exp)
        self.tt("dve", mold, lf[:, T:NT], mold, ALU.add)
        self.tt("dve", mt[:, T:NT], mold, ig[:, T:NT], ALU.max)
        self.tt("dve", acs, mold, mt[:, T:NT], ALU.subtract)
        self.act(acs, acs, AF.Exp)
        self.tt("dve", wss, ig[:, T:NT], mt[:, T:NT], ALU.subtract)
        self.act(wss, wss, AF.Exp)
        self.act(emrow, mt, AF.Exp, scale=-1.0)
        self.dmas(O["o_m"][l, :, 0:1], mt[:, T - 1:T], eng="act")
        S.dma(O["o_m"][l, :, 1:1 + NS], mt[:, T:NT], eng="act")
        ps = P.bank()
        i4 = C["ident_f"][0:4, 0:4]
        for bb in range(NB):
            self.mm(ps[:, bb * 4:bb * 4 + 4], aa[:, bb * 128:(bb + 1) * 128], i4)
            self.mm(ps[:, 256 + bb * 4:256 + bb * 4 + 4], erow[:, bb * 128:(bb + 1) * 128], i4)
        self.mm(ps[0:NS, 500:504], wss, i4)
        self.cp("act", acol, ps[:, 0:NB * 4].rearrange("p (b h) -> p b h", h=4))
        self.cp("act", ecol, ps[:, 256:256 + NB * 4].rearrange("p (b h) -> p b h", h=4))
        self.cp("act", wscol, ps[0:NS, 500:504])
        A.release(mkeep)
        self._ml_base = A.mark()
        for h in range(4):
            self.mlstm_head(l, h, dict(Ac=Ac, wrow=wrow, emrow=emrow, dec=dec, acs=acs, acol=acol, ecol=ecol,
                                       wscol=wscol))

    def bcast_rows(self, dst, rows4, h, ncols):
        c = 0
        while c < ncols:
            n = min(512, ncols - c)
            ps = self.P.bank()
            self.mm(ps[:, 0:n], self.C["sel"][:, h, :], rows4[:, c:c + n])
            self.cp("act", dst[:, c:c + n], ps[:, 0:n])
            c += n

    def conv_silu(self, l, ch0, cidx, raw, acc, cs):
        S, I = self.S, self.I
        T, NS, NT = self.T, self.NS, self.NT
        S.dma(raw, self.projT[MB + ch0:MB + ch0 + 128, :])
        S.dma(cs, I["s_conv"][l, :, ch0:ch0 + 128, :].rearrange("g p i -> p g i"))
        w = lambda j: self.V("conv_w", j * 16 + cidx)
        self.ts("dve", acc, raw, w(3), self.V("conv_b", cidx), ALU.mult, ALU.add)
        for lag in (1, 2, 3):
            self.stt(acc[:, lag:T], raw[:, 0:T - lag], w(3 - lag), acc[:, lag:T], ALU.mult, ALU.add)
            self.stt(acc[:, T:NT], cs[:, 3 - lag, :], w(3 - lag), acc[:, T:NT], ALU.mult, ALU.add)

    def mlstm_head(self, l, h, R):
        A, S, C, I, O, P = self.A, self.S, self.C, self.I, self.O, self.P
        T, NS, NT = self.T, self.NS, self.NT
        NB = T // 128
        tl = ttiles(NT)
        A.release(self._ml_base)
        GBA, GBw = A.f32(T), A.f32(T)
        GBem = A.f32(NT)
        GBd = A.f32(NB)
        GBac = A.f32(NS)
        self.bcast_rows(GBA, R["Ac"], h, T)
        self.bcast_rows(GBw, R["wrow"], h, T)
        self.bcast_rows(GBem, R["emrow"], h, NT)
        self.bcast_rows(GBd, R["dec"], h, NB)
        self.bcast_rows(GBac, R["acs"], h, NS)
        Qb = [A.bf(NT) for _ in range(2)]
        Kb = [A.bf(NT) for _ in range(2)]
        Qw = [A.bf(T) for _ in range(2)]
        qs = A.f32(2 * NS).rearrange("p (k i) -> p k i", k=2)
        ks = A.f32(2 * NS).rearrange("p (k i) -> p k i", k=2)
        vs = A.f32(2 * NS).rearrange("p (k i) -> p k i", k=2)
        Vtm = A.bf(NB * 257).rearrange("p (b n) -> p b n", b=NB)
        Ktm = A.bf(NB * 256).rearrange("p (b n) -> p b n", b=NB)
        Hh = [A.f32(NT) for _ in range(2)]
        m1 = A.mark()
        raw, acc = A.f32(NT), A.f32(NT)
        cs = A.f32(3 * NS).rearrange("p (g i) -> p g i", g=3)
        vb = A.bf(T)
        for kc in range(2):
            self.conv_silu(l, h * 256 + kc * 128, h * 2 + kc, raw, acc, cs)
            self.act(acc, acc, AF.Silu)
            self.cp("dve", Qb[kc], acc)
            self.cp("pool", qs[:, kc, :], acc[:, T:NT])
            self.tt("dve", Qw[kc], acc[:, 0:T], GBw, ALU.mult)
            self.conv_silu(l, BW + h * 256 + kc * 128, 8 + h * 2 + kc, raw, acc, cs)
            self.act(acc, acc, AF.Silu)
            self.ts("dve", Kb[kc], acc, 0.0625, None, ALU.mult)
            self.ts("pool", ks[:, kc, :], acc[:, T:NT], 0.0625, None, ALU.mult)
            for bb in range(NB):
                ps = P.bank()
                self.mm(ps[:, 0:128], Kb[kc][:, bb * 128:(bb + 1) * 128], C["ident_b"])
                self.ts("dve", Ktm[:, bb, kc * 128:(kc + 1) * 128], ps[:, 0:128], R["ecol"][:, bb, h:h + 1], None, ALU.mult)
        self.memset("pool", Vtm[:, :, 256:257], 1.0)
        for vc in range(2):
            S.dma(raw, self.projT[MB + 2048 + h * 256 + vc * 128:MB + 2048 + h * 256 + (vc + 1) * 128, :])
            self.cp("dve", vb, raw[:, 0:T])
            self.cp("pool", vs[:, vc, :], raw[:, T:NT])
            for bb in range(NB):
                ps = P.bank()
                self.mm(ps[:, 0:128], vb[:, bb * 128:(bb + 1) * 128], C["ident_b"])
                self.cp("act", Vtm[:, bb, vc * 128:(vc + 1) * 128], ps[:, 0:128])
        A.release(m1)
        CM = A.f32(2 * 257).rearrange("p (k n) -> p k n", k=2)
        Cb = A.bf(2 * 256).rearrange("p (k n) -> p k n", k=2)
        nbc = A.bf(2 * 128).rearrange("p (k n) -> p k n", k=2)
        tmp = A.f32(128)
        ET = A.f32(128)
        PT = A.bf(128)
        absd = A.f32(128)
        self.memset("dve", CM, 0.0)
        self.memset("dve", Cb, 0.0)
        self.memset("dve", nbc, 0.0)
        for bb in range(NB):
            blk = slice(bb * 128, (bb + 1) * 128)
            ps = P.bank()
            self.mm(ps[:, 0:128], Kb[0][:, blk], Qb[0][:, blk], start=True, stop=False)
            self.mm(ps[:, 0:128], Kb[1][:, blk], Qb[1][:, blk], start=False, stop=True)
            self.stt(tmp, GBA[:, blk], -1.0, C["negm"], ALU.mult, ALU.add)
            self.act(ET, tmp, AF.Exp, bias=R["acol"][:, bb, h:h + 1])
            self.tt("dve", PT, ps[:, 0:128], ET, ALU.mult)
            pn = P.bank()
            for vc in range(2):
                o = pn[:, vc * 128:(vc + 1) * 128]
                self.mm(o, Vtm[:, bb, vc * 128:(vc + 1) * 128], PT, start=True, stop=False)
                self.mm(o, Cb[:, 0, vc * 128:(vc + 1) * 128], Qw[0][:, blk], start=False, stop=False)
                self.mm(o, Cb[:, 1, vc * 128:(vc + 1) * 128], Qw[1][:, blk], start=False, stop=True)
            pd = P.bank()
            self.mm(pd[:, 0:128], C["ones_b"], PT, start=True, stop=False)
            self.mm(pd[:, 0:128], nbc[:, 0, :], Qw[0][:, blk], start=False, stop=False)
            self.mm(pd[:, 0:128], nbc[:, 1, :], Qw[1][:, blk], start=False, stop=True)
            self.act(absd, pd[:, 0:128], AF.Abs)
            self.tt("dve", absd, absd, GBem[:, blk], ALU.max)
            self.recip(absd, absd)
            for vc in range(2):
                self.tt("dve", Hh[vc][:, blk], pn[:, vc * 128:(vc + 1) * 128], absd, ALU.mult)
            for kc in range(2):
                pc = P.bank()
                self.mm(pc[:, 0:257], Ktm[:, bb, kc * 128:(kc + 1) * 128], Vtm[:, bb, :])
                self.stt(CM[:, kc, :], CM[:, kc, :], GBd[:, bb:bb + 1], pc[:, 0:257], ALU.mult, ALU.add)
                self.cp("act", Cb[:, kc, :], CM[:, kc, 0:256])
                self.cp("pool", nbc[:, kc, :], CM[:, kc, 256:257].to_broadcast([128, 128]))
        for kc in range(2):
            S.dma(O["p_C"][l, h, kc * 128:(kc + 1) * 128, :], CM[:, kc, 0:256], eng="sp")
            self.dmas(O["p_n"][l, h, kc * 128:(kc + 1) * 128].unsqueeze(1), CM[:, kc, 256:257], eng="sp")
        m2 = A.mark()
        ktm = A.f32(256, parts=NS)
        vau = A.f32(257, parts=NS)
        ps = P.bank()
        for kc in range(2):
            self.tr(ps[0:NS, kc * 128:(kc + 1) * 128], ks[:, kc, :], C["ident_f"])
            self.tr(ps[0:NS, 256 + kc * 128:256 + (kc + 1) * 128], vs[:, kc, :], C["ident_f"])
        self.ts("dve", ktm, ps[0:NS, 0:256], R["wscol"][:, h:h + 1], None, ALU.mult)
        self.cp("act", vau[:, 0:256], ps[0:NS, 256:512])
        self.memset("dve", vau[:, 256:257], 1.0)
        Kex = A.f32(NS * 256, parts=NS).rearrange("p (i c) -> p i c", i=NS)
        eye = C["ident_f"][0:NS, 0:NS].unsqueeze(2).to_broadcast([NS, NS, 256])
        self.tt("dve", Kex, ktm.unsqueeze(1).to_broadcast([NS, NS, 256]), eye, ALU.mult)
        CS = [A.f32(2 * 257).rearrange("p (k n) -> p k n", k=2) for _ in range(4)]
        nb2 = [A.f32(2 * 128).rearrange("p (k n) -> p k n", k=2) for _ in range(4)]
        rb, pN = P.reserve()
        for i in range(NS):
            k2 = i % 4
            cs_ = CS[k2]
            for kc in range(2):
                S.dma(cs_[:, kc, 0:256], I["s_C"][l, i, h, kc * 128:(kc + 1) * 128, :])
                self.dmas(cs_[:, kc, 256:257], I["s_n"][l, i, h, kc * 128:(kc + 1) * 128].unsqueeze(1))
            for kc in range(2):
                pc = P.bank()
                self.mm(pc[:, 0:257], Kex[:, i, kc * 128:(kc + 1) * 128], vau)
                self.stt(cs_[:, kc, :], cs_[:, kc, :], GBac[:, i:i + 1], pc[:, 0:257], ALU.mult, ALU.add)
                S.dma(O["s_C"][l, i, h, kc * 128:(kc + 1) * 128, :], cs_[:, kc, 0:256], eng="sp")
                self.dmas(O["s_n"][l, i, h, kc * 128:(kc + 1) * 128].unsqueeze(1), cs_[:, kc, 256:257], eng="sp")
                self.cp("pool", nb2[k2][:, kc, :], cs_[:, kc, 256:257].to_broadcast([128, 128]))
            for vc in range(2):
                o = pN[:, vc * NS + i:vc * NS + i + 1]
                self.mm(o, cs_[:, 0, vc * 128:(vc + 1) * 128], qs[:, 0, i:i + 1], start=True, stop=False)
                self.mm(o, cs_[:, 1, vc * 128:(vc + 1) * 128], qs[:, 1, i:i + 1], start=False, stop=True)
            o = pN[:, 2 * NS + i:2 * NS + i + 1]
            self.mm(o, nb2[k2][:, 0, :], qs[:, 0, i:i + 1], start=True, stop=False)
            self.mm(o, nb2[k2][:, 1, :], qs[:, 1, i:i + 1], start=False, stop=True)
        ad = A.f32(NS)
        self.act(ad, pN[:, 2 * NS:3 * NS], AF.Abs)
        self.tt("dve", ad, ad, GBem[:, T:NT], ALU.max)
        self.recip(ad, ad)
        for vc in range(2):
            self.tt("dve", Hh[vc][:, T:NT], pN[:, vc * NS:(vc + 1) * NS], ad, ALU.mult)
        P.unreserve(rb)
        A.release(m2)
        self.head_norm_out(Hh, MB + 3080 + h * 256, "m_ng", h * 2, AF.Sigmoid, BW + h * 256)

    def head_norm_out(self, Hh, gate_row0, gname, gidx0, gfunc, ob_row0):
        A, S, C, P = self.A, self.S, self.C, self.P
        NT = self.NT
        tl = ttiles(NT)
        m = A.mark()
        sq = [A.bf(NT) for _ in range(2)]
        rs = A.f32(NT)
        graw = A.f32(NT)
        obf = A.bf(NT)
        for vc in range(2):
            self.act(sq[vc], Hh[vc], AF.Square)
        for (t0, tn) in tl:
            ps = P.bank()
            self.mm(ps[:, 0:tn], C["ones_b"], sq[0][:, t0:t0 + tn], start=True, stop=False)
            self.mm(ps[:, 0:tn], C["ones_b"], sq[1][:, t0:t0 + tn], start=False, stop=True)
            self.rsqrt(rs[:, t0:t0 + tn], ps[:, 0:tn], scale=1.0 / 256, bias=C["eps"])
        for vc in range(2):
            S.dma(graw, self.projT[gate_row0 + vc * 128:gate_row0 + (vc + 1) * 128, :])
            self.act(graw, graw, gfunc)
            self.tt("dve", Hh[vc], Hh[vc], rs, ALU.mult)
            self.stt(obf, Hh[vc], self.V(gname, gidx0 + vc), graw, ALU.mult, ALU.mult)
            S.dma(self.obT[ob_row0 + vc * 128:ob_row0 + (vc + 1) * 128, :], obf, eng="sp")
        A.release(m)

    def gla(self, l):
        A, S, C, I, O, P = self.A, self.S, self.C, self.I, self.O, self.P
        T, NS, NT = self.T, self.NS, self.NT
        NB = T // 128
        tl = ttiles(NT)
        A.release(self.abase)
        a2b = A.bf(512, parts=16)
        xab = A.bf(NT, parts=16)
        m1 = A.mark()
        a2s = A.f32(512, parts=16)
        xas = A.f32(NT, parts=16)
        S.dma(a2s, I["ga2"][l])
        S.dma(xas, self.projT[GB_ + 2048:GB_ + 2064, :])
        self.cp("dve", a2b, a2s)
        self.cp("dve", xab, xas)
        A.release(m1)
        base = A.mark()
        v3 = lambda ap: ap[:, 0:T].rearrange("p (c j) -> p c j", j=128)
        for h in range(4):
            A.release(base)
            lg, bc, e1 = A.f32(NT), A.f32(NT), A.f32(NT)
            raw = A.f32(NT)
            Qh, Kh = A.bf(T), A.bf(T)
            WLg = A.f32(NB)
            qs, ksm, dgs = A.f32(NS), A.f32(NS), A.f32(NS)
            vs = A.f32(2 * NS).rearrange("p (k i) -> p k i", k=2)
            Vtm = A.bf(NB * 256).rearrange("p (b n) -> p b n", b=NB)
            Ktm = A.bf(NB * 128).rearrange("p (b n) -> p b n", b=NB)
            OG = [A.f32(NT) for _ in range(2)]
            vb = A.bf(T)
            for (t0, tn) in tl:
                ps = P.bank()
                self.mm(ps[:, 0:tn], a2b[:, h * 128:(h + 1) * 128], xab[:, t0:t0 + tn])
                self.act(lg[:, t0:t0 + tn], ps[:, 0:tn], AF.Sigmoid, bias=self.V("ga_b", h))
            self.act(lg, lg, AF.Ln)
            self.ts("pool", lg, lg, 1.0 / 16, None, ALU.mult)
            self.scan(bc[:, 0:T], C["rst128"], lg[:, 0:T], 0.0, ALU.mult, ALU.add)
            self.act(WLg, v3(bc)[:, :, 127], AF.Exp)
            self.act(dgs, lg[:, T:NT], AF.Exp)
            S.dma(raw, self.projT[GB_ + h * 128:GB_ + (h + 1) * 128, :])
            self.act(e1[:, 0:T], bc[:, 0:T], AF.Exp)
            self.stt(Qh, raw[:, 0:T], 128.0 ** -0.5, e1[:, 0:T], ALU.mult, ALU.mult)
            self.ts("pool", qs, raw[:, T:NT], 128.0 ** -0.5, None, ALU.mult)
            S.dma(raw, self.projT[GB_ + 512 + h * 128:GB_ + 512 + (h + 1) * 128, :])
            self.act(e1[:, 0:T], bc[:, 0:T], AF.Exp, scale=-1.0)
            self.tt("dve", Kh, raw[:, 0:T], e1[:, 0:T], ALU.mult)
            self.cp("pool", ksm, raw[:, T:NT])
            for bb in range(NB):
                ps = P.bank()
                self.mm(ps[:, 0:128], Kh[:, bb * 128:(bb + 1) * 128], C["ident_b"])
                self.cp("act", Ktm[:, bb, :], ps[:, 0:128])
            for vc in range(2):
                S.dma(raw, self.projT[GB_ + 1024 + h * 256 + vc * 128:GB_ + 1024 + h * 256 + (vc + 1) * 128, :])
                self.cp("dve", vb, raw[:, 0:T])
                self.cp("pool", vs[:, vc, :], raw[:, T:NT])
                for bb in range(NB):
                    ps = P.bank()
                    self.mm(ps[:, 0:128], vb[:, bb * 128:(bb + 1) * 128], C["ident_b"])
                    self.cp("act", Vtm[:, bb, vc * 128:(vc + 1) * 128], ps[:, 0:128])
            SM, SMs = A.f32(256), A.f32(256)
            Sb = A.bf(256)
            PTg = A.bf(128)
            self.memset("dve", SM, 0.0)
            self.memset("dve", Sb, 0.0)
            for bb in range(NB):
                blk = slice(bb * 128, (bb + 1) * 128)
                ps = P.bank()
                self.mm(ps[:, 0:128], Kh[:, blk], Qh[:, blk])
                self.tt("dve", PTg, ps[:, 0:128], C["m_le_b"], ALU.mult)
                po = P.bank()
                for vc in range(2):
                    o = po[:, vc * 128:(vc + 1) * 128]
                    self.mm(o, Vtm[:, bb, vc * 128:(vc + 1) * 128], PTg, start=True, stop=False)
                    self.mm(o, Sb[:, vc * 128:(vc + 1) * 128], Qh[:, blk], start=False, stop=True)
                    self.cp("act", OG[vc][:, blk], o)
                pz = P.bank()
                self.mm(pz[:, 0:256], Ktm[:, bb, :], Vtm[:, bb, :])
                self.ts("pool", SMs, SM, WLg[:, bb:bb + 1], None, ALU.mult)
                self.stt(SM, pz[:, 0:256], WLg[:, bb:bb + 1], SMs, ALU.mult, ALU.add)
                self.cp("act", Sb, SM)
            S.dma(O["p_gla"][l, h], SM, eng="sp")
            ktm = A.f32(128, parts=NS)
            vtm = A.f32(256, parts=NS)
            ps = P.bank()
            self.tr(ps[0:NS, 0:128], ksm, C["ident_f"])
            for vc in range(2):
                self.tr(ps[0:NS, 128 + vc * 128:128 + (vc + 1) * 128], vs[:, vc, :], C["ident_f"])
            self.cp("act", ktm, ps[0:NS, 0:128])
            self.cp("act", vtm, ps[0:NS, 128:384])
            Kex = A.f32(NS * 128, parts=NS).rearrange("p (i c) -> p i c", i=NS)
            eye = C["ident_f"][0:NS, 0:NS].unsqueeze(2).to_broadcast([NS, NS, 128])
            self.tt("dve", Kex, ktm.unsqueeze(1).to_broadcast([NS, NS, 128]), eye, ALU.mult)
            SS = [A.f32(256) for _ in range(4)]
            rb, pO = P.reserve()
            for i in range(NS):
                ss = SS[i % 4]
                S.dma(ss, I["s_gla"][l, i, h])
                pz = P.bank()
                self.mm(pz[:, 0:256], Kex[:, i, :], vtm)
                self.stt(ss, ss, dgs[:, i:i + 1], pz[:, 0:256], ALU.mult, ALU.add)
                S.dma(O["s_gla"][l, i, h], ss, eng="sp")
                for vc in range(2):
                    self.mm(pO[:, vc * NS + i:vc * NS + i + 1], ss[:, vc * 128:(vc + 1) * 128], qs[:, i:i + 1])
            for vc in range(2):
                self.cp("act", OG[vc][:, T:NT], pO[:, vc * NS:(vc + 1) * NS])
            P.unreserve(rb)
            self.head_norm_out(OG, GB_ + 2064 + h * 256, "g_ng", h * 2, AF.Silu, 2 * BW + h * 256)

    def merge(self, l):
        A, S, C, I, P = self.A, self.S, self.C, self.I, self.P
        NT = self.NT
        tl = ttiles(NT)
        A.release(self.abase)
        OB = A.bf(24 * NT).rearrange("p (c t) -> p c t", c=24)
        S.dma(OB, self.obT.rearrange("(c p) t -> p c t", p=128))
        wst = [A.f32(8 * 128).rearrange("p (k n) -> p k n", k=8) for _ in range(2)]
        wbf = [A.bf(8 * 128).rearrange("p (k n) -> p k n", k=8) for _ in range(2)]
        graw = [A.f32(NT) for _ in range(2)]
        accs = [A.f32(NT) for _ in range(2)]
        tmp = A.f32(512)
        mgb = [A.bf(NT) for _ in range(2)]
        it = 0
        for dc in range(16):
            acc = accs[dc % 2]
            for j in range(3):
                ws, wb, gr = wst[it % 2], wbf[it % 2], graw[it % 2]
                it += 1
                S.dma(ws, I["w_branch"][l, j, :, dc * 128:(dc + 1) * 128].rearrange("(k p) n -> p k n", p=128))
                self.cp("pool", wb, ws)
                S.dma(gr, self.projT[GTB + j * D + dc * 128:GTB + j * D + (dc + 1) * 128, :])
                self.act(gr, gr, AF.Sigmoid, bias=self.V("gate_b", j * 16 + dc))
                banks = P.banks(len(tl))
                for k in range(8):
                    for bi, (t0, tn) in enumerate(tl):
                        self.mm(banks[bi][:, 0:tn], wb[:, k, :], OB[:, j * 8 + k, t0:t0 + tn],
                                start=(k == 0), stop=(k == 7))
                for bi, (t0, tn) in enumerate(tl):
                    if j == 0:
                        self.tt("dve", acc[:, t0:t0 + tn], banks[bi][:, 0:tn], gr[:, t0:t0 + tn], ALU.mult)
                    else:
                        self.tt("dve", tmp[:, 0:tn], banks[bi][:, 0:tn], gr[:, t0:t0 + tn], ALU.mult)
                        self.tt("pool", acc[:, t0:t0 + tn], acc[:, t0:t0 + tn], tmp[:, 0:tn], ALU.add)
            self.cp("act", mgb[dc % 2], acc)
            S.dma(self.mgT[dc * 128:(dc + 1) * 128, :], mgb[dc % 2], eng="act")

    def resid_sink(self, it, m, rows, banks, tl):
        A, S = self.A, self.S
        A.release(self._dense_m0)
        stg = [A.f32(self.NT) for _ in range(2)]
        o = stg[it % 2]
        S.dma(o, self._res_src[m * 128:(m + 1) * 128, :])
        for bi, (t0, tn) in enumerate(tl):
            self.tt("dve", o[:, t0:t0 + tn], o[:, t0:t0 + tn], banks[bi][:, 0:tn], ALU.add)
        S.dma(self.xT[m * 128:(m + 1) * 128, :], o, eng="sp")

    def outproj(self, l):
        A, S, I = self.A, self.S, self.I
        NT = self.NT
        A.release(self.abase)
        MG = A.bf(16 * NT).rearrange("p (c t) -> p c t", c=16)
        S.dma(MG, self.mgT.rearrange("(c p) t -> p c t", p=128))
        self._res_src = self.src
        self.dense(I["w_out"][l], D, 16, MG, self.resid_sink)

    def ffn(self, l):
        A, S, I = self.A, self.S, self.I
        NT = self.NT
        NF = DFF // 128
        order = []
        for fc in range(NF):
            order += [fc, NF + fc]
        self.dense(I["w_gu"][l], 2 * DFF, 16, self.hT, self.gu_sink, order=order)
        self._res_src = self.xT
        for half in range(2):
            A.release(self.abase)
            AH = A.bf((NF // 2) * NT).rearrange("p (c t) -> p c t", c=NF // 2)
            r0 = half * (DFF // 2)
            S.dma(AH, self.actT[r0:r0 + DFF // 2, :].rearrange("(c p) t -> p c t", p=128))
            self.dense(I["w_down"][l, r0:r0 + DFF // 2, :], D, NF // 2, AH, self.resid_sink)

    def gu_sink(self, it, m, rows, banks, tl):
        A, S = self.A, self.S
        A.release(self._dense_m0)
        sil = A.f32(self.NT)
        ab = [A.bf(self.NT) for _ in range(2)]
        if it % 2 == 0:
            for bi, (t0, tn) in enumerate(tl):
                self.act(sil[:, t0:t0 + tn], banks[bi][:, 0:tn], AF.Silu)
        else:
            o = ab[(it // 2) % 2]
            fc = m - DFF // 128
            for bi, (t0, tn) in enumerate(tl):
                self.tt("dve", o[:, t0:t0 + tn], sil[:, t0:t0 + tn], banks[bi][:, 0:tn], ALU.mult)
            S.dma(self.actT[fc * 128:(fc + 1) * 128, :], o, eng="sp")


def _cols(v, n):
    return np.ascontiguousarray(np.asarray(v).reshape(n, 128).T)


def pack_vec(inp, l):
    out = np.zeros((128, VEC_N), np.float32)

    def put(name, arr):
        off, n = VEC_OFF[name]
        out[:arr.shape[0], off:off + n] = arr

    put("n1g", _cols(inp["norm1_g"][l], 16))
    put("n2g", _cols(inp["norm2_g"][l], 16))
    put("gate_b", _cols(inp["gate_b"][l].reshape(-1), 48))
    mu = np.asarray(inp["rwkv_mu"][l])
    put("mu", _cols(mu[:3072], 24))
    put("mul", np.ascontiguousarray(mu[3072:3264].reshape(3, 64).T))
    for nm, key in [("w0", "rwkv_w0"), ("a0", "rwkv_a0"), ("k_k", "rwkv_k_k"), ("k_a", "rwkv_k_a"),
                    ("ln_g", "rwkv_ln_g"), ("ln_b", "rwkv_ln_b"), ("m_ng", "mlstm_norm_g"), ("g_ng", "gla_norm_g")]:
        put(nm, _cols(inp[key][l], 8))
    put("r_k", _cols(np.asarray(inp["rwkv_r_k"][l]).reshape(-1), 8))
    put("conv_w", _cols(np.asarray(inp["mlstm_conv_w"][l]).reshape(-1), 64))
    put("conv_b", _cols(inp["mlstm_conv_b"][l], 16))
    put("ga_b", _cols(inp["gla_a_b"][l], 4))
    return out


_NC_CACHE = {}


def make_in_map(inp, L, xT, samp):
    f = lambda a: np.ascontiguousarray(np.asarray(a, dtype=np.float32))
    m = {}
    m["xT"] = f(xT)
    m["vec"] = np.stack([pack_vec(inp, l) for l in range(L)])
    m["rows"] = f(np.stack([np.stack([inp["mlstm_i_b"][l], inp["mlstm_f_b"][l]], axis=1) for l in range(L)]))
    m["w_in"] = f(inp["w_in"][:L])
    m["rw2"] = f(np.stack([np.stack([inp["rwkv_w2"][l], inp["rwkv_a2"][l], inp["rwkv_g2"][l]]) for l in range(L)]))
    m["ga2"] = f(inp["gla_a2"][:L])
    m["w_branch"] = f(inp["w_branch"][:L])
    m["w_out"] = f(inp["w_out"][:L])
    m["w_gu"] = f(inp["ffn_w_gu"][:L])
    m["w_down"] = f(inp["ffn_w_down"][:L])
    m["fng"] = _cols(inp["final_norm_g"], 16)
    m["s_shift"] = f(np.transpose(inp["state_rwkv_shift"][:L, samp], (0, 2, 1)))
    m["s_wkv"] = f(inp["state_rwkv_wkv"][:L, samp])
    m["s_conv"] = f(np.transpose(inp["state_mlstm_conv"][:L, samp], (0, 2, 3, 1)))
    m["s_C"] = f(inp["state_mlstm_C"][:L, samp])
    m["s_n"] = f(inp["state_mlstm_n"][:L, samp])
    m["s_m"] = f(np.transpose(inp["state_mlstm_m"][:L, samp], (0, 2, 1)))
    m["s_gla"] = f(inp["state_gla_S"][:L, samp])
    return m


def kernel(**inputs):
    inp = {k: np.asarray(v) for k, v in inputs.items()}
    B, T, _ = inp["x_prompt"].shape
    NSALL = inp["x_sample"].shape[0]
    L = inp["w_in"].shape[0]
    NS = NSALL // NCORES
    key = (T, NS, L)
    if key not in _NC_CACHE:
        _NC_CACHE[key] = Builder(T, NS, L).build()
    nc = _NC_CACHE[key]
    in_maps = []
    for c in range(NCORES):
        seq = c // 2
        samp = slice(c * NS, (c + 1) * NS)
        xT = np.concatenate([inp["x_prompt"][seq].T, inp["x_sample"][samp, 0, :].T], axis=1)
        in_maps.append(make_in_map(inp, L, xT, samp))
    res = run_bass_kernel_spmd(nc, in_maps, core_ids=list(range(NCORES)))
    R = res.results
    P = [R[2 * s] for s in range(B)]
    f32 = np.float32
    y_prompt = np.stack([P[s]["yT"][:, :T].T for s in range(B)]).astype(f32)
    y_sample = np.concatenate([R[c]["yT"][:, T:].T for c in range(NCORES)])[:, None, :].astype(f32)
    p_shift = np.stack([P[s]["o_shift"][:, :, 0] for s in range(B)], axis=1)
    s_shift = np.concatenate([np.transpose(R[c]["o_shift"][:, :, 1:], (0, 2, 1)) for c in range(NCORES)], axis=1)
    p_wkv = np.stack([P[s]["p_wkv"] for s in range(B)], axis=1)
    s_wkv = np.concatenate([R[c]["o_s_wkv"] for c in range(NCORES)], axis=1)
    p_conv = np.stack([np.transpose(P[s]["o_conv"][:, :, :, 0], (0, 1, 2)) for s in range(B)], axis=1)
    s_conv = np.concatenate([np.transpose(R[c]["o_conv"][:, :, :, 1:], (0, 3, 1, 2)) for c in range(NCORES)], axis=1)
    p_C = np.stack([P[s]["p_C"] for s in range(B)], axis=1)
    s_C = np.concatenate([R[c]["o_s_C"] for c in range(NCORES)], axis=1)
    p_n = np.stack([P[s]["p_n"] for s in range(B)], axis=1)
    s_n = np.concatenate([R[c]["o_s_n"] for c in range(NCORES)], axis=1)
    p_m = np.stack([P[s]["o_m"][:, :, 0] for s in range(B)], axis=1)
    s_m = np.concatenate([np.transpose(R[c]["o_m"][:, :, 1:], (0, 2, 1)) for c in range(NCORES)], axis=1)
    p_gla = np.stack([P[s]["p_gla"] for s in range(B)], axis=1)
    s_gla = np.concatenate([R[c]["o_s_gla"] for c in range(NCORES)], axis=1)
    outs = (y_prompt, y_sample, p_shift, p_wkv, p_conv, p_C, p_n, p_m, p_gla,
            s_shift, s_wkv, s_conv, s_C, s_n, s_m, s_gla)
    return tuple(np.ascontiguousarray(o, dtype=f32) for o in outs)
```

```python
from contextlib import ExitStack
import numpy as np
import concourse.bass as bass
import concourse.mybir as mybir
from concourse.bass_utils import run_bass_kernel_spmd

F32 = mybir.dt.float32
BF16 = mybir.dt.bfloat16
AF = mybir.ActivationFunctionType
ALU = mybir.AluOpType
AX = mybir.AxisListType
_ESZ = {F32: 4, BF16: 2, mybir.dt.int32: 4, mybir.dt.uint8: 1}

D = 2048
BW = 1024
R_COLS = 3264
M_COLS = 4104
G_COLS = 3088
N_IN = 16600
DFF = 5632
RB, MB, GB_, GTB = 0, 3264, 7368, 10456
EPS = 1e-6
R_GN_EPS = 64e-5
NCORES = 8


class _Rec:
    __slots__ = ("sp", "plo", "phi", "lo", "hi", "lw", "rde", "rdd", "cells", "dead")


class _Op:
    __slots__ = ("eng", "fn", "dma", "deps", "signal", "semval", "semidx")


def _region(ap):
    t = ap.tensor
    tn = type(t).__name__
    esz = _ESZ[ap.dtype]
    pairs = ap.ap
    if tn == "DRamTensorHandle":
        ext = 1
        for st, cnt in pairs:
            ext += (cnt - 1) * abs(st)
        lo = ap.offset * esz
        return ("d" + t.name, 0, 1, lo, lo + ext * esz)
    tshape = t.shape
    rowb = 1
    for s in tshape[1:]:
        rowb *= s
    rowb *= _ESZ[t.dtype]
    off = ap.offset * esz
    p0 = off // rowb
    c0 = off % rowb
    ext = 1
    for st, cnt in pairs[1:]:
        ext += (cnt - 1) * abs(st)
    np_ = pairs[0][1]
    if tn == "SBTensorHandle":
        return ("s", p0, p0 + np_, c0, c0 + ext * esz)
    lo = (c0 // 2048) * 2048
    hi = ((c0 + ext * esz + 2047) // 2048) * 2048
    return ("p", 0, 128, lo, hi)


class Sched:
    ENGS = ("pe", "act", "dve", "pool", "sp")

    def __init__(self, nc, n_dma_sems=64):
        self.nc = nc
        self.ops = []
        self.nds = n_dma_sems
        self.eng_obj = {"pe": nc.tensor, "act": nc.scalar, "dve": nc.vector, "pool": nc.gpsimd, "sp": nc.sync}
        self.recs = {}
        self.grid = {}
        self.cellsz = {}

    def _cells(self, key):
        sp, plo, phi, lo, hi = key
        if sp == "s":
            cs = 2048
        elif sp == "p":
            cs = 2048
        else:
            cs = 1 << 20
        return [(sp, c) for c in range(lo // cs, (hi - 1) // cs + 1)]

    def _get(self, key):
        r = self.recs.get(key)
        if r is None:
            r = _Rec()
            r.sp, r.plo, r.phi, r.lo, r.hi = key
            r.lw = -1
            r.rde = {}
            r.rdd = []
            r.cells = self._cells(key)
            r.dead = False
            self.recs[key] = r
            for c in r.cells:
                self.grid.setdefault(c, set()).add(key)
        return r

    def _overlaps(self, key):
        sp, plo, phi, lo, hi = key
        out = set()
        for c in self._cells(key):
            g = self.grid.get(c)
            if g:
                out |= g
        res = []
        for k in out:
            if k[1] < phi and plo < k[2] and k[3] < hi and lo < k[4]:
                res.append(k)
        return res

    def op(self, eng, fn, reads=(), writes=(), dma=False):
        i = len(self.ops)
        o = _Op()
        o.eng, o.fn, o.dma = eng, fn, dma
        o.signal = False
        o.semval = 0
        o.semidx = -1
        deps = set()
        rkeys = [_region(a) for a in reads]
        wkeys = [_region(a) for a in writes]
        for k in rkeys:
            for q in self._overlaps(k):
                r = self.recs[q]
                if r.lw >= 0:
                    deps.add(r.lw)
        for k in wkeys:
            for q in self._overlaps(k):
                r = self.recs[q]
                if r.lw >= 0:
                    deps.add(r.lw)
                for v in r.rde.values():
                    deps.add(v)
                for v in r.rdd:
                    deps.add(v)
        keep = []
        for d in deps:
            od = self.ops[d]
            if (not od.dma) and (not dma) and od.eng == eng and eng == "pe":
                continue
            keep.append(d)
            od.signal = True
        o.deps = keep
        self.ops.append(o)
        for k in rkeys:
            r = self._get(k)
            if dma:
                r.rdd.append(i)
            else:
                r.rde[eng] = i
        for k in wkeys:
            for q in self._overlaps(k):
                if q != k and q[1] >= k[1] and q[2] <= k[2] and q[3] >= k[3] and q[4] <= k[4]:
                    r = self.recs.pop(q)
                    for c in r.cells:
                        self.grid[c].discard(q)
            r = self._get(k)
            r.lw = i
            r.rde = {}
            r.rdd = []
        if dma:
            o.signal = True
        return o

    def dma(self, out, in_, eng="sp", **kw):
        return self.op(eng, lambda e: e.dma_start(out=out, in_=in_, **kw), [in_], [out], dma=True)

    def emit(self, stack):
        nc = self.nc
        cnt = {e: 0 for e in self.ENGS}
        dcnt = [0] * self.nds
        di = 0
        for o in self.ops:
            if not o.signal:
                continue
            if o.dma:
                o.semidx = di % self.nds
                dcnt[o.semidx] += 16
                o.semval = dcnt[o.semidx]
                di += 1
            else:
                cnt[o.eng] += 1
                o.semval = cnt[o.eng]
        sems = {e: stack.enter_context(nc.semaphore("s_" + e)) for e in self.ENGS}
        dsems = [stack.enter_context(nc.semaphore("d_%d" % i)) for i in range(self.nds)]
        waited = {e: {} for e in self.ENGS}
        nw = 0
        for o in self.ops:
            e = self.eng_obj[o.eng]
            need = {}
            for d in o.deps:
                od = self.ops[d]
                key = ("d", od.semidx) if od.dma else ("e", od.eng)
                if od.semval > need.get(key, 0):
                    need[key] = od.semval
            wd = waited[o.eng]
            if o.dma and o.semval > 16:
                k0 = ("d", o.semidx)
                if o.semval - 16 > need.get(k0, 0):
                    need[k0] = o.semval - 16
            for key, val in need.items():
                if wd.get(key, 0) >= val:
                    continue
                wd[key] = val
                e.wait_ge(dsems[key[1]] if key[0] == "d" else sems[key[1]], val)
                nw += 1
            ins = o.fn(e)
            if o.signal:
                if o.dma:
                    ins.then_inc(dsems[o.semidx], 16)
                else:
                    ins.then_inc(sems[o.eng], 1)
        for i, v in enumerate(dcnt):
            if v:
                nc.sync.wait_ge(dsems[i], v)
        return nw


class Arena:
    def __init__(self, t, ncols):
        self.t = t
        self.ncols = ncols
        self.top = 0

    def mark(self):
        return self.top

    def release(self, m):
        self.top = m

    def f32(self, cols, parts=128):
        c0 = self.top
        self.top += cols
        assert self.top <= self.ncols, ("arena overflow", self.top, self.ncols)
        return self.t[0:parts, c0:c0 + cols]

    def bf(self, cols, parts=128):
        n32 = (cols + 1) // 2
        a = self.f32(n32, parts)
        return a.bitcast(BF16)[:, 0:cols]


class Psum:
    def __init__(self, t):
        self.t = t
        self.nb = 0

    def bank(self):
        b = self.nb % 8
        self.nb += 1
        return self.t[:, b * 512:(b + 1) * 512]

    def banks(self, n):
        return [self.bank() for _ in range(n)]


def ttiles(NT):
    out = []
    c = 0
    while c < NT:
        n = min(512, NT - c)
        out.append((c, n))
        c += n
    return out


VEC_OFF = {}
_o = 0
for _n, _c in [("n1g", 16), ("n2g", 16), ("gate_b", 48), ("mu", 24), ("mul", 3), ("w0", 8), ("a0", 8), ("k_k", 8),
               ("k_a", 8), ("r_k", 8), ("ln_g", 8), ("ln_b", 8), ("conv_w", 64), ("conv_b", 16), ("m_ng", 8),
               ("g_ng", 8), ("ga_b", 4)]:
    VEC_OFF[_n] = (_o, _c)
    _o += _c
VEC_N = _o
DV_OFF = {"omu": (0, 24), "omul": (24, 3), "omk_a": (27, 8)}
DV_N = 35


class Psum2(Psum):
    def __init__(self, t):
        Psum.__init__(self, t)
        self.reserved = set()

    def bank(self):
        while True:
            b = self.nb % 8
            self.nb += 1
            if b not in self.reserved:
                return self.t[:, b * 512:(b + 1) * 512]

    def reserve(self):
        while True:
            b = self.nb % 8
            self.nb += 1
            if b not in self.reserved:
                self.reserved.add(b)
                return b, self.t[:, b * 512:(b + 1) * 512]

    def unreserve(self, b):
        self.reserved.discard(b)


class StopBuild(Exception):
    pass


class Builder:
    def chk(self, name):
        import os
        if os.environ.get("SUBSTOP") == name:
            raise StopBuild()

    def __init__(self, T, NS, depth, dbg=(), stop=None):
        self.T, self.NS, self.NT, self.depth = T, NS, T + NS, depth
        self.dbg = set(dbg)
        self.stop = stop
        assert T % 128 == 0

    def dram(self, name, shape, dt=F32, kind=None):
        if kind is None:
            kind = "ExternalOutput" if name in self.dbg else "Internal"
        return self.nc.dram_tensor(name, list(shape), dt, kind=kind).ap()

    def build(self):
        nc = bass.Bass("TRN2", target_bir_lowering=False)
        self.nc = nc
        T, NS, NT, L = self.T, self.NS, self.NT, self.depth
        inp = lambda n, s: self.dram(n, s, kind="ExternalInput")
        outp = lambda n, s: self.dram(n, s, kind="ExternalOutput")
        I = self.I = {}
        I["xT"] = inp("xT", [D, NT])
        I["vec"] = inp("vec", [L, 128, VEC_N])
        I["rows"] = inp("rows", [L, 4, 2])
        I["w_in"] = inp("w_in", [L, D, N_IN])
        I["rw2"] = inp("rw2", [L, 3, 64, BW])
        I["ga2"] = inp("ga2", [L, 16, 512])
        I["w_branch"] = inp("w_branch", [L, 3, BW, D])
        I["w_out"] = inp("w_out", [L, D, D])
        I["w_gu"] = inp("w_gu", [L, D, 2 * DFF])
        I["w_down"] = inp("w_down", [L, DFF, D])
        I["fng"] = inp("fng", [128, 16])
        I["s_shift"] = inp("s_shift", [L, R_COLS, NS])
        I["s_wkv"] = inp("s_wkv", [L, NS, 16, 64, 64])
        I["s_conv"] = inp("s_conv", [L, 3, D, NS])
        I["s_C"] = inp("s_C", [L, NS, 4, 256, 256])
        I["s_n"] = inp("s_n", [L, NS, 4, 256])
        I["s_m"] = inp("s_m", [L, 4, NS])
        I["s_gla"] = inp("s_gla", [L, NS, 4, 128, 256])
        O = self.O = {}
        O["yT"] = outp("yT", [D, NT])
        O["o_shift"] = outp("o_shift", [L, R_COLS, 1 + NS])
        O["p_wkv"] = outp("p_wkv", [L, 16, 64, 64])
        O["s_wkv"] = outp("o_s_wkv", [L, NS, 16, 64, 64])
        O["o_conv"] = outp("o_conv", [L, 3, D, 1 + NS])
        O["p_C"] = outp("p_C", [L, 4, 256, 256])
        O["p_n"] = outp("p_n", [L, 4, 256])
        O["s_C"] = outp("o_s_C", [L, NS, 4, 256, 256])
        O["s_n"] = outp("o_s_n", [L, NS, 4, 256])
        O["o_m"] = outp("o_m", [L, 4, 1 + NS])
        O["p_gla"] = outp("p_gla", [L, 4, 128, 256])
        O["s_gla"] = outp("o_s_gla", [L, NS, 4, 128, 256])
        self.xT = self.dram("x_scr", [D, NT])
        self.projT = self.dram("projT", [N_IN, NT])
        self.obT = self.dram("obT", [3 * BW, NT], BF16)
        self.mgT = self.dram("mgT", [D, NT], BF16)
        self.actT = self.dram("actT", [DFF, NT], BF16)

        with ExitStack() as st:
            ACOLS = 52800
            at = st.enter_context(nc.sbuf_tensor("arena", [128, ACOLS], F32))
            pt = st.enter_context(nc.psum_tensor("psum", [128, 4096], F32))
            self.A = Arena(at, ACOLS)
            self.P = Psum2(pt)
            self.S = Sched(nc)
            self.consts()
            done = True
            try:
                for l in range(L):
                    if not self.layer(l):
                        done = False
                        break
            except StopBuild:
                done = False
            if done:
                self.S.dma(self.C["vec"][:, 0:16], I["fng"])
                self.rmsnorm(self.xT, "n1g", out_dram=O["yT"])
            self.nwaits = self.S.emit(st)
        return nc

    def act(self, out, in_, func, bias=None, scale=1.0):
        kw = {}
        rd = [in_]
        if bias is not None:
            kw["bias"] = bias
            if not isinstance(bias, (int, float)):
                rd.append(bias)
        if not isinstance(scale, (int, float)):
            rd.append(scale)
        self.S.op("act", lambda e: e.activation(out=out, in_=in_, func=func, scale=scale, **kw), rd, [out])

    def ts(self, eng, out, in0, s1, s2, op0, op1=None):
        rd = [in0] + [s for s in (s1, s2) if s is not None and not isinstance(s, (int, float))]
        if op1 is None:
            self.S.op(eng, lambda e: e.tensor_scalar(out=out, in0=in0, scalar1=s1, scalar2=None, op0=op0), rd, [out])
        else:
            self.S.op(eng, lambda e: e.tensor_scalar(out=out, in0=in0, scalar1=s1, scalar2=s2, op0=op0, op1=op1),
                      rd, [out])

    def stt(self, out, in0, sc, in1, op0, op1):
        rd = [in0, in1] + ([] if isinstance(sc, (int, float)) else [sc])
        self.S.op("dve", lambda e: e.scalar_tensor_tensor(out=out, in0=in0, scalar=sc, in1=in1, op0=op0, op1=op1),
                  rd, [out])

    def tt(self, eng, out, in0, in1, op):
        self.S.op(eng, lambda e: e.tensor_tensor(out=out, in0=in0, in1=in1, op=op), [in0, in1], [out])

    def cp(self, eng, out, in_):
        if eng == "act":
            self.S.op("act", lambda e: e.copy(out=out, in_=in_), [in_], [out])
        else:
            self.S.op(eng, lambda e: e.tensor_copy(out=out, in_=in_), [in_], [out])

    def mm(self, out, lhsT, rhs, start=True, stop=True):
        self.S.op("pe", lambda e: e.matmul(out, lhsT=lhsT, rhs=rhs, start=start, stop=stop), [lhsT, rhs], [out])

    def tr(self, out, in_, ident):
        self.S.op("pe", lambda e: e.transpose(out, in_, ident), [in_, ident], [out])

    def memset(self, eng, ap, v):
        self.S.op(eng, lambda e: e.memset(ap, v), [], [ap])

    def rsqrt(self, out, in_, scale=1.0, bias=None):
        self.act(out, in_, AF.Ln, bias=bias, scale=scale)
        self.act(out, out, AF.Exp, scale=-0.5)

    def recip(self, out, in_):
        self.S.op("dve", lambda e: e.reciprocal(out=out, in_=in_), [in_], [out])

    def scan(self, out, d0, d1, init, op0, op1):
        self.S.op("dve", lambda e: e.tensor_tensor_scan(out=out, data0=d0, data1=d1, initial=init, op0=op0, op1=op1),
                  [d0, d1], [out])

    def dmas(self, out, in_, eng="sp"):
        self.S.op(eng, lambda e: e.dma_start(out=out, in_=in_, allow_slow_non_contiguous=True), [in_], [out], dma=True)

    def V(self, name, j=0, parts=128):
        off, n = VEC_OFF[name]
        return self.C["vec"][0:parts, off + j:off + j + 1]

    def DV(self, name, j=0, parts=128):
        off, n = DV_OFF[name]
        return self.C["dvec"][0:parts, off + j:off + j + 1]

    def consts(self):
        A, S, T = self.A, self.S, self.T
        C = self.C = {}
        io = A.f32(128)
        S.op("pool", lambda e: e.iota(io, [[1, 128]], base=0, channel_multiplier=-1,
                                      allow_small_or_imprecise_dtypes=True), [], [io])
        C["ident_f"] = A.f32(128)
        self.ts("dve", C["ident_f"], io, 0.0, None, ALU.is_equal)
        C["ident_b"] = A.bf(128)
        self.ts("dve", C["ident_b"], io, 0.0, None, ALU.is_equal)
        C["ones_b"] = A.bf(128)
        self.memset("dve", C["ones_b"], 1.0)
        C["m_le_b"] = A.bf(128)
        self.ts("dve", C["m_le_b"], io, 0.0, None, ALU.is_ge)
        C["negm"] = A.f32(128)
        self.ts("dve", C["negm"], io, 0.0, -30000.0, ALU.is_lt, ALU.mult)
        C["bo_b"] = A.bf(128)
        self.memset("dve", C["bo_b"], 0.0)
        self.memset("dve", C["bo_b"][0:64, 0:64], 1.0)
        self.memset("dve", C["bo_b"][64:128, 64:128], 1.0)
        io2 = A.f32(64)
        S.op("pool", lambda e: e.iota(io2[0:64, :], [[1, 64]], base=0, channel_multiplier=-1,
                                      allow_small_or_imprecise_dtypes=True), [], [io2[0:64, :]])
        S.op("pool", lambda e: e.iota(io2[64:128, :], [[1, 64]], base=0, channel_multiplier=-1,
                                      allow_small_or_imprecise_dtypes=True), [], [io2[64:128, :]])
        C["mk192"] = A.bf(192)
        self.ts("dve", C["mk192"][:, 0:64], io2, 0.0, None, ALU.is_gt)
        self.ts("dve", C["mk192"][:, 64:128], io2, 0.0, None, ALU.is_gt)
        self.ts("dve", C["mk192"][:, 128:192], io2, 0.0, None, ALU.is_ge)
        C["mk_lt"] = A.bf(128)
        self.ts("dve", C["mk_lt"][:, 0:64], io2, 0.0, None, ALU.is_lt)
        self.ts("dve", C["mk_lt"][:, 64:128], io2, 0.0, None, ALU.is_lt)
        C["rst64"] = A.bf(T)
        self.memset("pool", C["rst64"], 1.0)
        self.memset("pool", C["rst64"].rearrange("p (c j) -> p c j", j=64)[:, :, 0:1], 0.0)
        C["rst128"] = A.bf(T)
        self.memset("pool", C["rst128"], 1.0)
        self.memset("pool", C["rst128"].rearrange("p (c j) -> p c j", j=128)[:, :, 0:1], 0.0)
        C["sel"] = A.f32(4 * 128, parts=4).rearrange("p (h m) -> p h m", h=4)
        for h in range(4):
            self.cp("dve", C["sel"][:, h, :], C["ident_f"][0:4, h:h + 1].to_broadcast([4, 128]))
        C["vec"] = A.f32(VEC_N)
        C["dvec"] = A.f32(DV_N)
        C["rows"] = A.f32(2, parts=4)
        C["eps"] = A.f32(1)
        self.memset("dve", C["eps"], EPS)
        C["gneps"] = A.f32(1)
        self.memset("dve", C["gneps"], R_GN_EPS)
        self.abase = A.mark()

    def layer(self, l):
        S, I, O, C = self.S, self.I, self.O, self.C
        T, NS, NT = self.T, self.NS, self.NT
        S.dma(C["vec"], I["vec"][l])
        S.dma(C["rows"], I["rows"][l])
        mo, mn = VEC_OFF["mu"]
        self.ts("dve", C["dvec"][:, 0:27], C["vec"][:, mo:mo + 27], -1.0, 1.0, ALU.mult, ALU.add)
        ko, kn = VEC_OFF["k_a"]
        self.ts("dve", C["dvec"][:, 27:35], C["vec"][:, ko:ko + 8], -1.0, 1.0, ALU.mult, ALU.add)
        src = I["xT"] if l == 0 else self.xT
        self.src = src
        self.rmsnorm(src, "n1g")
        self.dense(I["w_in"][l], N_IN, 16, self.hT, self.proj_sink)
        if self.stop == (l, "proj"):
            return False
        for r0 in range(0, R_COLS, 816):
            self.dmas(O["o_shift"][l, r0:r0 + 816, 0:1], self.projT[r0:r0 + 816, T - 1:T])
            self.dmas(O["o_shift"][l, r0:r0 + 816, 1:1 + NS], self.projT[r0:r0 + 816, T:NT])
        for r0 in range(0, D, 1024):
            for lag in range(3):
                self.dmas(O["o_conv"][l, lag, r0:r0 + 1024, 0:1],
                          self.projT[MB + r0:MB + r0 + 1024, T - 3 + lag:T - 2 + lag])
            self.dmas(O["o_conv"][l, 0, r0:r0 + 1024, 1:1 + NS], I["s_conv"][l, 1, r0:r0 + 1024, :])
            self.dmas(O["o_conv"][l, 1, r0:r0 + 1024, 1:1 + NS], I["s_conv"][l, 2, r0:r0 + 1024, :])
            self.dmas(O["o_conv"][l, 2, r0:r0 + 1024, 1:1 + NS], self.projT[MB + r0:MB + r0 + 1024, T:NT])
        self.chk("rw0")
        self.rwkv(l)
        if self.stop == (l, "rwkv"):
            return False
        self.mlstm(l)
        if self.stop == (l, "mlstm"):
            return False
        self.gla(l)
        if self.stop == (l, "gla"):
            return False
        self.merge(l)
        self.outproj(l)
        if self.stop == (l, "attn"):
            return False
        self.rmsnorm(self.xT, "n2g")
        self.ffn(l)
        return True

    def rmsnorm(self, src, gname, out_dram=None):
        A, S, C = self.A, self.S, self.C
        NT = self.NT
        A.release(self.abase)
        if out_dram is None:
            self.hT = A.bf(16 * NT).rearrange("p (c t) -> p c t", c=16)
        m0 = A.mark()
        srcv = src.rearrange("(c p) t -> p c t", p=128)
        for (t0, tn) in ttiles(NT):
            A.release(m0)
            xt = A.f32(16 * tn).rearrange("p (c t) -> p c t", c=16)
            sq = A.bf(16 * tn).rearrange("p (c t) -> p c t", c=16)
            rs = A.f32(tn)
            S.dma(xt, srcv[:, :, t0:t0 + tn])
            ps = self.P.bank()[:, 0:tn]
            for c in range(16):
                self.act(sq[:, c, :], xt[:, c, :], AF.Square)
                self.mm(ps, C["ones_b"], sq[:, c, :], start=(c == 0), stop=(c == 15))
            self.rsqrt(rs, ps, scale=1.0 / D, bias=C["eps"])
            for c in range(16):
                dst = self.hT[:, c, t0:t0 + tn] if out_dram is None else xt[:, c, :]
                self.stt(dst, xt[:, c, :], self.V(gname, c), rs, ALU.mult, ALU.mult)
            if out_dram is not None:
                S.dma(out_dram.rearrange("(c p) t -> p c t", p=128)[:, :, t0:t0 + tn], xt, eng="sp")
        A.release(m0)

    def dense(self, W, N, KC, rhs, sink, order=None, presink=None):
        A, S = self.A, self.S
        NT = self.NT
        m0 = A.mark()
        nm = (N + 127) // 128
        Wv = W.rearrange("(kc p) n -> p kc n", p=128)
        NWB = 3
        wst = [A.f32(KC * 128).rearrange("p (k n) -> p k n", k=KC) for _ in range(NWB)]
        wbf = [A.bf(KC * 128).rearrange("p (k n) -> p k n", k=KC) for _ in range(NWB)]
        self._dense_m0 = A.mark()
        tl = ttiles(NT)
        order = order if order is not None else list(range(nm))
        n = len(order)

        def load(it):
            m = order[it]
            rows = min(128, N - m * 128)
            ws, wb = wst[it % NWB], wbf[it % NWB]
            S.dma(ws[:, :, 0:rows], Wv[:, :, m * 128:m * 128 + rows])
            self.cp("pool" if it % 2 else "dve", wb[:, :, 0:rows], ws[:, :, 0:rows])

        for it in range(min(2, n)):
            load(it)
        for it, m in enumerate(order):
            rows = min(128, N - m * 128)
            wb = wbf[it % NWB]
            if it + 2 < n:
                load(it + 2)
            if presink is not None:
                presink(it, m)
            banks = self.P.banks(len(tl))
            for k in range(KC):
                for bi, (t0, tn) in enumerate(tl):
                    self.mm(banks[bi][0:rows, 0:tn], wb[:, k, 0:rows], rhs[:, k, t0:t0 + tn],
                            start=(k == 0), stop=(k == KC - 1))
            sink(it, m, rows, banks, tl)
        A.release(m0)

    def proj_sink(self, it, m, rows, banks, tl):
        A, S = self.A, self.S
        A.release(self._dense_m0)
        stg = [A.f32(self.NT) for _ in range(2)]
        o = stg[it % 2]
        for bi, (t0, tn) in enumerate(tl):
            self.cp("act", o[0:rows, t0:t0 + tn], banks[bi][0:rows, 0:tn])
        S.dma(self.projT[m * 128:m * 128 + rows, :], o[0:rows, :], eng="act")

    def shiftmix(self, dst, p, mu, omu, st, parts=128):
        T, NT = self.T, self.NT
        self.ts("dve", dst[0:parts, :], p[0:parts, :], omu, None, ALU.mult)
        self.stt(dst[0:parts, 1:T], p[0:parts, 0:T - 1], mu, dst[0:parts, 1:T], ALU.mult, ALU.add)
        self.stt(dst[0:parts, T:NT], st, mu, dst[0:parts, T:NT], ALU.mult, ALU.add)

    def rwkv(self, l):
        A, S, C, I, O = self.A, self.S, self.C, self.I, self.O
        T, NS, NT = self.T, self.NS, self.NT
        A.release(self.abase)
        lw = A.bf(3 * BW, parts=64).rearrange("p (j n) -> p j n", j=3)
        lin = [A.bf(NT, parts=64) for _ in range(3)]
        m1 = A.mark()
        lw_st = A.f32(3 * BW, parts=64).rearrange("p (j n) -> p j n", j=3)
        S.dma(lw_st, I["rw2"][l].rearrange("j k n -> k j n"))
        self.cp("dve", lw, lw_st)
        self.chk("rw1a")
        raw = A.f32(NT, parts=64)
        xs = A.f32(NT, parts=64)
        sst = A.f32(NS, parts=64)
        for j in range(3):
            S.dma(raw, self.projT[3072 + 64 * j:3072 + 64 * (j + 1), :])
            S.dma(sst, I["s_shift"][l, 3072 + 64 * j:3072 + 64 * (j + 1), :])
            self.chk("rw1b")
            self.shiftmix(xs, raw, self.V("mul", j, 64), self.DV("omul", j, 64), sst, 64)
            self.chk("rw1c")
            if j == 1:
                self.cp("dve", lin[j], xs)
            else:
                self.act(lin[j], xs, [AF.Tanh, AF.Copy, AF.Sigmoid][j])
            self.chk("rw1d")
        A.release(m1)
        self._rw_base = A.mark()
        self.chk("rw1")
        for hp in range(8):
            self.rwkv_pair(l, hp, lw, lin)

    def rwkv_pair(self, l, hp, lw, lin):
        A, S, C, I, O, P = self.A, self.S, self.C, self.I, self.O, self.P
        T, NS, NT = self.T, self.NS, self.NT
        NCH = T // 64
        tl = ttiles(NT)
        A.release(self._rw_base)
        c0 = hp * 128
        H = [slice(0, 64), slice(64, 128)]
        bonus = A.f32(NT)
        g = A.f32(NT)
        Ofm = A.f32(NT)
        smp = A.f32(6 * NS).rearrange("p (q i) -> p q i", q=6)
        WL = A.f32(NCH)
        AR = A.bf(NCH * 192).rearrange("p (c n) -> p c n", c=NCH)
        Bb = A.bf(NCH * 128).rearrange("p (c n) -> p c n", c=NCH)
        Kb = A.bf(NCH * 128).rearrange("p (c n) -> p c n", c=NCH)
        Vb = A.bf(NCH * 128).rearrange("p (c n) -> p c n", c=NCH)
        mk = A.mark()
        t_r, t_k, t_v, xr, xk, xv, ld, a, kap, b, cc, e = [A.f32(NT) for _ in range(12)]
        sqb = A.bf(NT)
        sst = A.f32(3 * NS).rearrange("p (q i) -> p q i", q=3)
        for z in (AR, Bb, Kb, Vb):
            self.memset("pool", z, 0.0)
        for q, (tt_, xx) in enumerate([(t_r, xr), (t_k, xk), (t_v, xv)]):
            S.dma(tt_, self.projT[q * BW + c0:q * BW + c0 + 128, :])
            S.dma(sst[:, q, :], I["s_shift"][l, q * BW + c0:q * BW + c0 + 128, :])
            self.shiftmix(xx, tt_, self.V("mu", q * 8 + hp), self.DV("omu", q * 8 + hp), sst[:, q, :])
        for (t0, tn) in tl:
            ps = P.bank()
            self.mm(ps[:, 0:tn], lw[:, 0, c0:c0 + 128], lin[0][:, t0:t0 + tn])
            self.act(ld[:, t0:t0 + tn], ps[:, 0:tn], AF.Sigmoid, bias=self.V("w0", hp))
            ps = P.bank()
            self.mm(ps[:, 0:tn], lw[:, 1, c0:c0 + 128], lin[1][:, t0:t0 + tn])
            self.act(a[:, t0:t0 + tn], ps[:, 0:tn], AF.Sigmoid, bias=self.V("a0", hp))
            ps = P.bank()
            self.mm(ps[:, 0:tn], lw[:, 2, c0:c0 + 128], lin[2][:, t0:t0 + tn])
            self.cp("act", g[:, t0:t0 + tn], ps[:, 0:tn])
        self.ts("pool", ld, ld, -0.6065306597126334, None, ALU.mult)
        self.ts("dve", kap, xk, self.V("k_k", hp), None, ALU.mult)
        self.act(sqb, kap, AF.Square)
        for (t0, tn) in tl:
            ps = P.bank()
            self.mm(ps[:, 0:tn], C["bo_b"], sqb[:, t0:t0 + tn])
            self.ts("dve", e[:, t0:t0 + tn], ps[:, 0:tn], 1e-24, None, ALU.max)
        self.rsqrt(e, e)
        self.tt("dve", kap, kap, e, ALU.mult)
        self.ts("dve", e, a, self.V("k_a", hp), self.DV("omk_a", hp), ALU.mult, ALU.add)
        self.tt("dve", xk, xk, e, ALU.mult)
        self.tt("pool", b, kap, a, ALU.mult)
        self.stt(sqb, xr, self.V("r_k", hp), xk, ALU.mult, ALU.mult)
        for (t0, tn) in tl:
            ps = P.bank()
            self.mm(ps[:, 0:tn], C["bo_b"], sqb[:, t0:t0 + tn])
            self.tt("dve", bonus[:, t0:t0 + tn], ps[:, 0:tn], xv[:, t0:t0 + tn], ALU.mult)
        for q, src in enumerate([kap, b, xk, xv, xr]):
            self.cp("pool", smp[:, q, :], src[:, T:NT])
        self.act(smp[:, 5, :], ld[:, T:NT], AF.Exp)
        self.scan(cc[:, 0:T], C["rst64"], ld[:, 0:T], 0.0, ALU.mult, ALU.add)
        v3 = lambda ap: ap[:, 0:T].rearrange("p (c j) -> p c j", j=64)
        self.act(WL, v3(cc)[:, :, 63], AF.Exp)
        e1, e2, e3 = t_r, t_k, t_v
        self.act(e1[:, 0:T], cc[:, 0:T], AF.Exp)
        self.tt("dve", AR[:, :, 128:192], v3(xr), v3(e1), ALU.mult)
        self.tt("pool", e2[:, 0:T], cc[:, 0:T], ld[:, 0:T], ALU.subtract)
        self.act(e2[:, 0:T], e2[:, 0:T], AF.Exp)
        self.act(e3[:, 0:T], cc[:, 0:T], AF.Exp, scale=-1.0)
        for hh in range(2):
            hs = H[hh]
            self.stt(AR[hs, :, hh * 64:hh * 64 + 64], v3(kap)[hs], -1.0, v3(e2)[hs], ALU.mult, ALU.mult)
            self.tt("dve", Bb[hs, :, hh * 64:hh * 64 + 64], v3(b)[hs], v3(e3)[hs], ALU.mult)
            self.tt("dve", Kb[hs, :, hh * 64:hh * 64 + 64], v3(xk)[hs], v3(e3)[hs], ALU.mult)
            self.cp("pool", Vb[hs, :, hh * 64:hh * 64 + 64], v3(xv)[hs])
        self.chk("rw2")
        A.release(mk)
        Vtm = A.bf(NCH * 128).rearrange("p (c n) -> p c n", c=NCH)
        Btm = A.bf(NCH * 128).rearrange("p (c n) -> p c n", c=NCH)
        Ktm = A.bf(NCH * 128).rearrange("p (c n) -> p c n", c=NCH)
        TT = A.bf(NCH * 128).rearrange("p (c n) -> p c n", c=NCH)
        Aak = A.bf(NCH * 128).rearrange("p (c n) -> p c n", c=NCH)
        Abk = A.bf(NCH * 128).rearrange("p (c n) -> p c n", c=NCH)
        GW = 8
        PP = [[A.bf(256) for _ in range(2)] for _ in range(GW)]
        ZT = A.f32(128)
        ZTs = A.f32(128)
        ZTb = A.bf(128)
        Xsb = A.bf(128)
        Usb = A.bf(128)
        self.memset("dve", ZT, 0.0)
        self.memset("dve", ZTb, 0.0)
        ident = C["ident_b"]

        def st0(c, w):
            ps = P.bank()
            self.mm(ps[:, 0:192], Bb[:, c, :], AR[:, c, :])
            self.tt("dve", PP[w][0][:, 0:128], ps[:, 0:128], C["mk192"][:, 0:128], ALU.mult)
            self.tt("dve", Abk[:, c, 0:64], ps[:, 128:192], C["mk192"][:, 128:192], ALU.mult)
            self.chk("s0_1")
            ps2 = P.bank()
            self.mm(ps2[:, 0:192], Kb[:, c, :], AR[:, c, :])
            self.tt("dve", Aak[:, c, :], ps2[:, 0:128], C["mk192"][:, 0:128], ALU.mult)
            self.tt("dve", Abk[:, c, 64:128], ps2[:, 128:192], C["mk192"][:, 128:192], ALU.mult)
            ps3 = P.bank()
            self.mm(ps3[:, 0:128], AR[:, c, 0:128], Bb[:, c, :])
            self.tt("dve", PP[w][0][:, 128:256], ps3[:, 0:128], C["mk_lt"], ALU.mult)
            self.chk("s0_2")
            self.tt("pool", TT[:, c, :], PP[w][0][:, 0:128], ident, ALU.add)
            self.chk("s0_3")
            ps4 = P.bank()
            self.mm(ps4[:, 0:128], Vb[:, c, :], ident)
            self.mm(ps4[:, 128:256], Bb[:, c, :], ident)
            self.mm(ps4[:, 256:384], Kb[:, c, :], ident)
            self.cp("act", Vtm[:, c, :], ps4[:, 0:128])
            self.cp("act", Btm[:, c, :], ps4[:, 128:256])
            self.cp("act", Ktm[:, c, :], ps4[:, 256:384])
            self.chk("s0_4")

        def sq(c, w, j):
            src = PP[w][(j - 1) % 2]
            dst = PP[w][j % 2]
            ps = P.bank()
            self.mm(ps[:, 0:128], src[:, 128:256], src[:, 0:128])
            self.mm(ps[:, 128:256], src[:, 0:128], src[:, 128:256])
            self.cp("act", dst, ps[:, 0:256])

        def acc(c, w, j):
            dst = PP[w][j % 2]
            ps = P.bank()
            self.mm(ps[:, 0:128], dst[:, 128:256], TT[:, c, :])
            self.tt("dve", TT[:, c, :], TT[:, c, :], ps[:, 0:128], ALU.add)

        def chain(c):
            ps = P.bank()
            self.mm(ps[:, 0:128], AR[:, c, 0:128], ZTb, start=True, stop=False)
            self.mm(ps[:, 0:128], Aak[:, c, :], Vtm[:, c, :], start=False, stop=True)
            self.cp("act", Xsb, ps[:, 0:128])
            pu = P.bank()
            self.mm(pu[:, 0:128], TT[:, c, :], Xsb)
            self.cp("dve", Usb, pu[:, 0:128])
            po = P.bank()
            self.mm(po[:, 0:64], Usb, Abk[:, c, 0:64], start=True, stop=False)
            self.mm(po[:, 0:64], Vtm[:, c, :], Abk[:, c, 64:128], start=False, stop=False)
            self.mm(po[:, 0:64], ZTb, AR[:, c, 128:192], start=False, stop=True)
            self.cp("act", Ofm[:, c * 64:(c + 1) * 64], po[:, 0:64])
            pz = P.bank()
            self.mm(pz[:, 0:128], Btm[:, c, :], Usb, start=True, stop=False)
            self.mm(pz[:, 0:128], Ktm[:, c, :], Vtm[:, c, :], start=False, stop=True)
            self.ts("pool", ZTs, ZT, WL[:, c:c + 1], None, ALU.mult)
            self.stt(ZT, pz[:, 0:128], WL[:, c:c + 1], ZTs, ALU.mult, ALU.add)
            self.cp("act", ZTb, ZT)

        tm3 = A.f32(384, parts=NS)
        ps = P.bank()
        for q, idx in enumerate([1, 2, 3]):
            self.tr(ps[0:NS, q * 128:(q + 1) * 128], smp[:, idx, :], C["ident_f"])
        self.cp("act", tm3, ps[0:NS, 0:384])
        eye = C["ident_f"][0:NS, 0:NS].unsqueeze(2).to_broadcast([NS, NS, 128])
        Bex = A.f32(NS * 128, parts=NS).rearrange("p (i c) -> p i c", i=NS)
        Kex = A.f32(NS * 128, parts=NS).rearrange("p (i c) -> p i c", i=NS)
        self.tt("dve", Bex, tm3[:, 0:128].unsqueeze(1).to_broadcast([NS, NS, 128]), eye, ALU.mult)
        self.tt("dve", Kex, tm3[:, 128:256].unsqueeze(1).to_broadcast([NS, NS, 128]), eye, ALU.mult)
        vtm = tm3[:, 256:384]
        NBUF = 4
        Sbl = [A.f32(128) for _ in range(NBUF)]
        STt = [A.f32(128) for _ in range(NBUF)]
        SnT = [A.f32(128) for _ in range(NBUF)]
        Sou = [A.f32(128) for _ in range(NBUF)]
        nSK = [A.f32(128, parts=NS) for _ in range(NBUF)]
        for z in Sbl + SnT:
            self.memset("pool", z, 0.0)
        rb, pO = P.reserve()

        def samp_subs(i):
            k2 = i % NBUF

            def g1():
                for hh in range(2):
                    S.dma(Sbl[k2][H[hh], hh * 64:hh * 64 + 64], I["s_wkv"][l, i, 2 * hp + hh])
                ps_ = P.bank()
                self.tr(ps_[:, 0:128], Sbl[k2], C["ident_f"])
                self.cp("act", STt[k2], ps_[:, 0:128])

            def g2():
                p1 = P.bank()
                self.mm(p1[0:NS, 0:128], smp[:, 0, :], STt[k2])
                self.ts("dve", nSK[k2], p1[0:NS, 0:128], -1.0, None, ALU.mult)

            def g3():
                p2 = P.bank()
                self.mm(p2[:, 0:128], Bex[:, i, :], nSK[k2], start=True, stop=False)
                self.mm(p2[:, 0:128], Kex[:, i, :], vtm, start=False, stop=True)
                for hh in range(2):
                    hs = H[hh]
                    cs = slice(hh * 64, hh * 64 + 64)
                    self.stt(SnT[k2][hs, cs], STt[k2][hs, cs], smp[hs, 5, i:i + 1], p2[hs, cs], ALU.mult, ALU.add)

            def g4():
                self.mm(pO[:, i:i + 1], SnT[k2], smp[:, 4, i:i + 1])
                p3 = P.bank()
                self.tr(p3[:, 0:128], SnT[k2], C["ident_f"])
                self.cp("act", Sou[k2], p3[:, 0:128])
                for hh in range(2):
                    S.dma(O["s_wkv"][l, i, 2 * hp + hh], Sou[k2][H[hh], hh * 64:hh * 64 + 64], eng="act")
            return [g1, g2, g3, g4]

        def chain_subs(c):
            def f1():
                ps_ = P.bank()
                self.mm(ps_[:, 0:128], AR[:, c, 0:128], ZTb, start=True, stop=False)
                self.mm(ps_[:, 0:128], Aak[:, c, :], Vtm[:, c, :], start=False, stop=True)
                self.cp("act", Xsb, ps_[:, 0:128])

            def f2():
                pu = P.bank()
                self.mm(pu[:, 0:128], TT[:, c, :], Xsb)
                self.cp("dve", Usb, pu[:, 0:128])

            def f3():
                pz = P.bank()
                self.mm(pz[:, 0:128], Btm[:, c, :], Usb, start=True, stop=False)
                self.mm(pz[:, 0:128], Ktm[:, c, :], Vtm[:, c, :], start=False, stop=True)
                po = P.bank()
                self.mm(po[:, 0:64], Usb, Abk[:, c, 0:64], start=True, stop=False)
                self.mm(po[:, 0:64], Vtm[:, c, :], Abk[:, c, 64:128], start=False, stop=False)
                self.mm(po[:, 0:64], ZTb, AR[:, c, 128:192], start=False, stop=True)
                self.ts("pool", ZTs, ZT, WL[:, c:c + 1], None, ALU.mult)
                self.stt(ZT, pz[:, 0:128], WL[:, c:c + 1], ZTs, ALU.mult, ALU.add)
                self.cp("act", ZTb, ZT)
                self.cp("act", Ofm[:, c * 64:(c + 1) * 64], po[:, 0:64])
            return [f1, f2, f3]

        def bulk_ops(wave):
            ops_ = []
            for w, c in enumerate(wave):
                ops_.append((lambda c=c, w=w: st0(c, w)))
            for j in range(1, 6):
                for w, c in enumerate(wave):
                    ops_.append((lambda c=c, w=w, j=j: sq(c, w, j)))
                for w, c in enumerate(wave):
                    ops_.append((lambda c=c, w=w, j=j: acc(c, w, j)))
            return ops_

        waves = [list(range(c0_, min(NCH, c0_ + GW))) for c0_ in range(0, NCH, GW)]
        samp_all = []
        for i in range(NS):
            samp_all += samp_subs(i)
        for f in bulk_ops(waves[0]):
            f()
        sp_pos = 0
        for wi, wave in enumerate(waves):
            Al = []
            for c in wave:
                Al += chain_subs(c)
            Bl = bulk_ops(waves[wi + 1]) if wi + 1 < len(waves) else []
            nC = (len(samp_all) - sp_pos + (len(waves) - wi) - 1) // (len(waves) - wi)
            Cl = samp_all[sp_pos:sp_pos + nC]
            sp_pos += nC
            nA = len(Al)
            bi = ci = 0
            for k, f in enumerate(Al):
                f()
                tb = (len(Bl) * (k + 1) + nA - 1) // nA
                while bi < min(tb, len(Bl)):
                    Bl[bi]()
                    bi += 1
                tc = (len(Cl) * (k + 1) + nA - 1) // nA
                while ci < min(tc, len(Cl)):
                    Cl[ci]()
                    ci += 1
        self.chk("rw3")
        ps = P.bank()
        self.tr(ps[:, 0:128], ZT, C["ident_f"])
        So = A.f32(128)
        self.cp("act", So, ps[:, 0:128])
        for hh in range(2):
            S.dma(O["p_wkv"][l, 2 * hp + hh], So[H[hh], hh * 64:hh * 64 + 64], eng="act")
        self.cp("act", Ofm[:, T:NT], pO[:, 0:NS])
        P.unreserve(rb)
        self.chk("rw4")
        cen = A.f32(NT)
        rs = A.f32(NT)
        sb2 = A.bf(NT)
        obf = A.bf(NT)
        self.cp("pool", sb2, Ofm)
        for (t0, tn) in tl:
            ps = P.bank()
            self.mm(ps[:, 0:tn], C["bo_b"], sb2[:, t0:t0 + tn])
            self.stt(cen[:, t0:t0 + tn], ps[:, 0:tn], -1.0 / 64, Ofm[:, t0:t0 + tn], ALU.mult, ALU.add)
        self.act(sb2, cen, AF.Square)
        for (t0, tn) in tl:
            ps = P.bank()
            self.mm(ps[:, 0:tn], C["bo_b"], sb2[:, t0:t0 + tn])
            self.rsqrt(rs[:, t0:t0 + tn], ps[:, 0:tn], scale=1.0 / 64, bias=C["gneps"])
        self.tt("dve", cen, cen, rs, ALU.mult)
        self.ts("dve", cen, cen, self.V("ln_g", hp), self.V("ln_b", hp), ALU.mult, ALU.add)
        self.tt("pool", cen, cen, bonus, ALU.add)
        self.tt("dve", obf, cen, g, ALU.mult)
        S.dma(self.obT[c0:c0 + 128, :], obf, eng="sp")
        self.chk("rw5")

    def mlstm(self, l):
        A, S, C, I, O, P = self.A, self.S, self.C, self.I, self.O, self.P
        T, NS, NT = self.T, self.NS, self.NT
        NB = T // 128
        tl = ttiles(NT)
        A.release(self.abase)
        R4 = lambda n: A.f32(n, parts=4)
        Ac, wrow, emrow = [R4(NT) for _ in range(3)]
        dec, acs = R4(NB), R4(NS)
        acol = A.f32(NB * 4).rearrange("p (b h) -> p b h", h=4)
        ecol = A.f32(NB * 4).rearrange("p (b h) -> p b h", h=4)
        wscol = A.f32(4, parts=NS)
        mkeep = A.mark()
        ig, lf, G, aa, mt, erow = [R4(NT) for _ in range(6)]
        ones4 = R4(T)
        Ast, Aend = R4(NB), R4(NB)
        mold, wss = R4(NS), R4(NS)
        S.dma(ig, self.projT[MB + 3072:MB + 3076, :])
        S.dma(lf, self.projT[MB + 3076:MB + 3080, :])
        S.dma(mold, I["s_m"][l])
        self.memset("dve", ones4, 1.0)
        self.ts("dve", ig, ig, C["rows"][:, 0:1], None, ALU.add)
        self.act(lf, lf, AF.Sigmoid, bias=C["rows"][:, 1:2])
        self.act(lf, lf, AF.Ln)
        self.scan(G[:, 0:T], ones4, lf[:, 0:T], 0.0, ALU.mult, ALU.add)
        self.tt("dve", aa[:, 0:T], ig[:, 0:T], G[:, 0:T], ALU.subtract)
        self.scan(Ac[:, 0:T], ones4, aa[:, 0:T], 0.0, ALU.mult, ALU.max)
        self.tt("dve", mt[:, 0:T], G[:, 0:T], Ac[:, 0:T], ALU.add)
        v3 = lambda ap: ap[:, 0:T].rearrange("p (c j) -> p c j", j=128)
        self.cp("dve", Aend, v3(Ac)[:, :, 127])
        self.memset("dve", Ast[:, 0:1], 0.0)
        if NB > 1:
            self.cp("dve", Ast[:, 1:NB], Aend[:, 0:NB - 1])
        self.tt("dve", v3(wrow), Ast.unsqueeze(2).to_broadcast([4, NB, 128]), v3(Ac), ALU.subtract)
        self.act(wrow[:, 0:T], wrow[:, 0:T], AF.Exp)
        self.tt("dve", v3(erow), v3(aa), Aend.unsqueeze(2).to_broadcast([4, NB, 128]), ALU.subtract)
        self.act(erow[:, 0:T], erow[:, 0:T], AF.Exp)
        self.tt("dve", dec, Ast, Aend, ALU.subtract)
        self.act(dec, dec, AF.Exp)
        self.tt("dve", mold, lf[:, T:NT], mold, ALU.add)
        self.tt("dve", mt[:, T:NT], mold, ig[:, T:NT], ALU.max)
        self.tt("dve", acs, mold, mt[:, T:NT], ALU.subtract)
        self.act(acs, acs, AF.Exp)
        self.tt("dve", wss, ig[:, T:NT], mt[:, T:NT], ALU.subtract)
        self.act(wss, wss, AF.Exp)
        self.act(emrow, mt, AF.Exp, scale=-1.0)
        self.dmas(O["o_m"][l, :, 0:1], mt[:, T - 1:T], eng="act")
        S.dma(O["o_m"][l, :, 1:1 + NS], mt[:, T:NT], eng="act")
        ps = P.bank()
        i4 = C["ident_f"][0:4, 0:4]
        for bb in range(NB):
            self.mm(ps[:, bb * 4:bb * 4 + 4], aa[:, bb * 128:(bb + 1) * 128], i4)
            self.mm(ps[:, 256 + bb * 4:256 + bb * 4 + 4], erow[:, bb * 128:(bb + 1) * 128], i4)
        self.mm(ps[0:NS, 500:504], wss, i4)
        self.cp("act", acol, ps[:, 0:NB * 4].rearrange("p (b h) -> p b h", h=4))
        self.cp("act", ecol, ps[:, 256:256 + NB * 4].rearrange("p (b h) -> p b h", h=4))
        self.cp("act", wscol, ps[0:NS, 500:504])
        A.release(mkeep)
        self._ml_base = A.mark()
        for h in range(4):
            self.mlstm_head(l, h, dict(Ac=Ac, wrow=wrow, emrow=emrow, dec=dec, acs=acs, acol=acol, ecol=ecol,
                                       wscol=wscol))

    def bcast_rows(self, dst, rows4, h, ncols):
        c = 0
        while c < ncols:
            n = min(512, ncols - c)
            ps = self.P.bank()
            self.mm(ps[:, 0:n], self.C["sel"][:, h, :], rows4[:, c:c + n])
            self.cp("act", dst[:, c:c + n], ps[:, 0:n])
            c += n

    def conv_silu(self, l, ch0, cidx, raw, acc, cs):
        S, I = self.S, self.I
        T, NS, NT = self.T, self.NS, self.NT
        S.dma(raw, self.projT[MB + ch0:MB + ch0 + 128, :])
        S.dma(cs, I["s_conv"][l, :, ch0:ch0 + 128, :].rearrange("g p i -> p g i"))
        w = lambda j: self.V("conv_w", j * 16 + cidx)
        self.ts("dve", acc, raw, w(3), self.V("conv_b", cidx), ALU.mult, ALU.add)
        for lag in (1, 2, 3):
            self.stt(acc[:, lag:T], raw[:, 0:T - lag], w(3 - lag), acc[:, lag:T], ALU.mult, ALU.add)
            self.stt(acc[:, T:NT], cs[:, 3 - lag, :], w(3 - lag), acc[:, T:NT], ALU.mult, ALU.add)

    def mlstm_head(self, l, h, R):
        A, S, C, I, O, P = self.A, self.S, self.C, self.I, self.O, self.P
        T, NS, NT = self.T, self.NS, self.NT
        NB = T // 128
        tl = ttiles(NT)
        A.release(self._ml_base)
        GBA, GBw = A.f32(T), A.f32(T)
        GBem = A.f32(NT)
        GBd = A.f32(NB)
        GBac = A.f32(NS)
        self.bcast_rows(GBA, R["Ac"], h, T)
        self.bcast_rows(GBw, R["wrow"], h, T)
        self.bcast_rows(GBem, R["emrow"], h, NT)
        self.bcast_rows(GBd, R["dec"], h, NB)
        self.bcast_rows(GBac, R["acs"], h, NS)
        Qb = [A.bf(NT) for _ in range(2)]
        Kb = [A.bf(NT) for _ in range(2)]
        Qw = [A.bf(T) for _ in range(2)]
        qs = A.f32(2 * NS).rearrange("p (k i) -> p k i", k=2)
        ks = A.f32(2 * NS).rearrange("p (k i) -> p k i", k=2)
        vs = A.f32(2 * NS).rearrange("p (k i) -> p k i", k=2)
        Vtm = A.bf(NB * 257).rearrange("p (b n) -> p b n", b=NB)
        Ktm = A.bf(NB * 256).rearrange("p (b n) -> p b n", b=NB)
        Hh = [A.f32(NT) for _ in range(2)]
        m1 = A.mark()
        raw, acc = A.f32(NT), A.f32(NT)
        cs = A.f32(3 * NS).rearrange("p (g i) -> p g i", g=3)
        vb = A.bf(T)
        for kc in range(2):
            self.conv_silu(l, h * 256 + kc * 128, h * 2 + kc, raw, acc, cs)
            self.act(acc, acc, AF.Silu)
            self.cp("dve", Qb[kc], acc)
            self.cp("pool", qs[:, kc, :], acc[:, T:NT])
            self.tt("dve", Qw[kc], acc[:, 0:T], GBw, ALU.mult)
            self.conv_silu(l, BW + h * 256 + kc * 128, 8 + h * 2 + kc, raw, acc, cs)
            self.act(acc, acc, AF.Silu)
            self.ts("dve", Kb[kc], acc, 0.0625, None, ALU.mult)
            self.ts("pool", ks[:, kc, :], acc[:, T:NT], 0.0625, None, ALU.mult)
            for bb in range(NB):
                ps = P.bank()
                self.mm(ps[:, 0:128], Kb[kc][:, bb * 128:(bb + 1) * 128], C["ident_b"])
                self.ts("dve", Ktm[:, bb, kc * 128:(kc + 1) * 128], ps[:, 0:128], R["ecol"][:, bb, h:h + 1], None, ALU.mult)
        self.memset("pool", Vtm[:, :, 256:257], 1.0)
        for vc in range(2):
            S.dma(raw, self.projT[MB + 2048 + h * 256 + vc * 128:MB + 2048 + h * 256 + (vc + 1) * 128, :])
            self.cp("dve", vb, raw[:, 0:T])
            self.cp("pool", vs[:, vc, :], raw[:, T:NT])
            for bb in range(NB):
                ps = P.bank()
                self.mm(ps[:, 0:128], vb[:, bb * 128:(bb + 1) * 128], C["ident_b"])
                self.cp("act", Vtm[:, bb, vc * 128:(vc + 1) * 128], ps[:, 0:128])
        A.release(m1)
        CM = A.f32(2 * 257).rearrange("p (k n) -> p k n", k=2)
        Cb = A.bf(2 * 256).rearrange("p (k n) -> p k n", k=2)
        nbc = A.bf(2 * 128).rearrange("p (k n) -> p k n", k=2)
        tmp = A.f32(128)
        ET = A.f32(128)
        PT = A.bf(128)
        absd = A.f32(128)
        self.memset("dve", CM, 0.0)
        self.memset("dve", Cb, 0.0)
        self.memset("dve", nbc, 0.0)
        for bb in range(NB):
            blk = slice(bb * 128, (bb + 1) * 128)
            ps = P.bank()
            self.mm(ps[:, 0:128], Kb[0][:, blk], Qb[0][:, blk], start=True, stop=False)
            self.mm(ps[:, 0:128], Kb[1][:, blk], Qb[1][:, blk], start=False, stop=True)
            self.stt(tmp, GBA[:, blk], -1.0, C["negm"], ALU.mult, ALU.add)
            self.act(ET, tmp, AF.Exp, bias=R["acol"][:, bb, h:h + 1])
            self.tt("dve", PT, ps[:, 0:128], ET, ALU.mult)
            pn = P.bank()
            for vc in range(2):
                o = pn[:, vc * 128:(vc + 1) * 128]
                self.mm(o, Vtm[:, bb, vc * 128:(vc + 1) * 128], PT, start=True, stop=False)
                self.mm(o, Cb[:, 0, vc * 128:(vc + 1) * 128], Qw[0][:, blk], start=False, stop=False)
                self.mm(o, Cb[:, 1, vc * 128:(vc + 1) * 128], Qw[1][:, blk], start=False, stop=True)
            pd = P.bank()
            self.mm(pd[:, 0:128], C["ones_b"], PT, start=True, stop=False)
            self.mm(pd[:, 0:128], nbc[:, 0, :], Qw[0][:, blk], start=False, stop=False)
            self.mm(pd[:, 0:128], nbc[:, 1, :], Qw[1][:, blk], start=False, stop=True)
            self.act(absd, pd[:, 0:128], AF.Abs)
            self.tt("dve", absd, absd, GBem[:, blk], ALU.max)
            self.recip(absd, absd)
            for vc in range(2):
                self.tt("dve", Hh[vc][:, blk], pn[:, vc * 128:(vc + 1) * 128], absd, ALU.mult)
            for kc in range(2):
                pc = P.bank()
                self.mm(pc[:, 0:257], Ktm[:, bb, kc * 128:(kc + 1) * 128], Vtm[:, bb, :])
                self.stt(CM[:, kc, :], CM[:, kc, :], GBd[:, bb:bb + 1], pc[:, 0:257], ALU.mult, ALU.add)
                self.cp("act", Cb[:, kc, :], CM[:, kc, 0:256])
                self.cp("pool", nbc[:, kc, :], CM[:, kc, 256:257].to_broadcast([128, 128]))
        for kc in range(2):
            S.dma(O["p_C"][l, h, kc * 128:(kc + 1) * 128, :], CM[:, kc, 0:256], eng="sp")
            self.dmas(O["p_n"][l, h, kc * 128:(kc + 1) * 128].unsqueeze(1), CM[:, kc, 256:257], eng="sp")
        m2 = A.mark()
        ktm = A.f32(256, parts=NS)
        vau = A.f32(257, parts=NS)
        ps = P.bank()
        for kc in range(2):
            self.tr(ps[0:NS, kc * 128:(kc + 1) * 128], ks[:, kc, :], C["ident_f"])
            self.tr(ps[0:NS, 256 + kc * 128:256 + (kc + 1) * 128], vs[:, kc, :], C["ident_f"])
        self.ts("dve", ktm, ps[0:NS, 0:256], R["wscol"][:, h:h + 1], None, ALU.mult)
        self.cp("act", vau[:, 0:256], ps[0:NS, 256:512])
        self.memset("dve", vau[:, 256:257], 1.0)
        Kex = A.f32(NS * 256, parts=NS).rearrange("p (i c) -> p i c", i=NS)
        eye = C["ident_f"][0:NS, 0:NS].unsqueeze(2).to_broadcast([NS, NS, 256])
        self.tt("dve", Kex, ktm.unsqueeze(1).to_broadcast([NS, NS, 256]), eye, ALU.mult)
        CS = [A.f32(2 * 257).rearrange("p (k n) -> p k n", k=2) for _ in range(4)]
        nb2 = [A.f32(2 * 128).rearrange("p (k n) -> p k n", k=2) for _ in range(4)]
        rb, pN = P.reserve()
        for i in range(NS):
            k2 = i % 4
            cs_ = CS[k2]
            for kc in range(2):
                S.dma(cs_[:, kc, 0:256], I["s_C"][l, i, h, kc * 128:(kc + 1) * 128, :])
                self.dmas(cs_[:, kc, 256:257], I["s_n"][l, i, h, kc * 128:(kc + 1) * 128].unsqueeze(1))
            for kc in range(2):
                pc = P.bank()
                self.mm(pc[:, 0:257], Kex[:, i, kc * 128:(kc + 1) * 128], vau)
                self.stt(cs_[:, kc, :], cs_[:, kc, :], GBac[:, i:i + 1], pc[:, 0:257], ALU.mult, ALU.add)
                S.dma(O["s_C"][l, i, h, kc * 128:(kc + 1) * 128, :], cs_[:, kc, 0:256], eng="sp")
                self.dmas(O["s_n"][l, i, h, kc * 128:(kc + 1) * 128].unsqueeze(1), cs_[:, kc, 256:257], eng="sp")
                self.cp("pool", nb2[k2][:, kc, :], cs_[:, kc, 256:257].to_broadcast([128, 128]))
            for vc in range(2):
                o = pN[:, vc * NS + i:vc * NS + i + 1]
                self.mm(o, cs_[:, 0, vc * 128:(vc + 1) * 128], qs[:, 0, i:i + 1], start=True, stop=False)
                self.mm(o, cs_[:, 1, vc * 128:(vc + 1) * 128], qs[:, 1, i:i + 1], start=False, stop=True)
            o = pN[:, 2 * NS + i:2 * NS + i + 1]
            self.mm(o, nb2[k2][:, 0, :], qs[:, 0, i:i + 1], start=True, stop=False)
            self.mm(o, nb2[k2][:, 1, :], qs[:, 1, i:i + 1], start=False, stop=True)
        ad = A.f32(NS)
        self.act(ad, pN[:, 2 * NS:3 * NS], AF.Abs)
        self.tt("dve", ad, ad, GBem[:, T:NT], ALU.max)
        self.recip(ad, ad)
        for vc in range(2):
            self.tt("dve", Hh[vc][:, T:NT], pN[:, vc * NS:(vc + 1) * NS], ad, ALU.mult)
        P.unreserve(rb)
        A.release(m2)
        self.head_norm_out(Hh, MB + 3080 + h * 256, "m_ng", h * 2, AF.Sigmoid, BW + h * 256)

    def head_norm_out(self, Hh, gate_row0, gname, gidx0, gfunc, ob_row0):
        A, S, C, P = self.A, self.S, self.C, self.P
        NT = self.NT
        tl = ttiles(NT)
        m = A.mark()
        sq = [A.bf(NT) for _ in range(2)]
        rs = A.f32(NT)
        graw = A.f32(NT)
        obf = A.bf(NT)
        for vc in range(2):
            self.act(sq[vc], Hh[vc], AF.Square)
        for (t0, tn) in tl:
            ps = P.bank()
            self.mm(ps[:, 0:tn], C["ones_b"], sq[0][:, t0:t0 + tn], start=True, stop=False)
            self.mm(ps[:, 0:tn], C["ones_b"], sq[1][:, t0:t0 + tn], start=False, stop=True)
            self.rsqrt(rs[:, t0:t0 + tn], ps[:, 0:tn], scale=1.0 / 256, bias=C["eps"])
        for vc in range(2):
            S.dma(graw, self.projT[gate_row0 + vc * 128:gate_row0 + (vc + 1) * 128, :])
            self.act(graw, graw, gfunc)
            self.tt("dve", Hh[vc], Hh[vc], rs, ALU.mult)
            self.stt(obf, Hh[vc], self.V(gname, gidx0 + vc), graw, ALU.mult, ALU.mult)
            S.dma(self.obT[ob_row0 + vc * 128:ob_row0 + (vc + 1) * 128, :], obf, eng="sp")
        A.release(m)

    def gla(self, l):
        A, S, C, I, O, P = self.A, self.S, self.C, self.I, self.O, self.P
        T, NS, NT = self.T, self.NS, self.NT
        NB = T // 128
        tl = ttiles(NT)
        A.release(self.abase)
        a2b = A.bf(512, parts=16)
        xab = A.bf(NT, parts=16)
        m1 = A.mark()
        a2s = A.f32(512, parts=16)
        xas = A.f32(NT, parts=16)
        S.dma(a2s, I["ga2"][l])
        S.dma(xas, self.projT[GB_ + 2048:GB_ + 2064, :])
        self.cp("dve", a2b, a2s)
        self.cp("dve", xab, xas)
        A.release(m1)
        base = A.mark()
        v3 = lambda ap: ap[:, 0:T].rearrange("p (c j) -> p c j", j=128)
        for h in range(4):
            A.release(base)
            lg, bc, e1 = A.f32(NT), A.f32(NT), A.f32(NT)
            raw = A.f32(NT)
            Qh, Kh = A.bf(T), A.bf(T)
            WLg = A.f32(NB)
            qs, ksm, dgs = A.f32(NS), A.f32(NS), A.f32(NS)
            vs = A.f32(2 * NS).rearrange("p (k i) -> p k i", k=2)
            Vtm = A.bf(NB * 256).rearrange("p (b n) -> p b n", b=NB)
            Ktm = A.bf(NB * 128).rearrange("p (b n) -> p b n", b=NB)
            OG = [A.f32(NT) for _ in range(2)]
            vb = A.bf(T)
            for (t0, tn) in tl:
                ps = P.bank()
                self.mm(ps[:, 0:tn], a2b[:, h * 128:(h + 1) * 128], xab[:, t0:t0 + tn])
                self.act(lg[:, t0:t0 + tn], ps[:, 0:tn], AF.Sigmoid, bias=self.V("ga_b", h))
            self.act(lg, lg, AF.Ln)
            self.ts("pool", lg, lg, 1.0 / 16, None, ALU.mult)
            self.scan(bc[:, 0:T], C["rst128"], lg[:, 0:T], 0.0, ALU.mult, ALU.add)
            self.act(WLg, v3(bc)[:, :, 127], AF.Exp)
            self.act(dgs, lg[:, T:NT], AF.Exp)
            S.dma(raw, self.projT[GB_ + h * 128:GB_ + (h + 1) * 128, :])
            self.act(e1[:, 0:T], bc[:, 0:T], AF.Exp)
            self.stt(Qh, raw[:, 0:T], 128.0 ** -0.5, e1[:, 0:T], ALU.mult, ALU.mult)
            self.ts("pool", qs, raw[:, T:NT], 128.0 ** -0.5, None, ALU.mult)
            S.dma(raw, self.projT[GB_ + 512 + h * 128:GB_ + 512 + (h + 1) * 128, :])
            self.act(e1[:, 0:T], bc[:, 0:T], AF.Exp, scale=-1.0)
            self.tt("dve", Kh, raw[:, 0:T], e1[:, 0:T], ALU.mult)
            self.cp("pool", ksm, raw[:, T:NT])
            for bb in range(NB):
                ps = P.bank()
                self.mm(ps[:, 0:128], Kh[:, bb * 128:(bb + 1) * 128], C["ident_b"])
                self.cp("act", Ktm[:, bb, :], ps[:, 0:128])
            for vc in range(2):
                S.dma(raw, self.projT[GB_ + 1024 + h * 256 + vc * 128:GB_ + 1024 + h * 256 + (vc + 1) * 128, :])
                self.cp("dve", vb, raw[:, 0:T])
                self.cp("pool", vs[:, vc, :], raw[:, T:NT])
                for bb in range(NB):
                    ps = P.bank()
                    self.mm(ps[:, 0:128], vb[:, bb * 128:(bb + 1) * 128], C["ident_b"])
                    self.cp("act", Vtm[:, bb, vc * 128:(vc + 1) * 128], ps[:, 0:128])
            SM, SMs = A.f32(256), A.f32(256)
            Sb = A.bf(256)
            PTg = A.bf(128)
            self.memset("dve", SM, 0.0)
            self.memset("dve", Sb, 0.0)
            for bb in range(NB):
                blk = slice(bb * 128, (bb + 1) * 128)
                ps = P.bank()
                self.mm(ps[:, 0:128], Kh[:, blk], Qh[:, blk])
                self.tt("dve", PTg, ps[:, 0:128], C["m_le_b"], ALU.mult)
                po = P.bank()
                for vc in range(2):
                    o = po[:, vc * 128:(vc + 1) * 128]
                    self.mm(o, Vtm[:, bb, vc * 128:(vc + 1) * 128], PTg, start=True, stop=False)
                    self.mm(o, Sb[:, vc * 128:(vc + 1) * 128], Qh[:, blk], start=False, stop=True)
                    self.cp("act", OG[vc][:, blk], o)
                pz = P.bank()
                self.mm(pz[:, 0:256], Ktm[:, bb, :], Vtm[:, bb, :])
                self.ts("pool", SMs, SM, WLg[:, bb:bb + 1], None, ALU.mult)
                self.stt(SM, pz[:, 0:256], WLg[:, bb:bb + 1], SMs, ALU.mult, ALU.add)
                self.cp("act", Sb, SM)
            S.dma(O["p_gla"][l, h], SM, eng="sp")
            ktm = A.f32(128, parts=NS)
            vtm = A.f32(256, parts=NS)
            ps = P.bank()
            self.tr(ps[0:NS, 0:128], ksm, C["ident_f"])
            for vc in range(2):
                self.tr(ps[0:NS, 128 + vc * 128:128 + (vc + 1) * 128], vs[:, vc, :], C["ident_f"])
            self.cp("act", ktm, ps[0:NS, 0:128])
            self.cp("act", vtm, ps[0:NS, 128:384])
            Kex = A.f32(NS * 128, parts=NS).rearrange("p (i c) -> p i c", i=NS)
            eye = C["ident_f"][0:NS, 0:NS].unsqueeze(2).to_broadcast([NS, NS, 128])
            self.tt("dve", Kex, ktm.unsqueeze(1).to_broadcast([NS, NS, 128]), eye, ALU.mult)
            SS = [A.f32(256) for _ in range(4)]
            rb, pO = P.reserve()
            for i in range(NS):
                ss = SS[i % 4]
                S.dma(ss, I["s_gla"][l, i, h])
                pz = P.bank()
                self.mm(pz[:, 0:256], Kex[:, i, :], vtm)
                self.stt(ss, ss, dgs[:, i:i + 1], pz[:, 0:256], ALU.mult, ALU.add)
                S.dma(O["s_gla"][l, i, h], ss, eng="sp")
                for vc in range(2):
                    self.mm(pO[:, vc * NS + i:vc * NS + i + 1], ss[:, vc * 128:(vc + 1) * 128], qs[:, i:i + 1])
            for vc in range(2):
                self.cp("act", OG[vc][:, T:NT], pO[:, vc * NS:(vc + 1) * NS])
            P.unreserve(rb)
            self.head_norm_out(OG, GB_ + 2064 + h * 256, "g_ng", h * 2, AF.Silu, 2 * BW + h * 256)

    def merge(self, l):
        A, S, C, I, P = self.A, self.S, self.C, self.I, self.P
        NT = self.NT
        tl = ttiles(NT)
        A.release(self.abase)
        OB = A.bf(24 * NT).rearrange("p (c t) -> p c t", c=24)
        S.dma(OB, self.obT.rearrange("(c p) t -> p c t", p=128))
        wst = [A.f32(8 * 128).rearrange("p (k n) -> p k n", k=8) for _ in range(2)]
        wbf = [A.bf(8 * 128).rearrange("p (k n) -> p k n", k=8) for _ in range(2)]
        graw = [A.f32(NT) for _ in range(2)]
        accs = [A.f32(NT) for _ in range(2)]
        tmp = A.f32(512)
        mgb = [A.bf(NT) for _ in range(2)]
        groups = [(dc, j) for dc in range(16) for j in range(3)]

        def load(gi):
            dc, j = groups[gi]
            ws, wb, gr = wst[gi % 2], wbf[gi % 2], graw[gi % 2]
            S.dma(ws, I["w_branch"][l, j, :, dc * 128:(dc + 1) * 128].rearrange("(k p) n -> p k n", p=128))
            self.cp("pool", wb, ws)
            S.dma(gr, self.projT[GTB + j * D + dc * 128:GTB + j * D + (dc + 1) * 128, :])
            self.act(gr, gr, AF.Sigmoid, bias=self.V("gate_b", j * 16 + dc))

        load(0)
        for gi, (dc, j) in enumerate(groups):
            acc = accs[dc % 2]
            wb, gr = wbf[gi % 2], graw[gi % 2]
            banks = P.banks(len(tl))
            for k in range(8):
                for bi, (t0, tn) in enumerate(tl):
                    self.mm(banks[bi][:, 0:tn], wb[:, k, :], OB[:, j * 8 + k, t0:t0 + tn],
                            start=(k == 0), stop=(k == 7))
            if gi + 1 < len(groups):
                load(gi + 1)
            for bi, (t0, tn) in enumerate(tl):
                if j == 0:
                    self.tt("dve", acc[:, t0:t0 + tn], banks[bi][:, 0:tn], gr[:, t0:t0 + tn], ALU.mult)
                else:
                    self.tt("dve", tmp[:, 0:tn], banks[bi][:, 0:tn], gr[:, t0:t0 + tn], ALU.mult)
                    self.tt("pool", acc[:, t0:t0 + tn], acc[:, t0:t0 + tn], tmp[:, 0:tn], ALU.add)
            if j == 2:
                self.cp("act", mgb[dc % 2], acc)
                S.dma(self.mgT[dc * 128:(dc + 1) * 128, :], mgb[dc % 2], eng="act")

    def resid_pre(self, it, m):
        A, S = self.A, self.S
        A.release(self._dense_m0)
        stg = [A.f32(self.NT) for _ in range(2)]
        S.dma(stg[it % 2], self._res_src[m * 128:(m + 1) * 128, :])

    def resid_sink(self, it, m, rows, banks, tl):
        A, S = self.A, self.S
        A.release(self._dense_m0)
        stg = [A.f32(self.NT) for _ in range(2)]
        o = stg[it % 2]
        for bi, (t0, tn) in enumerate(tl):
            self.tt("dve", o[:, t0:t0 + tn], o[:, t0:t0 + tn], banks[bi][:, 0:tn], ALU.add)
        S.dma(self.xT[m * 128:(m + 1) * 128, :], o, eng="sp")

    def outproj(self, l):
        A, S, I = self.A, self.S, self.I
        NT = self.NT
        A.release(self.abase)
        MG = A.bf(16 * NT).rearrange("p (c t) -> p c t", c=16)
        S.dma(MG, self.mgT.rearrange("(c p) t -> p c t", p=128))
        self._res_src = self.src
        self.dense(I["w_out"][l], D, 16, MG, self.resid_sink, presink=self.resid_pre)

    def ffn(self, l):
        A, S, I = self.A, self.S, self.I
        NT = self.NT
        NF = DFF // 128
        order = []
        for fc in range(NF):
            order += [fc, NF + fc]
        self.dense(I["w_gu"][l], 2 * DFF, 16, self.hT, self.gu_sink, order=order)
        self._res_src = self.xT
        for half in range(2):
            A.release(self.abase)
            AH = A.bf((NF // 2) * NT).rearrange("p (c t) -> p c t", c=NF // 2)
            r0 = half * (DFF // 2)
            S.dma(AH, self.actT[r0:r0 + DFF // 2, :].rearrange("(c p) t -> p c t", p=128))
            self.dense(I["w_down"][l, r0:r0 + DFF // 2, :], D, NF // 2, AH, self.resid_sink, presink=self.resid_pre)

    def gu_sink(self, it, m, rows, banks, tl):
        A, S = self.A, self.S
        A.release(self._dense_m0)
        sil = A.f32(self.NT)
        ab = [A.bf(self.NT) for _ in range(2)]
        if it % 2 == 0:
            for bi, (t0, tn) in enumerate(tl):
                self.act(sil[:, t0:t0 + tn], banks[bi][:, 0:tn], AF.Silu)
        else:
            o = ab[(it // 2) % 2]
            fc = m - DFF // 128
            for bi, (t0, tn) in enumerate(tl):
                self.tt("dve", o[:, t0:t0 + tn], sil[:, t0:t0 + tn], banks[bi][:, 0:tn], ALU.mult)
            S.dma(self.actT[fc * 128:(fc + 1) * 128, :], o, eng="sp")


def _cols(v, n):
    return np.ascontiguousarray(np.asarray(v).reshape(n, 128).T)


def pack_vec(inp, l):
    out = np.zeros((128, VEC_N), np.float32)

    def put(name, arr):
        off, n = VEC_OFF[name]
        out[:arr.shape[0], off:off + n] = arr

    put("n1g", _cols(inp["norm1_g"][l], 16))
    put("n2g", _cols(inp["norm2_g"][l], 16))
    put("gate_b", _cols(inp["gate_b"][l].reshape(-1), 48))
    mu = np.asarray(inp["rwkv_mu"][l])
    put("mu", _cols(mu[:3072], 24))
    put("mul", np.ascontiguousarray(mu[3072:3264].reshape(3, 64).T))
    for nm, key in [("w0", "rwkv_w0"), ("a0", "rwkv_a0"), ("k_k", "rwkv_k_k"), ("k_a", "rwkv_k_a"),
                    ("ln_g", "rwkv_ln_g"), ("ln_b", "rwkv_ln_b"), ("m_ng", "mlstm_norm_g"), ("g_ng", "gla_norm_g")]:
        put(nm, _cols(inp[key][l], 8))
    put("r_k", _cols(np.asarray(inp["rwkv_r_k"][l]).reshape(-1), 8))
    put("conv_w", _cols(np.asarray(inp["mlstm_conv_w"][l]).reshape(-1), 64))
    put("conv_b", _cols(inp["mlstm_conv_b"][l], 16))
    put("ga_b", _cols(inp["gla_a_b"][l], 4))
    return out


_NC_CACHE = {}


def make_in_map(inp, L, xT, samp):
    f = lambda a: np.ascontiguousarray(np.asarray(a, dtype=np.float32))
    m = {}
    m["xT"] = f(xT)
    m["vec"] = np.stack([pack_vec(inp, l) for l in range(L)])
    m["rows"] = f(np.stack([np.stack([inp["mlstm_i_b"][l], inp["mlstm_f_b"][l]], axis=1) for l in range(L)]))
    m["w_in"] = f(inp["w_in"][:L])
    m["rw2"] = f(np.stack([np.stack([inp["rwkv_w2"][l], inp["rwkv_a2"][l], inp["rwkv_g2"][l]]) for l in range(L)]))
    m["ga2"] = f(inp["gla_a2"][:L])
    m["w_branch"] = f(inp["w_branch"][:L])
    m["w_out"] = f(inp["w_out"][:L])
    m["w_gu"] = f(inp["ffn_w_gu"][:L])
    m["w_down"] = f(inp["ffn_w_down"][:L])
    m["fng"] = _cols(inp["final_norm_g"], 16)
    m["s_shift"] = f(np.transpose(inp["state_rwkv_shift"][:L, samp], (0, 2, 1)))
    m["s_wkv"] = f(inp["state_rwkv_wkv"][:L, samp])
    m["s_conv"] = f(np.transpose(inp["state_mlstm_conv"][:L, samp], (0, 2, 3, 1)))
    m["s_C"] = f(inp["state_mlstm_C"][:L, samp])
    m["s_n"] = f(inp["state_mlstm_n"][:L, samp])
    m["s_m"] = f(np.transpose(inp["state_mlstm_m"][:L, samp], (0, 2, 1)))
    m["s_gla"] = f(inp["state_gla_S"][:L, samp])
    return m


def kernel(**inputs):
    inp = {k: np.asarray(v) for k, v in inputs.items()}
    B, T, _ = inp["x_prompt"].shape
    NSALL = inp["x_sample"].shape[0]
    L = inp["w_in"].shape[0]
    NS = NSALL // NCORES
    key = (T, NS, L)
    if key not in _NC_CACHE:
        _NC_CACHE[key] = Builder(T, NS, L).build()
    nc = _NC_CACHE[key]
    in_maps = []
    for c in range(NCORES):
        seq = c // 2
        samp = slice(c * NS, (c + 1) * NS)
        xT = np.concatenate([inp["x_prompt"][seq].T, inp["x_sample"][samp, 0, :].T], axis=1)
        in_maps.append(make_in_map(inp, L, xT, samp))
    res = run_bass_kernel_spmd(nc, in_maps, core_ids=list(range(NCORES)))
    R = res.results
    P = [R[2 * s] for s in range(B)]
    f32 = np.float32
    y_prompt = np.stack([P[s]["yT"][:, :T].T for s in range(B)]).astype(f32)
    y_sample = np.concatenate([R[c]["yT"][:, T:].T for c in range(NCORES)])[:, None, :].astype(f32)
    p_shift = np.stack([P[s]["o_shift"][:, :, 0] for s in range(B)], axis=1)
    s_shift = np.concatenate([np.transpose(R[c]["o_shift"][:, :, 1:], (0, 2, 1)) for c in range(NCORES)], axis=1)
    p_wkv = np.stack([P[s]["p_wkv"] for s in range(B)], axis=1)
    s_wkv = np.concatenate([R[c]["o_s_wkv"] for c in range(NCORES)], axis=1)
    p_conv = np.stack([np.transpose(P[s]["o_conv"][:, :, :, 0], (0, 1, 2)) for s in range(B)], axis=1)
    s_conv = np.concatenate([np.transpose(R[c]["o_conv"][:, :, :, 1:], (0, 3, 1, 2)) for c in range(NCORES)], axis=1)
    p_C = np.stack([P[s]["p_C"] for s in range(B)], axis=1)
    s_C = np.concatenate([R[c]["o_s_C"] for c in range(NCORES)], axis=1)
    p_n = np.stack([P[s]["p_n"] for s in range(B)], axis=1)
    s_n = np.concatenate([R[c]["o_s_n"] for c in range(NCORES)], axis=1)
    p_m = np.stack([P[s]["o_m"][:, :, 0] for s in range(B)], axis=1)
    s_m = np.concatenate([np.transpose(R[c]["o_m"][:, :, 1:], (0, 2, 1)) for c in range(NCORES)], axis=1)
    p_gla = np.stack([P[s]["p_gla"] for s in range(B)], axis=1)
    s_gla = np.concatenate([R[c]["o_s_gla"] for c in range(NCORES)], axis=1)
    outs = (y_prompt, y_sample, p_shift, p_wkv, p_conv, p_C, p_n, p_m, p_gla,
            s_shift, s_wkv, s_conv, s_C, s_n, s_m, s_gla)
    return tuple(np.ascontiguousarray(o, dtype=f32) for o in outs)
```

```python
from contextlib import ExitStack
import numpy as np
import concourse.bass as bass
import concourse.mybir as mybir
from concourse.bass_utils import run_bass_kernel_spmd

F32 = mybir.dt.float32
BF16 = mybir.dt.bfloat16
AF = mybir.ActivationFunctionType
ALU = mybir.AluOpType
AX = mybir.AxisListType
_ESZ = {F32: 4, BF16: 2, mybir.dt.int32: 4, mybir.dt.uint8: 1}

D = 2048
BW = 1024
R_COLS = 3264
M_COLS = 4104
G_COLS = 3088
N_IN = 16600
DFF = 5632
RB, MB, GB_, GTB = 0, 3264, 7368, 10456
EPS = 1e-6
R_GN_EPS = 64e-5
NCORES = 8


class _Rec:
    __slots__ = ("sp", "plo", "phi", "lo", "hi", "lw", "rde", "rdd", "cells", "dead")


class _Op:
    __slots__ = ("eng", "fn", "dma", "deps", "signal", "semval", "semidx")


def _region(ap):
    t = ap.tensor
    tn = type(t).__name__
    esz = _ESZ[ap.dtype]
    pairs = ap.ap
    if tn == "DRamTensorHandle":
        ext = 1
        for st, cnt in pairs:
            ext += (cnt - 1) * abs(st)
        lo = ap.offset * esz
        return ("d" + t.name, 0, 1, lo, lo + ext * esz)
    tshape = t.shape
    rowb = 1
    for s in tshape[1:]:
        rowb *= s
    rowb *= _ESZ[t.dtype]
    off = ap.offset * esz
    p0 = off // rowb
    c0 = off % rowb
    ext = 1
    for st, cnt in pairs[1:]:
        ext += (cnt - 1) * abs(st)
    np_ = pairs[0][1]
    if tn == "SBTensorHandle":
        return ("s", p0, p0 + np_, c0, c0 + ext * esz)
    lo = (c0 // 2048) * 2048
    hi = ((c0 + ext * esz + 2047) // 2048) * 2048
    return ("p", 0, 128, lo, hi)


class Sched:
    ENGS = ("pe", "act", "dve", "pool", "sp")

    def __init__(self, nc, n_dma_sems=64):
        self.nc = nc
        self.ops = []
        self.nds = n_dma_sems
        self.eng_obj = {"pe": nc.tensor, "act": nc.scalar, "dve": nc.vector, "pool": nc.gpsimd, "sp": nc.sync}
        self.recs = {}
        self.grid = {}
        self.cellsz = {}

    def _cells(self, key):
        sp, plo, phi, lo, hi = key
        if sp == "s":
            cs = 2048
        elif sp == "p":
            cs = 2048
        else:
            cs = 1 << 20
        return [(sp, c) for c in range(lo // cs, (hi - 1) // cs + 1)]

    def _get(self, key):
        r = self.recs.get(key)
        if r is None:
            r = _Rec()
            r.sp, r.plo, r.phi, r.lo, r.hi = key
            r.lw = -1
            r.rde = {}
            r.rdd = []
            r.cells = self._cells(key)
            r.dead = False
            self.recs[key] = r
            for c in r.cells:
                self.grid.setdefault(c, set()).add(key)
        return r

    def _overlaps(self, key):
        sp, plo, phi, lo, hi = key
        out = set()
        for c in self._cells(key):
            g = self.grid.get(c)
            if g:
                out |= g
        res = []
        for k in out:
            if k[1] < phi and plo < k[2] and k[3] < hi and lo < k[4]:
                res.append(k)
        return res

    def op(self, eng, fn, reads=(), writes=(), dma=False):
        i = len(self.ops)
        o = _Op()
        o.eng, o.fn, o.dma = eng, fn, dma
        o.signal = False
        o.semval = 0
        o.semidx = -1
        deps = set()
        rkeys = [_region(a) for a in reads]
        wkeys = [_region(a) for a in writes]
        for k in rkeys:
            for q in self._overlaps(k):
                r = self.recs[q]
                if r.lw >= 0:
                    deps.add(r.lw)
        for k in wkeys:
            for q in self._overlaps(k):
                r = self.recs[q]
                if r.lw >= 0:
                    deps.add(r.lw)
                for v in r.rde.values():
                    deps.add(v)
                for v in r.rdd:
                    deps.add(v)
        keep = []
        for d in deps:
            od = self.ops[d]
            if (not od.dma) and (not dma) and od.eng == eng and eng == "pe":
                continue
            keep.append(d)
            od.signal = True
        o.deps = keep
        self.ops.append(o)
        for k in rkeys:
            r = self._get(k)
            if dma:
                r.rdd.append(i)
            else:
                r.rde[eng] = i
        for k in wkeys:
            for q in self._overlaps(k):
                if q != k and q[1] >= k[1] and q[2] <= k[2] and q[3] >= k[3] and q[4] <= k[4]:
                    r = self.recs.pop(q)
                    for c in r.cells:
                        self.grid[c].discard(q)
            r = self._get(k)
            r.lw = i
            r.rde = {}
            r.rdd = []
        if dma:
            o.signal = True
        return o

    def dma(self, out, in_, eng="sp", **kw):
        return self.op(eng, lambda e: e.dma_start(out=out, in_=in_, **kw), [in_], [out], dma=True)

    def emit(self, stack):
        nc = self.nc
        cnt = {e: 0 for e in self.ENGS}
        dcnt = [0] * self.nds
        di = 0
        for o in self.ops:
            if not o.signal:
                continue
            if o.dma:
                o.semidx = di % self.nds
                dcnt[o.semidx] += 16
                o.semval = dcnt[o.semidx]
                di += 1
            else:
                cnt[o.eng] += 1
                o.semval = cnt[o.eng]
        sems = {e: stack.enter_context(nc.semaphore("s_" + e)) for e in self.ENGS}
        dsems = [stack.enter_context(nc.semaphore("d_%d" % i)) for i in range(self.nds)]
        waited = {e: {} for e in self.ENGS}
        nw = 0
        for o in self.ops:
            e = self.eng_obj[o.eng]
            need = {}
            for d in o.deps:
                od = self.ops[d]
                key = ("d", od.semidx) if od.dma else ("e", od.eng)
                if od.semval > need.get(key, 0):
                    need[key] = od.semval
            wd = waited[o.eng]
            if o.dma and o.semval > 16:
                k0 = ("d", o.semidx)
                if o.semval - 16 > need.get(k0, 0):
                    need[k0] = o.semval - 16
            for key, val in need.items():
                if wd.get(key, 0) >= val:
                    continue
                wd[key] = val
                e.wait_ge(dsems[key[1]] if key[0] == "d" else sems[key[1]], val)
                nw += 1
            ins = o.fn(e)
            if o.signal:
                if o.dma:
                    ins.then_inc(dsems[o.semidx], 16)
                else:
                    ins.then_inc(sems[o.eng], 1)
        for i, v in enumerate(dcnt):
            if v:
                nc.sync.wait_ge(dsems[i], v)
        return nw


class Arena:
    def __init__(self, t, ncols):
        self.t = t
        self.ncols = ncols
        self.top = 0

    def mark(self):
        return self.top

    def release(self, m):
        self.top = m

    def f32(self, cols, parts=128):
        c0 = self.top
        self.top += cols
        assert self.top <= self.ncols, ("arena overflow", self.top, self.ncols)
        return self.t[0:parts, c0:c0 + cols]

    def bf(self, cols, parts=128):
        n32 = (cols + 1) // 2
        a = self.f32(n32, parts)
        return a.bitcast(BF16)[:, 0:cols]


class Psum:
    def __init__(self, t):
        self.t = t
        self.nb = 0

    def bank(self):
        b = self.nb % 8
        self.nb += 1
        return self.t[:, b * 512:(b + 1) * 512]

    def banks(self, n):
        return [self.bank() for _ in range(n)]


def ttiles(NT):
    out = []
    c = 0
    while c < NT:
        n = min(512, NT - c)
        out.append((c, n))
        c += n
    return out


VEC_OFF = {}
_o = 0
for _n, _c in [("n1g", 16), ("n2g", 16), ("gate_b", 48), ("mu", 24), ("mul", 3), ("w0", 8), ("a0", 8), ("k_k", 8),
               ("k_a", 8), ("r_k", 8), ("ln_g", 8), ("ln_b", 8), ("conv_w", 64), ("conv_b", 16), ("m_ng", 8),
               ("g_ng", 8), ("ga_b", 4)]:
    VEC_OFF[_n] = (_o, _c)
    _o += _c
VEC_N = _o
DV_OFF = {"omu": (0, 24), "omul": (24, 3), "omk_a": (27, 8)}
DV_N = 35


class Psum2(Psum):
    def __init__(self, t):
        Psum.__init__(self, t)
        self.reserved = set()

    def bank(self):
        while True:
            b = self.nb % 8
            self.nb += 1
            if b not in self.reserved:
                return self.t[:, b * 512:(b + 1) * 512]

    def reserve(self):
        while True:
            b = self.nb % 8
            self.nb += 1
            if b not in self.reserved:
                self.reserved.add(b)
                return b, self.t[:, b * 512:(b + 1) * 512]

    def unreserve(self, b):
        self.reserved.discard(b)


class StopBuild(Exception):
    pass


class Builder:
    def chk(self, name):
        import os
        if os.environ.get("SUBSTOP") == name:
            raise StopBuild()

    def __init__(self, T, NS, depth, dbg=(), stop=None):
        self.T, self.NS, self.NT, self.depth = T, NS, T + NS, depth
        self.dbg = set(dbg)
        self.stop = stop
        assert T % 128 == 0

    def dram(self, name, shape, dt=F32, kind=None):
        if kind is None:
            kind = "ExternalOutput" if name in self.dbg else "Internal"
        return self.nc.dram_tensor(name, list(shape), dt, kind=kind).ap()

    def build(self):
        nc = bass.Bass("TRN2", target_bir_lowering=False)
        self.nc = nc
        T, NS, NT, L = self.T, self.NS, self.NT, self.depth
        inp = lambda n, s: self.dram(n, s, kind="ExternalInput")
        outp = lambda n, s: self.dram(n, s, kind="ExternalOutput")
        I = self.I = {}
        I["xT"] = inp("xT", [D, NT])
        I["vec"] = inp("vec", [L, 128, VEC_N])
        I["rows"] = inp("rows", [L, 4, 2])
        I["w_in"] = inp("w_in", [L, D, N_IN])
        I["rw2"] = inp("rw2", [L, 3, 64, BW])
        I["ga2"] = inp("ga2", [L, 16, 512])
        I["w_branch"] = inp("w_branch", [L, 3, BW, D])
        I["w_out"] = inp("w_out", [L, D, D])
        I["w_gu"] = inp("w_gu", [L, D, 2 * DFF])
        I["w_down"] = inp("w_down", [L, DFF, D])
        I["fng"] = inp("fng", [128, 16])
        I["s_shift"] = inp("s_shift", [L, R_COLS, NS])
        I["s_wkv"] = inp("s_wkv", [L, NS, 16, 64, 64])
        I["s_conv"] = inp("s_conv", [L, 3, D, NS])
        I["s_C"] = inp("s_C", [L, NS, 4, 256, 256])
        I["s_n"] = inp("s_n", [L, NS, 4, 256])
        I["s_m"] = inp("s_m", [L, 4, NS])
        I["s_gla"] = inp("s_gla", [L, NS, 4, 128, 256])
        O = self.O = {}
        O["yT"] = outp("yT", [D, NT])
        O["o_shift"] = outp("o_shift", [L, R_COLS, 1 + NS])
        O["p_wkv"] = outp("p_wkv", [L, 16, 64, 64])
        O["s_wkv"] = outp("o_s_wkv", [L, NS, 16, 64, 64])
        O["o_conv"] = outp("o_conv", [L, 3, D, 1 + NS])
        O["p_C"] = outp("p_C", [L, 4, 256, 256])
        O["p_n"] = outp("p_n", [L, 4, 256])
        O["s_C"] = outp("o_s_C", [L, NS, 4, 256, 256])
        O["s_n"] = outp("o_s_n", [L, NS, 4, 256])
        O["o_m"] = outp("o_m", [L, 4, 1 + NS])
        O["p_gla"] = outp("p_gla", [L, 4, 128, 256])
        O["s_gla"] = outp("o_s_gla", [L, NS, 4, 128, 256])
        self.xT = self.dram("x_scr", [D, NT])
        self.projT = self.dram("projT", [N_IN, NT])
        self.obT = self.dram("obT", [3 * BW, NT], BF16)
        self.mgT = self.dram("mgT", [D, NT], BF16)
        self.actT = self.dram("actT", [DFF, NT], BF16)

        with ExitStack() as st:
            ACOLS = 52800
            at = st.enter_context(nc.sbuf_tensor("arena", [128, ACOLS], F32))
            pt = st.enter_context(nc.psum_tensor("psum", [128, 4096], F32))
            self.A = Arena(at, ACOLS)
            self.P = Psum2(pt)
            self.S = Sched(nc)
            self.consts()
            done = True
            try:
                for l in range(L):
                    if not self.layer(l):
                        done = False
                        break
            except StopBuild:
                done = False
            if done:
                self.S.dma(self.C["vec"][:, 0:16], I["fng"])
                self.rmsnorm(self.xT, "n1g", out_dram=O["yT"])
            self.nwaits = self.S.emit(st)
        return nc

    def act(self, out, in_, func, bias=None, scale=1.0):
        kw = {}
        rd = [in_]
        if bias is not None:
            kw["bias"] = bias
            if not isinstance(bias, (int, float)):
                rd.append(bias)
        if not isinstance(scale, (int, float)):
            rd.append(scale)
        self.S.op("act", lambda e: e.activation(out=out, in_=in_, func=func, scale=scale, **kw), rd, [out])

    def ts(self, eng, out, in0, s1, s2, op0, op1=None):
        rd = [in0] + [s for s in (s1, s2) if s is not None and not isinstance(s, (int, float))]
        if op1 is None:
            self.S.op(eng, lambda e: e.tensor_scalar(out=out, in0=in0, scalar1=s1, scalar2=None, op0=op0), rd, [out])
        else:
            self.S.op(eng, lambda e: e.tensor_scalar(out=out, in0=in0, scalar1=s1, scalar2=s2, op0=op0, op1=op1),
                      rd, [out])

    def stt(self, out, in0, sc, in1, op0, op1):
        rd = [in0, in1] + ([] if isinstance(sc, (int, float)) else [sc])
        self.S.op("dve", lambda e: e.scalar_tensor_tensor(out=out, in0=in0, scalar=sc, in1=in1, op0=op0, op1=op1),
                  rd, [out])

    def tt(self, eng, out, in0, in1, op):
        self.S.op(eng, lambda e: e.tensor_tensor(out=out, in0=in0, in1=in1, op=op), [in0, in1], [out])

    def cp(self, eng, out, in_):
        if eng == "act":
            self.S.op("act", lambda e: e.copy(out=out, in_=in_), [in_], [out])
        else:
            self.S.op(eng, lambda e: e.tensor_copy(out=out, in_=in_), [in_], [out])

    def mm(self, out, lhsT, rhs, start=True, stop=True):
        self.S.op("pe", lambda e: e.matmul(out, lhsT=lhsT, rhs=rhs, start=start, stop=stop), [lhsT, rhs], [out])

    def tr(self, out, in_, ident):
        self.S.op("pe", lambda e: e.transpose(out, in_, ident), [in_, ident], [out])

    def memset(self, eng, ap, v):
        self.S.op(eng, lambda e: e.memset(ap, v), [], [ap])

    def rsqrt(self, out, in_, scale=1.0, bias=None):
        self.act(out, in_, AF.Ln, bias=bias, scale=scale)
        self.act(out, out, AF.Exp, scale=-0.5)

    def recip(self, out, in_):
        self.S.op("dve", lambda e: e.reciprocal(out=out, in_=in_), [in_], [out])

    def scan(self, out, d0, d1, init, op0, op1):
        self.S.op("dve", lambda e: e.tensor_tensor_scan(out=out, data0=d0, data1=d1, initial=init, op0=op0, op1=op1),
                  [d0, d1], [out])

    def dmas(self, out, in_, eng="sp"):
        self.S.op(eng, lambda e: e.dma_start(out=out, in_=in_, allow_slow_non_contiguous=True), [in_], [out], dma=True)

    def V(self, name, j=0, parts=128):
        off, n = VEC_OFF[name]
        return self.C["vec"][0:parts, off + j:off + j + 1]

    def DV(self, name, j=0, parts=128):
        off, n = DV_OFF[name]
        return self.C["dvec"][0:parts, off + j:off + j + 1]

    def consts(self):
        A, S, T = self.A, self.S, self.T
        C = self.C = {}
        io = A.f32(128)
        S.op("pool", lambda e: e.iota(io, [[1, 128]], base=0, channel_multiplier=-1,
                                      allow_small_or_imprecise_dtypes=True), [], [io])
        C["ident_f"] = A.f32(128)
        self.ts("dve", C["ident_f"], io, 0.0, None, ALU.is_equal)
        C["ident_b"] = A.bf(128)
        self.ts("dve", C["ident_b"], io, 0.0, None, ALU.is_equal)
        C["ones_b"] = A.bf(128)
        self.memset("dve", C["ones_b"], 1.0)
        C["m_le_b"] = A.bf(128)
        self.ts("dve", C["m_le_b"], io, 0.0, None, ALU.is_ge)
        C["negm"] = A.f32(128)
        self.ts("dve", C["negm"], io, 0.0, -30000.0, ALU.is_lt, ALU.mult)
        C["bo_b"] = A.bf(128)
        self.memset("dve", C["bo_b"], 0.0)
        self.memset("dve", C["bo_b"][0:64, 0:64], 1.0)
        self.memset("dve", C["bo_b"][64:128, 64:128], 1.0)
        io2 = A.f32(64)
        S.op("pool", lambda e: e.iota(io2[0:64, :], [[1, 64]], base=0, channel_multiplier=-1,
                                      allow_small_or_imprecise_dtypes=True), [], [io2[0:64, :]])
        S.op("pool", lambda e: e.iota(io2[64:128, :], [[1, 64]], base=0, channel_multiplier=-1,
                                      allow_small_or_imprecise_dtypes=True), [], [io2[64:128, :]])
        C["mk192"] = A.bf(192)
        self.ts("dve", C["mk192"][:, 0:64], io2, 0.0, None, ALU.is_gt)
        self.ts("dve", C["mk192"][:, 64:128], io2, 0.0, None, ALU.is_gt)
        self.ts("dve", C["mk192"][:, 128:192], io2, 0.0, None, ALU.is_ge)
        C["mk_lt"] = A.bf(128)
        self.ts("dve", C["mk_lt"][:, 0:64], io2, 0.0, None, ALU.is_lt)
        self.ts("dve", C["mk_lt"][:, 64:128], io2, 0.0, None, ALU.is_lt)
        C["rst64"] = A.bf(T)
        self.memset("pool", C["rst64"], 1.0)
        self.memset("pool", C["rst64"].rearrange("p (c j) -> p c j", j=64)[:, :, 0:1], 0.0)
        C["rst128"] = A.bf(T)
        self.memset("pool", C["rst128"], 1.0)
        self.memset("pool", C["rst128"].rearrange("p (c j) -> p c j", j=128)[:, :, 0:1], 0.0)
        C["sel"] = A.f32(4 * 128, parts=4).rearrange("p (h m) -> p h m", h=4)
        for h in range(4):
            self.cp("dve", C["sel"][:, h, :], C["ident_f"][0:4, h:h + 1].to_broadcast([4, 128]))
        C["vec"] = A.f32(VEC_N)
        C["dvec"] = A.f32(DV_N)
        C["rows"] = A.f32(2, parts=4)
        C["eps"] = A.f32(1)
        self.memset("dve", C["eps"], EPS)
        C["gneps"] = A.f32(1)
        self.memset("dve", C["gneps"], R_GN_EPS)
        self.abase = A.mark()

    def layer(self, l):
        S, I, O, C = self.S, self.I, self.O, self.C
        T, NS, NT = self.T, self.NS, self.NT
        S.dma(C["vec"], I["vec"][l])
        S.dma(C["rows"], I["rows"][l])
        mo, mn = VEC_OFF["mu"]
        self.ts("dve", C["dvec"][:, 0:27], C["vec"][:, mo:mo + 27], -1.0, 1.0, ALU.mult, ALU.add)
        ko, kn = VEC_OFF["k_a"]
        self.ts("dve", C["dvec"][:, 27:35], C["vec"][:, ko:ko + 8], -1.0, 1.0, ALU.mult, ALU.add)
        src = I["xT"] if l == 0 else self.xT
        self.src = src
        self.rmsnorm(src, "n1g")
        self.dense(I["w_in"][l], N_IN, 16, self.hT, self.proj_sink)
        if self.stop == (l, "proj"):
            return False
        for r0 in range(0, R_COLS, 816):
            self.dmas(O["o_shift"][l, r0:r0 + 816, 0:1], self.projT[r0:r0 + 816, T - 1:T])
            self.dmas(O["o_shift"][l, r0:r0 + 816, 1:1 + NS], self.projT[r0:r0 + 816, T:NT])
        for r0 in range(0, D, 1024):
            for lag in range(3):
                self.dmas(O["o_conv"][l, lag, r0:r0 + 1024, 0:1],
                          self.projT[MB + r0:MB + r0 + 1024, T - 3 + lag:T - 2 + lag])
            self.dmas(O["o_conv"][l, 0, r0:r0 + 1024, 1:1 + NS], I["s_conv"][l, 1, r0:r0 + 1024, :])
            self.dmas(O["o_conv"][l, 1, r0:r0 + 1024, 1:1 + NS], I["s_conv"][l, 2, r0:r0 + 1024, :])
            self.dmas(O["o_conv"][l, 2, r0:r0 + 1024, 1:1 + NS], self.projT[MB + r0:MB + r0 + 1024, T:NT])
        self.chk("rw0")
        self.rwkv(l)
        if self.stop == (l, "rwkv"):
            return False
        self.mlstm(l)
        if self.stop == (l, "mlstm"):
            return False
        self.gla(l)
        if self.stop == (l, "gla"):
            return False
        self.merge(l)
        self.outproj(l)
        if self.stop == (l, "attn"):
            return False
        self.rmsnorm(self.xT, "n2g")
        self.ffn(l)
        return True

    def rmsnorm(self, src, gname, out_dram=None):
        A, S, C = self.A, self.S, self.C
        NT = self.NT
        A.release(self.abase)
        if out_dram is None:
            self.hT = A.bf(16 * NT).rearrange("p (c t) -> p c t", c=16)
        m0 = A.mark()
        srcv = src.rearrange("(c p) t -> p c t", p=128)
        tls = ttiles(NT)
        xts = [A.f32(16 * 512).rearrange("p (c t) -> p c t", c=16) for _ in range(2)]
        sqs = [A.bf(16 * 512).rearrange("p (c t) -> p c t", c=16) for _ in range(2)]
        rss = [A.f32(512) for _ in range(2)]
        S.dma(xts[0][:, :, 0:tls[0][1]], srcv[:, :, tls[0][0]:tls[0][0] + tls[0][1]])
        for ti, (t0, tn) in enumerate(tls):
            xt = xts[ti % 2][:, :, 0:tn]
            sq = sqs[ti % 2][:, :, 0:tn]
            rs = rss[ti % 2][:, 0:tn]
            if ti + 1 < len(tls):
                n0, nn = tls[ti + 1]
                S.dma(xts[(ti + 1) % 2][:, :, 0:nn], srcv[:, :, n0:n0 + nn])
            ps = self.P.bank()[:, 0:tn]
            for c in range(16):
                self.act(sq[:, c, :], xt[:, c, :], AF.Square)
                self.mm(ps, C["ones_b"], sq[:, c, :], start=(c == 0), stop=(c == 15))
            self.rsqrt(rs, ps, scale=1.0 / D, bias=C["eps"])
            for c in range(16):
                dst = self.hT[:, c, t0:t0 + tn] if out_dram is None else xt[:, c, :]
                self.stt(dst, xt[:, c, :], self.V(gname, c), rs, ALU.mult, ALU.mult)
            if out_dram is not None:
                S.dma(out_dram.rearrange("(c p) t -> p c t", p=128)[:, :, t0:t0 + tn], xt, eng="sp")
        A.release(m0)

    def dense(self, W, N, KC, rhs, sink, order=None, presink=None):
        A, S = self.A, self.S
        NT = self.NT
        m0 = A.mark()
        nm = (N + 127) // 128
        Wv = W.rearrange("(kc p) n -> p kc n", p=128)
        NWB = 3
        wst = [A.f32(KC * 128).rearrange("p (k n) -> p k n", k=KC) for _ in range(NWB)]
        wbf = [A.bf(KC * 128).rearrange("p (k n) -> p k n", k=KC) for _ in range(NWB)]
        self._dense_m0 = A.mark()
        tl = ttiles(NT)
        order = order if order is not None else list(range(nm))
        n = len(order)

        def load(it):
            m = order[it]
            rows = min(128, N - m * 128)
            ws, wb = wst[it % NWB], wbf[it % NWB]
            S.dma(ws[:, :, 0:rows], Wv[:, :, m * 128:m * 128 + rows])
            self.cp("pool" if it % 2 else "dve", wb[:, :, 0:rows], ws[:, :, 0:rows])

        for it in range(min(2, n)):
            load(it)
        for it, m in enumerate(order):
            rows = min(128, N - m * 128)
            wb = wbf[it % NWB]
            if it + 2 < n:
                load(it + 2)
            if presink is not None:
                presink(it, m)
            banks = self.P.banks(len(tl))
            for k in range(KC):
                for bi, (t0, tn) in enumerate(tl):
                    self.mm(banks[bi][0:rows, 0:tn], wb[:, k, 0:rows], rhs[:, k, t0:t0 + tn],
                            start=(k == 0), stop=(k == KC - 1))
            sink(it, m, rows, banks, tl)
        A.release(m0)

    def proj_sink(self, it, m, rows, banks, tl):
        A, S = self.A, self.S
        A.release(self._dense_m0)
        stg = [A.f32(self.NT) for _ in range(2)]
        o = stg[it % 2]
        for bi, (t0, tn) in enumerate(tl):
            self.cp("act", o[0:rows, t0:t0 + tn], banks[bi][0:rows, 0:tn])
        S.dma(self.projT[m * 128:m * 128 + rows, :], o[0:rows, :], eng="act")

    def shiftmix(self, dst, p, mu, omu, st, parts=128):
        T, NT = self.T, self.NT
        self.ts("dve", dst[0:parts, :], p[0:parts, :], omu, None, ALU.mult)
        self.stt(dst[0:parts, 1:T], p[0:parts, 0:T - 1], mu, dst[0:parts, 1:T], ALU.mult, ALU.add)
        self.stt(dst[0:parts, T:NT], st, mu, dst[0:parts, T:NT], ALU.mult, ALU.add)

    def rwkv(self, l):
        A, S, C, I, O = self.A, self.S, self.C, self.I, self.O
        T, NS, NT = self.T, self.NS, self.NT
        A.release(self.abase)
        lw = A.bf(3 * BW, parts=64).rearrange("p (j n) -> p j n", j=3)
        lin = [A.bf(NT, parts=64) for _ in range(3)]
        m1 = A.mark()
        lw_st = A.f32(3 * BW, parts=64).rearrange("p (j n) -> p j n", j=3)
        S.dma(lw_st, I["rw2"][l].rearrange("j k n -> k j n"))
        self.cp("dve", lw, lw_st)
        self.chk("rw1a")
        raw = A.f32(NT, parts=64)
        xs = A.f32(NT, parts=64)
        sst = A.f32(NS, parts=64)
        for j in range(3):
            S.dma(raw, self.projT[3072 + 64 * j:3072 + 64 * (j + 1), :])
            S.dma(sst, I["s_shift"][l, 3072 + 64 * j:3072 + 64 * (j + 1), :])
            self.chk("rw1b")
            self.shiftmix(xs, raw, self.V("mul", j, 64), self.DV("omul", j, 64), sst, 64)
            self.chk("rw1c")
            if j == 1:
                self.cp("dve", lin[j], xs)
            else:
                self.act(lin[j], xs, [AF.Tanh, AF.Copy, AF.Sigmoid][j])
            self.chk("rw1d")
        A.release(m1)
        self._rw_base = A.mark()
        self.chk("rw1")
        for hp in range(8):
            self.rwkv_pair(l, hp, lw, lin)

    def rwkv_pair(self, l, hp, lw, lin):
        A, S, C, I, O, P = self.A, self.S, self.C, self.I, self.O, self.P
        T, NS, NT = self.T, self.NS, self.NT
        NCH = T // 64
        tl = ttiles(NT)
        A.release(self._rw_base)
        c0 = hp * 128
        H = [slice(0, 64), slice(64, 128)]
        bonus = A.f32(NT)
        g = A.f32(NT)
        Ofm = A.f32(NT)
        smp = A.f32(6 * NS).rearrange("p (q i) -> p q i", q=6)
        WL = A.f32(NCH)
        AR = A.bf(NCH * 192).rearrange("p (c n) -> p c n", c=NCH)
        Bb = A.bf(NCH * 128).rearrange("p (c n) -> p c n", c=NCH)
        Kb = A.bf(NCH * 128).rearrange("p (c n) -> p c n", c=NCH)
        Vb = A.bf(NCH * 128).rearrange("p (c n) -> p c n", c=NCH)
        mk = A.mark()
        t_r, t_k, t_v, xr, xk, xv, ld, a, kap, b, cc, e = [A.f32(NT) for _ in range(12)]
        sqb = A.bf(NT)
        sst = A.f32(3 * NS).rearrange("p (q i) -> p q i", q=3)
        for z in (AR, Bb, Kb, Vb):
            self.memset("pool", z, 0.0)
        for q, (tt_, xx) in enumerate([(t_r, xr), (t_k, xk), (t_v, xv)]):
            S.dma(tt_, self.projT[q * BW + c0:q * BW + c0 + 128, :])
            S.dma(sst[:, q, :], I["s_shift"][l, q * BW + c0:q * BW + c0 + 128, :])
            self.shiftmix(xx, tt_, self.V("mu", q * 8 + hp), self.DV("omu", q * 8 + hp), sst[:, q, :])
        for (t0, tn) in tl:
            ps = P.bank()
            self.mm(ps[:, 0:tn], lw[:, 0, c0:c0 + 128], lin[0][:, t0:t0 + tn])
            self.act(ld[:, t0:t0 + tn], ps[:, 0:tn], AF.Sigmoid, bias=self.V("w0", hp))
            ps = P.bank()
            self.mm(ps[:, 0:tn], lw[:, 1, c0:c0 + 128], lin[1][:, t0:t0 + tn])
            self.act(a[:, t0:t0 + tn], ps[:, 0:tn], AF.Sigmoid, bias=self.V("a0", hp))
            ps = P.bank()
            self.mm(ps[:, 0:tn], lw[:, 2, c0:c0 + 128], lin[2][:, t0:t0 + tn])
            self.cp("act", g[:, t0:t0 + tn], ps[:, 0:tn])
        self.ts("pool", ld, ld, -0.6065306597126334, None, ALU.mult)
        self.ts("dve", kap, xk, self.V("k_k", hp), None, ALU.mult)
        self.act(sqb, kap, AF.Square)
        for (t0, tn) in tl:
            ps = P.bank()
            self.mm(ps[:, 0:tn], C["bo_b"], sqb[:, t0:t0 + tn])
            self.ts("dve", e[:, t0:t0 + tn], ps[:, 0:tn], 1e-24, None, ALU.max)
        self.rsqrt(e, e)
        self.tt("dve", kap, kap, e, ALU.mult)
        self.ts("dve", e, a, self.V("k_a", hp), self.DV("omk_a", hp), ALU.mult, ALU.add)
        self.tt("dve", xk, xk, e, ALU.mult)
        self.tt("pool", b, kap, a, ALU.mult)
        self.stt(sqb, xr, self.V("r_k", hp), xk, ALU.mult, ALU.mult)
        for (t0, tn) in tl:
            ps = P.bank()
            self.mm(ps[:, 0:tn], C["bo_b"], sqb[:, t0:t0 + tn])
            self.tt("dve", bonus[:, t0:t0 + tn], ps[:, 0:tn], xv[:, t0:t0 + tn], ALU.mult)
        for q, src in enumerate([kap, b, xk, xv, xr]):
            self.cp("pool", smp[:, q, :], src[:, T:NT])
        self.act(smp[:, 5, :], ld[:, T:NT], AF.Exp)
        self.scan(cc[:, 0:T], C["rst64"], ld[:, 0:T], 0.0, ALU.mult, ALU.add)
        v3 = lambda ap: ap[:, 0:T].rearrange("p (c j) -> p c j", j=64)
        self.act(WL, v3(cc)[:, :, 63], AF.Exp)
        e1, e2, e3 = t_r, t_k, t_v
        self.act(e1[:, 0:T], cc[:, 0:T], AF.Exp)
        self.tt("dve", AR[:, :, 128:192], v3(xr), v3(e1), ALU.mult)
        self.tt("pool", e2[:, 0:T], cc[:, 0:T], ld[:, 0:T], ALU.subtract)
        self.act(e2[:, 0:T], e2[:, 0:T], AF.Exp)
        self.act(e3[:, 0:T], cc[:, 0:T], AF.Exp, scale=-1.0)
        for hh in range(2):
            hs = H[hh]
            self.stt(AR[hs, :, hh * 64:hh * 64 + 64], v3(kap)[hs], -1.0, v3(e2)[hs], ALU.mult, ALU.mult)
            self.tt("dve", Bb[hs, :, hh * 64:hh * 64 + 64], v3(b)[hs], v3(e3)[hs], ALU.mult)
            self.tt("dve", Kb[hs, :, hh * 64:hh * 64 + 64], v3(xk)[hs], v3(e3)[hs], ALU.mult)
            self.cp("pool", Vb[hs, :, hh * 64:hh * 64 + 64], v3(xv)[hs])
        self.chk("rw2")
        A.release(mk)
        Vtm = A.bf(NCH * 128).rearrange("p (c n) -> p c n", c=NCH)
        Btm = A.bf(NCH * 128).rearrange("p (c n) -> p c n", c=NCH)
        Ktm = A.bf(NCH * 128).rearrange("p (c n) -> p c n", c=NCH)
        TT = A.bf(NCH * 128).rearrange("p (c n) -> p c n", c=NCH)
        Aak = A.bf(NCH * 128).rearrange("p (c n) -> p c n", c=NCH)
        Abk = A.bf(NCH * 128).rearrange("p (c n) -> p c n", c=NCH)
        GW = 8
        PP = [[A.bf(256) for _ in range(2)] for _ in range(GW)]
        ZT = A.f32(128)
        ZTs = A.f32(128)
        ZTb = A.bf(128)
        Xsb = A.bf(128)
        Usb = A.bf(128)
        self.memset("dve", ZT, 0.0)
        self.memset("dve", ZTb, 0.0)
        ident = C["ident_b"]

        def st0(c, w):
            ps = P.bank()
            self.mm(ps[:, 0:192], Bb[:, c, :], AR[:, c, :])
            self.tt("dve", PP[w][0][:, 0:128], ps[:, 0:128], C["mk192"][:, 0:128], ALU.mult)
            self.tt("dve", Abk[:, c, 0:64], ps[:, 128:192], C["mk192"][:, 128:192], ALU.mult)
            self.chk("s0_1")
            ps2 = P.bank()
            self.mm(ps2[:, 0:192], Kb[:, c, :], AR[:, c, :])
            self.tt("dve", Aak[:, c, :], ps2[:, 0:128], C["mk192"][:, 0:128], ALU.mult)
            self.tt("dve", Abk[:, c, 64:128], ps2[:, 128:192], C["mk192"][:, 128:192], ALU.mult)
            ps3 = P.bank()
            self.mm(ps3[:, 0:128], AR[:, c, 0:128], Bb[:, c, :])
            self.tt("dve", PP[w][0][:, 128:256], ps3[:, 0:128], C["mk_lt"], ALU.mult)
            self.chk("s0_2")
            self.tt("pool", TT[:, c, :], PP[w][0][:, 0:128], ident, ALU.add)
            self.chk("s0_3")
            ps4 = P.bank()
            self.mm(ps4[:, 0:128], Vb[:, c, :], ident)
            self.mm(ps4[:, 128:256], Bb[:, c, :], ident)
            self.mm(ps4[:, 256:384], Kb[:, c, :], ident)
            self.cp("act", Vtm[:, c, :], ps4[:, 0:128])
            self.cp("act", Btm[:, c, :], ps4[:, 128:256])
            self.cp("act", Ktm[:, c, :], ps4[:, 256:384])
            self.chk("s0_4")

        def sq(c, w, j):
            src = PP[w][(j - 1) % 2]
            dst = PP[w][j % 2]
            ps = P.bank()
            self.mm(ps[:, 0:128], src[:, 128:256], src[:, 0:128])
            self.mm(ps[:, 128:256], src[:, 0:128], src[:, 128:256])
            self.cp("act", dst, ps[:, 0:256])

        def acc(c, w, j):
            dst = PP[w][j % 2]
            ps = P.bank()
            self.mm(ps[:, 0:128], dst[:, 128:256], TT[:, c, :])
            self.tt("dve", TT[:, c, :], TT[:, c, :], ps[:, 0:128], ALU.add)

        def chain(c):
            ps = P.bank()
            self.mm(ps[:, 0:128], AR[:, c, 0:128], ZTb, start=True, stop=False)
            self.mm(ps[:, 0:128], Aak[:, c, :], Vtm[:, c, :], start=False, stop=True)
            self.cp("act", Xsb, ps[:, 0:128])
            pu = P.bank()
            self.mm(pu[:, 0:128], TT[:, c, :], Xsb)
            self.cp("dve", Usb, pu[:, 0:128])
            po = P.bank()
            self.mm(po[:, 0:64], Usb, Abk[:, c, 0:64], start=True, stop=False)
            self.mm(po[:, 0:64], Vtm[:, c, :], Abk[:, c, 64:128], start=False, stop=False)
            self.mm(po[:, 0:64], ZTb, AR[:, c, 128:192], start=False, stop=True)
            self.cp("act", Ofm[:, c * 64:(c + 1) * 64], po[:, 0:64])
            pz = P.bank()
            self.mm(pz[:, 0:128], Btm[:, c, :], Usb, start=True, stop=False)
            self.mm(pz[:, 0:128], Ktm[:, c, :], Vtm[:, c, :], start=False, stop=True)
            self.ts("pool", ZTs, ZT, WL[:, c:c + 1], None, ALU.mult)
            self.stt(ZT, pz[:, 0:128], WL[:, c:c + 1], ZTs, ALU.mult, ALU.add)
            self.cp("act", ZTb, ZT)

        tm3 = A.f32(384, parts=NS)
        ps = P.bank()
        for q, idx in enumerate([1, 2, 3]):
            self.tr(ps[0:NS, q * 128:(q + 1) * 128], smp[:, idx, :], C["ident_f"])
        self.cp("act", tm3, ps[0:NS, 0:384])
        eye = C["ident_f"][0:NS, 0:NS].unsqueeze(2).to_broadcast([NS, NS, 128])
        Bex = A.f32(NS * 128, parts=NS).rearrange("p (i c) -> p i c", i=NS)
        Kex = A.f32(NS * 128, parts=NS).rearrange("p (i c) -> p i c", i=NS)
        self.tt("dve", Bex, tm3[:, 0:128].unsqueeze(1).to_broadcast([NS, NS, 128]), eye, ALU.mult)
        self.tt("dve", Kex, tm3[:, 128:256].unsqueeze(1).to_broadcast([NS, NS, 128]), eye, ALU.mult)
        vtm = tm3[:, 256:384]
        NBUF = 4
        Sbl = [A.f32(128) for _ in range(NBUF)]
        STt = [A.f32(128) for _ in range(NBUF)]
        SnT = [A.f32(128) for _ in range(NBUF)]
        Sou = [A.f32(128) for _ in range(NBUF)]
        nSK = [A.f32(128, parts=NS) for _ in range(NBUF)]
        for z in Sbl + SnT:
            self.memset("pool", z, 0.0)
        rb, pO = P.reserve()

        def samp_subs(i):
            k2 = i % NBUF

            def g1():
                for hh in range(2):
                    S.dma(Sbl[k2][H[hh], hh * 64:hh * 64 + 64], I["s_wkv"][l, i, 2 * hp + hh])
                ps_ = P.bank()
                self.tr(ps_[:, 0:128], Sbl[k2], C["ident_f"])
                self.cp("act", STt[k2], ps_[:, 0:128])

            def g2():
                p1 = P.bank()
                self.mm(p1[0:NS, 0:128], smp[:, 0, :], STt[k2])
                self.ts("dve", nSK[k2], p1[0:NS, 0:128], -1.0, None, ALU.mult)

            def g3():
                p2 = P.bank()
                self.mm(p2[:, 0:128], Bex[:, i, :], nSK[k2], start=True, stop=False)
                self.mm(p2[:, 0:128], Kex[:, i, :], vtm, start=False, stop=True)
                for hh in range(2):
                    hs = H[hh]
                    cs = slice(hh * 64, hh * 64 + 64)
                    self.stt(SnT[k2][hs, cs], STt[k2][hs, cs], smp[hs, 5, i:i + 1], p2[hs, cs], ALU.mult, ALU.add)

            def g4():
                self.mm(pO[:, i:i + 1], SnT[k2], smp[:, 4, i:i + 1])
                p3 = P.bank()
                self.tr(p3[:, 0:128], SnT[k2], C["ident_f"])
                self.cp("act", Sou[k2], p3[:, 0:128])
                for hh in range(2):
                    S.dma(O["s_wkv"][l, i, 2 * hp + hh], Sou[k2][H[hh], hh * 64:hh * 64 + 64], eng="act")
            return [g1, g2, g3, g4]

        def chain_subs(c):
            def f1():
                ps_ = P.bank()
                self.mm(ps_[:, 0:128], AR[:, c, 0:128], ZTb, start=True, stop=False)
                self.mm(ps_[:, 0:128], Aak[:, c, :], Vtm[:, c, :], start=False, stop=True)
                self.cp("act", Xsb, ps_[:, 0:128])

            def f2():
                pu = P.bank()
                self.mm(pu[:, 0:128], TT[:, c, :], Xsb)
                self.cp("dve", Usb, pu[:, 0:128])

            def f3():
                pz = P.bank()
                self.mm(pz[:, 0:128], Btm[:, c, :], Usb, start=True, stop=False)
                self.mm(pz[:, 0:128], Ktm[:, c, :], Vtm[:, c, :], start=False, stop=True)
                po = P.bank()
                self.mm(po[:, 0:64], Usb, Abk[:, c, 0:64], start=True, stop=False)
                self.mm(po[:, 0:64], Vtm[:, c, :], Abk[:, c, 64:128], start=False, stop=False)
                self.mm(po[:, 0:64], ZTb, AR[:, c, 128:192], start=False, stop=True)
                self.ts("pool", ZTs, ZT, WL[:, c:c + 1], None, ALU.mult)
                self.stt(ZT, pz[:, 0:128], WL[:, c:c + 1], ZTs, ALU.mult, ALU.add)
                self.cp("act", ZTb, ZT)
                self.cp("act", Ofm[:, c * 64:(c + 1) * 64], po[:, 0:64])
            return [f1, f2, f3]

        def bulk_ops(wave):
            ops_ = []
            for w, c in enumerate(wave):
                ops_.append((lambda c=c, w=w: st0(c, w)))
            for j in range(1, 6):
                for w, c in enumerate(wave):
                    ops_.append((lambda c=c, w=w, j=j: sq(c, w, j)))
                for w, c in enumerate(wave):
                    ops_.append((lambda c=c, w=w, j=j: acc(c, w, j)))
            return ops_

        waves = [list(range(c0_, min(NCH, c0_ + GW))) for c0_ in range(0, NCH, GW)]
        samp_all = []
        for i in range(NS):
            samp_all += samp_subs(i)
        for f in bulk_ops(waves[0]):
            f()
        sp_pos = 0
        for wi, wave in enumerate(waves):
            Al = []
            for c in wave:
                Al += chain_subs(c)
            Bl = bulk_ops(waves[wi + 1]) if wi + 1 < len(waves) else []
            nC = (len(samp_all) - sp_pos + (len(waves) - wi) - 1) // (len(waves) - wi)
            Cl = samp_all[sp_pos:sp_pos + nC]
            sp_pos += nC
            nA = len(Al)
            bi = ci = 0
            for k, f in enumerate(Al):
                f()
                tb = (len(Bl) * (k + 1) + nA - 1) // nA
                while bi < min(tb, len(Bl)):
                    Bl[bi]()
                    bi += 1
                tc = (len(Cl) * (k + 1) + nA - 1) // nA
                while ci < min(tc, len(Cl)):
                    Cl[ci]()
                    ci += 1
        self.chk("rw3")
        ps = P.bank()
        self.tr(ps[:, 0:128], ZT, C["ident_f"])
        So = A.f32(128)
        self.cp("act", So, ps[:, 0:128])
        for hh in range(2):
            S.dma(O["p_wkv"][l, 2 * hp + hh], So[H[hh], hh * 64:hh * 64 + 64], eng="act")
        self.cp("act", Ofm[:, T:NT], pO[:, 0:NS])
        P.unreserve(rb)
        self.chk("rw4")
        cen = A.f32(NT)
        rs = A.f32(NT)
        sb2 = A.bf(NT)
        obf = A.bf(NT)
        self.cp("pool", sb2, Ofm)
        for (t0, tn) in tl:
            ps = P.bank()
            self.mm(ps[:, 0:tn], C["bo_b"], sb2[:, t0:t0 + tn])
            self.stt(cen[:, t0:t0 + tn], ps[:, 0:tn], -1.0 / 64, Ofm[:, t0:t0 + tn], ALU.mult, ALU.add)
        self.act(sb2, cen, AF.Square)
        for (t0, tn) in tl:
            ps = P.bank()
            self.mm(ps[:, 0:tn], C["bo_b"], sb2[:, t0:t0 + tn])
            self.rsqrt(rs[:, t0:t0 + tn], ps[:, 0:tn], scale=1.0 / 64, bias=C["gneps"])
        self.tt("dve", cen, cen, rs, ALU.mult)
        self.ts("dve", cen, cen, self.V("ln_g", hp), self.V("ln_b", hp), ALU.mult, ALU.add)
        self.tt("pool", cen, cen, bonus, ALU.add)
        self.tt("dve", obf, cen, g, ALU.mult)
        S.dma(self.obT[c0:c0 + 128, :], obf, eng="sp")
        self.chk("rw5")

    def mlstm(self, l):
        A, S, C, I, O, P = self.A, self.S, self.C, self.I, self.O, self.P
        T, NS, NT = self.T, self.NS, self.NT
        NB = T // 128
        tl = ttiles(NT)
        A.release(self.abase)
        R4 = lambda n: A.f32(n, parts=4)
        Ac, wrow, emrow = [R4(NT) for _ in range(3)]
        dec, acs = R4(NB), R4(NS)
        acol = A.f32(NB * 4).rearrange("p (b h) -> p b h", h=4)
        ecol = A.f32(NB * 4).rearrange("p (b h) -> p b h", h=4)
        wscol = A.f32(4, parts=NS)
        mkeep = A.mark()
        ig, lf, G, aa, mt, erow = [R4(NT) for _ in range(6)]
        ones4 = R4(T)
        Ast, Aend = R4(NB), R4(NB)
        mold, wss = R4(NS), R4(NS)
        S.dma(ig, self.projT[MB + 3072:MB + 3076, :])
        S.dma(lf, self.projT[MB + 3076:MB + 3080, :])
        S.dma(mold, I["s_m"][l])
        self.memset("dve", ones4, 1.0)
        self.ts("dve", ig, ig, C["rows"][:, 0:1], None, ALU.add)
        self.act(lf, lf, AF.Sigmoid, bias=C["rows"][:, 1:2])
        self.act(lf, lf, AF.Ln)
        self.scan(G[:, 0:T], ones4, lf[:, 0:T], 0.0, ALU.mult, ALU.add)
        self.tt("dve", aa[:, 0:T], ig[:, 0:T], G[:, 0:T], ALU.subtract)
        self.scan(Ac[:, 0:T], ones4, aa[:, 0:T], 0.0, ALU.mult, ALU.max)
        self.tt("dve", mt[:, 0:T], G[:, 0:T], Ac[:, 0:T], ALU.add)
        v3 = lambda ap: ap[:, 0:T].rearrange("p (c j) -> p c j", j=128)
        self.cp("dve", Aend, v3(Ac)[:, :, 127])
        self.memset("dve", Ast[:, 0:1], 0.0)
        if NB > 1:
            self.cp("dve", Ast[:, 1:NB], Aend[:, 0:NB - 1])
        self.tt("dve", v3(wrow), Ast.unsqueeze(2).to_broadcast([4, NB, 128]), v3(Ac), ALU.subtract)
        self.act(wrow[:, 0:T], wrow[:, 0:T], AF.Exp)
        self.tt("dve", v3(erow), v3(aa), Aend.unsqueeze(2).to_broadcast([4, NB, 128]), ALU.subtract)
        self.act(erow[:, 0:T], erow[:, 0:T], AF.Exp)
        self.tt("dve", dec, Ast, Aend, ALU.subtract)
        self.act(dec, dec, AF.Exp)
        self.tt("dve", mold, lf[:, T:NT], mold, ALU.add)
        self.tt("dve", mt[:, T:NT], mold, ig[:, T:NT], ALU.max)
        self.tt("dve", acs, mold, mt[:, T:NT], ALU.subtract)
        self.act(acs, acs, AF.Exp)
        self.tt("dve", wss, ig[:, T:NT], mt[:, T:NT], ALU.subtract)
        self.act(wss, wss, AF.Exp)
        self.act(emrow, mt, AF.Exp, scale=-1.0)
        self.dmas(O["o_m"][l, :, 0:1], mt[:, T - 1:T], eng="act")
        S.dma(O["o_m"][l, :, 1:1 + NS], mt[:, T:NT], eng="act")
        ps = P.bank()
        i4 = C["ident_f"][0:4, 0:4]
        for bb in range(NB):
            self.mm(ps[:, bb * 4:bb * 4 + 4], aa[:, bb * 128:(bb + 1) * 128], i4)
            self.mm(ps[:, 256 + bb * 4:256 + bb * 4 + 4], erow[:, bb * 128:(bb + 1) * 128], i4)
        self.mm(ps[0:NS, 500:504], wss, i4)
        self.cp("act", acol, ps[:, 0:NB * 4].rearrange("p (b h) -> p b h", h=4))
        self.cp("act", ecol, ps[:, 256:256 + NB * 4].rearrange("p (b h) -> p b h", h=4))
        self.cp("act", wscol, ps[0:NS, 500:504])
        A.release(mkeep)
        self._ml_base = A.mark()
        for h in range(4):
            self.mlstm_head(l, h, dict(Ac=Ac, wrow=wrow, emrow=emrow, dec=dec, acs=acs, acol=acol, ecol=ecol,
                                       wscol=wscol))

    def bcast_rows(self, dst, rows4, h, ncols):
        c = 0
        while c < ncols:
            n = min(512, ncols - c)
            ps = self.P.bank()
            self.mm(ps[:, 0:n], self.C["sel"][:, h, :], rows4[:, c:c + n])
            self.cp("act", dst[:, c:c + n], ps[:, 0:n])
            c += n

    def conv_load(self, l, ch0, raw, cs):
        S, I = self.S, self.I
        S.dma(raw, self.projT[MB + ch0:MB + ch0 + 128, :])
        S.dma(cs, I["s_conv"][l, :, ch0:ch0 + 128, :].rearrange("g p i -> p g i"))

    def conv_silu(self, l, ch0, cidx, raw, acc, cs):
        S, I = self.S, self.I
        T, NS, NT = self.T, self.NS, self.NT
        w = lambda j: self.V("conv_w", j * 16 + cidx)
        self.ts("dve", acc, raw, w(3), self.V("conv_b", cidx), ALU.mult, ALU.add)
        for lag in (1, 2, 3):
            self.stt(acc[:, lag:T], raw[:, 0:T - lag], w(3 - lag), acc[:, lag:T], ALU.mult, ALU.add)
            self.stt(acc[:, T:NT], cs[:, 3 - lag, :], w(3 - lag), acc[:, T:NT], ALU.mult, ALU.add)

    def mlstm_head(self, l, h, R):
        A, S, C, I, O, P = self.A, self.S, self.C, self.I, self.O, self.P
        T, NS, NT = self.T, self.NS, self.NT
        NB = T // 128
        tl = ttiles(NT)
        A.release(self._ml_base)
        GBA, GBw = A.f32(T), A.f32(T)
        GBem = A.f32(NT)
        GBd = A.f32(NB)
        GBac = A.f32(NS)
        self.bcast_rows(GBA, R["Ac"], h, T)
        self.bcast_rows(GBw, R["wrow"], h, T)
        self.bcast_rows(GBem, R["emrow"], h, NT)
        self.bcast_rows(GBd, R["dec"], h, NB)
        self.bcast_rows(GBac, R["acs"], h, NS)
        Qb = [A.bf(NT) for _ in range(2)]
        Kb = [A.bf(NT) for _ in range(2)]
        Qw = [A.bf(T) for _ in range(2)]
        qs = A.f32(2 * NS).rearrange("p (k i) -> p k i", k=2)
        ks = A.f32(2 * NS).rearrange("p (k i) -> p k i", k=2)
        vs = A.f32(2 * NS).rearrange("p (k i) -> p k i", k=2)
        Vtm = A.bf(NB * 257).rearrange("p (b n) -> p b n", b=NB)
        Ktm = A.bf(NB * 256).rearrange("p (b n) -> p b n", b=NB)
        Hh = [A.f32(NT) for _ in range(2)]
        m1 = A.mark()
        acc = A.f32(NT)
        raws = [A.f32(NT) for _ in range(6)]
        css = [A.f32(3 * NS).rearrange("p (g i) -> p g i", g=3) for _ in range(4)]
        vb = A.bf(T)
        for kc in range(2):
            self.conv_load(l, h * 256 + kc * 128, raws[2 * kc], css[2 * kc])
            self.conv_load(l, BW + h * 256 + kc * 128, raws[2 * kc + 1], css[2 * kc + 1])
        for vc in range(2):
            S.dma(raws[4 + vc], self.projT[MB + 2048 + h * 256 + vc * 128:MB + 2048 + h * 256 + (vc + 1) * 128, :])
        for kc in range(2):
            raw, cs = raws[2 * kc], css[2 * kc]
            self.conv_silu(l, h * 256 + kc * 128, h * 2 + kc, raw, acc, cs)
            self.act(acc, acc, AF.Silu)
            self.cp("dve", Qb[kc], acc)
            self.cp("pool", qs[:, kc, :], acc[:, T:NT])
            self.tt("dve", Qw[kc], acc[:, 0:T], GBw, ALU.mult)
            raw, cs = raws[2 * kc + 1], css[2 * kc + 1]
            self.conv_silu(l, BW + h * 256 + kc * 128, 8 + h * 2 + kc, raw, acc, cs)
            self.act(acc, acc, AF.Silu)
            self.ts("dve", Kb[kc], acc, 0.0625, None, ALU.mult)
            self.ts("pool", ks[:, kc, :], acc[:, T:NT], 0.0625, None, ALU.mult)
            for bb in range(NB):
                ps = P.bank()
                self.mm(ps[:, 0:128], Kb[kc][:, bb * 128:(bb + 1) * 128], C["ident_b"])
                self.ts("dve", Ktm[:, bb, kc * 128:(kc + 1) * 128], ps[:, 0:128], R["ecol"][:, bb, h:h + 1], None, ALU.mult)
        self.memset("pool", Vtm[:, :, 256:257], 1.0)
        for vc in range(2):
            raw = raws[4 + vc]
            self.cp("dve", vb, raw[:, 0:T])
            self.cp("pool", vs[:, vc, :], raw[:, T:NT])
            for bb in range(NB):
                ps = P.bank()
                self.mm(ps[:, 0:128], vb[:, bb * 128:(bb + 1) * 128], C["ident_b"])
                self.cp("act", Vtm[:, bb, vc * 128:(vc + 1) * 128], ps[:, 0:128])
        A.release(m1)
        CM = A.f32(2 * 257).rearrange("p (k n) -> p k n", k=2)
        Cb = A.bf(2 * 256).rearrange("p (k n) -> p k n", k=2)
        nbc = A.bf(2 * 128).rearrange("p (k n) -> p k n", k=2)
        tmp = A.f32(128)
        ET = A.f32(128)
        PT = A.bf(128)
        absd = A.f32(128)
        self.memset("dve", CM, 0.0)
        self.memset("dve", Cb, 0.0)
        self.memset("dve", nbc, 0.0)
        for bb in range(NB):
            blk = slice(bb * 128, (bb + 1) * 128)
            ps = P.bank()
            self.mm(ps[:, 0:128], Kb[0][:, blk], Qb[0][:, blk], start=True, stop=False)
            self.mm(ps[:, 0:128], Kb[1][:, blk], Qb[1][:, blk], start=False, stop=True)
            self.stt(tmp, GBA[:, blk], -1.0, C["negm"], ALU.mult, ALU.add)
            self.act(ET, tmp, AF.Exp, bias=R["acol"][:, bb, h:h + 1])
            self.tt("dve", PT, ps[:, 0:128], ET, ALU.mult)
            pn = P.bank()
            for vc in range(2):
                o = pn[:, vc * 128:(vc + 1) * 128]
                self.mm(o, Vtm[:, bb, vc * 128:(vc + 1) * 128], PT, start=True, stop=False)
                self.mm(o, Cb[:, 0, vc * 128:(vc + 1) * 128], Qw[0][:, blk], start=False, stop=False)
                self.mm(o, Cb[:, 1, vc * 128:(vc + 1) * 128], Qw[1][:, blk], start=False, stop=True)
            pd = P.bank()
            self.mm(pd[:, 0:128], C["ones_b"], PT, start=True, stop=False)
            self.mm(pd[:, 0:128], nbc[:, 0, :], Qw[0][:, blk], start=False, stop=False)
            self.mm(pd[:, 0:128], nbc[:, 1, :], Qw[1][:, blk], start=False, stop=True)
            self.act(absd, pd[:, 0:128], AF.Abs)
            self.tt("dve", absd, absd, GBem[:, blk], ALU.max)
            self.recip(absd, absd)
            for vc in range(2):
                self.tt("dve", Hh[vc][:, blk], pn[:, vc * 128:(vc + 1) * 128], absd, ALU.mult)
            for kc in range(2):
                pc = P.bank()
                self.mm(pc[:, 0:257], Ktm[:, bb, kc * 128:(kc + 1) * 128], Vtm[:, bb, :])
                self.stt(CM[:, kc, :], CM[:, kc, :], GBd[:, bb:bb + 1], pc[:, 0:257], ALU.mult, ALU.add)
                self.cp("act", Cb[:, kc, :], CM[:, kc, 0:256])
                self.cp("pool", nbc[:, kc, :], CM[:, kc, 256:257].to_broadcast([128, 128]))
        for kc in range(2):
            S.dma(O["p_C"][l, h, kc * 128:(kc + 1) * 128, :], CM[:, kc, 0:256], eng="sp")
            self.dmas(O["p_n"][l, h, kc * 128:(kc + 1) * 128].unsqueeze(1), CM[:, kc, 256:257], eng="sp")
        m2 = A.mark()
        ktm = A.f32(256, parts=NS)
        vau = A.f32(257, parts=NS)
        ps = P.bank()
        for kc in range(2):
            self.tr(ps[0:NS, kc * 128:(kc + 1) * 128], ks[:, kc, :], C["ident_f"])
            self.tr(ps[0:NS, 256 + kc * 128:256 + (kc + 1) * 128], vs[:, kc, :], C["ident_f"])
        self.ts("dve", ktm, ps[0:NS, 0:256], R["wscol"][:, h:h + 1], None, ALU.mult)
        self.cp("act", vau[:, 0:256], ps[0:NS, 256:512])
        self.memset("dve", vau[:, 256:257], 1.0)
        Kex = A.f32(NS * 256, parts=NS).rearrange("p (i c) -> p i c", i=NS)
        eye = C["ident_f"][0:NS, 0:NS].unsqueeze(2).to_broadcast([NS, NS, 256])
        self.tt("dve", Kex, ktm.unsqueeze(1).to_broadcast([NS, NS, 256]), eye, ALU.mult)
        CS = [A.f32(2 * 257).rearrange("p (k n) -> p k n", k=2) for _ in range(4)]
        nb2 = [A.f32(2 * 128).rearrange("p (k n) -> p k n", k=2) for _ in range(4)]
        rb, pN = P.reserve()
        for i in range(NS):
            k2 = i % 4
            cs_ = CS[k2]
            for kc in range(2):
                S.dma(cs_[:, kc, 0:256], I["s_C"][l, i, h, kc * 128:(kc + 1) * 128, :])
                self.dmas(cs_[:, kc, 256:257], I["s_n"][l, i, h, kc * 128:(kc + 1) * 128].unsqueeze(1))
            for kc in range(2):
                pc = P.bank()
                self.mm(pc[:, 0:257], Kex[:, i, kc * 128:(kc + 1) * 128], vau)
                self.stt(cs_[:, kc, :], cs_[:, kc, :], GBac[:, i:i + 1], pc[:, 0:257], ALU.mult, ALU.add)
                S.dma(O["s_C"][l, i, h, kc * 128:(kc + 1) * 128, :], cs_[:, kc, 0:256], eng="sp")
                self.dmas(O["s_n"][l, i, h, kc * 128:(kc + 1) * 128].unsqueeze(1), cs_[:, kc, 256:257], eng="sp")
                self.cp("pool", nb2[k2][:, kc, :], cs_[:, kc, 256:257].to_broadcast([128, 128]))
            for vc in range(2):
                o = pN[:, vc * NS + i:vc * NS + i + 1]
                self.mm(o, cs_[:, 0, vc * 128:(vc + 1) * 128], qs[:, 0, i:i + 1], start=True, stop=False)
                self.mm(o, cs_[:, 1, vc * 128:(vc + 1) * 128], qs[:, 1, i:i + 1], start=False, stop=True)
            o = pN[:, 2 * NS + i:2 * NS + i + 1]
            self.mm(o, nb2[k2][:, 0, :], qs[:, 0, i:i + 1], start=True, stop=False)
            self.mm(o, nb2[k2][:, 1, :], qs[:, 1, i:i + 1], start=False, stop=True)
        ad = A.f32(NS)
        self.act(ad, pN[:, 2 * NS:3 * NS], AF.Abs)
        self.tt("dve", ad, ad, GBem[:, T:NT], ALU.max)
        self.recip(ad, ad)
        for vc in range(2):
            self.tt("dve", Hh[vc][:, T:NT], pN[:, vc * NS:(vc + 1) * NS], ad, ALU.mult)
        P.unreserve(rb)
        A.release(m2)
        self.head_norm_out(Hh, MB + 3080 + h * 256, "m_ng", h * 2, AF.Sigmoid, BW + h * 256)

    def head_norm_out(self, Hh, gate_row0, gname, gidx0, gfunc, ob_row0):
        A, S, C, P = self.A, self.S, self.C, self.P
        NT = self.NT
        tl = ttiles(NT)
        m = A.mark()
        sq = [A.bf(NT) for _ in range(2)]
        rs = A.f32(NT)
        graw = A.f32(NT)
        obf = A.bf(NT)
        for vc in range(2):
            self.act(sq[vc], Hh[vc], AF.Square)
        for (t0, tn) in tl:
            ps = P.bank()
            self.mm(ps[:, 0:tn], C["ones_b"], sq[0][:, t0:t0 + tn], start=True, stop=False)
            self.mm(ps[:, 0:tn], C["ones_b"], sq[1][:, t0:t0 + tn], start=False, stop=True)
            self.rsqrt(rs[:, t0:t0 + tn], ps[:, 0:tn], scale=1.0 / 256, bias=C["eps"])
        for vc in range(2):
            S.dma(graw, self.projT[gate_row0 + vc * 128:gate_row0 + (vc + 1) * 128, :])
            self.act(graw, graw, gfunc)
            self.tt("dve", Hh[vc], Hh[vc], rs, ALU.mult)
            self.stt(obf, Hh[vc], self.V(gname, gidx0 + vc), graw, ALU.mult, ALU.mult)
            S.dma(self.obT[ob_row0 + vc * 128:ob_row0 + (vc + 1) * 128, :], obf, eng="sp")
        A.release(m)

    def gla(self, l):
        A, S, C, I, O, P = self.A, self.S, self.C, self.I, self.O, self.P
        T, NS, NT = self.T, self.NS, self.NT
        NB = T // 128
        tl = ttiles(NT)
        A.release(self.abase)
        a2b = A.bf(512, parts=16)
        xab = A.bf(NT, parts=16)
        m1 = A.mark()
        a2s = A.f32(512, parts=16)
        xas = A.f32(NT, parts=16)
        S.dma(a2s, I["ga2"][l])
        S.dma(xas, self.projT[GB_ + 2048:GB_ + 2064, :])
        self.cp("dve", a2b, a2s)
        self.cp("dve", xab, xas)
        A.release(m1)
        base = A.mark()
        v3 = lambda ap: ap[:, 0:T].rearrange("p (c j) -> p c j", j=128)
        for h in range(4):
            A.release(base)
            lg, bc, e1 = A.f32(NT), A.f32(NT), A.f32(NT)
            graws = [A.f32(NT) for _ in range(4)]
            S.dma(graws[0], self.projT[GB_ + h * 128:GB_ + (h + 1) * 128, :])
            S.dma(graws[1], self.projT[GB_ + 512 + h * 128:GB_ + 512 + (h + 1) * 128, :])
            for vc in range(2):
                S.dma(graws[2 + vc],
                      self.projT[GB_ + 1024 + h * 256 + vc * 128:GB_ + 1024 + h * 256 + (vc + 1) * 128, :])
            raw = graws[0]
            Qh, Kh = A.bf(T), A.bf(T)
            WLg = A.f32(NB)
            qs, ksm, dgs = A.f32(NS), A.f32(NS), A.f32(NS)
            vs = A.f32(2 * NS).rearrange("p (k i) -> p k i", k=2)
            Vtm = A.bf(NB * 256).rearrange("p (b n) -> p b n", b=NB)
            Ktm = A.bf(NB * 128).rearrange("p (b n) -> p b n", b=NB)
            OG = [A.f32(NT) for _ in range(2)]
            vb = A.bf(T)
            for (t0, tn) in tl:
                ps = P.bank()
                self.mm(ps[:, 0:tn], a2b[:, h * 128:(h + 1) * 128], xab[:, t0:t0 + tn])
                self.act(lg[:, t0:t0 + tn], ps[:, 0:tn], AF.Sigmoid, bias=self.V("ga_b", h))
            self.act(lg, lg, AF.Ln)
            self.ts("pool", lg, lg, 1.0 / 16, None, ALU.mult)
            self.scan(bc[:, 0:T], C["rst128"], lg[:, 0:T], 0.0, ALU.mult, ALU.add)
            self.act(WLg, v3(bc)[:, :, 127], AF.Exp)
            self.act(dgs, lg[:, T:NT], AF.Exp)
            self.act(e1[:, 0:T], bc[:, 0:T], AF.Exp)
            self.stt(Qh, raw[:, 0:T], 128.0 ** -0.5, e1[:, 0:T], ALU.mult, ALU.mult)
            self.ts("pool", qs, raw[:, T:NT], 128.0 ** -0.5, None, ALU.mult)
            raw = graws[1]
            self.act(e1[:, 0:T], bc[:, 0:T], AF.Exp, scale=-1.0)
            self.tt("dve", Kh, raw[:, 0:T], e1[:, 0:T], ALU.mult)
            self.cp("pool", ksm, raw[:, T:NT])
            for bb in range(NB):
                ps = P.bank()
                self.mm(ps[:, 0:128], Kh[:, bb * 128:(bb + 1) * 128], C["ident_b"])
                self.cp("act", Ktm[:, bb, :], ps[:, 0:128])
            for vc in range(2):
                raw = graws[2 + vc]
                self.cp("dve", vb, raw[:, 0:T])
                self.cp("pool", vs[:, vc, :], raw[:, T:NT])
                for bb in range(NB):
                    ps = P.bank()
                    self.mm(ps[:, 0:128], vb[:, bb * 128:(bb + 1) * 128], C["ident_b"])
                    self.cp("act", Vtm[:, bb, vc * 128:(vc + 1) * 128], ps[:, 0:128])
            SM, SMs = A.f32(256), A.f32(256)
            Sb = A.bf(256)
            PTg = A.bf(128)
            self.memset("dve", SM, 0.0)
            self.memset("dve", Sb, 0.0)
            for bb in range(NB):
                blk = slice(bb * 128, (bb + 1) * 128)
                ps = P.bank()
                self.mm(ps[:, 0:128], Kh[:, blk], Qh[:, blk])
                self.tt("dve", PTg, ps[:, 0:128], C["m_le_b"], ALU.mult)
                po = P.bank()
                for vc in range(2):
                    o = po[:, vc * 128:(vc + 1) * 128]
                    self.mm(o, Vtm[:, bb, vc * 128:(vc + 1) * 128], PTg, start=True, stop=False)
                    self.mm(o, Sb[:, vc * 128:(vc + 1) * 128], Qh[:, blk], start=False, stop=True)
                    self.cp("act", OG[vc][:, blk], o)
                pz = P.bank()
                self.mm(pz[:, 0:256], Ktm[:, bb, :], Vtm[:, bb, :])
                self.ts("pool", SMs, SM, WLg[:, bb:bb + 1], None, ALU.mult)
                self.stt(SM, pz[:, 0:256], WLg[:, bb:bb + 1], SMs, ALU.mult, ALU.add)
                self.cp("act", Sb, SM)
            S.dma(O["p_gla"][l, h], SM, eng="sp")
            ktm = A.f32(128, parts=NS)
            vtm = A.f32(256, parts=NS)
            ps = P.bank()
            self.tr(ps[0:NS, 0:128], ksm, C["ident_f"])
            for vc in range(2):
                self.tr(ps[0:NS, 128 + vc * 128:128 + (vc + 1) * 128], vs[:, vc, :], C["ident_f"])
            self.cp("act", ktm, ps[0:NS, 0:128])
            self.cp("act", vtm, ps[0:NS, 128:384])
            Kex = A.f32(NS * 128, parts=NS).rearrange("p (i c) -> p i c", i=NS)
            eye = C["ident_f"][0:NS, 0:NS].unsqueeze(2).to_broadcast([NS, NS, 128])
            self.tt("dve", Kex, ktm.unsqueeze(1).to_broadcast([NS, NS, 128]), eye, ALU.mult)
            SS = [A.f32(256) for _ in range(4)]
            rb, pO = P.reserve()
            for i in range(NS):
                ss = SS[i % 4]
                S.dma(ss, I["s_gla"][l, i, h])
                pz = P.bank()
                self.mm(pz[:, 0:256], Kex[:, i, :], vtm)
                self.stt(ss, ss, dgs[:, i:i + 1], pz[:, 0:256], ALU.mult, ALU.add)
                S.dma(O["s_gla"][l, i, h], ss, eng="sp")
                for vc in range(2):
                    self.mm(pO[:, vc * NS + i:vc * NS + i + 1], ss[:, vc * 128:(vc + 1) * 128], qs[:, i:i + 1])
            for vc in range(2):
                self.cp("act", OG[vc][:, T:NT], pO[:, vc * NS:(vc + 1) * NS])
            P.unreserve(rb)
            self.head_norm_out(OG, GB_ + 2064 + h * 256, "g_ng", h * 2, AF.Silu, 2 * BW + h * 256)

    def merge(self, l):
        A, S, C, I, P = self.A, self.S, self.C, self.I, self.P
        NT = self.NT
        tl = ttiles(NT)
        A.release(self.abase)
        OB = A.bf(24 * NT).rearrange("p (c t) -> p c t", c=24)
        S.dma(OB, self.obT.rearrange("(c p) t -> p c t", p=128))
        wst = [A.f32(8 * 128).rearrange("p (k n) -> p k n", k=8) for _ in range(2)]
        wbf = [A.bf(8 * 128).rearrange("p (k n) -> p k n", k=8) for _ in range(2)]
        graw = [A.f32(NT) for _ in range(2)]
        accs = [A.f32(NT) for _ in range(2)]
        tmp = A.f32(512)
        mgb = [A.bf(NT) for _ in range(2)]
        groups = [(dc, j) for dc in range(16) for j in range(3)]

        def load(gi):
            dc, j = groups[gi]
            ws, wb, gr = wst[gi % 2], wbf[gi % 2], graw[gi % 2]
            S.dma(ws, I["w_branch"][l, j, :, dc * 128:(dc + 1) * 128].rearrange("(k p) n -> p k n", p=128))
            self.cp("pool", wb, ws)
            S.dma(gr, self.projT[GTB + j * D + dc * 128:GTB + j * D + (dc + 1) * 128, :])
            self.act(gr, gr, AF.Sigmoid, bias=self.V("gate_b", j * 16 + dc))

        load(0)
        for gi, (dc, j) in enumerate(groups):
            acc = accs[dc % 2]
            wb, gr = wbf[gi % 2], graw[gi % 2]
            banks = P.banks(len(tl))
            for k in range(8):
                for bi, (t0, tn) in enumerate(tl):
                    self.mm(banks[bi][:, 0:tn], wb[:, k, :], OB[:, j * 8 + k, t0:t0 + tn],
                            start=(k == 0), stop=(k == 7))
            if gi + 1 < len(groups):
                load(gi + 1)
            for bi, (t0, tn) in enumerate(tl):
                if j == 0:
                    self.tt("dve", acc[:, t0:t0 + tn], banks[bi][:, 0:tn], gr[:, t0:t0 + tn], ALU.mult)
                else:
                    self.tt("dve", tmp[:, 0:tn], banks[bi][:, 0:tn], gr[:, t0:t0 + tn], ALU.mult)
                    self.tt("pool", acc[:, t0:t0 + tn], acc[:, t0:t0 + tn], tmp[:, 0:tn], ALU.add)
            if j == 2:
                self.cp("act", mgb[dc % 2], acc)
                S.dma(self.mgT[dc * 128:(dc + 1) * 128, :], mgb[dc % 2], eng="act")

    def resid_pre(self, it, m):
        A, S = self.A, self.S
        A.release(self._dense_m0)
        stg = [A.f32(self.NT) for _ in range(2)]
        S.dma(stg[it % 2], self._res_src[m * 128:(m + 1) * 128, :])

    def resid_sink(self, it, m, rows, banks, tl):
        A, S = self.A, self.S
        A.release(self._dense_m0)
        stg = [A.f32(self.NT) for _ in range(2)]
        o = stg[it % 2]
        for bi, (t0, tn) in enumerate(tl):
            self.tt("dve", o[:, t0:t0 + tn], o[:, t0:t0 + tn], banks[bi][:, 0:tn], ALU.add)
        S.dma(self.xT[m * 128:(m + 1) * 128, :], o, eng="sp")

    def outproj(self, l):
        A, S, I = self.A, self.S, self.I
        NT = self.NT
        A.release(self.abase)
        MG = A.bf(16 * NT).rearrange("p (c t) -> p c t", c=16)
        S.dma(MG, self.mgT.rearrange("(c p) t -> p c t", p=128))
        self._res_src = self.src
        self.dense(I["w_out"][l], D, 16, MG, self.resid_sink, presink=self.resid_pre)

    def ffn(self, l):
        A, S, I = self.A, self.S, self.I
        NT = self.NT
        NF = DFF // 128
        order = []
        for fc in range(NF):
            order += [fc, NF + fc]
        self.dense(I["w_gu"][l], 2 * DFF, 16, self.hT, self.gu_sink, order=order)
        self._res_src = self.xT
        for half in range(2):
            A.release(self.abase)
            AH = A.bf((NF // 2) * NT).rearrange("p (c t) -> p c t", c=NF // 2)
            r0 = half * (DFF // 2)
            S.dma(AH, self.actT[r0:r0 + DFF // 2, :].rearrange("(c p) t -> p c t", p=128))
            self.dense(I["w_down"][l, r0:r0 + DFF // 2, :], D, NF // 2, AH, self.resid_sink, presink=self.resid_pre)

    def gu_sink(self, it, m, rows, banks, tl):
        A, S = self.A, self.S
        A.release(self._dense_m0)
        sil = A.f32(self.NT)
        ab = [A.bf(self.NT) for _ in range(2)]
        if it % 2 == 0:
            for bi, (t0, tn) in enumerate(tl):
                self.act(sil[:, t0:t0 + tn], banks[bi][:, 0:tn], AF.Silu)
        else:
            o = ab[(it // 2) % 2]
            fc = m - DFF // 128
            for bi, (t0, tn) in enumerate(tl):
                self.tt("dve", o[:, t0:t0 + tn], sil[:, t0:t0 + tn], banks[bi][:, 0:tn], ALU.mult)
            S.dma(self.actT[fc * 128:(fc + 1) * 128, :], o, eng="sp")


def _cols(v, n):
    return np.ascontiguousarray(np.asarray(v).reshape(n, 128).T)


def pack_vec(inp, l):
    out = np.zeros((128, VEC_N), np.float32)

    def put(name, arr):
        off, n = VEC_OFF[name]
        out[:arr.shape[0], off:off + n] = arr

    put("n1g", _cols(inp["norm1_g"][l], 16))
    put("n2g", _cols(inp["norm2_g"][l], 16))
    put("gate_b", _cols(inp["gate_b"][l].reshape(-1), 48))
    mu = np.asarray(inp["rwkv_mu"][l])
    put("mu", _cols(mu[:3072], 24))
    put("mul", np.ascontiguousarray(mu[3072:3264].reshape(3, 64).T))
    for nm, key in [("w0", "rwkv_w0"), ("a0", "rwkv_a0"), ("k_k", "rwkv_k_k"), ("k_a", "rwkv_k_a"),
                    ("ln_g", "rwkv_ln_g"), ("ln_b", "rwkv_ln_b"), ("m_ng", "mlstm_norm_g"), ("g_ng", "gla_norm_g")]:
        put(nm, _cols(inp[key][l], 8))
    put("r_k", _cols(np.asarray(inp["rwkv_r_k"][l]).reshape(-1), 8))
    put("conv_w", _cols(np.asarray(inp["mlstm_conv_w"][l]).reshape(-1), 64))
    put("conv_b", _cols(inp["mlstm_conv_b"][l], 16))
    put("ga_b", _cols(inp["gla_a_b"][l], 4))
    return out


_NC_CACHE = {}


def make_in_map(inp, L, xT, samp):
    f = lambda a: np.ascontiguousarray(np.asarray(a, dtype=np.float32))
    m = {}
    m["xT"] = f(xT)
    m["vec"] = np.stack([pack_vec(inp, l) for l in range(L)])
    m["rows"] = f(np.stack([np.stack([inp["mlstm_i_b"][l], inp["mlstm_f_b"][l]], axis=1) for l in range(L)]))
    m["w_in"] = f(inp["w_in"][:L])
    m["rw2"] = f(np.stack([np.stack([inp["rwkv_w2"][l], inp["rwkv_a2"][l], inp["rwkv_g2"][l]]) for l in range(L)]))
    m["ga2"] = f(inp["gla_a2"][:L])
    m["w_branch"] = f(inp["w_branch"][:L])
    m["w_out"] = f(inp["w_out"][:L])
    m["w_gu"] = f(inp["ffn_w_gu"][:L])
    m["w_down"] = f(inp["ffn_w_down"][:L])
    m["fng"] = _cols(inp["final_norm_g"], 16)
    m["s_shift"] = f(np.transpose(inp["state_rwkv_shift"][:L, samp], (0, 2, 1)))
    m["s_wkv"] = f(inp["state_rwkv_wkv"][:L, samp])
    m["s_conv"] = f(np.transpose(inp["state_mlstm_conv"][:L, samp], (0, 2, 3, 1)))
    m["s_C"] = f(inp["state_mlstm_C"][:L, samp])
    m["s_n"] = f(inp["state_mlstm_n"][:L, samp])
    m["s_m"] = f(np.transpose(inp["state_mlstm_m"][:L, samp], (0, 2, 1)))
    m["s_gla"] = f(inp["state_gla_S"][:L, samp])
    return m


def kernel(**inputs):
    inp = {k: np.asarray(v) for k, v in inputs.items()}
    B, T, _ = inp["x_prompt"].shape
    NSALL = inp["x_sample"].shape[0]
    L = inp["w_in"].shape[0]
    NS = NSALL // NCORES
    key = (T, NS, L)
    if key not in _NC_CACHE:
        _NC_CACHE[key] = Builder(T, NS, L).build()
    nc = _NC_CACHE[key]
    in_maps = []
    for c in range(NCORES):
        seq = c // 2
        samp = slice(c * NS, (c + 1) * NS)
        xT = np.concatenate([inp["x_prompt"][seq].T, inp["x_sample"][samp, 0, :].T], axis=1)
        in_maps.append(make_in_map(inp, L, xT, samp))
    res = run_bass_kernel_spmd(nc, in_maps, core_ids=list(range(NCORES)))
    R = res.results
    P = [R[2 * s] for s in range(B)]
    f32 = np.float32
    y_prompt = np.stack([P[s]["yT"][:, :T].T for s in range(B)]).astype(f32)
    y_sample = np.concatenate([R[c]["yT"][:, T:].T for c in range(NCORES)])[:, None, :].astype(f32)
    p_shift = np.stack([P[s]["o_shift"][:, :, 0] for s in range(B)], axis=1)
    s_shift = np.concatenate([np.transpose(R[c]["o_shift"][:, :, 1:], (0, 2, 1)) for c in range(NCORES)], axis=1)
    p_wkv = np.stack([P[s]["p_wkv"] for s in range(B)], axis=1)
    s_wkv = np.concatenate([R[c]["o_s_wkv"] for c in range(NCORES)], axis=1)
    p_conv = np.stack([np.transpose(P[s]["o_conv"][:, :, :, 0], (0, 1, 2)) for s in range(B)], axis=1)
    s_conv = np.concatenate([np.transpose(R[c]["o_conv"][:, :, :, 1:], (0, 3, 1, 2)) for c in range(NCORES)], axis=1)
    p_C = np.stack([P[s]["p_C"] for s in range(B)], axis=1)
    s_C = np.concatenate([R[c]["o_s_C"] for c in range(NCORES)], axis=1)
    p_n = np.stack([P[s]["p_n"] for s in range(B)], axis=1)
    s_n = np.concatenate([R[c]["o_s_n"] for c in range(NCORES)], axis=1)
    p_m = np.stack([P[s]["o_m"][:, :, 0] for s in range(B)], axis=1)
    s_m = np.concatenate([np.transpose(R[c]["o_m"][:, :, 1:], (0, 2, 1)) for c in range(NCORES)], axis=1)
    p_gla = np.stack([P[s]["p_gla"] for s in range(B)], axis=1)
    s_gla = np.concatenate([R[c]["o_s_gla"] for c in range(NCORES)], axis=1)
    outs = (y_prompt, y_sample, p_shift, p_wkv, p_conv, p_C, p_n, p_m, p_gla,
            s_shift, s_wkv, s_conv, s_C, s_n, s_m, s_gla)
    return tuple(np.ascontiguousarray(o, dtype=f32) for o in outs)
```

```python
from contextlib import ExitStack
import numpy as np
import concourse.bass as bass
import concourse.mybir as mybir
from concourse.bass_utils import run_bass_kernel_spmd

F32 = mybir.dt.float32
BF16 = mybir.dt.bfloat16
AF = mybir.ActivationFunctionType
ALU = mybir.AluOpType
AX = mybir.AxisListType
_ESZ = {F32: 4, BF16: 2, mybir.dt.int32: 4, mybir.dt.uint8: 1}

D = 2048
BW = 1024
R_COLS = 3264
M_COLS = 4104
G_COLS = 3088
N_IN = 16600
DFF = 5632
RB, MB, GB_, GTB = 0, 3264, 7368, 10456
EPS = 1e-6
R_GN_EPS = 64e-5
NCORES = 8


class _Rec:
    __slots__ = ("sp", "plo", "phi", "lo", "hi", "lw", "rde", "rdd", "cells", "dead")


class _Op:
    __slots__ = ("eng", "fn", "dma", "deps", "signal", "semval", "semidx")


def _region(ap):
    t = ap.tensor
    tn = type(t).__name__
    esz = _ESZ[ap.dtype]
    pairs = ap.ap
    if tn == "DRamTensorHandle":
        ext = 1
        for st, cnt in pairs:
            ext += (cnt - 1) * abs(st)
        lo = ap.offset * esz
        return ("d" + t.name, 0, 1, lo, lo + ext * esz)
    tshape = t.shape
    rowb = 1
    for s in tshape[1:]:
        rowb *= s
    rowb *= _ESZ[t.dtype]
    off = ap.offset * esz
    p0 = off // rowb
    c0 = off % rowb
    ext = 1
    for st, cnt in pairs[1:]:
        ext += (cnt - 1) * abs(st)
    np_ = pairs[0][1]
    if tn == "SBTensorHandle":
        return ("s", p0, p0 + np_, c0, c0 + ext * esz)
    lo = (c0 // 2048) * 2048
    hi = ((c0 + ext * esz + 2047) // 2048) * 2048
    return ("p", 0, 128, lo, hi)


class Sched:
    ENGS = ("pe", "act", "dve", "pool", "sp")

    def __init__(self, nc, n_dma_sems=64):
        self.nc = nc
        self.ops = []
        self.nds = n_dma_sems
        self.eng_obj = {"pe": nc.tensor, "act": nc.scalar, "dve": nc.vector, "pool": nc.gpsimd, "sp": nc.sync}
        self.recs = {}
        self.grid = {}
        self.cellsz = {}

    def _cells(self, key):
        sp, plo, phi, lo, hi = key
        if sp == "s":
            cs = 2048
        elif sp == "p":
            cs = 2048
        else:
            cs = 1 << 20
        return [(sp, c) for c in range(lo // cs, (hi - 1) // cs + 1)]

    def _get(self, key):
        r = self.recs.get(key)
        if r is None:
            r = _Rec()
            r.sp, r.plo, r.phi, r.lo, r.hi = key
            r.lw = -1
            r.rde = {}
            r.rdd = []
            r.cells = self._cells(key)
            r.dead = False
            self.recs[key] = r
            for c in r.cells:
                self.grid.setdefault(c, set()).add(key)
        return r

    def _overlaps(self, key):
        sp, plo, phi, lo, hi = key
        out = set()
        for c in self._cells(key):
            g = self.grid.get(c)
            if g:
                out |= g
        res = []
        for k in out:
            if k[1] < phi and plo < k[2] and k[3] < hi and lo < k[4]:
                res.append(k)
        return res

    def op(self, eng, fn, reads=(), writes=(), dma=False):
        i = len(self.ops)
        o = _Op()
        o.eng, o.fn, o.dma = eng, fn, dma
        o.signal = False
        o.semval = 0
        o.semidx = -1
        deps = set()
        rkeys = [_region(a) for a in reads]
        wkeys = [_region(a) for a in writes]
        for k in rkeys:
            for q in self._overlaps(k):
                r = self.recs[q]
                if r.lw >= 0:
                    deps.add(r.lw)
        for k in wkeys:
            for q in self._overlaps(k):
                r = self.recs[q]
                if r.lw >= 0:
                    deps.add(r.lw)
                for v in r.rde.values():
                    deps.add(v)
                for v in r.rdd:
                    deps.add(v)
        keep = []
        for d in deps:
            od = self.ops[d]
            if (not od.dma) and (not dma) and od.eng == eng and eng == "pe":
                continue
            keep.append(d)
            od.signal = True
        o.deps = keep
        self.ops.append(o)
        for k in rkeys:
            r = self._get(k)
            if dma:
                r.rdd.append(i)
            else:
                r.rde[eng] = i
        for k in wkeys:
            for q in self._overlaps(k):
                if q != k and q[1] >= k[1] and q[2] <= k[2] and q[3] >= k[3] and q[4] <= k[4]:
                    r = self.recs.pop(q)
                    for c in r.cells:
                        self.grid[c].discard(q)
            r = self._get(k)
            r.lw = i
            r.rde = {}
            r.rdd = []
        if dma:
            o.signal = True
        return o

    def dma(self, out, in_, eng="sp", **kw):
        return self.op(eng, lambda e: e.dma_start(out=out, in_=in_, **kw), [in_], [out], dma=True)

    def emit(self, stack):
        nc = self.nc
        cnt = {e: 0 for e in self.ENGS}
        dcnt = [0] * self.nds
        di = 0
        for o in self.ops:
            if not o.signal:
                continue
            if o.dma:
                o.semidx = di % self.nds
                dcnt[o.semidx] += 16
                o.semval = dcnt[o.semidx]
                di += 1
            else:
                cnt[o.eng] += 1
                o.semval = cnt[o.eng]
        sems = {e: stack.enter_context(nc.semaphore("s_" + e)) for e in self.ENGS}
        dsems = [stack.enter_context(nc.semaphore("d_%d" % i)) for i in range(self.nds)]
        waited = {e: {} for e in self.ENGS}
        nw = 0
        for o in self.ops:
            e = self.eng_obj[o.eng]
            need = {}
            for d in o.deps:
                od = self.ops[d]
                key = ("d", od.semidx) if od.dma else ("e", od.eng)
                if od.semval > need.get(key, 0):
                    need[key] = od.semval
            wd = waited[o.eng]
            if o.dma and o.semval > 16:
                k0 = ("d", o.semidx)
                if o.semval - 16 > need.get(k0, 0):
                    need[k0] = o.semval - 16
            for key, val in need.items():
                if wd.get(key, 0) >= val:
                    continue
                wd[key] = val
                e.wait_ge(dsems[key[1]] if key[0] == "d" else sems[key[1]], val)
                nw += 1
            ins = o.fn(e)
            if o.signal:
                if o.dma:
                    ins.then_inc(dsems[o.semidx], 16)
                else:
                    ins.then_inc(sems[o.eng], 1)
        for i, v in enumerate(dcnt):
            if v:
                nc.sync.wait_ge(dsems[i], v)
        return nw


class Arena:
    def __init__(self, t, ncols):
        self.t = t
        self.ncols = ncols
        self.top = 0

    def mark(self):
        return self.top

    def release(self, m):
        self.top = m

    def f32(self, cols, parts=128):
        c0 = self.top
        self.top += cols
        assert self.top <= self.ncols, ("arena overflow", self.top, self.ncols)
        return self.t[0:parts, c0:c0 + cols]

    def bf(self, cols, parts=128):
        n32 = (cols + 1) // 2
        a = self.f32(n32, parts)
        return a.bitcast(BF16)[:, 0:cols]


class Psum:
    def __init__(self, t):
        self.t = t
        self.nb = 0

    def bank(self):
        b = self.nb % 8
        self.nb += 1
        return self.t[:, b * 512:(b + 1) * 512]

    def banks(self, n):
        return [self.bank() for _ in range(n)]


def ttiles(NT):
    out = []
    c = 0
    while c < NT:
        n = min(512, NT - c)
        out.append((c, n))
        c += n
    return out


VEC_OFF = {}
_o = 0
for _n, _c in [("n1g", 16), ("n2g", 16), ("gate_b", 48), ("mu", 24), ("mul", 3), ("w0", 8), ("a0", 8), ("k_k", 8),
               ("k_a", 8), ("r_k", 8), ("ln_g", 8), ("ln_b", 8), ("conv_w", 64), ("conv_b", 16), ("m_ng", 8),
               ("g_ng", 8), ("ga_b", 4)]:
    VEC_OFF[_n] = (_o, _c)
    _o += _c
VEC_N = _o
DV_OFF = {"omu": (0, 24), "omul": (24, 3), "omk_a": (27, 8)}
DV_N = 35


class Psum2(Psum):
    def __init__(self, t):
        Psum.__init__(self, t)
        self.reserved = set()

    def bank(self):
        while True:
            b = self.nb % 8
            self.nb += 1
            if b not in self.reserved:
                return self.t[:, b * 512:(b + 1) * 512]

    def reserve(self):
        while True:
            b = self.nb % 8
            self.nb += 1
            if b not in self.reserved:
                self.reserved.add(b)
                return b, self.t[:, b * 512:(b + 1) * 512]

    def unreserve(self, b):
        self.reserved.discard(b)


class StopBuild(Exception):
    pass


class Builder:
    def chk(self, name):
        import os
        if os.environ.get("SUBSTOP") == name:
            raise StopBuild()

    def __init__(self, T, NS, depth, dbg=(), stop=None):
        self.T, self.NS, self.NT, self.depth = T, NS, T + NS, depth
        self.dbg = set(dbg)
        self.stop = stop
        assert T % 128 == 0

    def dram(self, name, shape, dt=F32, kind=None):
        if kind is None:
            kind = "ExternalOutput" if name in self.dbg else "Internal"
        return self.nc.dram_tensor(name, list(shape), dt, kind=kind).ap()

    def build(self):
        nc = bass.Bass("TRN2", target_bir_lowering=False)
        self.nc = nc
        T, NS, NT, L = self.T, self.NS, self.NT, self.depth
        inp = lambda n, s: self.dram(n, s, kind="ExternalInput")
        outp = lambda n, s: self.dram(n, s, kind="ExternalOutput")
        I = self.I = {}
        I["xT"] = inp("xT", [D, NT])
        I["vec"] = inp("vec", [L, 128, VEC_N])
        I["rows"] = inp("rows", [L, 4, 2])
        I["w_in"] = inp("w_in", [L, D, N_IN])
        I["rw2"] = inp("rw2", [L, 3, 64, BW])
        I["ga2"] = inp("ga2", [L, 16, 512])
        I["w_branch"] = inp("w_branch", [L, 3, BW, D])
        I["w_out"] = inp("w_out", [L, D, D])
        I["w_gu"] = inp("w_gu", [L, D, 2 * DFF])
        I["w_down"] = inp("w_down", [L, DFF, D])
        I["fng"] = inp("fng", [128, 16])
        I["s_shift"] = inp("s_shift", [L, R_COLS, NS])
        I["s_wkv"] = inp("s_wkv", [L, NS, 16, 64, 64])
        I["s_conv"] = inp("s_conv", [L, 3, D, NS])
        I["s_C"] = inp("s_C", [L, NS, 4, 256, 256])
        I["s_n"] = inp("s_n", [L, NS, 4, 256])
        I["s_m"] = inp("s_m", [L, 4, NS])
        I["s_gla"] = inp("s_gla", [L, NS, 4, 128, 256])
        O = self.O = {}
        O["yT"] = outp("yT", [D, NT])
        O["o_shift"] = outp("o_shift", [L, R_COLS, 1 + NS])
        O["p_wkv"] = outp("p_wkv", [L, 16, 64, 64])
        O["s_wkv"] = outp("o_s_wkv", [L, NS, 16, 64, 64])
        O["o_conv"] = outp("o_conv", [L, 3, D, 1 + NS])
        O["p_C"] = outp("p_C", [L, 4, 256, 256])
        O["p_n"] = outp("p_n", [L, 4, 256])
        O["s_C"] = outp("o_s_C", [L, NS, 4, 256, 256])
        O["s_n"] = outp("o_s_n", [L, NS, 4, 256])
        O["o_m"] = outp("o_m", [L, 4, 1 + NS])
        O["p_gla"] = outp("p_gla", [L, 4, 128, 256])
        O["s_gla"] = outp("o_s_gla", [L, NS, 4, 128, 256])
        self.xT = self.dram("x_scr", [D, NT])
        self.projT = self.dram("projT", [N_IN, NT])
        self.obT = self.dram("obT", [3 * BW, NT], BF16)
        self.mgT = self.dram("mgT", [D, NT], BF16)
        self.actT = self.dram("actT", [DFF, NT], BF16)

        with ExitStack() as st:
            ACOLS = 52800
            at = st.enter_context(nc.sbuf_tensor("arena", [128, ACOLS], F32))
            pt = st.enter_context(nc.psum_tensor("psum", [128, 4096], F32))
            self.A = Arena(at, ACOLS)
            self.P = Psum2(pt)
            self.S = Sched(nc)
            self.consts()
            done = True
            try:
                for l in range(L):
                    if not self.layer(l):
                        done = False
                        break
            except StopBuild:
                done = False
            if done:
                self.S.dma(self.C["vec"][:, 0:16], I["fng"])
                self.rmsnorm(self.xT, "n1g", out_dram=O["yT"])
            self.nwaits = self.S.emit(st)
        return nc

    def act(self, out, in_, func, bias=None, scale=1.0):
        kw = {}
        rd = [in_]
        if bias is not None:
            kw["bias"] = bias
            if not isinstance(bias, (int, float)):
                rd.append(bias)
        if not isinstance(scale, (int, float)):
            rd.append(scale)
        self.S.op("act", lambda e: e.activation(out=out, in_=in_, func=func, scale=scale, **kw), rd, [out])

    def ts(self, eng, out, in0, s1, s2, op0, op1=None):
        rd = [in0] + [s for s in (s1, s2) if s is not None and not isinstance(s, (int, float))]
        if op1 is None:
            self.S.op(eng, lambda e: e.tensor_scalar(out=out, in0=in0, scalar1=s1, scalar2=None, op0=op0), rd, [out])
        else:
            self.S.op(eng, lambda e: e.tensor_scalar(out=out, in0=in0, scalar1=s1, scalar2=s2, op0=op0, op1=op1),
                      rd, [out])

    def stt(self, out, in0, sc, in1, op0, op1):
        rd = [in0, in1] + ([] if isinstance(sc, (int, float)) else [sc])
        self.S.op("dve", lambda e: e.scalar_tensor_tensor(out=out, in0=in0, scalar=sc, in1=in1, op0=op0, op1=op1),
                  rd, [out])

    def tt(self, eng, out, in0, in1, op):
        self.S.op(eng, lambda e: e.tensor_tensor(out=out, in0=in0, in1=in1, op=op), [in0, in1], [out])

    def cp(self, eng, out, in_):
        if eng == "act":
            self.S.op("act", lambda e: e.copy(out=out, in_=in_), [in_], [out])
        else:
            self.S.op(eng, lambda e: e.tensor_copy(out=out, in_=in_), [in_], [out])

    def mm(self, out, lhsT, rhs, start=True, stop=True):
        self.S.op("pe", lambda e: e.matmul(out, lhsT=lhsT, rhs=rhs, start=start, stop=stop), [lhsT, rhs], [out])

    def tr(self, out, in_, ident):
        self.S.op("pe", lambda e: e.transpose(out, in_, ident), [in_, ident], [out])

    def memset(self, eng, ap, v):
        self.S.op(eng, lambda e: e.memset(ap, v), [], [ap])

    def rsqrt(self, out, in_, scale=1.0, bias=None):
        self.act(out, in_, AF.Ln, bias=bias, scale=scale)
        self.act(out, out, AF.Exp, scale=-0.5)

    def recip(self, out, in_):
        self.S.op("dve", lambda e: e.reciprocal(out=out, in_=in_), [in_], [out])

    def scan(self, out, d0, d1, init, op0, op1):
        self.S.op("dve", lambda e: e.tensor_tensor_scan(out=out, data0=d0, data1=d1, initial=init, op0=op0, op1=op1),
                  [d0, d1], [out])

    def dmas(self, out, in_, eng="sp"):
        self.S.op(eng, lambda e: e.dma_start(out=out, in_=in_, allow_slow_non_contiguous=True), [in_], [out], dma=True)

    def V(self, name, j=0, parts=128):
        off, n = VEC_OFF[name]
        return self.C["vec"][0:parts, off + j:off + j + 1]

    def DV(self, name, j=0, parts=128):
        off, n = DV_OFF[name]
        return self.C["dvec"][0:parts, off + j:off + j + 1]

    def consts(self):
        A, S, T = self.A, self.S, self.T
        C = self.C = {}
        io = A.f32(128)
        S.op("pool", lambda e: e.iota(io, [[1, 128]], base=0, channel_multiplier=-1,
                                      allow_small_or_imprecise_dtypes=True), [], [io])
        C["ident_f"] = A.f32(128)
        self.ts("dve", C["ident_f"], io, 0.0, None, ALU.is_equal)
        C["ident_b"] = A.bf(128)
        self.ts("dve", C["ident_b"], io, 0.0, None, ALU.is_equal)
        C["ones_b"] = A.bf(128)
        self.memset("dve", C["ones_b"], 1.0)
        C["m_le_b"] = A.bf(128)
        self.ts("dve", C["m_le_b"], io, 0.0, None, ALU.is_ge)
        C["negm"] = A.f32(128)
        self.ts("dve", C["negm"], io, 0.0, -30000.0, ALU.is_lt, ALU.mult)
        C["bo_b"] = A.bf(128)
        self.memset("dve", C["bo_b"], 0.0)
        self.memset("dve", C["bo_b"][0:64, 0:64], 1.0)
        self.memset("dve", C["bo_b"][64:128, 64:128], 1.0)
        io2 = A.f32(64)
        S.op("pool", lambda e: e.iota(io2[0:64, :], [[1, 64]], base=0, channel_multiplier=-1,
                                      allow_small_or_imprecise_dtypes=True), [], [io2[0:64, :]])
        S.op("pool", lambda e: e.iota(io2[64:128, :], [[1, 64]], base=0, channel_multiplier=-1,
                                      allow_small_or_imprecise_dtypes=True), [], [io2[64:128, :]])
        C["mk192"] = A.bf(192)
        self.ts("dve", C["mk192"][:, 0:64], io2, 0.0, None, ALU.is_gt)
        self.ts("dve", C["mk192"][:, 64:128], io2, 0.0, None, ALU.is_gt)
        self.ts("dve", C["mk192"][:, 128:192], io2, 0.0, None, ALU.is_ge)
        C["mk_lt"] = A.bf(128)
        self.ts("dve", C["mk_lt"][:, 0:64], io2, 0.0, None, ALU.is_lt)
        self.ts("dve", C["mk_lt"][:, 64:128], io2, 0.0, None, ALU.is_lt)
        C["rst64"] = A.bf(T)
        self.memset("pool", C["rst64"], 1.0)
        self.memset("pool", C["rst64"].rearrange("p (c j) -> p c j", j=64)[:, :, 0:1], 0.0)
        C["rst128"] = A.bf(T)
        self.memset("pool", C["rst128"], 1.0)
        self.memset("pool", C["rst128"].rearrange("p (c j) -> p c j", j=128)[:, :, 0:1], 0.0)
        C["sel"] = A.f32(4 * 128, parts=4).rearrange("p (h m) -> p h m", h=4)
        for h in range(4):
            self.cp("dve", C["sel"][:, h, :], C["ident_f"][0:4, h:h + 1].to_broadcast([4, 128]))
        C["vec"] = A.f32(VEC_N)
        C["dvec"] = A.f32(DV_N)
        C["rows"] = A.f32(2, parts=4)
        C["eps"] = A.f32(1)
        self.memset("dve", C["eps"], EPS)
        C["gneps"] = A.f32(1)
        self.memset("dve", C["gneps"], R_GN_EPS)
        self.abase = A.mark()

    def layer(self, l):
        S, I, O, C = self.S, self.I, self.O, self.C
        T, NS, NT = self.T, self.NS, self.NT
        S.dma(C["vec"], I["vec"][l])
        S.dma(C["rows"], I["rows"][l])
        mo, mn = VEC_OFF["mu"]
        self.ts("dve", C["dvec"][:, 0:27], C["vec"][:, mo:mo + 27], -1.0, 1.0, ALU.mult, ALU.add)
        ko, kn = VEC_OFF["k_a"]
        self.ts("dve", C["dvec"][:, 27:35], C["vec"][:, ko:ko + 8], -1.0, 1.0, ALU.mult, ALU.add)
        src = I["xT"] if l == 0 else self.xT
        self.src = src
        self.rmsnorm(src, "n1g")
        self.dense(I["w_in"][l], N_IN, 16, self.hT, self.proj_sink)
        if self.stop == (l, "proj"):
            return False
        for r0 in range(0, R_COLS, 816):
            self.dmas(O["o_shift"][l, r0:r0 + 816, 0:1], self.projT[r0:r0 + 816, T - 1:T])
            self.dmas(O["o_shift"][l, r0:r0 + 816, 1:1 + NS], self.projT[r0:r0 + 816, T:NT])
        for r0 in range(0, D, 1024):
            for lag in range(3):
                self.dmas(O["o_conv"][l, lag, r0:r0 + 1024, 0:1],
                          self.projT[MB + r0:MB + r0 + 1024, T - 3 + lag:T - 2 + lag])
            self.dmas(O["o_conv"][l, 0, r0:r0 + 1024, 1:1 + NS], I["s_conv"][l, 1, r0:r0 + 1024, :])
            self.dmas(O["o_conv"][l, 1, r0:r0 + 1024, 1:1 + NS], I["s_conv"][l, 2, r0:r0 + 1024, :])
            self.dmas(O["o_conv"][l, 2, r0:r0 + 1024, 1:1 + NS], self.projT[MB + r0:MB + r0 + 1024, T:NT])
        self.chk("rw0")
        self.rwkv(l)
        if self.stop == (l, "rwkv"):
            return False
        self.mlstm(l)
        if self.stop == (l, "mlstm"):
            return False
        self.gla(l)
        if self.stop == (l, "gla"):
            return False
        self.merge(l)
        self.outproj(l)
        if self.stop == (l, "attn"):
            return False
        self.rmsnorm(self.xT, "n2g")
        self.ffn(l)
        return True

    def rmsnorm(self, src, gname, out_dram=None):
        A, S, C = self.A, self.S, self.C
        NT = self.NT
        A.release(self.abase)
        if out_dram is None:
            self.hT = A.bf(16 * NT).rearrange("p (c t) -> p c t", c=16)
        m0 = A.mark()
        srcv = src.rearrange("(c p) t -> p c t", p=128)
        tls = ttiles(NT)
        xts = [A.f32(16 * 512).rearrange("p (c t) -> p c t", c=16) for _ in range(2)]
        sqs = [A.bf(16 * 512).rearrange("p (c t) -> p c t", c=16) for _ in range(2)]
        rss = [A.f32(512) for _ in range(2)]
        S.dma(xts[0][:, :, 0:tls[0][1]], srcv[:, :, tls[0][0]:tls[0][0] + tls[0][1]])
        for ti, (t0, tn) in enumerate(tls):
            xt = xts[ti % 2][:, :, 0:tn]
            sq = sqs[ti % 2][:, :, 0:tn]
            rs = rss[ti % 2][:, 0:tn]
            if ti + 1 < len(tls):
                n0, nn = tls[ti + 1]
                S.dma(xts[(ti + 1) % 2][:, :, 0:nn], srcv[:, :, n0:n0 + nn])
            ps = self.P.bank()[:, 0:tn]
            for c in range(16):
                self.act(sq[:, c, :], xt[:, c, :], AF.Square)
                self.mm(ps, C["ones_b"], sq[:, c, :], start=(c == 0), stop=(c == 15))
            self.rsqrt(rs, ps, scale=1.0 / D, bias=C["eps"])
            for c in range(16):
                dst = self.hT[:, c, t0:t0 + tn] if out_dram is None else xt[:, c, :]
                self.stt(dst, xt[:, c, :], self.V(gname, c), rs, ALU.mult, ALU.mult)
            if out_dram is not None:
                S.dma(out_dram.rearrange("(c p) t -> p c t", p=128)[:, :, t0:t0 + tn], xt, eng="sp")
        A.release(m0)

    def dense(self, W, N, KC, rhs, sink, order=None, presink=None):
        A, S = self.A, self.S
        NT = self.NT
        m0 = A.mark()
        nm = (N + 127) // 128
        Wv = W.rearrange("(kc p) n -> p kc n", p=128)
        NWB = 3
        wst = [A.f32(KC * 128).rearrange("p (k n) -> p k n", k=KC) for _ in range(NWB)]
        wbf = [A.bf(KC * 128).rearrange("p (k n) -> p k n", k=KC) for _ in range(NWB)]
        self._dense_m0 = A.mark()
        tl = ttiles(NT)
        order = order if order is not None else list(range(nm))
        n = len(order)

        def load(it):
            m = order[it]
            rows = min(128, N - m * 128)
            ws, wb = wst[it % NWB], wbf[it % NWB]
            S.dma(ws[:, :, 0:rows], Wv[:, :, m * 128:m * 128 + rows])
            self.cp("pool" if it % 2 else "dve", wb[:, :, 0:rows], ws[:, :, 0:rows])

        for it in range(min(2, n)):
            load(it)
        for it, m in enumerate(order):
            rows = min(128, N - m * 128)
            wb = wbf[it % NWB]
            if it + 2 < n:
                load(it + 2)
            if presink is not None:
                presink(it, m)
            banks = self.P.banks(len(tl))
            for k in range(KC):
                for bi, (t0, tn) in enumerate(tl):
                    self.mm(banks[bi][0:rows, 0:tn], wb[:, k, 0:rows], rhs[:, k, t0:t0 + tn],
                            start=(k == 0), stop=(k == KC - 1))
            sink(it, m, rows, banks, tl)
        A.release(m0)

    def proj_sink(self, it, m, rows, banks, tl):
        A, S = self.A, self.S
        A.release(self._dense_m0)
        stg = [A.f32(self.NT) for _ in range(2)]
        o = stg[it % 2]
        for bi, (t0, tn) in enumerate(tl):
            self.cp("act", o[0:rows, t0:t0 + tn], banks[bi][0:rows, 0:tn])
        S.dma(self.projT[m * 128:m * 128 + rows, :], o[0:rows, :], eng="act")

    def shiftmix(self, dst, p, mu, omu, st, parts=128):
        T, NT = self.T, self.NT
        self.ts("dve", dst[0:parts, :], p[0:parts, :], omu, None, ALU.mult)
        self.stt(dst[0:parts, 1:T], p[0:parts, 0:T - 1], mu, dst[0:parts, 1:T], ALU.mult, ALU.add)
        self.stt(dst[0:parts, T:NT], st, mu, dst[0:parts, T:NT], ALU.mult, ALU.add)

    def rwkv(self, l):
        A, S, C, I, O = self.A, self.S, self.C, self.I, self.O
        T, NS, NT = self.T, self.NS, self.NT
        A.release(self.abase)
        lw = A.bf(3 * BW, parts=64).rearrange("p (j n) -> p j n", j=3)
        lin = [A.bf(NT, parts=64) for _ in range(3)]
        m1 = A.mark()
        lw_st = A.f32(3 * BW, parts=64).rearrange("p (j n) -> p j n", j=3)
        S.dma(lw_st, I["rw2"][l].rearrange("j k n -> k j n"))
        self.cp("dve", lw, lw_st)
        self.chk("rw1a")
        raw = A.f32(NT, parts=64)
        xs = A.f32(NT, parts=64)
        sst = A.f32(NS, parts=64)
        for j in range(3):
            S.dma(raw, self.projT[3072 + 64 * j:3072 + 64 * (j + 1), :])
            S.dma(sst, I["s_shift"][l, 3072 + 64 * j:3072 + 64 * (j + 1), :])
            self.chk("rw1b")
            self.shiftmix(xs, raw, self.V("mul", j, 64), self.DV("omul", j, 64), sst, 64)
            self.chk("rw1c")
            if j == 1:
                self.cp("dve", lin[j], xs)
            else:
                self.act(lin[j], xs, [AF.Tanh, AF.Copy, AF.Sigmoid][j])
            self.chk("rw1d")
        A.release(m1)
        self._rw_base = A.mark()
        self.chk("rw1")
        for hp in range(8):
            self.rwkv_pair(l, hp, lw, lin)

    def rwkv_pair(self, l, hp, lw, lin):
        A, S, C, I, O, P = self.A, self.S, self.C, self.I, self.O, self.P
        T, NS, NT = self.T, self.NS, self.NT
        NCH = T // 64
        tl = ttiles(NT)
        A.release(self._rw_base)
        c0 = hp * 128
        H = [slice(0, 64), slice(64, 128)]
        bonus = A.f32(NT)
        g = A.f32(NT)
        Ofm = A.f32(NT)
        smp = A.f32(6 * NS).rearrange("p (q i) -> p q i", q=6)
        WL = A.f32(NCH)
        AR = A.bf(NCH * 192).rearrange("p (c n) -> p c n", c=NCH)
        Bb = A.bf(NCH * 128).rearrange("p (c n) -> p c n", c=NCH)
        Kb = A.bf(NCH * 128).rearrange("p (c n) -> p c n", c=NCH)
        Vb = A.bf(NCH * 128).rearrange("p (c n) -> p c n", c=NCH)
        mk = A.mark()
        t_r, t_k, t_v, xr, xk, xv, ld, a, kap, b, cc, e = [A.f32(NT) for _ in range(12)]
        sqb = A.bf(NT)
        sst = A.f32(3 * NS).rearrange("p (q i) -> p q i", q=3)
        for z in (AR, Bb, Kb, Vb):
            self.memset("pool", z, 0.0)
        pref = getattr(self, "_rw_pref", None) == (l, hp)
        self._rw_tiles = (t_r, t_k, t_v)
        for q, (tt_, xx) in enumerate([(t_r, xr), (t_k, xk), (t_v, xv)]):
            if not pref:
                S.dma(tt_, self.projT[q * BW + c0:q * BW + c0 + 128, :])
            S.dma(sst[:, q, :], I["s_shift"][l, q * BW + c0:q * BW + c0 + 128, :])
            self.shiftmix(xx, tt_, self.V("mu", q * 8 + hp), self.DV("omu", q * 8 + hp), sst[:, q, :])
        for (t0, tn) in tl:
            ps = P.bank()
            self.mm(ps[:, 0:tn], lw[:, 0, c0:c0 + 128], lin[0][:, t0:t0 + tn])
            self.act(ld[:, t0:t0 + tn], ps[:, 0:tn], AF.Sigmoid, bias=self.V("w0", hp))
            ps = P.bank()
            self.mm(ps[:, 0:tn], lw[:, 1, c0:c0 + 128], lin[1][:, t0:t0 + tn])
            self.act(a[:, t0:t0 + tn], ps[:, 0:tn], AF.Sigmoid, bias=self.V("a0", hp))
            ps = P.bank()
            self.mm(ps[:, 0:tn], lw[:, 2, c0:c0 + 128], lin[2][:, t0:t0 + tn])
            self.cp("act", g[:, t0:t0 + tn], ps[:, 0:tn])
        self.ts("pool", ld, ld, -0.6065306597126334, None, ALU.mult)
        self.ts("dve", kap, xk, self.V("k_k", hp), None, ALU.mult)
        self.act(sqb, kap, AF.Square)
        for (t0, tn) in tl:
            ps = P.bank()
            self.mm(ps[:, 0:tn], C["bo_b"], sqb[:, t0:t0 + tn])
            self.ts("dve", e[:, t0:t0 + tn], ps[:, 0:tn], 1e-24, None, ALU.max)
        self.rsqrt(e, e)
        self.tt("dve", kap, kap, e, ALU.mult)
        self.ts("dve", e, a, self.V("k_a", hp), self.DV("omk_a", hp), ALU.mult, ALU.add)
        self.tt("dve", xk, xk, e, ALU.mult)
        self.tt("pool", b, kap, a, ALU.mult)
        self.stt(sqb, xr, self.V("r_k", hp), xk, ALU.mult, ALU.mult)
        for (t0, tn) in tl:
            ps = P.bank()
            self.mm(ps[:, 0:tn], C["bo_b"], sqb[:, t0:t0 + tn])
            self.tt("dve", bonus[:, t0:t0 + tn], ps[:, 0:tn], xv[:, t0:t0 + tn], ALU.mult)
        for q, src in enumerate([kap, b, xk, xv, xr]):
            self.cp("pool", smp[:, q, :], src[:, T:NT])
        self.act(smp[:, 5, :], ld[:, T:NT], AF.Exp)
        self.scan(cc[:, 0:T], C["rst64"], ld[:, 0:T], 0.0, ALU.mult, ALU.add)
        v3 = lambda ap: ap[:, 0:T].rearrange("p (c j) -> p c j", j=64)
        self.act(WL, v3(cc)[:, :, 63], AF.Exp)
        e1, e2, e3 = t_r, t_k, t_v
        self.act(e1[:, 0:T], cc[:, 0:T], AF.Exp)
        self.tt("dve", AR[:, :, 128:192], v3(xr), v3(e1), ALU.mult)
        self.tt("pool", e2[:, 0:T], cc[:, 0:T], ld[:, 0:T], ALU.subtract)
        self.act(e2[:, 0:T], e2[:, 0:T], AF.Exp)
        self.act(e3[:, 0:T], cc[:, 0:T], AF.Exp, scale=-1.0)
        for hh in range(2):
            hs = H[hh]
            self.stt(AR[hs, :, hh * 64:hh * 64 + 64], v3(kap)[hs], -1.0, v3(e2)[hs], ALU.mult, ALU.mult)
            self.tt("dve", Bb[hs, :, hh * 64:hh * 64 + 64], v3(b)[hs], v3(e3)[hs], ALU.mult)
            self.tt("dve", Kb[hs, :, hh * 64:hh * 64 + 64], v3(xk)[hs], v3(e3)[hs], ALU.mult)
            self.cp("pool", Vb[hs, :, hh * 64:hh * 64 + 64], v3(xv)[hs])
        self.chk("rw2")
        A.release(mk)
        Vtm = A.bf(NCH * 128).rearrange("p (c n) -> p c n", c=NCH)
        Btm = A.bf(NCH * 128).rearrange("p (c n) -> p c n", c=NCH)
        Ktm = A.bf(NCH * 128).rearrange("p (c n) -> p c n", c=NCH)
        TT = A.bf(NCH * 128).rearrange("p (c n) -> p c n", c=NCH)
        Aak = A.bf(NCH * 128).rearrange("p (c n) -> p c n", c=NCH)
        Abk = A.bf(NCH * 128).rearrange("p (c n) -> p c n", c=NCH)
        GW = 8
        PP = [[A.bf(256) for _ in range(2)] for _ in range(GW)]
        ZT = A.f32(128)
        ZTs = A.f32(128)
        ZTb = A.bf(128)
        Xsb = A.bf(128)
        Usb = A.bf(128)
        self.memset("dve", ZT, 0.0)
        self.memset("dve", ZTb, 0.0)
        ident = C["ident_b"]

        def st0(c, w):
            ps = P.bank()
            self.mm(ps[:, 0:192], Bb[:, c, :], AR[:, c, :])
            self.tt("dve", PP[w][0][:, 0:128], ps[:, 0:128], C["mk192"][:, 0:128], ALU.mult)
            self.tt("dve", Abk[:, c, 0:64], ps[:, 128:192], C["mk192"][:, 128:192], ALU.mult)
            self.chk("s0_1")
            ps2 = P.bank()
            self.mm(ps2[:, 0:192], Kb[:, c, :], AR[:, c, :])
            self.tt("dve", Aak[:, c, :], ps2[:, 0:128], C["mk192"][:, 0:128], ALU.mult)
            self.tt("dve", Abk[:, c, 64:128], ps2[:, 128:192], C["mk192"][:, 128:192], ALU.mult)
            ps3 = P.bank()
            self.mm(ps3[:, 0:128], AR[:, c, 0:128], Bb[:, c, :])
            self.tt("dve", PP[w][0][:, 128:256], ps3[:, 0:128], C["mk_lt"], ALU.mult)
            self.chk("s0_2")
            self.tt("pool", TT[:, c, :], PP[w][0][:, 0:128], ident, ALU.add)
            self.chk("s0_3")
            ps4 = P.bank()
            self.mm(ps4[:, 0:128], Vb[:, c, :], ident)
            self.mm(ps4[:, 128:256], Bb[:, c, :], ident)
            self.mm(ps4[:, 256:384], Kb[:, c, :], ident)
            self.cp("act", Vtm[:, c, :], ps4[:, 0:128])
            self.cp("act", Btm[:, c, :], ps4[:, 128:256])
            self.cp("act", Ktm[:, c, :], ps4[:, 256:384])
            self.chk("s0_4")

        def sq(c, w, j):
            src = PP[w][(j - 1) % 2]
            dst = PP[w][j % 2]
            ps = P.bank()
            self.mm(ps[:, 0:128], src[:, 128:256], src[:, 0:128])
            self.mm(ps[:, 128:256], src[:, 0:128], src[:, 128:256])
            self.cp("act", dst, ps[:, 0:256])

        def acc(c, w, j):
            dst = PP[w][j % 2]
            ps = P.bank()
            self.mm(ps[:, 0:128], dst[:, 128:256], TT[:, c, :])
            self.tt("dve", TT[:, c, :], TT[:, c, :], ps[:, 0:128], ALU.add)

        def chain(c):
            ps = P.bank()
            self.mm(ps[:, 0:128], AR[:, c, 0:128], ZTb, start=True, stop=False)
            self.mm(ps[:, 0:128], Aak[:, c, :], Vtm[:, c, :], start=False, stop=True)
            self.cp("act", Xsb, ps[:, 0:128])
            pu = P.bank()
            self.mm(pu[:, 0:128], TT[:, c, :], Xsb)
            self.cp("dve", Usb, pu[:, 0:128])
            po = P.bank()
            self.mm(po[:, 0:64], Usb, Abk[:, c, 0:64], start=True, stop=False)
            self.mm(po[:, 0:64], Vtm[:, c, :], Abk[:, c, 64:128], start=False, stop=False)
            self.mm(po[:, 0:64], ZTb, AR[:, c, 128:192], start=False, stop=True)
            self.cp("act", Ofm[:, c * 64:(c + 1) * 64], po[:, 0:64])
            pz = P.bank()
            self.mm(pz[:, 0:128], Btm[:, c, :], Usb, start=True, stop=False)
            self.mm(pz[:, 0:128], Ktm[:, c, :], Vtm[:, c, :], start=False, stop=True)
            self.ts("pool", ZTs, ZT, WL[:, c:c + 1], None, ALU.mult)
            self.stt(ZT, pz[:, 0:128], WL[:, c:c + 1], ZTs, ALU.mult, ALU.add)
            self.cp("act", ZTb, ZT)

        tm3 = A.f32(384, parts=NS)
        ps = P.bank()
        for q, idx in enumerate([1, 2, 3]):
            self.tr(ps[0:NS, q * 128:(q + 1) * 128], smp[:, idx, :], C["ident_f"])
        self.cp("act", tm3, ps[0:NS, 0:384])
        eye = C["ident_f"][0:NS, 0:NS].unsqueeze(2).to_broadcast([NS, NS, 128])
        Bex = A.f32(NS * 128, parts=NS).rearrange("p (i c) -> p i c", i=NS)
        Kex = A.f32(NS * 128, parts=NS).rearrange("p (i c) -> p i c", i=NS)
        self.tt("dve", Bex, tm3[:, 0:128].unsqueeze(1).to_broadcast([NS, NS, 128]), eye, ALU.mult)
        self.tt("dve", Kex, tm3[:, 128:256].unsqueeze(1).to_broadcast([NS, NS, 128]), eye, ALU.mult)
        vtm = tm3[:, 256:384]
        NBUF = 4
        Sbl = [A.f32(128) for _ in range(NBUF)]
        STt = [A.f32(128) for _ in range(NBUF)]
        SnT = [A.f32(128) for _ in range(NBUF)]
        Sou = [A.f32(128) for _ in range(NBUF)]
        nSK = [A.f32(128, parts=NS) for _ in range(NBUF)]
        for z in Sbl + SnT:
            self.memset("pool", z, 0.0)
        rb, pO = P.reserve()

        def samp_subs(i):
            k2 = i % NBUF

            def g1():
                for hh in range(2):
                    S.dma(Sbl[k2][H[hh], hh * 64:hh * 64 + 64], I["s_wkv"][l, i, 2 * hp + hh])
                ps_ = P.bank()
                self.tr(ps_[:, 0:128], Sbl[k2], C["ident_f"])
                self.cp("act", STt[k2], ps_[:, 0:128])

            def g2():
                p1 = P.bank()
                self.mm(p1[0:NS, 0:128], smp[:, 0, :], STt[k2])
                self.ts("dve", nSK[k2], p1[0:NS, 0:128], -1.0, None, ALU.mult)

            def g3():
                p2 = P.bank()
                self.mm(p2[:, 0:128], Bex[:, i, :], nSK[k2], start=True, stop=False)
                self.mm(p2[:, 0:128], Kex[:, i, :], vtm, start=False, stop=True)
                for hh in range(2):
                    hs = H[hh]
                    cs = slice(hh * 64, hh * 64 + 64)
                    self.stt(SnT[k2][hs, cs], STt[k2][hs, cs], smp[hs, 5, i:i + 1], p2[hs, cs], ALU.mult, ALU.add)

            def g4():
                self.mm(pO[:, i:i + 1], SnT[k2], smp[:, 4, i:i + 1])
                p3 = P.bank()
                self.tr(p3[:, 0:128], SnT[k2], C["ident_f"])
                self.cp("act", Sou[k2], p3[:, 0:128])
                for hh in range(2):
                    S.dma(O["s_wkv"][l, i, 2 * hp + hh], Sou[k2][H[hh], hh * 64:hh * 64 + 64], eng="act")
            return [g1, g2, g3, g4]

        def chain_subs(c):
            def f1():
                ps_ = P.bank()
                self.mm(ps_[:, 0:128], AR[:, c, 0:128], ZTb, start=True, stop=False)
                self.mm(ps_[:, 0:128], Aak[:, c, :], Vtm[:, c, :], start=False, stop=True)
                self.cp("act", Xsb, ps_[:, 0:128])

            def f2():
                pu = P.bank()
                self.mm(pu[:, 0:128], TT[:, c, :], Xsb)
                self.cp("dve", Usb, pu[:, 0:128])

            def f3():
                pz = P.bank()
                self.mm(pz[:, 0:128], Btm[:, c, :], Usb, start=True, stop=False)
                self.mm(pz[:, 0:128], Ktm[:, c, :], Vtm[:, c, :], start=False, stop=True)
                po = P.bank()
                self.mm(po[:, 0:64], Usb, Abk[:, c, 0:64], start=True, stop=False)
                self.mm(po[:, 0:64], Vtm[:, c, :], Abk[:, c, 64:128], start=False, stop=False)
                self.mm(po[:, 0:64], ZTb, AR[:, c, 128:192], start=False, stop=True)
                self.ts("pool", ZTs, ZT, WL[:, c:c + 1], None, ALU.mult)
                self.stt(ZT, pz[:, 0:128], WL[:, c:c + 1], ZTs, ALU.mult, ALU.add)
                self.cp("act", ZTb, ZT)
                self.cp("act", Ofm[:, c * 64:(c + 1) * 64], po[:, 0:64])
            return [f1, f2, f3]

        def bulk_ops(wave):
            ops_ = []
            for w, c in enumerate(wave):
                ops_.append((lambda c=c, w=w: st0(c, w)))
            for j in range(1, 6):
                for w, c in enumerate(wave):
                    ops_.append((lambda c=c, w=w, j=j: sq(c, w, j)))
                for w, c in enumerate(wave):
                    ops_.append((lambda c=c, w=w, j=j: acc(c, w, j)))
            return ops_

        waves = [list(range(c0_, min(NCH, c0_ + GW))) for c0_ in range(0, NCH, GW)]
        samp_all = []
        for i in range(NS):
            samp_all += samp_subs(i)
        for f in bulk_ops(waves[0]):
            f()
        sp_pos = 0
        for wi, wave in enumerate(waves):
            Al = []
            for c in wave:
                Al += chain_subs(c)
            Bl = bulk_ops(waves[wi + 1]) if wi + 1 < len(waves) else []
            nC = (len(samp_all) - sp_pos + (len(waves) - wi) - 1) // (len(waves) - wi)
            Cl = samp_all[sp_pos:sp_pos + nC]
            sp_pos += nC
            nA = len(Al)
            bi = ci = 0
            for k, f in enumerate(Al):
                f()
                tb = (len(Bl) * (k + 1) + nA - 1) // nA
                while bi < min(tb, len(Bl)):
                    Bl[bi]()
                    bi += 1
                tc = (len(Cl) * (k + 1) + nA - 1) // nA
                while ci < min(tc, len(Cl)):
                    Cl[ci]()
                    ci += 1
        self.chk("rw3")
        ps = P.bank()
        self.tr(ps[:, 0:128], ZT, C["ident_f"])
        So = A.f32(128)
        self.cp("act", So, ps[:, 0:128])
        for hh in range(2):
            S.dma(O["p_wkv"][l, 2 * hp + hh], So[H[hh], hh * 64:hh * 64 + 64], eng="act")
        self.cp("act", Ofm[:, T:NT], pO[:, 0:NS])
        P.unreserve(rb)
        self.chk("rw4")
        if hp + 1 < 8:
            cn = (hp + 1) * 128
            for q, tt_ in enumerate(self._rw_tiles):
                S.dma(tt_, self.projT[q * BW + cn:q * BW + cn + 128, :])
            self._rw_pref = (l, hp + 1)
        cen = A.f32(NT)
        rs = A.f32(NT)
        sb2 = A.bf(NT)
        obf = A.bf(NT)
        self.cp("pool", sb2, Ofm)
        for (t0, tn) in tl:
            ps = P.bank()
            self.mm(ps[:, 0:tn], C["bo_b"], sb2[:, t0:t0 + tn])
            self.stt(cen[:, t0:t0 + tn], ps[:, 0:tn], -1.0 / 64, Ofm[:, t0:t0 + tn], ALU.mult, ALU.add)
        self.act(sb2, cen, AF.Square)
        for (t0, tn) in tl:
            ps = P.bank()
            self.mm(ps[:, 0:tn], C["bo_b"], sb2[:, t0:t0 + tn])
            self.rsqrt(rs[:, t0:t0 + tn], ps[:, 0:tn], scale=1.0 / 64, bias=C["gneps"])
        self.tt("dve", cen, cen, rs, ALU.mult)
        self.ts("dve", cen, cen, self.V("ln_g", hp), self.V("ln_b", hp), ALU.mult, ALU.add)
        self.tt("pool", cen, cen, bonus, ALU.add)
        self.tt("dve", obf, cen, g, ALU.mult)
        S.dma(self.obT[c0:c0 + 128, :], obf, eng="sp")
        self.chk("rw5")

    def mlstm(self, l):
        A, S, C, I, O, P = self.A, self.S, self.C, self.I, self.O, self.P
        T, NS, NT = self.T, self.NS, self.NT
        NB = T // 128
        tl = ttiles(NT)
        A.release(self.abase)
        R4 = lambda n: A.f32(n, parts=4)
        Ac, wrow, emrow = [R4(NT) for _ in range(3)]
        dec, acs = R4(NB), R4(NS)
        acol = A.f32(NB * 4).rearrange("p (b h) -> p b h", h=4)
        ecol = A.f32(NB * 4).rearrange("p (b h) -> p b h", h=4)
        wscol = A.f32(4, parts=NS)
        mkeep = A.mark()
        ig, lf, G, aa, mt, erow = [R4(NT) for _ in range(6)]
        ones4 = R4(T)
        Ast, Aend = R4(NB), R4(NB)
        mold, wss = R4(NS), R4(NS)
        S.dma(ig, self.projT[MB + 3072:MB + 3076, :])
        S.dma(lf, self.projT[MB + 3076:MB + 3080, :])
        S.dma(mold, I["s_m"][l])
        self.memset("dve", ones4, 1.0)
        self.ts("dve", ig, ig, C["rows"][:, 0:1], None, ALU.add)
        self.act(lf, lf, AF.Sigmoid, bias=C["rows"][:, 1:2])
        self.act(lf, lf, AF.Ln)
        self.scan(G[:, 0:T], ones4, lf[:, 0:T], 0.0, ALU.mult, ALU.add)
        self.tt("dve", aa[:, 0:T], ig[:, 0:T], G[:, 0:T], ALU.subtract)
        self.scan(Ac[:, 0:T], ones4, aa[:, 0:T], 0.0, ALU.mult, ALU.max)
        self.tt("dve", mt[:, 0:T], G[:, 0:T], Ac[:, 0:T], ALU.add)
        v3 = lambda ap: ap[:, 0:T].rearrange("p (c j) -> p c j", j=128)
        self.cp("dve", Aend, v3(Ac)[:, :, 127])
        self.memset("dve", Ast[:, 0:1], 0.0)
        if NB > 1:
            self.cp("dve", Ast[:, 1:NB], Aend[:, 0:NB - 1])
        self.tt("dve", v3(wrow), Ast.unsqueeze(2).to_broadcast([4, NB, 128]), v3(Ac), ALU.subtract)
        self.act(wrow[:, 0:T], wrow[:, 0:T], AF.Exp)
        self.tt("dve", v3(erow), v3(aa), Aend.unsqueeze(2).to_broadcast([4, NB, 128]), ALU.subtract)
        self.act(erow[:, 0:T], erow[:, 0:T], AF.Exp)
        self.tt("dve", dec, Ast, Aend, ALU.subtract)
        self.act(dec, dec, AF.Exp)
        self.tt("dve", mold, lf[:, T:NT], mold, ALU.add)
        self.tt("dve", mt[:, T:NT], mold, ig[:, T:NT], ALU.max)
        self.tt("dve", acs, mold, mt[:, T:NT], ALU.subtract)
        self.act(acs, acs, AF.Exp)
        self.tt("dve", wss, ig[:, T:NT], mt[:, T:NT], ALU.subtract)
        self.act(wss, wss, AF.Exp)
        self.act(emrow, mt, AF.Exp, scale=-1.0)
        self.dmas(O["o_m"][l, :, 0:1], mt[:, T - 1:T], eng="act")
        S.dma(O["o_m"][l, :, 1:1 + NS], mt[:, T:NT], eng="act")
        ps = P.bank()
        i4 = C["ident_f"][0:4, 0:4]
        for bb in range(NB):
            self.mm(ps[:, bb * 4:bb * 4 + 4], aa[:, bb * 128:(bb + 1) * 128], i4)
            self.mm(ps[:, 256 + bb * 4:256 + bb * 4 + 4], erow[:, bb * 128:(bb + 1) * 128], i4)
        self.mm(ps[0:NS, 500:504], wss, i4)
        self.cp("act", acol, ps[:, 0:NB * 4].rearrange("p (b h) -> p b h", h=4))
        self.cp("act", ecol, ps[:, 256:256 + NB * 4].rearrange("p (b h) -> p b h", h=4))
        self.cp("act", wscol, ps[0:NS, 500:504])
        A.release(mkeep)
        self._ml_base = A.mark()
        for h in range(4):
            self.mlstm_head(l, h, dict(Ac=Ac, wrow=wrow, emrow=emrow, dec=dec, acs=acs, acol=acol, ecol=ecol,
                                       wscol=wscol))

    def bcast_rows(self, dst, rows4, h, ncols):
        c = 0
        while c < ncols:
            n = min(512, ncols - c)
            ps = self.P.bank()
            self.mm(ps[:, 0:n], self.C["sel"][:, h, :], rows4[:, c:c + n])
            self.cp("act", dst[:, c:c + n], ps[:, 0:n])
            c += n

    def conv_load(self, l, ch0, raw, cs):
        S, I = self.S, self.I
        S.dma(raw, self.projT[MB + ch0:MB + ch0 + 128, :])
        S.dma(cs, I["s_conv"][l, :, ch0:ch0 + 128, :].rearrange("g p i -> p g i"))

    def conv_silu(self, l, ch0, cidx, raw, acc, cs):
        S, I = self.S, self.I
        T, NS, NT = self.T, self.NS, self.NT
        w = lambda j: self.V("conv_w", j * 16 + cidx)
        self.ts("dve", acc, raw, w(3), self.V("conv_b", cidx), ALU.mult, ALU.add)
        for lag in (1, 2, 3):
            self.stt(acc[:, lag:T], raw[:, 0:T - lag], w(3 - lag), acc[:, lag:T], ALU.mult, ALU.add)
            self.stt(acc[:, T:NT], cs[:, 3 - lag, :], w(3 - lag), acc[:, T:NT], ALU.mult, ALU.add)

    def mlstm_head(self, l, h, R):
        A, S, C, I, O, P = self.A, self.S, self.C, self.I, self.O, self.P
        T, NS, NT = self.T, self.NS, self.NT
        NB = T // 128
        tl = ttiles(NT)
        A.release(self._ml_base)
        GBA, GBw = A.f32(T), A.f32(T)
        GBem = A.f32(NT)
        GBd = A.f32(NB)
        GBac = A.f32(NS)
        self.bcast_rows(GBA, R["Ac"], h, T)
        self.bcast_rows(GBw, R["wrow"], h, T)
        self.bcast_rows(GBem, R["emrow"], h, NT)
        self.bcast_rows(GBd, R["dec"], h, NB)
        self.bcast_rows(GBac, R["acs"], h, NS)
        Qb = [A.bf(NT) for _ in range(2)]
        Kb = [A.bf(NT) for _ in range(2)]
        Qw = [A.bf(T) for _ in range(2)]
        qs = A.f32(2 * NS).rearrange("p (k i) -> p k i", k=2)
        ks = A.f32(2 * NS).rearrange("p (k i) -> p k i", k=2)
        vs = A.f32(2 * NS).rearrange("p (k i) -> p k i", k=2)
        Vtm = A.bf(NB * 257).rearrange("p (b n) -> p b n", b=NB)
        Ktm = A.bf(NB * 256).rearrange("p (b n) -> p b n", b=NB)
        Hh = [A.f32(NT) for _ in range(2)]
        m1 = A.mark()
        acc = A.f32(NT)
        raws = [A.f32(NT) for _ in range(6)]
        css = [A.f32(3 * NS).rearrange("p (g i) -> p g i", g=3) for _ in range(4)]
        vb = A.bf(T)
        for kc in range(2):
            self.conv_load(l, h * 256 + kc * 128, raws[2 * kc], css[2 * kc])
            self.conv_load(l, BW + h * 256 + kc * 128, raws[2 * kc + 1], css[2 * kc + 1])
        for vc in range(2):
            S.dma(raws[4 + vc], self.projT[MB + 2048 + h * 256 + vc * 128:MB + 2048 + h * 256 + (vc + 1) * 128, :])
        for kc in range(2):
            raw, cs = raws[2 * kc], css[2 * kc]
            self.conv_silu(l, h * 256 + kc * 128, h * 2 + kc, raw, acc, cs)
            self.act(acc, acc, AF.Silu)
            self.cp("dve", Qb[kc], acc)
            self.cp("pool", qs[:, kc, :], acc[:, T:NT])
            self.tt("dve", Qw[kc], acc[:, 0:T], GBw, ALU.mult)
            raw, cs = raws[2 * kc + 1], css[2 * kc + 1]
            self.conv_silu(l, BW + h * 256 + kc * 128, 8 + h * 2 + kc, raw, acc, cs)
            self.act(acc, acc, AF.Silu)
            self.ts("dve", Kb[kc], acc, 0.0625, None, ALU.mult)
            self.ts("pool", ks[:, kc, :], acc[:, T:NT], 0.0625, None, ALU.mult)
            for bb in range(NB):
                ps = P.bank()
                self.mm(ps[:, 0:128], Kb[kc][:, bb * 128:(bb + 1) * 128], C["ident_b"])
                self.ts("dve", Ktm[:, bb, kc * 128:(kc + 1) * 128], ps[:, 0:128], R["ecol"][:, bb, h:h + 1], None, ALU.mult)
        self.memset("pool", Vtm[:, :, 256:257], 1.0)
        for vc in range(2):
            raw = raws[4 + vc]
            self.cp("dve", vb, raw[:, 0:T])
            self.cp("pool", vs[:, vc, :], raw[:, T:NT])
            for bb in range(NB):
                ps = P.bank()
                self.mm(ps[:, 0:128], vb[:, bb * 128:(bb + 1) * 128], C["ident_b"])
                self.cp("act", Vtm[:, bb, vc * 128:(vc + 1) * 128], ps[:, 0:128])
        A.release(m1)
        CM = A.f32(2 * 257).rearrange("p (k n) -> p k n", k=2)
        Cb = A.bf(2 * 256).rearrange("p (k n) -> p k n", k=2)
        nbc = A.bf(2 * 128).rearrange("p (k n) -> p k n", k=2)
        tmp = A.f32(128)
        ET = A.f32(128)
        PT = A.bf(128)
        absd = A.f32(128)
        self.memset("dve", CM, 0.0)
        self.memset("dve", Cb, 0.0)
        self.memset("dve", nbc, 0.0)
        for bb in range(NB):
            blk = slice(bb * 128, (bb + 1) * 128)
            ps = P.bank()
            self.mm(ps[:, 0:128], Kb[0][:, blk], Qb[0][:, blk], start=True, stop=False)
            self.mm(ps[:, 0:128], Kb[1][:, blk], Qb[1][:, blk], start=False, stop=True)
            self.stt(tmp, GBA[:, blk], -1.0, C["negm"], ALU.mult, ALU.add)
            self.act(ET, tmp, AF.Exp, bias=R["acol"][:, bb, h:h + 1])
            self.tt("dve", PT, ps[:, 0:128], ET, ALU.mult)
            pn = P.bank()
            for vc in range(2):
                o = pn[:, vc * 128:(vc + 1) * 128]
                self.mm(o, Vtm[:, bb, vc * 128:(vc + 1) * 128], PT, start=True, stop=False)
                self.mm(o, Cb[:, 0, vc * 128:(vc + 1) * 128], Qw[0][:, blk], start=False, stop=False)
                self.mm(o, Cb[:, 1, vc * 128:(vc + 1) * 128], Qw[1][:, blk], start=False, stop=True)
            pd = P.bank()
            self.mm(pd[:, 0:128], C["ones_b"], PT, start=True, stop=False)
            self.mm(pd[:, 0:128], nbc[:, 0, :], Qw[0][:, blk], start=False, stop=False)
            self.mm(pd[:, 0:128], nbc[:, 1, :], Qw[1][:, blk], start=False, stop=True)
            self.act(absd, pd[:, 0:128], AF.Abs)
            self.tt("dve", absd, absd, GBem[:, blk], ALU.max)
            self.recip(absd, absd)
            for vc in range(2):
                self.tt("dve", Hh[vc][:, blk], pn[:, vc * 128:(vc + 1) * 128], absd, ALU.mult)
            for kc in range(2):
                pc = P.bank()
                self.mm(pc[:, 0:257], Ktm[:, bb, kc * 128:(kc + 1) * 128], Vtm[:, bb, :])
                self.stt(CM[:, kc, :], CM[:, kc, :], GBd[:, bb:bb + 1], pc[:, 0:257], ALU.mult, ALU.add)
                self.cp("act", Cb[:, kc, :], CM[:, kc, 0:256])
                self.cp("pool", nbc[:, kc, :], CM[:, kc, 256:257].to_broadcast([128, 128]))
        for kc in range(2):
            S.dma(O["p_C"][l, h, kc * 128:(kc + 1) * 128, :], CM[:, kc, 0:256], eng="sp")
            self.dmas(O["p_n"][l, h, kc * 128:(kc + 1) * 128].unsqueeze(1), CM[:, kc, 256:257], eng="sp")
        m2 = A.mark()
        ktm = A.f32(256, parts=NS)
        vau = A.f32(257, parts=NS)
        ps = P.bank()
        for kc in range(2):
            self.tr(ps[0:NS, kc * 128:(kc + 1) * 128], ks[:, kc, :], C["ident_f"])
            self.tr(ps[0:NS, 256 + kc * 128:256 + (kc + 1) * 128], vs[:, kc, :], C["ident_f"])
        self.ts("dve", ktm, ps[0:NS, 0:256], R["wscol"][:, h:h + 1], None, ALU.mult)
        self.cp("act", vau[:, 0:256], ps[0:NS, 256:512])
        self.memset("dve", vau[:, 256:257], 1.0)
        Kex = A.f32(NS * 256, parts=NS).rearrange("p (i c) -> p i c", i=NS)
        eye = C["ident_f"][0:NS, 0:NS].unsqueeze(2).to_broadcast([NS, NS, 256])
        self.tt("dve", Kex, ktm.unsqueeze(1).to_broadcast([NS, NS, 256]), eye, ALU.mult)
        CS = [A.f32(2 * 257).rearrange("p (k n) -> p k n", k=2) for _ in range(4)]
        nb2 = [A.f32(2 * 128).rearrange("p (k n) -> p k n", k=2) for _ in range(4)]
        rb, pN = P.reserve()
        for i in range(NS):
            k2 = i % 4
            cs_ = CS[k2]
            for kc in range(2):
                S.dma(cs_[:, kc, 0:256], I["s_C"][l, i, h, kc * 128:(kc + 1) * 128, :])
                self.dmas(cs_[:, kc, 256:257], I["s_n"][l, i, h, kc * 128:(kc + 1) * 128].unsqueeze(1))
            for kc in range(2):
                pc = P.bank()
                self.mm(pc[:, 0:257], Kex[:, i, kc * 128:(kc + 1) * 128], vau)
                self.stt(cs_[:, kc, :], cs_[:, kc, :], GBac[:, i:i + 1], pc[:, 0:257], ALU.mult, ALU.add)
                S.dma(O["s_C"][l, i, h, kc * 128:(kc + 1) * 128, :], cs_[:, kc, 0:256], eng="sp")
                self.dmas(O["s_n"][l, i, h, kc * 128:(kc + 1) * 128].unsqueeze(1), cs_[:, kc, 256:257], eng="sp")
                self.cp("pool", nb2[k2][:, kc, :], cs_[:, kc, 256:257].to_broadcast([128, 128]))
            for vc in range(2):
                o = pN[:, vc * NS + i:vc * NS + i + 1]
                self.mm(o, cs_[:, 0, vc * 128:(vc + 1) * 128], qs[:, 0, i:i + 1], start=True, stop=False)
                self.mm(o, cs_[:, 1, vc * 128:(vc + 1) * 128], qs[:, 1, i:i + 1], start=False, stop=True)
            o = pN[:, 2 * NS + i:2 * NS + i + 1]
            self.mm(o, nb2[k2][:, 0, :], qs[:, 0, i:i + 1], start=True, stop=False)
            self.mm(o, nb2[k2][:, 1, :], qs[:, 1, i:i + 1], start=False, stop=True)
        ad = A.f32(NS)
        self.act(ad, pN[:, 2 * NS:3 * NS], AF.Abs)
        self.tt("dve", ad, ad, GBem[:, T:NT], ALU.max)
        self.recip(ad, ad)
        for vc in range(2):
            self.tt("dve", Hh[vc][:, T:NT], pN[:, vc * NS:(vc + 1) * NS], ad, ALU.mult)
        P.unreserve(rb)
        A.release(m2)
        self.head_norm_out(Hh, MB + 3080 + h * 256, "m_ng", h * 2, AF.Sigmoid, BW + h * 256)

    def head_norm_out(self, Hh, gate_row0, gname, gidx0, gfunc, ob_row0):
        A, S, C, P = self.A, self.S, self.C, self.P
        NT = self.NT
        tl = ttiles(NT)
        m = A.mark()
        sq = [A.bf(NT) for _ in range(2)]
        rs = A.f32(NT)
        graw = A.f32(NT)
        obf = A.bf(NT)
        for vc in range(2):
            self.act(sq[vc], Hh[vc], AF.Square)
        for (t0, tn) in tl:
            ps = P.bank()
            self.mm(ps[:, 0:tn], C["ones_b"], sq[0][:, t0:t0 + tn], start=True, stop=False)
            self.mm(ps[:, 0:tn], C["ones_b"], sq[1][:, t0:t0 + tn], start=False, stop=True)
            self.rsqrt(rs[:, t0:t0 + tn], ps[:, 0:tn], scale=1.0 / 256, bias=C["eps"])
        for vc in range(2):
            S.dma(graw, self.projT[gate_row0 + vc * 128:gate_row0 + (vc + 1) * 128, :])
            self.act(graw, graw, gfunc)
            self.tt("dve", Hh[vc], Hh[vc], rs, ALU.mult)
            self.stt(obf, Hh[vc], self.V(gname, gidx0 + vc), graw, ALU.mult, ALU.mult)
            S.dma(self.obT[ob_row0 + vc * 128:ob_row0 + (vc + 1) * 128, :], obf, eng="sp")
        A.release(m)

    def gla(self, l):
        A, S, C, I, O, P = self.A, self.S, self.C, self.I, self.O, self.P
        T, NS, NT = self.T, self.NS, self.NT
        NB = T // 128
        tl = ttiles(NT)
        A.release(self.abase)
        a2b = A.bf(512, parts=16)
        xab = A.bf(NT, parts=16)
        m1 = A.mark()
        a2s = A.f32(512, parts=16)
        xas = A.f32(NT, parts=16)
        S.dma(a2s, I["ga2"][l])
        S.dma(xas, self.projT[GB_ + 2048:GB_ + 2064, :])
        self.cp("dve", a2b, a2s)
        self.cp("dve", xab, xas)
        A.release(m1)
        base = A.mark()
        v3 = lambda ap: ap[:, 0:T].rearrange("p (c j) -> p c j", j=128)
        for h in range(4):
            A.release(base)
            lg, bc, e1 = A.f32(NT), A.f32(NT), A.f32(NT)
            graws = [A.f32(NT) for _ in range(4)]
            S.dma(graws[0], self.projT[GB_ + h * 128:GB_ + (h + 1) * 128, :])
            S.dma(graws[1], self.projT[GB_ + 512 + h * 128:GB_ + 512 + (h + 1) * 128, :])
            for vc in range(2):
                S.dma(graws[2 + vc],
                      self.projT[GB_ + 1024 + h * 256 + vc * 128:GB_ + 1024 + h * 256 + (vc + 1) * 128, :])
            raw = graws[0]
            Qh, Kh = A.bf(T), A.bf(T)
            WLg = A.f32(NB)
            qs, ksm, dgs = A.f32(NS), A.f32(NS), A.f32(NS)
            vs = A.f32(2 * NS).rearrange("p (k i) -> p k i", k=2)
            Vtm = A.bf(NB * 256).rearrange("p (b n) -> p b n", b=NB)
            Ktm = A.bf(NB * 128).rearrange("p (b n) -> p b n", b=NB)
            OG = [A.f32(NT) for _ in range(2)]
            vb = A.bf(T)
            for (t0, tn) in tl:
                ps = P.bank()
                self.mm(ps[:, 0:tn], a2b[:, h * 128:(h + 1) * 128], xab[:, t0:t0 + tn])
                self.act(lg[:, t0:t0 + tn], ps[:, 0:tn], AF.Sigmoid, bias=self.V("ga_b", h))
            self.act(lg, lg, AF.Ln)
            self.ts("pool", lg, lg, 1.0 / 16, None, ALU.mult)
            self.scan(bc[:, 0:T], C["rst128"], lg[:, 0:T], 0.0, ALU.mult, ALU.add)
            self.act(WLg, v3(bc)[:, :, 127], AF.Exp)
            self.act(dgs, lg[:, T:NT], AF.Exp)
            self.act(e1[:, 0:T], bc[:, 0:T], AF.Exp)
            self.stt(Qh, raw[:, 0:T], 128.0 ** -0.5, e1[:, 0:T], ALU.mult, ALU.mult)
            self.ts("pool", qs, raw[:, T:NT], 128.0 ** -0.5, None, ALU.mult)
            raw = graws[1]
            self.act(e1[:, 0:T], bc[:, 0:T], AF.Exp, scale=-1.0)
            self.tt("dve", Kh, raw[:, 0:T], e1[:, 0:T], ALU.mult)
            self.cp("pool", ksm, raw[:, T:NT])
            for bb in range(NB):
                ps = P.bank()
                self.mm(ps[:, 0:128], Kh[:, bb * 128:(bb + 1) * 128], C["ident_b"])
                self.cp("act", Ktm[:, bb, :], ps[:, 0:128])
            for vc in range(2):
                raw = graws[2 + vc]
                self.cp("dve", vb, raw[:, 0:T])
                self.cp("pool", vs[:, vc, :], raw[:, T:NT])
                for bb in range(NB):
                    ps = P.bank()
                    self.mm(ps[:, 0:128], vb[:, bb * 128:(bb + 1) * 128], C["ident_b"])
                    self.cp("act", Vtm[:, bb, vc * 128:(vc + 1) * 128], ps[:, 0:128])
            SM, SMs = A.f32(256), A.f32(256)
            Sb = A.bf(256)
            PTg = A.bf(128)
            self.memset("dve", SM, 0.0)
            self.memset("dve", Sb, 0.0)
            for bb in range(NB):
                blk = slice(bb * 128, (bb + 1) * 128)
                ps = P.bank()
                self.mm(ps[:, 0:128], Kh[:, blk], Qh[:, blk])
                self.tt("dve", PTg, ps[:, 0:128], C["m_le_b"], ALU.mult)
                po = P.bank()
                for vc in range(2):
                    o = po[:, vc * 128:(vc + 1) * 128]
                    self.mm(o, Vtm[:, bb, vc * 128:(vc + 1) * 128], PTg, start=True, stop=False)
                    self.mm(o, Sb[:, vc * 128:(vc + 1) * 128], Qh[:, blk], start=False, stop=True)
                    self.cp("act", OG[vc][:, blk], o)
                pz = P.bank()
                self.mm(pz[:, 0:256], Ktm[:, bb, :], Vtm[:, bb, :])
                self.ts("pool", SMs, SM, WLg[:, bb:bb + 1], None, ALU.mult)
                self.stt(SM, pz[:, 0:256], WLg[:, bb:bb + 1], SMs, ALU.mult, ALU.add)
                self.cp("act", Sb, SM)
            S.dma(O["p_gla"][l, h], SM, eng="sp")
            ktm = A.f32(128, parts=NS)
            vtm = A.f32(256, parts=NS)
            ps = P.bank()
            self.tr(ps[0:NS, 0:128], ksm, C["ident_f"])
            for vc in range(2):
                self.tr(ps[0:NS, 128 + vc * 128:128 + (vc + 1) * 128], vs[:, vc, :], C["ident_f"])
            self.cp("act", ktm, ps[0:NS, 0:128])
            self.cp("act", vtm, ps[0:NS, 128:384])
            Kex = A.f32(NS * 128, parts=NS).rearrange("p (i c) -> p i c", i=NS)
            eye = C["ident_f"][0:NS, 0:NS].unsqueeze(2).to_broadcast([NS, NS, 128])
            self.tt("dve", Kex, ktm.unsqueeze(1).to_broadcast([NS, NS, 128]), eye, ALU.mult)
            SS = [A.f32(256) for _ in range(4)]
            rb, pO = P.reserve()
            for i in range(NS):
                ss = SS[i % 4]
                S.dma(ss, I["s_gla"][l, i, h])
                pz = P.bank()
                self.mm(pz[:, 0:256], Kex[:, i, :], vtm)
                self.stt(ss, ss, dgs[:, i:i + 1], pz[:, 0:256], ALU.mult, ALU.add)
                S.dma(O["s_gla"][l, i, h], ss, eng="sp")
                for vc in range(2):
                    self.mm(pO[:, vc * NS + i:vc * NS + i + 1], ss[:, vc * 128:(vc + 1) * 128], qs[:, i:i + 1])
            for vc in range(2):
                self.cp("act", OG[vc][:, T:NT], pO[:, vc * NS:(vc + 1) * NS])
            P.unreserve(rb)
            self.head_norm_out(OG, GB_ + 2064 + h * 256, "g_ng", h * 2, AF.Silu, 2 * BW + h * 256)

    def merge(self, l):
        A, S, C, I, P = self.A, self.S, self.C, self.I, self.P
        NT = self.NT
        tl = ttiles(NT)
        A.release(self.abase)
        OB = A.bf(24 * NT).rearrange("p (c t) -> p c t", c=24)
        S.dma(OB, self.obT.rearrange("(c p) t -> p c t", p=128))
        wst = [A.f32(8 * 128).rearrange("p (k n) -> p k n", k=8) for _ in range(2)]
        wbf = [A.bf(8 * 128).rearrange("p (k n) -> p k n", k=8) for _ in range(2)]
        graw = [A.f32(NT) for _ in range(2)]
        accs = [A.f32(NT) for _ in range(2)]
        tmp = A.f32(512)
        mgb = [A.bf(NT) for _ in range(2)]
        groups = [(dc, j) for dc in range(16) for j in range(3)]

        def load(gi):
            dc, j = groups[gi]
            ws, wb, gr = wst[gi % 2], wbf[gi % 2], graw[gi % 2]
            S.dma(ws, I["w_branch"][l, j, :, dc * 128:(dc + 1) * 128].rearrange("(k p) n -> p k n", p=128))
            self.cp("pool", wb, ws)
            S.dma(gr, self.projT[GTB + j * D + dc * 128:GTB + j * D + (dc + 1) * 128, :])
            self.act(gr, gr, AF.Sigmoid, bias=self.V("gate_b", j * 16 + dc))

        load(0)
        for gi, (dc, j) in enumerate(groups):
            acc = accs[dc % 2]
            wb, gr = wbf[gi % 2], graw[gi % 2]
            banks = P.banks(len(tl))
            for k in range(8):
                for bi, (t0, tn) in enumerate(tl):
                    self.mm(banks[bi][:, 0:tn], wb[:, k, :], OB[:, j * 8 + k, t0:t0 + tn],
                            start=(k == 0), stop=(k == 7))
            if gi + 1 < len(groups):
                load(gi + 1)
            for bi, (t0, tn) in enumerate(tl):
                if j == 0:
                    self.tt("dve", acc[:, t0:t0 + tn], banks[bi][:, 0:tn], gr[:, t0:t0 + tn], ALU.mult)
                else:
                    self.tt("dve", tmp[:, 0:tn], banks[bi][:, 0:tn], gr[:, t0:t0 + tn], ALU.mult)
                    self.tt("pool", acc[:, t0:t0 + tn], acc[:, t0:t0 + tn], tmp[:, 0:tn], ALU.add)
            if j == 2:
                self.cp("act", mgb[dc % 2], acc)
                S.dma(self.mgT[dc * 128:(dc + 1) * 128, :], mgb[dc % 2], eng="act")

    def resid_pre(self, it, m):
        A, S = self.A, self.S
        A.release(self._dense_m0)
        stg = [A.f32(self.NT) for _ in range(2)]
        S.dma(stg[it % 2], self._res_src[m * 128:(m + 1) * 128, :])

    def resid_sink(self, it, m, rows, banks, tl):
        A, S = self.A, self.S
        A.release(self._dense_m0)
        stg = [A.f32(self.NT) for _ in range(2)]
        o = stg[it % 2]
        for bi, (t0, tn) in enumerate(tl):
            self.tt("dve", o[:, t0:t0 + tn], o[:, t0:t0 + tn], banks[bi][:, 0:tn], ALU.add)
        S.dma(self.xT[m * 128:(m + 1) * 128, :], o, eng="sp")

    def outproj(self, l):
        A, S, I = self.A, self.S, self.I
        NT = self.NT
        A.release(self.abase)
        MG = A.bf(16 * NT).rearrange("p (c t) -> p c t", c=16)
        S.dma(MG, self.mgT.rearrange("(c p) t -> p c t", p=128))
        self._res_src = self.src
        self.dense(I["w_out"][l], D, 16, MG, self.resid_sink, presink=self.resid_pre)

    def ffn(self, l):
        A, S, I = self.A, self.S, self.I
        NT = self.NT
        NF = DFF // 128
        order = []
        for fc in range(NF):
            order += [fc, NF + fc]
        self.dense(I["w_gu"][l], 2 * DFF, 16, self.hT, self.gu_sink, order=order)
        self._res_src = self.xT
        for half in range(2):
            A.release(self.abase)
            AH = A.bf((NF // 2) * NT).rearrange("p (c t) -> p c t", c=NF // 2)
            r0 = half * (DFF // 2)
            S.dma(AH, self.actT[r0:r0 + DFF // 2, :].rearrange("(c p) t -> p c t", p=128))
            self.dense(I["w_down"][l, r0:r0 + DFF // 2, :], D, NF // 2, AH, self.resid_sink, presink=self.resid_pre)

    def gu_sink(self, it, m, rows, banks, tl):
        A, S = self.A, self.S
        A.release(self._dense_m0)
        sil = A.f32(self.NT)
        ab = [A.bf(self.NT) for _ in range(2)]
        if it % 2 == 0:
            for bi, (t0, tn) in enumerate(tl):
                self.act(sil[:, t0:t0 + tn], banks[bi][:, 0:tn], AF.Silu)
        else:
            o = ab[(it // 2) % 2]
            fc = m - DFF // 128
            for bi, (t0, tn) in enumerate(tl):
                self.tt("dve", o[:, t0:t0 + tn], sil[:, t0:t0 + tn], banks[bi][:, 0:tn], ALU.mult)
            S.dma(self.actT[fc * 128:(fc + 1) * 128, :], o, eng="sp")


def _cols(v, n):
    return np.ascontiguousarray(np.asarray(v).reshape(n, 128).T)


def pack_vec(inp, l):
    out = np.zeros((128, VEC_N), np.float32)

    def put(name, arr):
        off, n = VEC_OFF[name]
        out[:arr.shape[0], off:off + n] = arr

    put("n1g", _cols(inp["norm1_g"][l], 16))
    put("n2g", _cols(inp["norm2_g"][l], 16))
    put("gate_b", _cols(inp["gate_b"][l].reshape(-1), 48))
    mu = np.asarray(inp["rwkv_mu"][l])
    put("mu", _cols(mu[:3072], 24))
    put("mul", np.ascontiguousarray(mu[3072:3264].reshape(3, 64).T))
    for nm, key in [("w0", "rwkv_w0"), ("a0", "rwkv_a0"), ("k_k", "rwkv_k_k"), ("k_a", "rwkv_k_a"),
                    ("ln_g", "rwkv_ln_g"), ("ln_b", "rwkv_ln_b"), ("m_ng", "mlstm_norm_g"), ("g_ng", "gla_norm_g")]:
        put(nm, _cols(inp[key][l], 8))
    put("r_k", _cols(np.asarray(inp["rwkv_r_k"][l]).reshape(-1), 8))
    put("conv_w", _cols(np.asarray(inp["mlstm_conv_w"][l]).reshape(-1), 64))
    put("conv_b", _cols(inp["mlstm_conv_b"][l], 16))
    put("ga_b", _cols(inp["gla_a_b"][l], 4))
    return out


_NC_CACHE = {}


def make_in_map(inp, L, xT, samp):
    f = lambda a: np.ascontiguousarray(np.asarray(a, dtype=np.float32))
    m = {}
    m["xT"] = f(xT)
    m["vec"] = np.stack([pack_vec(inp, l) for l in range(L)])
    m["rows"] = f(np.stack([np.stack([inp["mlstm_i_b"][l], inp["mlstm_f_b"][l]], axis=1) for l in range(L)]))
    m["w_in"] = f(inp["w_in"][:L])
    m["rw2"] = f(np.stack([np.stack([inp["rwkv_w2"][l], inp["rwkv_a2"][l], inp["rwkv_g2"][l]]) for l in range(L)]))
    m["ga2"] = f(inp["gla_a2"][:L])
    m["w_branch"] = f(inp["w_branch"][:L])
    m["w_out"] = f(inp["w_out"][:L])
    m["w_gu"] = f(inp["ffn_w_gu"][:L])
    m["w_down"] = f(inp["ffn_w_down"][:L])
    m["fng"] = _cols(inp["final_norm_g"], 16)
    m["s_shift"] = f(np.transpose(inp["state_rwkv_shift"][:L, samp], (0, 2, 1)))
    m["s_wkv"] = f(inp["state_rwkv_wkv"][:L, samp])
    m["s_conv"] = f(np.transpose(inp["state_mlstm_conv"][:L, samp], (0, 2, 3, 1)))
    m["s_C"] = f(inp["state_mlstm_C"][:L, samp])
    m["s_n"] = f(inp["state_mlstm_n"][:L, samp])
    m["s_m"] = f(np.transpose(inp["state_mlstm_m"][:L, samp], (0, 2, 1)))
    m["s_gla"] = f(inp["state_gla_S"][:L, samp])
    return m


def kernel(**inputs):
    inp = {k: np.asarray(v) for k, v in inputs.items()}
    B, T, _ = inp["x_prompt"].shape
    NSALL = inp["x_sample"].shape[0]
    L = inp["w_in"].shape[0]
    NS = NSALL // NCORES
    key = (T, NS, L)
    if key not in _NC_CACHE:
        _NC_CACHE[key] = Builder(T, NS, L).build()
    nc = _NC_CACHE[key]
    in_maps = []
    for c in range(NCORES):
        seq = c // 2
        samp = slice(c * NS, (c + 1) * NS)
        xT = np.concatenate([inp["x_prompt"][seq].T, inp["x_sample"][samp, 0, :].T], axis=1)
        in_maps.append(make_in_map(inp, L, xT, samp))
    res = run_bass_kernel_spmd(nc, in_maps, core_ids=list(range(NCORES)))
    R = res.results
    P = [R[2 * s] for s in range(B)]
    f32 = np.float32
    y_prompt = np.stack([P[s]["yT"][:, :T].T for s in range(B)]).astype(f32)
    y_sample = np.concatenate([R[c]["yT"][:, T:].T for c in range(NCORES)])[:, None, :].astype(f32)
    p_shift = np.stack([P[s]["o_shift"][:, :, 0] for s in range(B)], axis=1)
    s_shift = np.concatenate([np.transpose(R[c]["o_shift"][:, :, 1:], (0, 2, 1)) for c in range(NCORES)], axis=1)
    p_wkv = np.stack([P[s]["p_wkv"] for s in range(B)], axis=1)
    s_wkv = np.concatenate([R[c]["o_s_wkv"] for c in range(NCORES)], axis=1)
    p_conv = np.stack([np.transpose(P[s]["o_conv"][:, :, :, 0], (0, 1, 2)) for s in range(B)], axis=1)
    s_conv = np.concatenate([np.transpose(R[c]["o_conv"][:, :, :, 1:], (0, 3, 1, 2)) for c in range(NCORES)], axis=1)
    p_C = np.stack([P[s]["p_C"] for s in range(B)], axis=1)
    s_C = np.concatenate([R[c]["o_s_C"] for c in range(NCORES)], axis=1)
    p_n = np.stack([P[s]["p_n"] for s in range(B)], axis=1)
    s_n = np.concatenate([R[c]["o_s_n"] for c in range(NCORES)], axis=1)
    p_m = np.stack([P[s]["o_m"][:, :, 0] for s in range(B)], axis=1)
    s_m = np.concatenate([np.transpose(R[c]["o_m"][:, :, 1:], (0, 2, 1)) for c in range(NCORES)], axis=1)
    p_gla = np.stack([P[s]["p_gla"] for s in range(B)], axis=1)
    s_gla = np.concatenate([R[c]["o_s_gla"] for c in range(NCORES)], axis=1)
    outs = (y_prompt, y_sample, p_shift, p_wkv, p_conv, p_C, p_n, p_m, p_gla,
            s_shift, s_wkv, s_conv, s_C, s_n, s_m, s_gla)
    return tuple(np.ascontiguousarray(o, dtype=f32) for o in outs)
```
